# Optimizing a Trainium2 kernel written in Bass

```python
import math
import jax
import jax.numpy as jnp
from jax import lax
import numpy as np

D_MODEL = 1024
BATCH = 8
SEQ = 2048
DEPTH = 2

GRID_W = 64
CTX_LEN = 256
NORM_EPS = 1e-6

RG_WIDTH = 512
RG_HEADS = 8
RG_HEAD_DIM = RG_WIDTH // RG_HEADS
RG_CONV = 4
RG_C = 8.0

HY_WIDTH = 512
HY_ORDER = 2
HY_SHORT_CONV = 3
HY_EMB_DIM = 33
HY_FILTER_HIDDEN = 64
HY_DECAY_TARGET = 1e-2
HY_FAST_DECAY_PCT = 0.3
HY_SLOW_DECAY_PCT = 1.5

EVEN_IN = 2 * RG_WIDTH + (HY_ORDER + 1) * HY_WIDTH
EVEN_OUT = RG_WIDTH + HY_WIDTH

MLA_HEADS = 8
Q_LORA = 512
KV_LORA = 256
QK_NOPE = 128
QK_ROPE = 64
V_HEAD = 128
ODD_IN = Q_LORA + KV_LORA + QK_ROPE
ROPE_THETA = 10000.0
Q_BLOCK = 128

D_FF = 2816
FFN_CONV = 3

N_EVEN = (DEPTH + 1) // 2
N_ODD = DEPTH // 2

kernel_name = 'hybrid_rglru_hyena_mla_prefix_dit'


def rmsnorm(x, g):
    xf = x.astype(jnp.float32)
    y = xf * lax.rsqrt(jnp.mean(xf * xf, axis=-1, keepdims=True) + NORM_EPS)
    return (y * g.astype(jnp.float32)).astype(x.dtype)


def dwconv(x, w, b):
    k, ch = w.shape
    left = k // 2
    y = lax.conv_general_dilated(x, w[:, None, :].astype(x.dtype), window_strides=(1,),
                                 padding=[(left, k - 1 - left)],
                                 dimension_numbers=('NWC', 'WIO', 'NWC'),
                                 feature_group_count=ch)
    return y + b


def linear_scan(a, b, h0, reverse):
    if reverse:
        a = jnp.flip(a, 1)
        b = jnp.flip(b, 1)

    def combine(e1, e2):
        a1, b1 = e1
        a2, b2 = e2
        return a1 * a2, a2 * b1 + b2

    a_cum, b_cum = lax.associative_scan(combine, (a, b), axis=1)
    h = a_cum * h0[:, None, :] + b_cum
    return jnp.flip(h, 1) if reverse else h


def rglru_coeffs(xc, wa, ba, wx, bx, lam):
    bsz, n, w = xc.shape
    xh = xc.reshape(bsz, n, RG_HEADS, RG_HEAD_DIM)
    r = jax.nn.sigmoid(jnp.einsum('blhi,hij->blhj', xh, wa).reshape(bsz, n, w) + ba)
    i = jax.nn.sigmoid(jnp.einsum('blhi,hij->blhj', xh, wx).reshape(bsz, n, w) + bx)
    log_a = -RG_C * r.astype(jnp.float32) * jax.nn.softplus(-lam.astype(jnp.float32))
    a = jnp.exp(log_a)
    b = jnp.sqrt(-jnp.expm1(2.0 * log_a)) * (i * xc).astype(jnp.float32)
    return a, b


def rglru_bidir(x_lat, x_ctx, conv_w, conv_b, wa, ba, wx, bx, lam, need_ctx):
    x_lat = dwconv(x_lat, conv_w, conv_b)
    x_ctx = dwconv(x_ctx, conv_w, conv_b)
    ys_lat = []
    ys_ctx = []
    for d in range(2):
        rev = d == 1
        a_c, b_c = rglru_coeffs(x_ctx, wa[d], ba[d], wx[d], bx[d], lam[d])
        h_c = linear_scan(a_c, b_c, jnp.zeros_like(b_c[:, 0]), rev)
        h0 = h_c[:, 0] if rev else h_c[:, -1]
        a_l, b_l = rglru_coeffs(x_lat, wa[d], ba[d], wx[d], bx[d], lam[d])
        ys_lat.append(linear_scan(a_l, b_l, h0, rev))
        ys_ctx.append(h_c)
    y_lat = (ys_lat[0] + ys_lat[1]).astype(x_lat.dtype)
    y_ctx = (ys_ctx[0] + ys_ctx[1]).astype(x_ctx.dtype) if need_ctx else None
    return y_lat, y_ctx


def hyena_filters(n, f_w1, f_b1, f_freq, f_w2, f_b2, f_w3):
    f32 = jnp.float32
    pos = jnp.arange(n, dtype=f32)
    t = jnp.linspace(0.0, 1.0, n, dtype=f32)[:, None]
    bands = (HY_EMB_DIM - 1) // 2
    w = 2.0 * math.pi * pos / n
    f = jnp.linspace(1e-4, bands - 1, bands, dtype=f32)
    ang = w[:, None] * f[None, :]
    z = jnp.concatenate([t, jnp.cos(ang), -jnp.sin(ang)], axis=-1)
    freq = f_freq.astype(f32)
    h = jnp.sin(freq * (z @ f_w1.astype(f32) + f_b1.astype(f32)))
    h = jnp.sin(freq * (h @ f_w2.astype(f32) + f_b2.astype(f32)))
    h = (h @ f_w3.astype(f32)).reshape(n, HY_ORDER, 2, HY_WIDTH)
    max_decay = math.log(HY_DECAY_TARGET) / HY_FAST_DECAY_PCT
    min_decay = math.log(HY_DECAY_TARGET) / HY_SLOW_DECAY_PCT
    deltas = jnp.linspace(min_decay, max_decay, HY_WIDTH, dtype=f32)
    decay = jnp.exp(-t * jnp.abs(deltas)[None, :])
    h = h * decay[:, None, None, :]
    k = jnp.concatenate([h[:, :, 0],
                         jnp.zeros((1, HY_ORDER, HY_WIDTH), f32),
                         jnp.flip(h[1:, :, 1], 0)], axis=0)
    return k / jnp.sum(jnp.abs(k), axis=0, keepdims=True)


def hyena(z, conv_w, conv_b, f_w1, f_b1, f_freq, f_w2, f_b2, f_w3, bias):
    bsz, n, _ = z.shape
    z = dwconv(z, conv_w, conv_b)
    v, x1, x2 = jnp.split(z, 3, axis=-1)
    k_f = jnp.fft.rfft(hyena_filters(n, f_w1, f_b1, f_freq, f_w2, f_b2, f_w3), axis=0)
    u = v
    for o, gate in enumerate((x1, x2)):
        u_f = jnp.fft.rfft(u.astype(jnp.float32), n=2 * n, axis=1)
        y = jnp.fft.irfft(u_f * k_f[None, :, o], n=2 * n, axis=1)[:, :n].astype(z.dtype)
        u = gate * (y + u * bias[o])
    return u


def axial_rope(n):
    f32 = jnp.float32
    rows = n // GRID_W
    row = jnp.repeat(jnp.arange(rows, dtype=f32), GRID_W)
    col = jnp.tile(jnp.arange(GRID_W, dtype=f32), rows)
    n_freq = QK_ROPE // 4
    inv = ROPE_THETA ** (-jnp.arange(n_freq, dtype=f32) / n_freq)
    ang = jnp.concatenate([row[:, None] * inv[None, :], col[:, None] * inv[None, :]], axis=-1)
    return jnp.cos(ang), jnp.sin(ang)


def apply_rope(x, cos, sin):
    xr = x.reshape(x.shape[:-1] + (x.shape[-1] // 2, 2))
    x1 = xr[..., 0]
    x2 = xr[..., 1]
    out = jnp.stack([x1 * cos - x2 * sin, x1 * sin + x2 * cos], axis=-1)
    return out.reshape(x.shape).astype(x.dtype)


def attend(qn, qr, kn, kr, v):
    scale = (QK_NOPE + QK_ROPE) ** -0.5
    s = jnp.einsum('bqhd,bkhd->bhqk', qn, kn) + jnp.einsum('bqhr,bkr->bhqk', qr, kr)
    p = jax.nn.softmax(s.astype(jnp.float32) * scale, axis=-1)
    return jnp.einsum('bhqk,bkhd->bqhd', p.astype(v.dtype), v)


def mla_kv(zkv, kv_g, w_ukv):
    bsz, n, _ = zkv.shape
    ckv = rmsnorm(zkv[..., :KV_LORA], kv_g)
    kv = (ckv @ w_ukv).reshape(bsz, n, MLA_HEADS, QK_NOPE + V_HEAD)
    return kv[..., :QK_NOPE], zkv[..., KV_LORA:], kv[..., QK_NOPE:]


def mla_q(zq, q_g, w_uq):
    bsz, n, _ = zq.shape
    q = (rmsnorm(zq, q_g) @ w_uq).reshape(bsz, n, MLA_HEADS, QK_NOPE + QK_ROPE)
    return q[..., :QK_NOPE], q[..., QK_NOPE:]


def mla_mixer(h_l, h_c, w_in, q_g, kv_g, w_uq, w_ukv, w_o, cos, sin, need_ctx):
    bsz, n, _ = h_l.shape
    z_l = h_l @ w_in
    kn_l, kr_l, v_l = mla_kv(z_l[..., Q_LORA:], kv_g, w_ukv)
    kr_l = apply_rope(kr_l, cos[None], sin[None])
    kn_c, kr_c, v_c = mla_kv(h_c @ w_in[:, Q_LORA:], kv_g, w_ukv)
    kn = jnp.concatenate([kn_l, kn_c], axis=1)
    kr = jnp.concatenate([kr_l, kr_c], axis=1)
    v = jnp.concatenate([v_l, v_c], axis=1)
    qn_l, qr_l = mla_q(z_l[..., :Q_LORA], q_g, w_uq)
    qr_l = apply_rope(qr_l, cos[None, :, None], sin[None, :, None])
    nb = n // Q_BLOCK

    def blocks(t):
        return jnp.swapaxes(t.reshape((bsz, nb, Q_BLOCK) + t.shape[2:]), 0, 1)

    o = lax.map(lambda q: attend(q[0], q[1], kn, kr, v), (blocks(qn_l), blocks(qr_l)))
    out_l = jnp.swapaxes(o, 0, 1).reshape(bsz, n, MLA_HEADS * V_HEAD) @ w_o
    out_c = None
    if need_ctx:
        qn_c, qr_c = mla_q(h_c @ w_in[:, :Q_LORA], q_g, w_uq)
        o_c = attend(qn_c, qr_c, kn_c, kr_c, v_c)
        out_c = o_c.reshape(bsz, o_c.shape[1], MLA_HEADS * V_HEAD) @ w_o
    return out_l, out_c


def even_mixer(h_l, h_c, w_in, rg_conv_w, rg_conv_b, rg_wa, rg_ba, rg_wx, rg_bx, rg_lambda,
               hy_conv_w, hy_conv_b, f_w1, f_b1, f_freq, f_w2, f_b2, f_w3, hy_bias, w_out, need_ctx):
    z_l = h_l @ w_in
    zc_rg = h_c @ w_in[:, :RG_WIDTH]
    y_rg_l, y_rg_c = rglru_bidir(z_l[..., :RG_WIDTH], zc_rg, rg_conv_w, rg_conv_b,
                                 rg_wa, rg_ba, rg_wx, rg_bx, rg_lambda, need_ctx)
    a_l = y_rg_l * jax.nn.gelu(z_l[..., RG_WIDTH:2 * RG_WIDTH])
    b_l = hyena(z_l[..., 2 * RG_WIDTH:], hy_conv_w, hy_conv_b, f_w1, f_b1, f_freq, f_w2, f_b2, f_w3, hy_bias)
    out_l = jnp.concatenate([a_l, b_l], axis=-1) @ w_out
    out_c = None
    if need_ctx:
        zc = h_c @ w_in[:, RG_WIDTH:]
        a_c = y_rg_c * jax.nn.gelu(zc[..., :RG_WIDTH])
        b_c = hyena(zc[..., RG_WIDTH:], hy_conv_w, hy_conv_b, f_w1, f_b1, f_freq, f_w2, f_b2, f_w3, hy_bias)
        out_c = jnp.concatenate([a_c, b_c], axis=-1) @ w_out
    return out_l, out_c


def conv_ffn(h, w_up, conv_w, conv_b, w_down):
    u = dwconv(h @ w_up, conv_w, conv_b)
    g, v = jnp.split(u, 2, axis=-1)
    return (jax.nn.silu(g) * v) @ w_down


def setup_inputs(seed: int = 0) -> dict:
    key = jax.random.key(seed)
    ks = iter(jax.random.split(key, 64))
    f32 = jnp.float32
    D = D_MODEL
    F2 = 2 * D_FF

    def nrm(shape, scale):
        return jax.random.normal(next(ks), shape, f32) * scale

    def gain(shape):
        return 1.0 + nrm(shape, 0.05)

    u = jax.random.uniform(next(ks), (N_EVEN, 2, RG_WIDTH), f32, 0.9, 0.999)
    a = u ** (1.0 / RG_C)
    lam = jnp.log(a) - jnp.log1p(-a)
    return {
        'x': nrm((BATCH, SEQ, D), 1.0),
        'c': nrm((BATCH, D), 1.0),
        'ctx': nrm((BATCH, CTX_LEN, D), 1.0),
        'c_ctx': nrm((D,), 1.0),
        'ada_w': nrm((DEPTH, D, 6 * D), 0.5 * D ** -0.5),
        'ada_b': nrm((DEPTH, 6 * D), 0.02),
        'norm1_g': gain((DEPTH, D)),
        'norm2_g': gain((DEPTH, D)),
        'ev_w_in': nrm((N_EVEN, D, EVEN_IN), D ** -0.5),
        'ev_rg_conv_w': nrm((N_EVEN, RG_CONV, RG_WIDTH), RG_CONV ** -0.5),
        'ev_rg_conv_b': nrm((N_EVEN, RG_WIDTH), 0.02),
        'ev_rg_wa': nrm((N_EVEN, 2, RG_HEADS, RG_HEAD_DIM, RG_HEAD_DIM), RG_HEAD_DIM ** -0.5),
        'ev_rg_ba': nrm((N_EVEN, 2, RG_WIDTH), 0.02),
        'ev_rg_wx': nrm((N_EVEN, 2, RG_HEADS, RG_HEAD_DIM, RG_HEAD_DIM), RG_HEAD_DIM ** -0.5),
        'ev_rg_bx': nrm((N_EVEN, 2, RG_WIDTH), 0.02),
        'ev_rg_lambda': lam,
        'ev_hy_conv_w': nrm((N_EVEN, HY_SHORT_CONV, (HY_ORDER + 1) * HY_WIDTH), HY_SHORT_CONV ** -0.5),
        'ev_hy_conv_b': nrm((N_EVEN, (HY_ORDER + 1) * HY_WIDTH), 0.02),
        'ev_hy_f_w1': nrm((N_EVEN, HY_EMB_DIM, HY_FILTER_HIDDEN), HY_EMB_DIM ** -0.5),
        'ev_hy_f_b1': nrm((N_EVEN, HY_FILTER_HIDDEN), 0.1),
        'ev_hy_f_freq': 1.0 + nrm((N_EVEN, HY_FILTER_HIDDEN), 0.1),
        'ev_hy_f_w2': nrm((N_EVEN, HY_FILTER_HIDDEN, HY_FILTER_HIDDEN), HY_FILTER_HIDDEN ** -0.5),
        'ev_hy_f_b2': nrm((N_EVEN, HY_FILTER_HIDDEN), 0.1),
        'ev_hy_f_w3': nrm((N_EVEN, HY_FILTER_HIDDEN, HY_ORDER * 2 * HY_WIDTH), HY_FILTER_HIDDEN ** -0.5),
        'ev_hy_bias': nrm((N_EVEN, HY_ORDER, HY_WIDTH), 0.5),
        'ev_w_out': nrm((N_EVEN, EVEN_OUT, D), EVEN_OUT ** -0.5),
        'od_w_in': nrm((N_ODD, D, ODD_IN), D ** -0.5),
        'od_q_norm_g': gain((N_ODD, Q_LORA)),
        'od_kv_norm_g': gain((N_ODD, KV_LORA)),
        'od_w_uq': nrm((N_ODD, Q_LORA, MLA_HEADS * (QK_NOPE + QK_ROPE)), Q_LORA ** -0.5),
        'od_w_ukv': nrm((N_ODD, KV_LORA, MLA_HEADS * (QK_NOPE + V_HEAD)), KV_LORA ** -0.5),
        'od_w_o': nrm((N_ODD, MLA_HEADS * V_HEAD, D), (MLA_HEADS * V_HEAD) ** -0.5),
        'ffn_w_up': nrm((DEPTH, D, F2), D ** -0.5),
        'ffn_conv_w': nrm((DEPTH, FFN_CONV, F2), FFN_CONV ** -0.5),
        'ffn_conv_b': nrm((DEPTH, F2), 0.02),
        'ffn_w_down': nrm((DEPTH, D_FF, D), D_FF ** -0.5),
        'final_g': gain((D,)),
    }


def reference(x, c, ctx, c_ctx, ada_w, ada_b, norm1_g, norm2_g,
              ev_w_in, ev_rg_conv_w, ev_rg_conv_b, ev_rg_wa, ev_rg_ba, ev_rg_wx, ev_rg_bx, ev_rg_lambda,
              ev_hy_conv_w, ev_hy_conv_b, ev_hy_f_w1, ev_hy_f_b1, ev_hy_f_freq, ev_hy_f_w2, ev_hy_f_b2,
              ev_hy_f_w3, ev_hy_bias, ev_w_out,
              od_w_in, od_q_norm_g, od_kv_norm_g, od_w_uq, od_w_ukv, od_w_o,
              ffn_w_up, ffn_conv_w, ffn_conv_b, ffn_w_down, final_g):
    n = x.shape[1]
    cos, sin = axial_rope(n)
    s_c = jax.nn.silu(c)
    s_cc = jax.nn.silu(c_ctx)
    xc = ctx
    for layer in range(DEPTH):
        need_ctx = layer < DEPTH - 1
        mod_l = (s_c @ ada_w[layer] + ada_b[layer])[:, None, :]
        mod_c = (s_cc @ ada_w[layer] + ada_b[layer])[None, None, :]
        sh1_l, sc1_l, g1_l, sh2_l, sc2_l, g2_l = jnp.split(mod_l, 6, axis=-1)
        sh1_c, sc1_c, g1_c, sh2_c, sc2_c, g2_c = jnp.split(mod_c, 6, axis=-1)
        h_l = rmsnorm(x, norm1_g[layer]) * (1.0 + sc1_l) + sh1_l
        h_c = rmsnorm(xc, norm1_g[layer]) * (1.0 + sc1_c) + sh1_c
        e = layer // 2
        if layer % 2 == 0:
            out_l, out_c = even_mixer(h_l, h_c, ev_w_in[e], ev_rg_conv_w[e], ev_rg_conv_b[e],
                                      ev_rg_wa[e], ev_rg_ba[e], ev_rg_wx[e], ev_rg_bx[e], ev_rg_lambda[e],
                                      ev_hy_conv_w[e], ev_hy_conv_b[e], ev_hy_f_w1[e], ev_hy_f_b1[e],
                                      ev_hy_f_freq[e], ev_hy_f_w2[e], ev_hy_f_b2[e], ev_hy_f_w3[e],
                                      ev_hy_bias[e], ev_w_out[e], need_ctx)
        else:
            out_l, out_c = mla_mixer(h_l, h_c, od_w_in[e], od_q_norm_g[e], od_kv_norm_g[e],
                                     od_w_uq[e], od_w_ukv[e], od_w_o[e], cos, sin, need_ctx)
        x = x + g1_l * out_l
        h2_l = rmsnorm(x, norm2_g[layer]) * (1.0 + sc2_l) + sh2_l
        x = x + g2_l * conv_ffn(h2_l, ffn_w_up[layer], ffn_conv_w[layer], ffn_conv_b[layer], ffn_w_down[layer])
        if need_ctx:
            xc = xc + g1_c * out_c
            h2_c = rmsnorm(xc, norm2_g[layer]) * (1.0 + sc2_c) + sh2_c
            xc = xc + g2_c * conv_ffn(h2_c, ffn_w_up[layer], ffn_conv_w[layer], ffn_conv_b[layer], ffn_w_down[layer])
    return rmsnorm(x, final_g)
```

```python
import contextlib
import math
import numpy as np
import ml_dtypes
import concourse.bass as bass
import concourse.mybir as mybir
from concourse.bass_utils import run_bass_kernel_spmd

dt = mybir.dt
F32 = dt.float32
BF16 = dt.bfloat16
AF = mybir.ActivationFunctionType
ALU = mybir.AluOpType
AX = mybir.AxisListType

D = 1024
S = 2048
CT = 256
NT = S + CT
EPS = 1e-6
ENG = ['pe', 'dve', 'act', 'pool', 'sp']
SAME_SYNC = True


class Res:
    __slots__ = ('w', 'r')

    def __init__(self):
        self.w = None
        self.r = {}


class Prog:
    def __init__(self, nc, stack, n_dma=40):
        self.nc = nc
        self.ops = {e: [] for e in ENG}
        self.seq = {e: 0 for e in ENG}
        self.known = {e: {} for e in ENG}
        self.esem = {e: stack.enter_context(nc.semaphore('s_' + e)) for e in ENG}
        self.dsem = [stack.enter_context(nc.semaphore('d%d' % i)) for i in range(n_dma)]
        self.dtgt = [0] * n_dma
        self.drr = 0

    def _sem(self, key):
        return self.esem[key[1]] if key[0] == 'e' else self.dsem[key[1]]

    def _collect(self, eng, R, W):
        need = {}

        def add(key, val):
            if val > need.get(key, 0):
                need[key] = val
        for r in R:
            if r.w:
                add(*r.w)
        for w in W:
            if w.w:
                add(*w.w)
            for k, v in w.r.items():
                add(k, v)
        waits = []
        kn = self.known[eng]
        for key, val in need.items():
            if key == ('e', eng) and (eng == 'pe' or not SAME_SYNC):
                continue
            if kn.get(key, 0) >= val:
                continue
            kn[key] = val
            waits.append((self._sem(key), val))
        return waits

    def _mark(self, ev, R, W):
        for r in R:
            if ev[1] > r.r.get(ev[0], 0):
                r.r[ev[0]] = ev[1]
        for w in W:
            w.w = ev
            w.r = {}

    def op(self, eng, fn, R=(), W=()):
        waits = self._collect(eng, R, W)
        self.seq[eng] += 1
        ev = (('e', eng), self.seq[eng])
        es = self.esem[eng]

        def emit(e):
            for s, v in waits:
                e.wait_ge(s, v)
            fn(e).then_inc(es, 1)
        self.ops[eng].append(emit)
        self._mark(ev, R, W)

    def dma(self, q, out, in_, R=(), W=(), **kw):
        waits = self._collect(q, R, W)
        si = self.drr
        self.drr = (self.drr + 1) % len(self.dsem)
        prev = self.dtgt[si]
        key = ('d', si)
        if prev > 0 and self.known[q].get(key, 0) < prev:
            self.known[q][key] = prev
            waits.append((self.dsem[si], prev))
        tgt = prev + 16
        self.dtgt[si] = tgt
        ds = self.dsem[si]

        def emit(e):
            for s, v in waits:
                e.wait_ge(s, v)
            e.dma_start(out=out, in_=in_, **kw).then_inc(ds, 16)
        self.ops[q].append(emit)
        self._mark((key, tgt), R, W)

    def barrier(self):
        for e in ENG:
            waits = []
            kn = self.known[e]
            for f in ENG:
                if f == e and e == 'pe':
                    continue
                v = self.seq[f]
                key = ('e', f)
                if v > kn.get(key, 0):
                    kn[key] = v
                    waits.append((self.esem[f], v))
            for si, t in enumerate(self.dtgt):
                key = ('d', si)
                if t > kn.get(key, 0):
                    kn[key] = t
                    waits.append((self.dsem[si], t))

            def emit(eng, waits=waits):
                for s, v in waits:
                    eng.wait_ge(s, v)
            self.ops[e].append(emit)

    def play(self):
        with self.nc.Block() as blk:
            @blk.tensor
            def _(e):
                for f in self.ops['pe']:
                    f(e)

            @blk.vector
            def _(e):
                for f in self.ops['dve']:
                    f(e)

            @blk.scalar
            def _(e):
                for f in self.ops['act']:
                    f(e)

            @blk.gpsimd
            def _(e):
                for f in self.ops['pool']:
                    f(e)

            @blk.sync
            def _(e):
                for f in self.ops['sp']:
                    f(e)


class Mem:
    def __init__(self, nc, stack, words=49152):
        self.arena = stack.enter_context(nc.sbuf_tensor('arena', [128, words], F32))
        self.psum = stack.enter_context(nc.psum_tensor('psum', [128, 4096], F32))
        self.words = words
        self.ptr = 0
        self.mark = 0
        self.v32 = self.arena
        self.v16 = self.arena.bitcast(BF16)
        self.p16 = self.psum.bitcast(BF16)

    def tile(self, shape, dtype=F32):
        shape = list(shape) if isinstance(shape, (list, tuple)) else [shape]
        n = int(np.prod(shape))
        esz = 4 if dtype == F32 else 2
        self.ptr = (self.ptr + 63) // 64 * 64
        off = self.ptr // esz
        self.ptr += n * esz
        assert self.ptr <= self.words * 4, 'SBUF arena overflow %d' % self.ptr
        base = self.v32 if dtype == F32 else self.v16
        ap = base[:, off:off + n]
        if len(shape) == 2:
            ap = ap.rearrange('p (a b) -> p a b', a=shape[0])
        elif len(shape) == 3:
            ap = ap.rearrange('p (a b c) -> p a b c', a=shape[0], b=shape[1])
        return ap

    def persist(self):
        self.mark = self.ptr

    def reset(self):
        self.ptr = self.mark

    def bank(self, i, n=512, dtype=F32, off=0):
        if dtype == F32:
            return self.psum[:, i * 512 + off:i * 512 + off + n]
        return self.p16[:, i * 1024 + off:i * 1024 + off + n]


def token_blocks():
    return [(0, 512, 0), (512, 512, 0), (1024, 512, 0), (1536, 512, 0), (2048, 256, 1)]


class Builder:
    def __init__(self, dbg=False, upto=99, mode=None, feed=()):
        self.dbg = dbg
        self.upto = upto
        self.mode = mode
        self.feed = set(feed)
        self.nc = bass.Bass('TRN2', target_bir_lowering=False)
        self.stack = contextlib.ExitStack()
        self.din = {}
        self.dout = {}

    def inp(self, name, shape, dtype=F32):
        t = self.nc.dram_tensor(name, list(shape), dtype, kind='ExternalInput').ap()
        self.din[name] = t
        return t

    def scratch(self, name, shape, dtype=F32, out=False):
        kind = 'ExternalOutput' if (out or self.dbg) else 'Internal'
        if name in self.feed:
            kind = 'ExternalInput'
        t = self.nc.dram_tensor(name, list(shape), dtype, kind=kind).ap()
        if kind == 'ExternalInput':
            self.din[name] = t
        if kind == 'ExternalOutput':
            self.dout[name] = t
        return t

    def build(self):
        nc = self.nc
        st = self.stack
        P = self.P = Prog(nc, st)
        M = self.M = Mem(nc, st)
        I = self.inp
        self.xT = I('xT', [D, NT])
        self.cvec = I('cvec', [2, D])
        self.ada_w = I('ada_w', [2, D, 6 * D])
        self.ada_b = I('ada_b', [2, 6 * D])
        self.norm1_g = I('norm1_g', [2, D])
        self.norm2_g = I('norm2_g', [2, D])
        self.ev_w_in = I('ev_w_in', [D, 2560])
        self.ident = I('ident', [128, 128])
        self.rg_conv_w = I('rg_conv_w', [4, 512])
        self.rg_conv_b = I('rg_conv_b', [512])
        self.rg_wa = I('rg_wa', [2, 8, 64, 64])
        self.rg_ba = I('rg_ba', [2, 512])
        self.rg_wx = I('rg_wx', [2, 8, 64, 64])
        self.rg_bx = I('rg_bx', [2, 512])
        self.rg_lambda = I('rg_lambda', [2, 512])
        self.hy_conv_w = I('hy_conv_w', [3, 1536])
        self.hy_conv_b = I('hy_conv_b', [1536])
        self.f_w1 = I('f_w1', [33, 64]); self.f_b1 = I('f_b1', [64]); self.f_freq = I('f_freq', [64])
        self.f_w2 = I('f_w2', [64, 64]); self.f_b2 = I('f_b2', [64]); self.f_w3 = I('f_w3', [64, 2048])
        self.hy_bias = I('hy_bias', [2, 512])
        self.ev_w_out = I('ev_w_out', [D, D])
        self.ffn_w_up = I('ffn_w_up', [2, D, 5632])
        self.ffn_conv_w = I('ffn_conv_w', [2, 3, 5632])
        self.ffn_conv_b = I('ffn_conv_b', [2, 5632])
        self.ffn_w_down = I('ffn_w_down', [2, 2816, D])
        self.final_g = I('final_g', [D])
        self.xa0 = self.scratch('xa0', [D, NT])
        self.xb0 = self.scratch('xb0', [D, NT])
        self.od_w_in = I('od_w_in', [D, 896])
        self.od_q_norm_g = I('od_q_norm_g', [512]); self.od_kv_norm_g = I('od_kv_norm_g', [256])
        self.od_w_uq = I('od_w_uq', [512, 1536]); self.od_w_ukv = I('od_w_ukv', [256, 2048])
        self.od_w_o = I('od_w_o', [D, D])
        self.cos2 = I('cos2', [64, S]); self.sin2 = I('sin2', [64, S]); self.rotm = I('rotm', [64, 64])
        self.z1 = self.scratch('z1', [896, NT])
        self.qT = self.scratch('qT', [8, 192, S], BF16)
        self.kT = self.scratch('kT', [8, 128, NT], BF16)
        self.krT = self.scratch('krT', [64, NT], BF16)
        self.Vtok = self.scratch('Vtok', [NT, D], BF16)
        self.oT = self.scratch('oT', [D, S], BF16)
        self.xa1 = self.scratch('xa1', [D, S])
        self.outT = self.scratch('outT', [D, S], out=True)
        self.hc = {}
        for n in (2048, 256):
            nj = n // 128
            self.hc[n] = dict(zT=I('zT%d' % n, [33, n]), dec=I('dec%d' % n, [n, 512]), decb=I('decb%d' % n, [n, 512]),
                              C=I('Ccol%d' % n, [nj, 128, nj, 128], BF16), S=I('Scol%d' % n, [nj, 128, nj, 128], BF16),
                              wk=I('wk%d' % n, [128, nj]), Kf=self.scratch('Kf%d' % n, [2, 2, n, 512]))
        self.vtok = self.scratch('vtok', [3, NT, 512])
        self.u2tok = self.scratch('u2tok', [NT, 512])
        self.z0 = self.scratch('z0', [2560, NT])
        self.mixT = self.scratch('mixT', [D, NT], BF16)
        self.consts()
        if self.mode == 'attn':
            self.phase_attn()
            P.barrier()
            P.play()
            return nc
        if self.upto >= 1:
            for l in range(2):
                self.phase_mod(l)
        if self.upto >= 2:
            self.phase_norm_proj(0, self.xT, 1, self.ev_w_in, 2560, self.z0)
        if self.upto >= 3:
            self.phase_rg()
        if self.upto >= 4:
            for n in (2048, 256):
                h = self.hc[n]
                self.phase_hy_filters(n, h['zT'], h['dec'], h['decb'], h['C'], h['S'], h['wk'], h['Kf'])
            self.phase_hy_prep()
            for n, tok0 in ((2048, 0), (256, S)):
                h = self.hc[n]
                self.phase_hy_conv(n, tok0, h['C'], h['S'], h['Kf'])
        if self.upto >= 5:
            self.phase_out_proj(0, self.mixT, self.ev_w_out, self.xT, self.xa0, token_blocks())
        if self.upto >= 6:
            self.phase_ffn(0, self.xa0, self.xb0, [(0, S, 0), (S, NT, 1)])
        lat = token_blocks()[:4]
        if self.upto >= 7:
            self.phase_norm_proj(1, self.xb0, 1, self.od_w_in, 896, self.z1)
            self.phase_mla_prep()
        if self.upto >= 8:
            self.phase_attn()
            self.phase_out_proj(1, self.oT, self.od_w_o, self.xb0, self.xa1, lat)
        if self.upto >= 9:
            self.phase_ffn(1, self.xa1, None, [(0, S, 0)], final_out=self.outT)
        P.barrier()
        P.play()
        return nc

    def consts(self):
        P, M = self.P, self.M
        self.ones_bf = M.tile([128], BF16)
        self.r_ones = Res()
        P.op('dve', lambda e: e.memset(self.ones_bf, 1.0), W=[self.r_ones])
        self.id32 = M.tile([128], F32)
        self.idbf = M.tile([128], BF16)
        self.r_id = Res()
        P.dma('sp', self.id32, self.ident, W=[self.r_id])
        self.r_idbf = Res()
        P.op('dve', lambda e: e.tensor_copy(self.idbf, self.id32), R=[self.r_id], W=[self.r_idbf])
        self.mod = [M.tile([48, 2], F32) for _ in range(2)]
        self.A1 = [M.tile([8, 2], F32) for _ in range(2)]
        self.A2 = [M.tile([8, 2], F32) for _ in range(2)]
        self.r_mod = [Res() for _ in range(2)]
        M.persist()

    def phase_mod(self, l):
        P, M = self.P, self.M
        M.reset()
        cv = M.tile([8, 2], F32)
        sT = M.tile([8, 2], BF16)
        bT = M.tile([48], F32)
        g1 = M.tile([8], F32)
        g2 = M.tile([8], F32)
        r_cv, r_sT, r_b, r_g = Res(), Res(), Res(), Res()
        for j in range(2):
            P.dma('sp', cv[:, :, j], self.cvec[j].rearrange('(k p) -> p k', p=128), W=[r_cv],
                  allow_slow_non_contiguous=True)
        P.dma('sp', bT, self.ada_b[l].rearrange('(j p) -> p j', p=128), W=[r_b], allow_slow_non_contiguous=True)
        P.dma('sp', g1, self.norm1_g[l].rearrange('(k p) -> p k', p=128), W=[r_g], allow_slow_non_contiguous=True)
        P.dma('sp', g2, self.norm2_g[l].rearrange('(k p) -> p k', p=128), W=[r_g], allow_slow_non_contiguous=True)
        P.op('act', lambda e: e.activation(sT, cv, AF.Silu), R=[r_cv], W=[r_sT])
        wt = [M.tile([8, 2048], BF16) for _ in range(2)]
        r_wt = [Res(), Res()]
        ps = M.bank(0, 96).rearrange('p (j c) -> p j c', c=2)
        r_ps = Res()
        for sec in range(3):
            w = wt[sec % 2]
            rw = r_wt[sec % 2]
            for k in range(8):
                P.dma('pool', w[:, k, :], self.ada_w[l, k * 128:(k + 1) * 128, sec * 2048:(sec + 1) * 2048], W=[rw])
            for fc in range(16):
                j = sec * 16 + fc
                for k in range(8):
                    P.op('pe', lambda e, w=w, k=k, fc=fc, j=j: e.matmul(
                        ps[:, j, :], w[:, k, fc * 128:(fc + 1) * 128], sT[:, k, :],
                        start=(k == 0), stop=(k == 7)), R=[rw, r_sT], W=[r_ps])
        mod = self.mod[l]
        rm = self.r_mod[l]
        for c in range(2):
            P.op('dve', lambda e, c=c: e.tensor_tensor(mod[:, :, c], ps[:, :, c], bT, ALU.add),
                 R=[r_ps, r_b], W=[rm])
        for c in range(2):
            P.op('dve', lambda e, c=c: e.scalar_tensor_tensor(
                self.A1[l][:, :, c], mod[:, 8:16, c], 1.0, g1, ALU.add, ALU.mult), R=[rm, r_g], W=[rm])
            P.op('dve', lambda e, c=c: e.scalar_tensor_tensor(
                self.A2[l][:, :, c], mod[:, 32:40, c], 1.0, g2, ALU.add, ALU.mult), R=[rm, r_g], W=[rm])
        P.barrier()

    def norm_block(self, src, t0, T, c, A, B, hdst, r_hdst, bufs, i):
        P, M = self.P, self.M
        xb, sq, rstd, tmp, rx, rsq, rrs, rtmp, psb, rps = bufs[i % 2]
        P.dma('sp', xb[:, :, 0:T], src[:, t0:t0 + T].rearrange('(k p) t -> p k t', p=128), W=[rx])
        P.op('act', lambda e: e.activation(sq[:, :, 0:T], xb[:, :, 0:T], AF.Square), R=[rx], W=[rsq])
        for k in range(8):
            P.op('pe', lambda e, k=k: e.matmul(psb[:, 0:T], self.ones_bf, sq[:, k, 0:T],
                                               start=(k == 0), stop=(k == 7)),
                 R=[rsq, self.r_ones], W=[rps])
        P.op('act', lambda e: e.activation(rstd[:, 0:T], psb[:, 0:T], AF.Sqrt, bias=EPS, scale=1.0 / D),
             R=[rps], W=[rrs])
        P.op('dve', lambda e: e.reciprocal(rstd[:, 0:T], rstd[:, 0:T]), R=[rrs], W=[rrs])
        for k in range(8):
            P.op('dve', lambda e, k=k: e.scalar_tensor_tensor(
                tmp[:, k, 0:T], xb[:, k, 0:T], A[:, k, c:c + 1], rstd[:, 0:T], ALU.mult, ALU.mult),
                R=[rx, rrs, self.r_mod[0], self.r_mod[1]], W=[rtmp[k]])
            P.op('act', lambda e, k=k: e.activation(hdst[:, k, :], tmp[:, k, 0:T], AF.Identity,
                                                    bias=B[:, k, c:c + 1]),
                 R=[rtmp[k], self.r_mod[0], self.r_mod[1]], W=[r_hdst])

    def norm_bufs(self, psbanks):
        M = self.M
        bufs = []
        for i in range(2):
            bufs.append((M.tile([8, 512], F32), M.tile([8, 512], BF16), M.tile([512], F32),
                         M.tile([8, 512], F32), Res(), Res(), Res(), [Res() for _ in range(8)],
                         M.bank(psbanks[i]), Res()))
        return bufs

    def phase_norm_proj(self, l, src, which, w_dram, F, zdst, blocks=None):
        P, M = self.P, self.M
        M.reset()
        blocks = blocks or token_blocks()
        A = self.A1[l] if which == 1 else self.A2[l]
        B = self.mod[l][:, 0:8, :] if which == 1 else self.mod[l][:, 24:32, :]
        hT = M.tile([8, NT], BF16)
        r_h = [Res() for _ in blocks]
        nfc = F // 128
        wt = M.tile([8, F], BF16)
        r_w = [Res() for _ in range(nfc)]
        for g in range(0, F, 2048):
            gw = min(2048, F - g)
            for k in range(8):
                P.dma('pool', wt[:, k, g:g + gw], w_dram[k * 128:(k + 1) * 128, g:g + gw],
                      W=[r_w[j] for j in range(g // 128, (g + gw) // 128)])
        bufs = self.norm_bufs([6, 7])
        for i, (t0, T, c) in enumerate(blocks):
            self.norm_block(src, t0, T, c, A, B, hT[:, :, t0:t0 + T], r_h[i], bufs, i)
        stg = [M.tile([512], F32) for _ in range(4)]
        r_stg = [Res() for _ in range(4)]
        r_ps = [Res() for _ in range(4)]
        n = 0
        for fc in range(nfc):
            for i, (t0, T, c) in enumerate(blocks):
                b = n % 4
                ps = M.bank(b)
                for k in range(8):
                    P.op('pe', lambda e, k=k, fc=fc, t0=t0, T=T, ps=ps: e.matmul(
                        ps[:, 0:T], wt[:, k, fc * 128:(fc + 1) * 128], hT[:, k, t0:t0 + T],
                        start=(k == 0), stop=(k == 7)), R=[r_w[fc], r_h[i]], W=[r_ps[b]])
                ev = 'act' if n % 2 == 0 else 'dve'
                if ev == 'act':
                    P.op('act', lambda e, b=b, T=T, ps=ps: e.copy(stg[b][:, 0:T], ps[:, 0:T]),
                         R=[r_ps[b]], W=[r_stg[b]])
                else:
                    P.op('dve', lambda e, b=b, T=T, ps=ps: e.tensor_copy(stg[b][:, 0:T], ps[:, 0:T]),
                         R=[r_ps[b]], W=[r_stg[b]])
                P.dma('sp', zdst[fc * 128:(fc + 1) * 128, t0:t0 + T], stg[b][:, 0:T], R=[r_stg[b]])
                n += 1
        P.barrier()


def rev(ap, n):
    pat = [list(p) for p in ap.ap]
    assert len(pat) == 2 and pat[1][1] == n
    return bass.AP(ap.tensor, ap.offset + (n - 1) * pat[1][0], [pat[0], [-pat[1][0], n]])


def _phase_rg(self):
    P, M = self.P, self.M
    M.reset()
    f = lambda: M.tile([NT], F32)
    zx, gate, xc, hf, hb, ga = [f() for _ in range(6)]
    rgs, igs, aas, bbs = [[f(), f()] for _ in range(4)]
    xcb = M.tile([NT], BF16)
    ob = M.tile([NT], BF16)
    cw = M.tile([4], F32)
    cb = M.tile([1], F32)
    bias4 = M.tile([4], F32)
    lam = M.tile([2], F32)
    cl = M.tile([4], F32)
    wbd = [M.tile([128], BF16) for _ in range(4)]
    segs = [(0, S), (S, CT)]
    blocks = token_blocks()
    for cc in range(4):
        R = {k: Res() for k in ['zx', 'gate', 'xc', 'xcb', 'r0', 'i0', 'a0', 'b0', 'r1', 'i1', 'a1', 'b1', 'hf', 'hb', 'small', 'cl', 'ob', 'ga']}
        rw = [Res() for _ in range(4)]
        rps = [Res() for _ in range(4)]
        fs = slice(cc * 128, (cc + 1) * 128)
        P.dma('sp', zx, self.z0[cc * 128:(cc + 1) * 128, :], W=[R['zx']])
        P.dma('sp', gate, self.z0[512 + cc * 128:512 + (cc + 1) * 128, :], W=[R['gate']])
        P.dma('sp', cw, self.rg_conv_w[:, fs].rearrange('j p -> p j'), W=[R['small']], allow_slow_non_contiguous=True)
        P.dma('sp', cb, self.rg_conv_b[fs].rearrange('(p o) -> p o', o=1), W=[R['small']])
        for d in range(2):
            P.dma('sp', bias4[:, 2 * d:2 * d + 1], self.rg_ba[d, fs].rearrange('(p o) -> p o', o=1), W=[R['small']])
            P.dma('sp', bias4[:, 2 * d + 1:2 * d + 2], self.rg_bx[d, fs].rearrange('(p o) -> p o', o=1), W=[R['small']])
            P.dma('sp', lam[:, d:d + 1], self.rg_lambda[d, fs].rearrange('(p o) -> p o', o=1), W=[R['small']])
        for d in range(2):
            for ty, wsrc in enumerate([self.rg_wa, self.rg_wx]):
                w = wbd[2 * d + ty]
                r_ = rw[2 * d + ty]
                P.op('dve', lambda e, w=w: e.memset(w, 0.0), W=[r_])
                for hh in range(2):
                    P.dma('pool', w[hh * 64:(hh + 1) * 64, hh * 64:(hh + 1) * 64], wsrc[d, 2 * cc + hh], W=[r_])
        P.op('act', lambda e: e.activation(cl[:, 0:2], lam, AF.Exp, scale=-1.0), R=[R['small']], W=[R['cl']])
        P.op('act', lambda e: e.activation(cl[:, 0:2], cl[:, 0:2], AF.Ln, bias=1.0), R=[R['cl']], W=[R['cl']])
        P.op('dve', lambda e: e.tensor_scalar(cl[:, 2:4], cl[:, 0:2], -16.0, None, ALU.mult), R=[R['cl']], W=[R['cl']])
        P.op('dve', lambda e: e.tensor_scalar(cl[:, 0:2], cl[:, 0:2], -8.0, None, ALU.mult), R=[R['cl']], W=[R['cl']])
        P.op('act', lambda e: e.activation(xc, zx, AF.Identity, bias=cb[:, 0:1], scale=cw[:, 2:3]),
             R=[R['zx'], R['small']], W=[R['xc']])
        for j in (0, 1, 3):
            off = j - 2
            for (s0, n) in segs:
                lo = max(0, -off)
                hi = n - max(0, off)
                P.op('dve', lambda e, j=j, off=off, s0=s0, lo=lo, hi=hi: e.scalar_tensor_tensor(
                    xc[:, s0 + lo:s0 + hi], zx[:, s0 + lo + off:s0 + hi + off], cw[:, j:j + 1],
                    xc[:, s0 + lo:s0 + hi], ALU.mult, ALU.add), R=[R['zx'], R['small'], R['xc']], W=[R['xc']])
        P.op('act', lambda e: e.copy(xcb, xc), R=[R['xc']], W=[R['xcb']])
        P.op('act', lambda e: e.activation(ga, gate, AF.Square), R=[R['gate']], W=[R['ga']])
        P.op('pool', lambda e: e.tensor_scalar(ga, ga, 0.044715, 1.0, ALU.mult, ALU.add), R=[R['ga']], W=[R['ga']])
        P.op('pool', lambda e: e.tensor_tensor(ga, ga, gate, ALU.mult), R=[R['ga'], R['gate']], W=[R['ga']])
        P.op('act', lambda e: e.activation(ga, ga, AF.Sigmoid, scale=1.5957691216), R=[R['ga']], W=[R['ga']])
        P.op('pool', lambda e: e.tensor_tensor(ga, ga, gate, ALU.mult), R=[R['ga'], R['gate']], W=[R['ga']])
        for d in range(2):
            rg, ig, aa, bb = rgs[d], igs[d], aas[d], bbs[d]
            kr_, ki_, ka_, kb_ = 'r%d' % d, 'i%d' % d, 'a%d' % d, 'b%d' % d
            for ty, dst, rk in ((0, rg, kr_), (1, ig, ki_)):
                for bi, (t0, T, c) in enumerate(blocks):
                    bk = (bi + ty) % 4
                    ps = M.bank(bk)
                    P.op('pe', lambda e, ps=ps, d=d, ty=ty, t0=t0, T=T: e.matmul(
                        ps[:, 0:T], wbd[2 * d + ty], xcb[:, t0:t0 + T], start=True, stop=True),
                        R=[rw[2 * d + ty], R['xcb']], W=[rps[bk]])
                    P.op('act', lambda e, ps=ps, dst=dst, d=d, ty=ty, t0=t0, T=T: e.activation(
                        dst[:, t0:t0 + T], ps[:, 0:T], AF.Sigmoid, bias=bias4[:, 2 * d + ty:2 * d + ty + 1]),
                        R=[rps[bk], R['small']], W=[R[rk]])
            P.op('act', lambda e, d=d, aa=aa, rg=rg: e.activation(aa, rg, AF.Exp, scale=cl[:, d:d + 1]), R=[R[kr_], R['cl']], W=[R[ka_]])
            P.op('act', lambda e, d=d, bb=bb, rg=rg: e.activation(bb, rg, AF.Exp, scale=cl[:, 2 + d:3 + d]), R=[R[kr_], R['cl']], W=[R[kb_]])
            P.op('act', lambda e, bb=bb: e.activation(bb, bb, AF.Sqrt, bias=1.0, scale=-1.0), R=[R[kb_]], W=[R[kb_]])
            P.op('pool', lambda e, bb=bb, ig=ig: e.tensor_tensor(bb, bb, ig, ALU.mult), R=[R[kb_], R[ki_]], W=[R[kb_]])
            P.op('pool', lambda e, bb=bb: e.tensor_tensor(bb, bb, xc, ALU.mult), R=[R[kb_], R['xc']], W=[R[kb_]])
            h = hf if d == 0 else hb
            hk = 'hf' if d == 0 else 'hb'
            if d == 0:
                P.op('dve', lambda e, h=h, aa=aa, bb=bb: e.tensor_tensor_scan(h[:, S:NT], aa[:, S:NT], bb[:, S:NT], 0.0, ALU.mult, ALU.add),
                     R=[R[ka_], R[kb_]], W=[R[hk]])
                P.op('dve', lambda e, h=h, aa=aa, bb=bb: e.tensor_tensor_scan(h[:, 0:S], aa[:, 0:S], bb[:, 0:S], h[:, NT - 1:NT], ALU.mult, ALU.add),
                     R=[R[ka_], R[kb_], R[hk]], W=[R[hk]])
            else:
                P.op('dve', lambda e, h=h, aa=aa, bb=bb: e.tensor_tensor_scan(rev(h[:, S:NT], CT), rev(aa[:, S:NT], CT), rev(bb[:, S:NT], CT),
                                                                               0.0, ALU.mult, ALU.add), R=[R[ka_], R[kb_]], W=[R[hk]])
                P.op('dve', lambda e, h=h, aa=aa, bb=bb: e.tensor_tensor_scan(rev(h[:, 0:S], S), rev(aa[:, 0:S], S), rev(bb[:, 0:S], S),
                                                                               h[:, S:S + 1], ALU.mult, ALU.add),
                     R=[R[ka_], R[kb_], R[hk]], W=[R[hk]])
        P.op('pool', lambda e: e.tensor_tensor(hf, hf, hb, ALU.add), R=[R['hf'], R['hb']], W=[R['hf']])
        P.op('dve', lambda e: e.tensor_tensor(ob, ga, hf, ALU.mult), R=[R['ga'], R['hf']], W=[R['ob']])
        P.dma('sp', self.mixT[cc * 128:(cc + 1) * 128, :], ob, R=[R['ob']])
        P.barrier()


Builder.phase_rg = _phase_rg


def bcast_rows(ap_row, nparts=128):
    pat = [list(p) for p in ap_row.ap]
    return bass.AP(ap_row.tensor, ap_row.offset, [[0, nparts]] + pat)


def _phase_hy_filters(self, n, zT_d, dec_d, decb_d, Ccol, Scol, wk_d, Kf_d):
    P, M = self.P, self.M
    M.reset()
    nj = n // 128
    w1 = M.tile([64], F32)
    w2 = M.tile([64], F32)
    w3 = M.tile([2048], F32)
    sm = M.tile([3], F32)
    zT = M.tile([n], F32)
    h1 = M.tile([n], F32)
    h2 = M.tile([n], F32)
    tm = M.tile([512], F32)
    r_w, r_z, r_h1, r_h2, r_tm = Res(), Res(), Res(), Res(), Res()
    P.dma('sp', w1[0:33, :], self.f_w1, W=[r_w])
    P.dma('sp', w2[0:64, :], self.f_w2, W=[r_w])
    P.dma('sp', w3[0:64, :], self.f_w3, W=[r_w])
    P.dma('sp', sm[0:64, 0:1], self.f_b1.rearrange('(p o) -> p o', o=1), W=[r_w])
    P.dma('sp', sm[0:64, 1:2], self.f_freq.rearrange('(p o) -> p o', o=1), W=[r_w])
    P.dma('sp', sm[0:64, 2:3], self.f_b2.rearrange('(p o) -> p o', o=1), W=[r_w])
    P.dma('sp', zT[0:33, :], zT_d, W=[r_z])
    rps = [Res() for _ in range(8)]
    PI = math.pi

    def sin_layer(lhsT, kk, src, rsrc, bcol, dst, rdst):
        for bi, t0 in enumerate(range(0, n, 512)):
            T = min(512, n - t0)
            ps = M.bank(bi % 2)
            P.op('pe', lambda e, ps=ps, t0=t0, T=T: e.matmul(ps[0:64, 0:T], lhsT, src[0:kk, t0:t0 + T], start=True, stop=True),
                 R=[r_w, rsrc], W=[rps[bi % 2]])
            P.op('dve', lambda e, ps=ps, T=T: e.tensor_scalar(tm[0:64, 0:T], ps[0:64, 0:T], sm[0:64, bcol:bcol + 1],
                                                           sm[0:64, 1:2], ALU.add, ALU.mult), R=[rps[bi % 2], r_w, r_tm], W=[r_tm])
            for thr, op, mul in ((PI, ALU.is_gt, -2 * PI), (-PI, ALU.is_lt, 2 * PI)):
                P.op('dve', lambda e, T=T, t0=t0, thr=thr, op=op, mul=mul: e.tensor_scalar(
                    dst[0:64, t0:t0 + T], tm[0:64, 0:T], thr, mul, op, ALU.mult), R=[r_tm], W=[rdst])
                P.op('dve', lambda e, T=T, t0=t0: e.tensor_tensor(tm[0:64, 0:T], tm[0:64, 0:T], dst[0:64, t0:t0 + T], ALU.add),
                     R=[r_tm, rdst], W=[r_tm])
            P.op('act', lambda e, T=T, t0=t0: e.activation(dst[0:64, t0:t0 + T], tm[0:64, 0:T], AF.Sin), R=[r_tm], W=[rdst])

    sin_layer(w1[0:33, :], 33, zT, r_z, 0, h1, r_h1)
    sin_layer(w2[0:64, :], 64, h1, r_h1, 2, h2, r_h2)
    ks = M.tile([nj, 1024], BF16)
    kd = M.tile([nj, 1024], BF16)
    dec = [M.tile([512], F32) for _ in range(2)]
    decb = [M.tile([512], F32) for _ in range(2)]
    kf = [M.tile([512], F32) for _ in range(2)]
    kb = [M.tile([512], F32) for _ in range(2)]
    ab = [M.tile([512], BF16) for _ in range(2)]
    r_dec = [Res(), Res()]
    r_kf = [Res(), Res()]
    r_kb = [Res(), Res()]
    r_ab = [Res(), Res()]
    r_ks = Res()
    r_nrm = Res()
    cnt = 0
    for j in range(nj):
        P.dma('sp', dec[j % 2], dec_d[j * 128:(j + 1) * 128, :], W=[r_dec[j % 2]])
        P.dma('sp', decb[j % 2], decb_d[j * 128:(j + 1) * 128, :], W=[r_dec[j % 2]])
        for o in range(2):
            q = cnt % 2
            cnt += 1
            for dr in range(2):
                bk = 2 + 2 * q + dr
                ps = M.bank(bk)
                c0 = o * 1024 + dr * 512
                P.op('pe', lambda e, ps=ps, j=j, c0=c0: e.matmul(ps, h2[0:64, j * 128:(j + 1) * 128], w3[0:64, c0:c0 + 512],
                                                             start=True, stop=True), R=[r_h2, r_w], W=[rps[bk]])
                dst, rd, dc = (kf[q], r_kf[q], dec[j % 2]) if dr == 0 else (kb[q], r_kb[q], decb[j % 2])
                P.op('dve', lambda e, ps=ps, dst=dst, dc=dc: e.tensor_tensor(dst, ps, dc, ALU.mult),
                     R=[rps[bk], r_dec[j % 2]], W=[rd])
                P.op('act', lambda e, dst=dst, q=q: e.activation(ab[q], dst, AF.Abs), R=[rd], W=[r_ab[q]])
                first = (j == 0 and dr == 0)
                last = (j == nj - 1 and dr == 1)
                P.op('pe', lambda e, o=o, q=q, first=first, last=last: e.matmul(M.bank(6 + o), self.ones_bf, ab[q], start=first, stop=last),
                     R=[r_ab[q], self.r_ones], W=[r_nrm])
            P.op('dve', lambda e, q=q, j=j, o=o: e.tensor_tensor(ks[:, j, o * 512:(o + 1) * 512], kf[q], kb[q], ALU.add),
                 R=[r_kf[q], r_kb[q]], W=[r_ks])
            P.op('dve', lambda e, q=q, j=j, o=o: e.tensor_tensor(kd[:, j, o * 512:(o + 1) * 512], kb[q], kf[q], ALU.subtract),
                 R=[r_kf[q], r_kb[q]], W=[r_ks])
    rn = M.tile([1024], F32)
    r_rn = Res()
    for o in range(2):
        P.op('dve', lambda e, o=o: e.reciprocal(rn[:, o * 512:(o + 1) * 512], M.bank(6 + o)), R=[r_nrm], W=[r_rn])
    wk = M.tile([nj], F32)
    P.dma('sp', wk, wk_d, W=[r_rn])
    cs = [[M.tile([nj, 128], BF16) for _ in range(2)] for _ in range(2)]
    r_cs = [Res(), Res()]
    ot = [M.tile([512], F32) for _ in range(4)]
    r_ot = [Res() for _ in range(4)]
    m = 0
    for f in range(nj):
        q = f % 2
        P.dma('sp', cs[q][0], Ccol[f], W=[r_cs[q]])
        P.dma('sp', cs[q][1], Scol[f], W=[r_cs[q]])
        for ri in range(2):
            src = ks if ri == 0 else kd
            for o in range(2):
                bk = m % 4
                ps = M.bank(bk)
                for j in range(nj):
                    P.op('pe', lambda e, ps=ps, q=q, ri=ri, j=j, o=o, src=src: e.matmul(
                        ps, cs[q][ri][:, j, :], src[:, j, o * 512:(o + 1) * 512], start=(j == 0), stop=(j == nj - 1)),
                        R=[r_cs[q], r_ks], W=[rps[bk]])
                P.op('dve', lambda e, ps=ps, bk=bk, f=f, o=o: e.scalar_tensor_tensor(
                    ot[bk], ps, wk[:, f:f + 1], rn[:, o * 512:(o + 1) * 512], ALU.mult, ALU.mult),
                    R=[rps[bk], r_rn], W=[r_ot[bk]])
                P.dma('sp', Kf_d[o, ri, f * 128:(f + 1) * 128, :], ot[bk], R=[r_ot[bk]])
                m += 1
    P.barrier()


def _phase_hy_prep(self):
    P, M = self.P, self.M
    M.reset()
    zz = [M.tile([NT], F32) for _ in range(2)]
    zc = [M.tile([NT], F32) for _ in range(2)]
    cw = [M.tile([4], F32) for _ in range(2)]
    stg = [M.tile([4, 128], F32) for _ in range(2)]
    r_zz = [Res(), Res()]
    r_zc = [Res(), Res()]
    r_cw = [Res(), Res()]
    r_stg = [Res(), Res()]
    r_ps = [Res(), Res()]
    segs = [(0, S), (S, CT)]
    m = 0
    for ch in range(12):
        q = ch % 2
        g, cc = ch // 4, ch % 4
        fs = slice(ch * 128, (ch + 1) * 128)
        P.dma('sp', zz[q], self.z0[1024 + ch * 128:1024 + (ch + 1) * 128, :], W=[r_zz[q]])
        P.dma('sp', cw[q][:, 0:3], self.hy_conv_w[:, fs].rearrange('j p -> p j'), W=[r_cw[q]], allow_slow_non_contiguous=True)
        P.dma('sp', cw[q][:, 3:4], self.hy_conv_b[fs].rearrange('(p o) -> p o', o=1), W=[r_cw[q]])
        P.op('act', lambda e, q=q: e.activation(zc[q], zz[q], AF.Identity, bias=cw[q][:, 3:4], scale=cw[q][:, 1:2]),
             R=[r_zz[q], r_cw[q]], W=[r_zc[q]])
        for j in (0, 2):
            off = j - 1
            for (s0, n) in segs:
                lo = max(0, -off)
                hi = n - max(0, off)
                P.op('dve', lambda e, q=q, j=j, off=off, s0=s0, lo=lo, hi=hi: e.scalar_tensor_tensor(
                    zc[q][:, s0 + lo:s0 + hi], zz[q][:, s0 + lo + off:s0 + hi + off], cw[q][:, j:j + 1],
                    zc[q][:, s0 + lo:s0 + hi], ALU.mult, ALU.add), R=[r_zz[q], r_cw[q], r_zc[q]], W=[r_zc[q]])
        for tg in range(0, NT // 128, 4):
            ntl = min(4, NT // 128 - tg)
            b = m % 2
            m += 1
            ps = M.bank(b)
            for i in range(ntl):
                P.op('pe', lambda e, ps=ps, q=q, tg=tg, i=i: e.transpose(
                    ps[:, i * 128:(i + 1) * 128], zc[q][:, (tg + i) * 128:(tg + i + 1) * 128], self.id32),
                    R=[r_zc[q], self.r_id], W=[r_ps[b]])
            P.op('act' if b == 0 else 'dve',
                 (lambda e, ps=ps, b=b, ntl=ntl: e.copy(stg[b][:, 0:ntl, :], ps[:, 0:ntl * 128].rearrange('p (a c) -> p a c', c=128)))
                 if b == 0 else
                 (lambda e, ps=ps, b=b, ntl=ntl: e.tensor_copy(stg[b][:, 0:ntl, :], ps[:, 0:ntl * 128].rearrange('p (a c) -> p a c', c=128))),
                 R=[r_ps[b]], W=[r_stg[b]])
            P.dma('sp', self.vtok[g, tg * 128:(tg + ntl) * 128, cc * 128:(cc + 1) * 128].rearrange('(a p) c -> p a c', p=128),
                  stg[b][:, 0:ntl, :], R=[r_stg[b]])
    P.barrier()


def _phase_hy_conv(self, n, tok0, Ccol, Scol, Kf_d):
    P, M = self.P, self.M
    M.reset()
    nj = n // 128
    u = M.tile([nj, 512], BF16)
    Y = M.tile([nj, 2, 512], BF16)
    r_u = [Res() for _ in range(nj)]
    r_Y = [Res() for _ in range(nj)]
    cs = [[M.tile([nj, 128], BF16) for _ in range(2)] for _ in range(3)]
    r_cs = [Res(), Res(), Res()]
    kk = [[M.tile([512], F32) for _ in range(2)] for _ in range(3)]
    r_kk = [Res(), Res(), Res()]
    t1 = [M.tile([512], F32) for _ in range(2)]
    t2 = [M.tile([512], F32) for _ in range(2)]
    t3 = [M.tile([512], F32) for _ in range(2)]
    t4 = [M.tile([512], F32) for _ in range(2)]
    r_t1 = [Res(), Res()]
    r_t2 = [Res(), Res()]
    r_t3 = [Res(), Res()]
    r_t4 = [Res(), Res()]
    biasb = M.tile([2, 512], F32)
    r_bias = Res()
    for o in range(2):
        P.dma('sp', biasb[:, o, :], bcast_rows(self.hy_bias[o]), W=[r_bias])
    uf = [M.tile([512], F32) for _ in range(3)]
    xg = [M.tile([512], F32) for _ in range(3)]
    un = [M.tile([512], F32) for _ in range(2)]
    ub = [M.tile([512], BF16) for _ in range(2)]
    ot = [M.tile([512], BF16) for _ in range(2)]
    r_uf = [Res(), Res(), Res()]
    r_xg = [Res(), Res(), Res()]
    r_un = [Res(), Res()]
    r_ub = [Res(), Res()]
    r_ot = [Res(), Res()]
    rps = [Res() for _ in range(8)]
    for j in range(nj):
        P.dma('pool', u[:, j, :], self.vtok[0, tok0 + j * 128:tok0 + (j + 1) * 128, :], W=[r_u[j]])
    for o in range(2):
        for f in range(nj):
            q3 = f % 3
            q = f % 2
            P.dma('sp', cs[q3][0], Ccol[f], W=[r_cs[q3]])
            P.dma('sp', cs[q3][1], Scol[f], W=[r_cs[q3]])
            for ri in range(2):
                P.dma('sp', kk[q3][ri], Kf_d[o, ri, f * 128:(f + 1) * 128, :], W=[r_kk[q3]])
            pr, pi = M.bank(2 * q), M.bank(2 * q + 1)
            for ri, ps in ((0, pr), (1, pi)):
                for j in range(nj):
                    P.op('pe', lambda e, ps=ps, q3=q3, ri=ri, j=j: e.matmul(ps, cs[q3][ri][:, j, :], u[:, j, :],
                                                                          start=(j == 0), stop=(j == nj - 1)),
                         R=[r_cs[q3], r_u[j]], W=[rps[2 * q + ri]])
            R_ = [rps[2 * q], rps[2 * q + 1], r_kk[q3]]
            P.op('dve', lambda e, q=q, q3=q3, pr=pr: e.tensor_tensor(t1[q], pr, kk[q3][0], ALU.mult), R=R_ + [r_t1[q]], W=[r_t1[q]])
            P.op('dve', lambda e, q=q, q3=q3, pi=pi: e.tensor_tensor(t2[q], pi, kk[q3][1], ALU.mult), R=R_ + [r_t2[q]], W=[r_t2[q]])
            P.op('dve', lambda e, q=q, q3=q3, pi=pi: e.tensor_tensor(t3[q], pi, kk[q3][0], ALU.mult), R=R_ + [r_t3[q]], W=[r_t3[q]])
            P.op('dve', lambda e, q=q, q3=q3, pr=pr: e.tensor_tensor(t4[q], pr, kk[q3][1], ALU.mult), R=R_ + [r_t4[q]], W=[r_t4[q]])
            P.op('pool', lambda e, q=q, f=f: e.tensor_tensor(Y[:, f, 0, :], t1[q], t2[q], ALU.add), R=[r_t1[q], r_t2[q]], W=[r_Y[f]])
            P.op('pool', lambda e, q=q, f=f: e.tensor_tensor(Y[:, f, 1, :], t3[q], t4[q], ALU.subtract), R=[r_t3[q], r_t4[q]], W=[r_Y[f]])
        for t in range(nj):
            q = t % 2
            q3 = t % 3
            P.dma('sp', cs[q3][0], Ccol[t], W=[r_cs[q3]])
            P.dma('sp', cs[q3][1], Scol[t], W=[r_cs[q3]])
            rows = slice(tok0 + t * 128, tok0 + (t + 1) * 128)
            if o == 0:
                P.dma('sp', uf[q3], self.vtok[0, rows, :], W=[r_uf[q3]])
            else:
                P.dma('sp', uf[q3], self.u2tok[rows, :], W=[r_uf[q3]])
            P.dma('sp', xg[q3], self.vtok[1 + o, rows, :], W=[r_xg[q3]])
            ps = M.bank(4 + q)
            for ri in range(2):
                for j in range(nj):
                    P.op('pe', lambda e, ps=ps, q3=q3, ri=ri, j=j: e.matmul(ps, cs[q3][ri][:, j, :], Y[:, j, ri, :],
                                                                          start=(ri == 0 and j == 0), stop=(ri == 1 and j == nj - 1)),
                         R=[r_cs[q3], r_Y[j]], W=[rps[4 + q]])
            P.op('pool', lambda e, q=q, q3=q3, o=o: e.tensor_tensor(un[q], uf[q3], biasb[:, o, :], ALU.mult), R=[r_uf[q3], r_bias], W=[r_un[q]])
            P.op('dve', lambda e, q=q, ps=ps: e.tensor_tensor(un[q], un[q], ps, ALU.add), R=[r_un[q], rps[4 + q]], W=[r_un[q]])
            if o == 0:
                P.op('dve', lambda e, q=q, q3=q3: e.tensor_tensor(un[q], un[q], xg[q3], ALU.mult), R=[r_un[q], r_xg[q3]], W=[r_un[q]])
                P.dma('sp', self.u2tok[rows, :], un[q], R=[r_un[q]])
                P.op('act', lambda e, q=q, t=t: e.copy(u[:, t, :], un[q]), R=[r_un[q]], W=[r_u[t]])
            else:
                P.op('dve', lambda e, q=q, q3=q3: e.tensor_tensor(ub[q], un[q], xg[q3], ALU.mult), R=[r_un[q], r_xg[q3]], W=[r_ub[q]])
                pt = M.bank(6 + q, 512, BF16)
                for cc in range(4):
                    P.op('pe', lambda e, pt=pt, q=q, cc=cc: e.transpose(pt[:, cc * 128:(cc + 1) * 128], ub[q][:, cc * 128:(cc + 1) * 128], self.idbf),
                         R=[r_ub[q], self.r_idbf], W=[rps[6 + q]])
                P.op('act', lambda e, pt=pt, q=q: e.copy(ot[q], pt), R=[rps[6 + q]], W=[r_ot[q]])
                P.dma('sp', self.mixT[512:1024, tok0 + t * 128:tok0 + (t + 1) * 128].rearrange('(a p) t -> p a t', p=128),
                      ot[q].rearrange('p (a t) -> p a t', a=4), R=[r_ot[q]])
    P.barrier()


Builder.phase_hy_filters = _phase_hy_filters
Builder.phase_hy_prep = _phase_hy_prep
Builder.phase_hy_conv = _phase_hy_conv


def hyena_consts():
    out = {}
    f32 = np.float32
    max_decay = math.log(1e-2) / 0.3
    min_decay = math.log(1e-2) / 1.5
    deltas = np.linspace(min_decay, max_decay, 512, dtype=f32)
    for n in (2048, 256):
        pos = np.arange(n, dtype=f32)
        t = np.linspace(0.0, 1.0, n, dtype=f32)[:, None]
        w = (f32(2.0 * math.pi) * pos / f32(n)).astype(f32)
        fr = np.linspace(1e-4, 15, 16, dtype=f32)
        ang = (w[:, None] * fr[None, :]).astype(f32)
        z = np.concatenate([t, np.cos(ang), -np.sin(ang)], axis=-1).astype(f32)
        out['zT%d' % n] = np.ascontiguousarray(z.T)
        dec = np.exp(-t * np.abs(deltas)[None, :]).astype(f32)
        out['dec%d' % n] = dec
        decb = dec.copy()
        decb[0] = 0.0
        out['decb%d' % n] = decb
        N = 2 * n - 1
        sk = (np.arange(n, dtype=np.int64)[:, None] * np.arange(n, dtype=np.int64)[None, :]) % N
        ang = 2.0 * np.pi * sk.astype(np.float64) / N
        nj = n // 128
        for nm, mat in (('C', np.cos(ang)), ('S', np.sin(ang))):
            m4 = mat.reshape(nj, 128, nj, 128).transpose(2, 1, 0, 3)
            out['%scol%d' % (nm, n)] = np.ascontiguousarray(m4).astype(ml_dtypes.bfloat16)
        wk = np.full(n, 2.0 / N, dtype=f32)
        wk[0] = 1.0 / N
        out['wk%d' % n] = np.ascontiguousarray(wk.reshape(nj, 128).T)
    return out


def _phase_out_proj(self, l, src_bf, w_dram, xsrc, xdst, blocks):
    P, M = self.P, self.M
    M.reset()
    ntok = max(t0 + T for t0, T, c in blocks)
    mx = M.tile([8, ntok], BF16)
    wt = M.tile([8, D], BF16)
    r_mx, r_w = Res(), Res()
    for k in range(8):
        P.dma('sp', mx[:, k, :], src_bf[k * 128:(k + 1) * 128, 0:ntok], W=[r_mx])
        P.dma('pool', wt[:, k, :], w_dram[k * 128:(k + 1) * 128, :], W=[r_w])
    xb = [M.tile([512], F32) for _ in range(3)]
    ob = [M.tile([512], F32) for _ in range(3)]
    r_xb = [Res() for _ in range(3)]
    r_ob = [Res() for _ in range(3)]
    rps = [Res() for _ in range(4)]
    n = 0
    mod = self.mod[l]
    for dc in range(8):
        for (t0, T, c) in blocks:
            b = n % 4
            q = n % 3
            n += 1
            ps = M.bank(b)
            P.dma('sp', xb[q][:, 0:T], xsrc[dc * 128:(dc + 1) * 128, t0:t0 + T], W=[r_xb[q]])
            for k in range(8):
                P.op('pe', lambda e, ps=ps, k=k, dc=dc, t0=t0, T=T: e.matmul(
                    ps[:, 0:T], wt[:, k, dc * 128:(dc + 1) * 128], mx[:, k, t0:t0 + T], start=(k == 0), stop=(k == 7)),
                    R=[r_w, r_mx], W=[rps[b]])
            P.op('dve', lambda e, ps=ps, q=q, dc=dc, c=c, T=T: e.scalar_tensor_tensor(
                ob[q][:, 0:T], ps[:, 0:T], mod[:, 16 + dc, c:c + 1], xb[q][:, 0:T], ALU.mult, ALU.add),
                R=[rps[b], r_xb[q], self.r_mod[l]], W=[r_ob[q]])
            P.dma('sp', xdst[dc * 128:(dc + 1) * 128, t0:t0 + T], ob[q][:, 0:T], R=[r_ob[q]])
    P.barrier()


def _phase_ffn(self, l, xsrc, xdst, segs, final_out=None):
    P, M = self.P, self.M
    M.reset()
    TB = 256
    NP = 22
    wu = M.tile([8, 5632], BF16)
    wd = M.tile([NP, D], BF16)
    r_wu = [Res() for _ in range(44)]
    r_wd = Res()
    for g in range(0, 5632, 2048):
        gw = min(2048, 5632 - g)
        for k in range(8):
            P.dma('pool', wu[:, k, g:g + gw], self.ffn_w_up[l, k * 128:(k + 1) * 128, g:g + gw],
                  W=[r_wu[j] for j in range(g // 128, (g + gw) // 128)])
    for p in range(NP):
        P.dma('pool', wd[:, p, :], self.ffn_w_down[l, p * 128:(p + 1) * 128, :], W=[r_wd])
    cw = M.tile([44, 4], F32)
    r_cw = Res()
    for j in range(3):
        P.dma('sp', cw[:, :, j], self.ffn_conv_w[l, j].rearrange('(c p) -> p c', p=128), W=[r_cw], allow_slow_non_contiguous=True)
    P.dma('sp', cw[:, :, 3], self.ffn_conv_b[l].rearrange('(c p) -> p c', p=128), W=[r_cw], allow_slow_non_contiguous=True)
    NC = TB + 2
    xb = M.tile([8, NC], F32)
    sq = M.tile([8, NC], BF16)
    h2 = M.tile([8, NC], BF16)
    rstd = M.tile([NC], F32)
    tmp = [M.tile([NC], F32) for _ in range(2)]
    mm = M.tile([NP, TB], BF16)
    cg = [M.tile([TB], F32) for _ in range(3)]
    cv = [M.tile([TB], F32) for _ in range(3)]
    xn = M.tile([8, TB], F32)
    r_xb, r_sq, r_h2, r_rs, r_xn = Res(), Res(), Res(), Res(), Res()
    r_tmp = [Res(), Res()]
    r_mm = [Res() for _ in range(NP)]
    r_cg = [Res(), Res(), Res()]
    r_cv = [Res(), Res(), Res()]
    rps = [Res() for _ in range(8)]
    A = self.A2[l]
    mod = self.mod[l]
    if final_out is not None:
        fg = M.tile([8], F32)
        r_fg = Res()
        P.dma('sp', fg, self.final_g.rearrange('(k p) -> p k', p=128), W=[r_fg], allow_slow_non_contiguous=True)
        sq2 = M.tile([8, TB], BF16)
        r_sq2 = Res()
        fo = M.tile([8, TB], F32)
        r_fo = Res()
    RM = [self.r_mod[l]]
    nblk = 0
    for (s0, s1, c) in segs:
        for t0 in range(s0, s1, TB):
            T = min(TB, s1 - t0)
            lo = max(s0, t0 - 1)
            hi = min(s1, t0 + T + 1)
            nc_ = hi - lo
            c0 = t0 - lo
            P.dma('sp', xb[:, :, 0:nc_], xsrc[:, lo:hi].rearrange('(k p) t -> p k t', p=128), W=[r_xb])
            P.op('act', lambda e, nc_=nc_: e.activation(sq[:, :, 0:nc_], xb[:, :, 0:nc_], AF.Square), R=[r_xb], W=[r_sq])
            psn = M.bank(6)
            for k in range(8):
                P.op('pe', lambda e, k=k, nc_=nc_, psn=psn: e.matmul(psn[:, 0:nc_], self.ones_bf, sq[:, k, 0:nc_], start=(k == 0), stop=(k == 7)),
                     R=[r_sq, self.r_ones], W=[rps[6]])
            P.op('act', lambda e, nc_=nc_, psn=psn: e.activation(rstd[:, 0:nc_], psn[:, 0:nc_], AF.Sqrt, bias=EPS, scale=1.0 / D),
                 R=[rps[6]], W=[r_rs])
            P.op('dve', lambda e, nc_=nc_: e.reciprocal(rstd[:, 0:nc_], rstd[:, 0:nc_]), R=[r_rs], W=[r_rs])
            for k in range(8):
                q = k % 2
                P.op('pool', lambda e, k=k, q=q, nc_=nc_: e.tensor_tensor(
                    tmp[q][:, 0:nc_], xb[:, k, 0:nc_], rstd[:, 0:nc_], ALU.mult), R=[r_xb, r_rs], W=[r_tmp[q]])
                P.op('act', lambda e, k=k, q=q, nc_=nc_, c=c: e.activation(
                    h2[:, k, 0:nc_], tmp[q][:, 0:nc_], AF.Identity, bias=mod[:, 24 + k, c:c + 1], scale=A[:, k, c:c + 1]),
                    R=[r_tmp[q]] + RM, W=[r_h2])
            for p in range(NP):
                q = p % 3
                pg, pv = M.bank(2 * q), M.bank(2 * q + 1)
                for ch, ps, bk in ((p, pg, 2 * q), (NP + p, pv, 2 * q + 1)):
                    for k in range(8):
                        P.op('pe', lambda e, ps=ps, k=k, ch=ch, nc_=nc_: e.matmul(
                            ps[:, 0:nc_], wu[:, k, ch * 128:(ch + 1) * 128], h2[:, k, 0:nc_], start=(k == 0), stop=(k == 7)),
                            R=[r_wu[ch], r_h2], W=[rps[bk]])
                pairs = ((p, pg, 2 * q, cg[q], r_cg[q]), (NP + p, pv, 2 * q + 1, cv[q], r_cv[q]))
                for ch, ps, bk, dst, rd in pairs:
                    P.op('act', lambda e, ps=ps, ch=ch, dst=dst, T=T, c0=c0: e.activation(
                        dst[:, 0:T], ps[:, c0:c0 + T], AF.Identity, bias=cw[:, ch, 3:4], scale=cw[:, ch, 1:2]),
                        R=[rps[bk], r_cw], W=[rd])
                a = 1 if t0 == s0 else 0
                bnd = 1 if t0 + T == s1 else 0
                for ch, ps, bk, dst, rd in pairs:
                    P.op('dve', lambda e, ps=ps, ch=ch, dst=dst, T=T, c0=c0, a=a: e.scalar_tensor_tensor(
                        dst[:, a:T], ps[:, c0 - 1 + a:c0 - 1 + T], cw[:, ch, 0:1], dst[:, a:T], ALU.mult, ALU.add),
                        R=[rps[bk], r_cw, rd], W=[rd])
                for ch, ps, bk, dst, rd in pairs:
                    P.op('dve', lambda e, ps=ps, ch=ch, dst=dst, T=T, c0=c0, bnd=bnd: e.scalar_tensor_tensor(
                        dst[:, 0:T - bnd], ps[:, c0 + 1:c0 + 1 + T - bnd], cw[:, ch, 2:3], dst[:, 0:T - bnd], ALU.mult, ALU.add),
                        R=[rps[bk], r_cw, rd], W=[rd])
                P.op('act', lambda e, q=q, T=T: e.activation(cg[q][:, 0:T], cg[q][:, 0:T], AF.Silu), R=[r_cg[q]], W=[r_cg[q]])
                P.op('pool', lambda e, q=q, p=p, T=T: e.tensor_tensor(mm[:, p, 0:T], cg[q][:, 0:T], cv[q][:, 0:T], ALU.mult),
                     R=[r_cg[q], r_cv[q]], W=[r_mm[p]])
            for dc in range(8):
                bk = 6 + dc % 2
                ps = M.bank(bk)
                for p in range(NP):
                    P.op('pe', lambda e, ps=ps, p=p, dc=dc, T=T: e.matmul(
                        ps[:, 0:T], wd[:, p, dc * 128:(dc + 1) * 128], mm[:, p, 0:T], start=(p == 0), stop=(p == NP - 1)),
                        R=[r_wd, r_mm[p]], W=[rps[bk]])
                P.op('dve', lambda e, ps=ps, dc=dc, T=T, c0=c0, c=c: e.scalar_tensor_tensor(
                    xn[:, dc, 0:T], ps[:, 0:T], mod[:, 40 + dc, c:c + 1], xb[:, dc, c0:c0 + T], ALU.mult, ALU.add),
                    R=[rps[bk], r_xb] + RM, W=[r_xn])
            if final_out is None:
                P.dma('sp', xdst[:, t0:t0 + T].rearrange('(k p) t -> p k t', p=128), xn[:, :, 0:T], R=[r_xn])
            else:
                P.op('act', lambda e, T=T: e.activation(sq2[:, :, 0:T], xn[:, :, 0:T], AF.Square), R=[r_xn], W=[r_sq2])
                psn = M.bank(7)
                for k in range(8):
                    P.op('pe', lambda e, k=k, T=T, psn=psn: e.matmul(psn[:, 0:T], self.ones_bf, sq2[:, k, 0:T], start=(k == 0), stop=(k == 7)),
                         R=[r_sq2, self.r_ones], W=[rps[7]])
                P.op('act', lambda e, T=T, psn=psn: e.activation(rstd[:, 0:T], psn[:, 0:T], AF.Sqrt, bias=EPS, scale=1.0 / D),
                     R=[rps[7], r_rs], W=[r_rs])
                P.op('dve', lambda e, T=T: e.reciprocal(rstd[:, 0:T], rstd[:, 0:T]), R=[r_rs], W=[r_rs])
                for k in range(8):
                    P.op('dve', lambda e, k=k, T=T: e.scalar_tensor_tensor(
                        fo[:, k, 0:T], xn[:, k, 0:T], fg[:, k:k + 1], rstd[:, 0:T], ALU.mult, ALU.mult),
                        R=[r_xn, r_rs, r_fg], W=[r_fo])
                P.dma('sp', final_out[:, t0:t0 + T].rearrange('(k p) t -> p k t', p=128), fo[:, :, 0:T], R=[r_fo])
            nblk += 1
    P.barrier()


Builder.phase_out_proj = _phase_out_proj
Builder.phase_ffn = _phase_ffn


def _rms_feat(self, zt, nk, T, g, dst, rz, rg_, rdst, tmp, r_tmp, sq, r_sq, rstd, r_rs, psb, rps):
    P = self.P
    P.op('act', lambda e: e.activation(sq[:, 0:nk, 0:T], zt[:, :, 0:T], AF.Square), R=[rz], W=[r_sq])
    for k in range(nk):
        P.op('pe', lambda e, k=k: e.matmul(psb[:, 0:T], self.ones_bf, sq[:, k, 0:T], start=(k == 0), stop=(k == nk - 1)),
             R=[r_sq, self.r_ones], W=[rps])
    P.op('act', lambda e: e.activation(rstd[:, 0:T], psb[:, 0:T], AF.Sqrt, bias=EPS, scale=1.0 / (nk * 128)), R=[rps], W=[r_rs])
    P.op('dve', lambda e: e.reciprocal(rstd[:, 0:T], rstd[:, 0:T]), R=[r_rs], W=[r_rs])
    for k in range(nk):
        P.op('dve', lambda e, k=k: e.scalar_tensor_tensor(dst[:, k, :], zt[:, k, 0:T], g[:, k:k + 1], rstd[:, 0:T], ALU.mult, ALU.mult),
             R=[rz, r_rs, rg_], W=[rdst])


def _phase_mla_prep(self):
    P, M = self.P, self.M
    M.reset()
    z1 = self.z1
    wq = M.tile([4, 1536], BF16)
    wkv = M.tile([2, 2048], BF16)
    r_w = Res()
    for k in range(4):
        P.dma('pool', wq[:, k, :], self.od_w_uq[k * 128:(k + 1) * 128, :], W=[r_w])
    for k in range(2):
        for g in range(2):
            P.dma('pool', wkv[:, k, g * 1024:(g + 1) * 1024], self.od_w_ukv[k * 128:(k + 1) * 128, g * 1024:(g + 1) * 1024], W=[r_w])
    gq = M.tile([4], F32)
    gkv = M.tile([2], F32)
    rotm = M.tile([64], F32)
    cos2 = M.tile([S], F32)
    sin2 = M.tile([S], F32)
    r_c = Res()
    P.dma('sp', gq, self.od_q_norm_g.rearrange('(k p) -> p k', p=128), W=[r_c], allow_slow_non_contiguous=True)
    P.dma('sp', gkv, self.od_kv_norm_g.rearrange('(k p) -> p k', p=128), W=[r_c], allow_slow_non_contiguous=True)
    P.dma('sp', rotm[0:64, :], self.rotm, W=[r_c])
    P.dma('sp', cos2[0:64, :], self.cos2, W=[r_c])
    P.dma('sp', sin2[0:64, :], self.sin2, W=[r_c])
    qn = M.tile([4, S], BF16)
    ckv = M.tile([2, NT], BF16)
    r_qn = [Res() for _ in range(4)]
    r_ckv = [Res() for _ in range(5)]
    zt = [M.tile([4, 512], F32) for _ in range(2)]
    r_zt = [Res(), Res()]
    sq = M.tile([4, 512], BF16)
    rstd = M.tile([512], F32)
    tmp = None
    r_sq, r_rs = Res(), Res()
    rps = [Res() for _ in range(8)]
    blocks = token_blocks()
    for i, (t0, T, c) in enumerate(blocks[:4]):
        q = i % 2
        P.dma('sp', zt[q][:, :, 0:T], z1[0:512, t0:t0 + T].rearrange('(k p) t -> p k t', p=128), W=[r_zt[q]])
        self.rms_feat(zt[q], 4, T, gq, qn[:, :, t0:t0 + T], r_zt[q], r_c, r_qn[i], None, None, sq, r_sq, rstd, r_rs, M.bank(7), rps[7])
    for i, (t0, T, c) in enumerate(blocks):
        q = i % 2
        P.dma('sp', zt[q][:, 0:2, 0:T], z1[512:768, t0:t0 + T].rearrange('(k p) t -> p k t', p=128), W=[r_zt[q]])
        self.rms_feat(zt[q][:, 0:2, :], 2, T, gkv, ckv[:, :, t0:t0 + T], r_zt[q], r_c, r_ckv[i], None, None, sq, r_sq, rstd, r_rs, M.bank(7), rps[7])
    stb = [M.tile([512], BF16) for _ in range(4)]
    r_stb = [Res() for _ in range(4)]
    xr = [M.tile([512], F32) for _ in range(2)]
    r_xr = [Res(), Res()]
    tt_ = [M.tile([512], F32) for _ in range(2)]
    r_tt = [Res(), Res()]
    n = 0

    def rope_out(src_sb, r_src, t0, T, dst_dram, latent, q):
        nonlocal n
        b = n % 4
        n += 1
        if latent:
            pr = M.bank(4 + q)
            P.op('pe', lambda e: e.matmul(pr[0:64, 0:T], rotm[0:64, :], src_sb[0:64, 0:T], start=True, stop=True),
                 R=[r_src, r_c], W=[rps[4 + q]])
            P.op('dve', lambda e: e.tensor_tensor(tt_[q][0:64, 0:T], pr[0:64, 0:T], sin2[0:64, t0:t0 + T], ALU.mult),
                 R=[rps[4 + q], r_c], W=[r_tt[q]])
            P.op('dve', lambda e: e.tensor_tensor(src_sb[0:64, 0:T], src_sb[0:64, 0:T], cos2[0:64, t0:t0 + T], ALU.mult),
                 R=[r_src, r_c], W=[r_src])
            P.op('dve', lambda e: e.tensor_tensor(stb[b][0:64, 0:T], src_sb[0:64, 0:T], tt_[q][0:64, 0:T], ALU.add),
                 R=[r_src, r_tt[q]], W=[r_stb[b]])
        else:
            P.op('dve', lambda e: e.tensor_copy(stb[b][0:64, 0:T], src_sb[0:64, 0:T]), R=[r_src], W=[r_stb[b]])
        P.dma('sp', dst_dram, stb[b][0:64, 0:T], R=[r_stb[b]])

    for h in range(8):
        for i, (t0, T, c) in enumerate(blocks[:4]):
            b = n % 4
            n += 1
            ps = M.bank(b)
            for k in range(4):
                P.op('pe', lambda e, ps=ps, k=k, h=h, t0=t0, T=T: e.matmul(
                    ps[:, 0:T], wq[:, k, h * 192:h * 192 + 128], qn[:, k, t0:t0 + T], start=(k == 0), stop=(k == 3)),
                    R=[r_w, r_qn[i]], W=[rps[b]])
            P.op('act', lambda e, ps=ps, b=b, T=T: e.copy(stb[b][:, 0:T], ps[:, 0:T]), R=[rps[b]], W=[r_stb[b]])
            P.dma('sp', self.qT[h, 0:128, t0:t0 + T], stb[b][:, 0:T], R=[r_stb[b]])
            q = i % 2
            b2 = n % 4
            n += 1
            ps2 = M.bank(b2)
            for k in range(4):
                P.op('pe', lambda e, ps2=ps2, k=k, h=h, t0=t0, T=T: e.matmul(
                    ps2[0:64, 0:T], wq[:, k, h * 192 + 128:h * 192 + 192], qn[:, k, t0:t0 + T], start=(k == 0), stop=(k == 3)),
                    R=[r_w, r_qn[i]], W=[rps[b2]])
            P.op('act', lambda e, ps2=ps2, q=q, T=T: e.copy(xr[q][0:64, 0:T], ps2[0:64, 0:T]), R=[rps[b2]], W=[r_xr[q]])
            rope_out(xr[q], r_xr[q], t0, T, self.qT[h, 128:192, t0:t0 + T], True, q)
    for h in range(8):
        for i, (t0, T, c) in enumerate(blocks):
            b = n % 4
            n += 1
            ps = M.bank(b)
            for k in range(2):
                P.op('pe', lambda e, ps=ps, k=k, h=h, t0=t0, T=T: e.matmul(
                    ps[:, 0:T], wkv[:, k, h * 128:(h + 1) * 128], ckv[:, k, t0:t0 + T], start=(k == 0), stop=(k == 1)),
                    R=[r_w, r_ckv[i]], W=[rps[b]])
            P.op('act', lambda e, ps=ps, b=b, T=T: e.copy(stb[b][:, 0:T], ps[:, 0:T]), R=[rps[b]], W=[r_stb[b]])
            P.dma('sp', self.kT[h, :, t0:t0 + T], stb[b][:, 0:T], R=[r_stb[b]])
    for tt in range(NT // 128):
        i = min(tt // 4, 4)
        for g in range(2):
            b = n % 4
            n += 1
            ps = M.bank(b)
            for k in range(2):
                P.op('pe', lambda e, ps=ps, k=k, g=g, tt=tt: e.matmul(
                    ps, ckv[:, k, tt * 128:(tt + 1) * 128], wkv[:, k, 1024 + g * 512:1024 + (g + 1) * 512], start=(k == 0), stop=(k == 1)),
                    R=[r_w, r_ckv[i]], W=[rps[b]])
            P.op('dve', lambda e, ps=ps, b=b: e.tensor_copy(stb[b], ps), R=[rps[b]], W=[r_stb[b]])
            P.dma('sp', self.Vtok[tt * 128:(tt + 1) * 128, g * 512:(g + 1) * 512], stb[b], R=[r_stb[b]])
    for i, (t0, T, c) in enumerate(blocks):
        q = i % 2
        P.dma('sp', xr[q][0:64, 0:T], z1[768:832, t0:t0 + T], W=[r_xr[q]])
        rope_out(xr[q], r_xr[q], t0, T, self.krT[:, t0:t0 + T], c == 0, q)
    P.barrier()


def _phase_attn(self):
    P, M = self.P, self.M
    M.reset()
    SC = 192.0 ** -0.5
    V = M.tile([18, 1024], BF16)
    kra = M.tile([NT], BF16)
    colsel = M.tile([128], BF16)
    r_v, r_kra, r_cs = Res(), Res(), Res()
    for tt in range(18):
        P.dma('sp', V[:, tt, :], self.Vtok[tt * 128:(tt + 1) * 128, :], W=[r_v])
    P.op('dve', lambda e: e.memset(kra[64:128, :], 0.0), W=[r_kra])
    P.op('dve', lambda e: e.memset(kra[64:65, :], 1.0), W=[r_kra])
    P.dma('sp', kra[0:64, :], self.krT, W=[r_kra])
    P.op('dve', lambda e: e.memset(colsel, 0.0), W=[r_cs])
    P.op('dve', lambda e: e.memset(colsel[:, 64:65], 1.0), W=[r_cs])
    qn = [M.tile([S], BF16) for _ in range(2)]
    qrz = [M.tile([S], BF16) for _ in range(2)]
    kn = [M.tile([NT], BF16) for _ in range(2)]
    r_hd = [Res(), Res()]
    for i in range(2):
        P.op('dve', lambda e, i=i: e.memset(qrz[i][64:128, :], 0.0), W=[r_hd[i]])
    NB = 3
    qra = [M.tile([512], BF16) for _ in range(NB)]
    r_qra = [Res() for _ in range(NB)]
    mx = [M.tile([8], F32) for _ in range(4)]
    r_mx = [Res() for _ in range(4)]
    dg = [M.tile([128], BF16) for _ in range(2)]
    r_dg = [Res(), Res()]
    PTt = [M.tile([512], BF16) for _ in range(4)]
    r_PT = [Res() for _ in range(4)]
    rs = [M.tile([512], F32) for _ in range(2)]
    r_rs = [Res(), Res()]
    ob = [M.tile([512], BF16) for _ in range(2)]
    r_ob = [Res(), Res()]
    rps = [Res() for _ in range(8)]
    kblocks = token_blocks()
    cnt = {'sc': 0, 'mx': 0, 'dg': 0, 'p2': 0, 'pt': 0}
    if getattr(self, 'attn_hook', None):
        self.attn_hook()

    def load_head(h):
        i = h % 2
        P.dma('sp', qn[i], self.qT[h, 0:128, :], W=[r_hd[i]])
        P.dma('sp', qrz[i][0:64, :], self.qT[h, 128:192, :], W=[r_hd[i]])
        P.dma('sp', kn[i], self.kT[h], W=[r_hd[i]])

    def pass1(h, qb, bi):
        i = h % 2
        u = bi % NB
        P.dma('sp', qra[u][0:64, :], self.qT[h, 128:192, qb * 512:(qb + 1) * 512], W=[r_qra[u]])
        for qt in range(4):
            qs = slice(qb * 512 + qt * 128, qb * 512 + (qt + 1) * 128)
            m_ = cnt['mx'] % 4
            cnt['mx'] += 1
            for j, (k0, T, c) in enumerate(kblocks):
                b = cnt['sc'] % 3
                cnt['sc'] += 1
                ps = M.bank(b)
                P.op('pe', lambda e, ps=ps, i=i, qs=qs, k0=k0, T=T: e.matmul(ps[:, 0:T], qn[i][:, qs], kn[i][:, k0:k0 + T], start=True, stop=False),
                     R=[r_hd[i]], W=[rps[b]])
                P.op('pe', lambda e, ps=ps, i=i, qs=qs, k0=k0, T=T: e.matmul(ps[:, 0:T], qrz[i][:, qs], kra[:, k0:k0 + T], start=False, stop=True),
                     R=[r_hd[i], r_kra], W=[rps[b]])
                P.op('dve', lambda e, ps=ps, m_=m_, j=j, T=T: e.tensor_reduce(mx[m_][:, j:j + 1], ps[:, 0:T], AX.X, ALU.max),
                     R=[rps[b]], W=[r_mx[m_]])
                yield
            P.op('dve', lambda e, m_=m_: e.tensor_reduce(mx[m_][:, 5:6], mx[m_][:, 0:5], AX.X, ALU.max, negate=True),
                 R=[r_mx[m_]], W=[r_mx[m_]])
            g = cnt['dg'] % 2
            cnt['dg'] += 1
            P.op('dve', lambda e, g=g, m_=m_: e.tensor_scalar(dg[g], self.idbf, mx[m_][:, 5:6], None, ALU.mult),
                 R=[r_mx[m_], self.r_idbf], W=[r_dg[g]])
            P.op('pe', lambda e, g=g, qt=qt: e.matmul(M.bank(3)[:, qt * 128:(qt + 1) * 128], colsel, dg[g], start=True, stop=True),
                 R=[r_cs, r_dg[g]], W=[rps[3]])
        P.op('act', lambda e, u=u: e.copy(qra[u][64:128, :], M.bank(3)[64:128, :]), R=[rps[3]], W=[r_qra[u]])

    def pass2(h, qb, bi):
        i = h % 2
        u = bi % NB
        qcols = slice(qb * 512, (qb + 1) * 512)
        def emit_s(kt):
            b = 4 + kt % 2
            ps = M.bank(b)
            ks = slice(kt * 128, (kt + 1) * 128)
            P.op('pe', lambda e, ps=ps, i=i, ks=ks, qcols=qcols: e.matmul(ps, kn[i][:, ks], qn[i][:, qcols], start=True, stop=False),
                 R=[r_hd[i]], W=[rps[b]])
            P.op('pe', lambda e, ps=ps, ks=ks, u=u: e.matmul(ps, kra[:, ks], qra[u], start=False, stop=True),
                 R=[r_kra, r_qra[u]], W=[rps[b]])

        emit_s(0)
        for kt in range(18):
            if kt + 1 < 18:
                emit_s(kt + 1)
            b = 4 + kt % 2
            ps = M.bank(b)
            t = cnt['pt'] % 4
            cnt['pt'] += 1
            P.op('act', lambda e, ps=ps, t=t: e.activation(PTt[t], ps, AF.Exp, scale=SC), R=[rps[b]], W=[r_PT[t]])
            P.op('pe', lambda e, t=t, kt=kt: e.matmul(M.bank(6), self.ones_bf, PTt[t], start=(kt == 0), stop=(kt == 17)),
                 R=[r_PT[t], self.r_ones], W=[rps[6]])
            P.op('pe', lambda e, t=t, kt=kt, h=h: e.matmul(M.bank(7), V[:, kt, h * 128:(h + 1) * 128], PTt[t], start=(kt == 0), stop=(kt == 17)),
                 R=[r_PT[t], r_v], W=[rps[7]])
            yield
        o = bi % 2
        P.op('dve', lambda e, o=o: e.reciprocal(rs[o], M.bank(6)), R=[rps[6]], W=[r_rs[o]])
        P.op('dve', lambda e, o=o: e.tensor_tensor(ob[o], M.bank(7), rs[o], ALU.mult), R=[rps[7], r_rs[o]], W=[r_ob[o]])
        P.dma('sp', self.oT[h * 128:(h + 1) * 128, qcols], ob[o], R=[r_ob[o]])

    blocks = [(h, qb) for h in range(8) for qb in range(4)]
    def run_both(g2, g1):
        a, b_ = True, True
        while a or b_:
            if a:
                try:
                    next(g2)
                except StopIteration:
                    a = False
            if b_:
                try:
                    next(g1)
                except StopIteration:
                    b_ = False

    load_head(0)
    for _ in pass1(0, 0, 0):
        pass
    for bi, (h, qb) in enumerate(blocks):
        g2 = pass2(h, qb, bi)
        if bi + 1 < len(blocks):
            h1, qb1 = blocks[bi + 1]
            if qb1 == 0:
                pass
            g1 = pass1(h1, qb1, bi + 1)
        else:
            g1 = iter(())
        run_both(g2, g1)
        if qb == 0 and h + 1 < 8:
            load_head(h + 1)
    P.barrier()


Builder.rms_feat = _rms_feat
Builder.phase_mla_prep = _phase_mla_prep
Builder.phase_attn = _phase_attn


def mla_consts():
    f32 = np.float32
    n = S
    row = np.repeat(np.arange(n // 64, dtype=f32), 64)
    col = np.tile(np.arange(64, dtype=f32), n // 64)
    inv = (f32(10000.0) ** (-np.arange(16, dtype=f32) / f32(16))).astype(f32)
    ang = np.concatenate([row[:, None] * inv[None, :], col[:, None] * inv[None, :]], axis=-1).astype(f32)
    cos2 = np.repeat(np.cos(ang).astype(f32), 2, axis=1).T
    sin2 = np.repeat(np.sin(ang).astype(f32), 2, axis=1).T
    rotm = np.zeros((64, 64), f32)
    for i in range(32):
        rotm[2 * i + 1, 2 * i] = -1.0
        rotm[2 * i, 2 * i + 1] = 1.0
    return {'cos2': np.ascontiguousarray(cos2), 'sin2': np.ascontiguousarray(sin2), 'rotm': rotm}

def make_inputs(inputs, b):
    x = np.asarray(inputs['x'][b], np.float32)
    ctx = np.asarray(inputs['ctx'][b], np.float32)
    m = {}
    m['xT'] = np.ascontiguousarray(np.concatenate([x.T, ctx.T], axis=1))
    m['cvec'] = np.ascontiguousarray(np.stack([inputs['c'][b], inputs['c_ctx']]).astype(np.float32))
    m['ada_w'] = np.ascontiguousarray(inputs['ada_w'], np.float32)
    m['ada_b'] = np.ascontiguousarray(inputs['ada_b'], np.float32)
    m['norm1_g'] = np.ascontiguousarray(inputs['norm1_g'], np.float32)
    m['norm2_g'] = np.ascontiguousarray(inputs['norm2_g'], np.float32)
    m['ev_w_in'] = np.ascontiguousarray(inputs['ev_w_in'][0], np.float32)
    m['ident'] = np.eye(128, dtype=np.float32)
    f = lambda k: np.ascontiguousarray(inputs[k][0], np.float32)
    m['rg_conv_w'] = f('ev_rg_conv_w'); m['rg_conv_b'] = f('ev_rg_conv_b')
    m['rg_wa'] = f('ev_rg_wa'); m['rg_ba'] = f('ev_rg_ba'); m['rg_wx'] = f('ev_rg_wx'); m['rg_bx'] = f('ev_rg_bx')
    m['rg_lambda'] = f('ev_rg_lambda')
    m['hy_conv_w'] = f('ev_hy_conv_w'); m['hy_conv_b'] = f('ev_hy_conv_b')
    m['f_w1'] = f('ev_hy_f_w1'); m['f_b1'] = f('ev_hy_f_b1'); m['f_freq'] = f('ev_hy_f_freq')
    m['f_w2'] = f('ev_hy_f_w2'); m['f_b2'] = f('ev_hy_f_b2'); m['f_w3'] = f('ev_hy_f_w3')
    m['hy_bias'] = f('ev_hy_bias')
    m.update(HC())
    m['ev_w_out'] = f('ev_w_out')
    wi = np.zeros((D, 896), np.float32)
    wi[:, :832] = inputs['od_w_in'][0]
    m['od_w_in'] = wi
    m['od_q_norm_g'] = f('od_q_norm_g'); m['od_kv_norm_g'] = f('od_kv_norm_g')
    m['od_w_uq'] = f('od_w_uq'); m['od_w_o'] = f('od_w_o')
    wkv = np.asarray(inputs['od_w_ukv'][0], np.float32).reshape(256, 8, 256)
    m['od_w_ukv'] = np.ascontiguousarray(np.concatenate([wkv[:, :, :128].reshape(256, 1024), wkv[:, :, 128:].reshape(256, 1024)], axis=1))
    m.update(MC())
    for k in ('ffn_w_up', 'ffn_conv_w', 'ffn_conv_b', 'ffn_w_down', 'final_g'):
        m[k] = np.ascontiguousarray(inputs[k], np.float32)
    return m


_HC = {}


def HC():
    if not _HC:
        _HC.update(hyena_consts())
    return _HC


_MC = {}


def MC():
    if not _MC:
        _MC.update(mla_consts())
    return _MC


def kernel(**inputs):
    inputs = {k: np.asarray(v) for k, v in inputs.items()}
    nb = inputs['x'].shape[0]
    B = Builder(dbg=True)
    nc = B.build()
    shared = make_inputs(inputs, 0)
    in_maps = []
    for b in range(nb):
        m = dict(shared)
        x = np.asarray(inputs['x'][b], np.float32)
        ctx = np.asarray(inputs['ctx'][b], np.float32)
        m['xT'] = np.ascontiguousarray(np.concatenate([x.T, ctx.T], axis=1))
        m['cvec'] = np.ascontiguousarray(np.stack([inputs['c'][b], inputs['c_ctx']]).astype(np.float32))
        in_maps.append({k: v for k, v in m.items() if k in B.din})
    res = run_bass_kernel_spmd(nc, in_maps, core_ids=list(range(nb)))
    out = np.stack([np.asarray(res.results[b]['outT'], np.float32).T for b in range(nb)], axis=0)
    return np.ascontiguousarray(out)
```

```python
import contextlib
import math
import numpy as np
import ml_dtypes
import concourse.bass as bass
import concourse.mybir as mybir
from concourse.bass_utils import run_bass_kernel_spmd

dt = mybir.dt
F32 = dt.float32
BF16 = dt.bfloat16
AF = mybir.ActivationFunctionType
ALU = mybir.AluOpType
AX = mybir.AxisListType

D = 1024
S = 2048
CT = 256
NT = S + CT
EPS = 1e-6
ENG = ['pe', 'dve', 'act', 'pool', 'sp']
SAME_SYNC = True


class Res:
    __slots__ = ('w', 'r')

    def __init__(self):
        self.w = None
        self.r = {}


class Prog:
    def __init__(self, nc, stack, n_dma=40):
        self.nc = nc
        self.ops = {e: [] for e in ENG}
        self.seq = {e: 0 for e in ENG}
        self.known = {e: {} for e in ENG}
        self.esem = {e: stack.enter_context(nc.semaphore('s_' + e)) for e in ENG}
        self.dsem = [stack.enter_context(nc.semaphore('d%d' % i)) for i in range(n_dma)]
        self.dtgt = [0] * n_dma
        self.drr = 0

    def _sem(self, key):
        return self.esem[key[1]] if key[0] == 'e' else self.dsem[key[1]]

    def _collect(self, eng, R, W):
        need = {}

        def add(key, val):
            if val > need.get(key, 0):
                need[key] = val
        for r in R:
            if r.w:
                add(*r.w)
        for w in W:
            if w.w:
                add(*w.w)
            for k, v in w.r.items():
                add(k, v)
        waits = []
        kn = self.known[eng]
        for key, val in need.items():
            if key == ('e', eng) and (eng == 'pe' or not SAME_SYNC):
                continue
            if kn.get(key, 0) >= val:
                continue
            kn[key] = val
            waits.append((self._sem(key), val))
        return waits

    def _mark(self, ev, R, W):
        for r in R:
            if ev[1] > r.r.get(ev[0], 0):
                r.r[ev[0]] = ev[1]
        for w in W:
            w.w = ev
            w.r = {}

    def op(self, eng, fn, R=(), W=()):
        waits = self._collect(eng, R, W)
        self.seq[eng] += 1
        ev = (('e', eng), self.seq[eng])
        es = self.esem[eng]

        def emit(e):
            for s, v in waits:
                e.wait_ge(s, v)
            fn(e).then_inc(es, 1)
        self.ops[eng].append(emit)
        self._mark(ev, R, W)

    def dma(self, q, out, in_, R=(), W=(), **kw):
        waits = self._collect(q, R, W)
        si = self.drr
        self.drr = (self.drr + 1) % len(self.dsem)
        prev = self.dtgt[si]
        key = ('d', si)
        if prev > 0 and self.known[q].get(key, 0) < prev:
            self.known[q][key] = prev
            waits.append((self.dsem[si], prev))
        tgt = prev + 16
        self.dtgt[si] = tgt
        ds = self.dsem[si]

        def emit(e):
            for s, v in waits:
                e.wait_ge(s, v)
            e.dma_start(out=out, in_=in_, **kw).then_inc(ds, 16)
        self.ops[q].append(emit)
        self._mark((key, tgt), R, W)

    def barrier(self):
        for e in ENG:
            waits = []
            kn = self.known[e]
            for f in ENG:
                if f == e and e == 'pe':
                    continue
                v = self.seq[f]
                key = ('e', f)
                if v > kn.get(key, 0):
                    kn[key] = v
                    waits.append((self.esem[f], v))
            for si, t in enumerate(self.dtgt):
                key = ('d', si)
                if t > kn.get(key, 0):
                    kn[key] = t
                    waits.append((self.dsem[si], t))

            def emit(eng, waits=waits):
                for s, v in waits:
                    eng.wait_ge(s, v)
            self.ops[e].append(emit)

    def play(self):
        with self.nc.Block() as blk:
            @blk.tensor
            def _(e):
                for f in self.ops['pe']:
                    f(e)

            @blk.vector
            def _(e):
                for f in self.ops['dve']:
                    f(e)

            @blk.scalar
            def _(e):
                for f in self.ops['act']:
                    f(e)

            @blk.gpsimd
            def _(e):
                for f in self.ops['pool']:
                    f(e)

            @blk.sync
            def _(e):
                for f in self.ops['sp']:
                    f(e)


class Mem:
    def __init__(self, nc, stack, words=49152):
        self.arena = stack.enter_context(nc.sbuf_tensor('arena', [128, words], F32))
        self.psum = stack.enter_context(nc.psum_tensor('psum', [128, 4096], F32))
        self.words = words
        self.ptr = 0
        self.mark = 0
        self.v32 = self.arena
        self.v16 = self.arena.bitcast(BF16)
        self.p16 = self.psum.bitcast(BF16)

    def tile(self, shape, dtype=F32):
        shape = list(shape) if isinstance(shape, (list, tuple)) else [shape]
        n = int(np.prod(shape))
        esz = 4 if dtype == F32 else 2
        self.ptr = (self.ptr + 63) // 64 * 64
        off = self.ptr // esz
        self.ptr += n * esz
        assert self.ptr <= self.words * 4, 'SBUF arena overflow %d' % self.ptr
        base = self.v32 if dtype == F32 else self.v16
        ap = base[:, off:off + n]
        if len(shape) == 2:
            ap = ap.rearrange('p (a b) -> p a b', a=shape[0])
        elif len(shape) == 3:
            ap = ap.rearrange('p (a b c) -> p a b c', a=shape[0], b=shape[1])
        return ap

    def persist(self):
        self.mark = self.ptr

    def reset(self):
        self.ptr = self.mark

    def bank(self, i, n=512, dtype=F32, off=0):
        if dtype == F32:
            return self.psum[:, i * 512 + off:i * 512 + off + n]
        return self.p16[:, i * 1024 + off:i * 1024 + off + n]


def token_blocks():
    return [(0, 512, 0), (512, 512, 0), (1024, 512, 0), (1536, 512, 0), (2048, 256, 1)]


class Builder:
    def __init__(self, dbg=False, upto=99, mode=None, feed=()):
        self.dbg = dbg
        self.upto = upto
        self.mode = mode
        self.feed = set(feed)
        self.nc = bass.Bass('TRN2', target_bir_lowering=False)
        self.stack = contextlib.ExitStack()
        self.din = {}
        self.dout = {}

    def inp(self, name, shape, dtype=F32):
        t = self.nc.dram_tensor(name, list(shape), dtype, kind='ExternalInput').ap()
        self.din[name] = t
        return t

    def scratch(self, name, shape, dtype=F32, out=False):
        kind = 'ExternalOutput' if (out or self.dbg) else 'Internal'
        if name in self.feed:
            kind = 'ExternalInput'
        t = self.nc.dram_tensor(name, list(shape), dtype, kind=kind).ap()
        if kind == 'ExternalInput':
            self.din[name] = t
        if kind == 'ExternalOutput':
            self.dout[name] = t
        return t

    def build(self):
        nc = self.nc
        st = self.stack
        P = self.P = Prog(nc, st)
        M = self.M = Mem(nc, st)
        I = self.inp
        self.xT = I('xT', [D, NT])
        self.cvec = I('cvec', [2, D])
        self.ada_w = I('ada_w', [2, D, 6 * D])
        self.ada_b = I('ada_b', [2, 6 * D])
        self.norm1_g = I('norm1_g', [2, D])
        self.norm2_g = I('norm2_g', [2, D])
        self.ev_w_in = I('ev_w_in', [D, 2560])
        self.ident = I('ident', [128, 128])
        self.rg_conv_w = I('rg_conv_w', [4, 512])
        self.rg_conv_b = I('rg_conv_b', [512])
        self.rg_wa = I('rg_wa', [2, 8, 64, 64])
        self.rg_ba = I('rg_ba', [2, 512])
        self.rg_wx = I('rg_wx', [2, 8, 64, 64])
        self.rg_bx = I('rg_bx', [2, 512])
        self.rg_lambda = I('rg_lambda', [2, 512])
        self.hy_conv_w = I('hy_conv_w', [3, 1536])
        self.hy_conv_b = I('hy_conv_b', [1536])
        self.f_w1 = I('f_w1', [33, 64]); self.f_b1 = I('f_b1', [64]); self.f_freq = I('f_freq', [64])
        self.f_w2 = I('f_w2', [64, 64]); self.f_b2 = I('f_b2', [64]); self.f_w3 = I('f_w3', [64, 2048])
        self.hy_bias = I('hy_bias', [2, 512])
        self.ev_w_out = I('ev_w_out', [D, D])
        self.ffn_w_up = I('ffn_w_up', [2, D, 5632])
        self.ffn_conv_w = I('ffn_conv_w', [2, 3, 5632])
        self.ffn_conv_b = I('ffn_conv_b', [2, 5632])
        self.ffn_w_down = I('ffn_w_down', [2, 2816, D])
        self.final_g = I('final_g', [D])
        self.xa0 = self.scratch('xa0', [D, NT])
        self.xb0 = self.scratch('xb0', [D, NT])
        self.od_w_in = I('od_w_in', [D, 896])
        self.od_q_norm_g = I('od_q_norm_g', [512]); self.od_kv_norm_g = I('od_kv_norm_g', [256])
        self.od_w_uq = I('od_w_uq', [512, 1536]); self.od_w_ukv = I('od_w_ukv', [256, 2048])
        self.od_w_o = I('od_w_o', [D, D])
        self.cos2 = I('cos2', [64, S]); self.sin2 = I('sin2', [64, S]); self.rotm = I('rotm', [64, 64])
        self.z1 = self.scratch('z1', [896, NT])
        self.qT = self.scratch('qT', [8, 192, S], BF16)
        self.kT = self.scratch('kT', [8, 128, NT], BF16)
        self.krT = self.scratch('krT', [64, NT], BF16)
        self.Vtok = self.scratch('Vtok', [NT, D], BF16)
        self.oT = self.scratch('oT', [D, S], BF16)
        self.xa1 = self.scratch('xa1', [D, S])
        self.outT = self.scratch('outT', [D, S], out=True)
        self.hc = {}
        for n in (2048, 256):
            nj = n // 128
            self.hc[n] = dict(zT=I('zT%d' % n, [33, n]), dec=I('dec%d' % n, [n, 512]), decb=I('decb%d' % n, [n, 512]),
                              C=I('Ccol%d' % n, [nj, 128, nj, 128], BF16), S=I('Scol%d' % n, [nj, 128, nj, 128], BF16),
                              wk=I('wk%d' % n, [128, nj]), Kf=self.scratch('Kf%d' % n, [2, 2, n, 512]))
        self.vtok = self.scratch('vtok', [3, NT, 512])
        self.u2tok = self.scratch('u2tok', [NT, 512])
        self.z0 = self.scratch('z0', [2560, NT])
        self.mixT = self.scratch('mixT', [D, NT], BF16)
        self.consts()
        if self.mode == 'attn':
            self.phase_attn()
            P.barrier()
            P.play()
            return nc
        if self.upto >= 1:
            for l in range(2):
                self.phase_mod(l)
        if self.upto >= 2:
            self.phase_norm_proj(0, self.xT, 1, self.ev_w_in, 2560, self.z0)
        if self.upto >= 3:
            self.phase_rg()
        if self.upto >= 4:
            for n in (2048, 256):
                h = self.hc[n]
                self.phase_hy_filters(n, h['zT'], h['dec'], h['decb'], h['C'], h['S'], h['wk'], h['Kf'])
            self.phase_hy_prep()
            for n, tok0 in ((2048, 0), (256, S)):
                h = self.hc[n]
                self.phase_hy_conv(n, tok0, h['C'], h['S'], h['Kf'])
        if self.upto >= 5:
            self.phase_out_proj(0, self.mixT, self.ev_w_out, self.xT, self.xa0, token_blocks())
        if self.upto >= 6:
            self.phase_ffn(0, self.xa0, self.xb0, [(0, S, 0), (S, NT, 1)])
        lat = token_blocks()[:4]
        if self.upto >= 7:
            self.phase_norm_proj(1, self.xb0, 1, self.od_w_in, 896, self.z1)
            self.phase_mla_prep()
        if self.upto >= 8:
            self.phase_attn()
            self.phase_out_proj(1, self.oT, self.od_w_o, self.xb0, self.xa1, lat)
        if self.upto >= 9:
            self.phase_ffn(1, self.xa1, None, [(0, S, 0)], final_out=self.outT)
        P.barrier()
        P.play()
        return nc

    def consts(self):
        P, M = self.P, self.M
        self.ones_bf = M.tile([128], BF16)
        self.r_ones = Res()
        P.op('dve', lambda e: e.memset(self.ones_bf, 1.0), W=[self.r_ones])
        self.id32 = M.tile([128], F32)
        self.idbf = M.tile([128], BF16)
        self.r_id = Res()
        P.dma('sp', self.id32, self.ident, W=[self.r_id])
        self.r_idbf = Res()
        P.op('dve', lambda e: e.tensor_copy(self.idbf, self.id32), R=[self.r_id], W=[self.r_idbf])
        self.mod = [M.tile([48, 2], F32) for _ in range(2)]
        self.A1 = [M.tile([8, 2], F32) for _ in range(2)]
        self.A2 = [M.tile([8, 2], F32) for _ in range(2)]
        self.r_mod = [Res() for _ in range(2)]
        M.persist()

    def phase_mod(self, l):
        P, M = self.P, self.M
        M.reset()
        cv = M.tile([8, 2], F32)
        sT = M.tile([8, 2], BF16)
        bT = M.tile([48], F32)
        g1 = M.tile([8], F32)
        g2 = M.tile([8], F32)
        r_cv, r_sT, r_b, r_g = Res(), Res(), Res(), Res()
        for j in range(2):
            P.dma('sp', cv[:, :, j], self.cvec[j].rearrange('(k p) -> p k', p=128), W=[r_cv],
                  allow_slow_non_contiguous=True)
        P.dma('sp', bT, self.ada_b[l].rearrange('(j p) -> p j', p=128), W=[r_b], allow_slow_non_contiguous=True)
        P.dma('sp', g1, self.norm1_g[l].rearrange('(k p) -> p k', p=128), W=[r_g], allow_slow_non_contiguous=True)
        P.dma('sp', g2, self.norm2_g[l].rearrange('(k p) -> p k', p=128), W=[r_g], allow_slow_non_contiguous=True)
        P.op('act', lambda e: e.activation(sT, cv, AF.Silu), R=[r_cv], W=[r_sT])
        wt = [M.tile([8, 2048], BF16) for _ in range(2)]
        r_wt = [Res(), Res()]
        ps = M.bank(0, 96).rearrange('p (j c) -> p j c', c=2)
        r_ps = Res()
        for sec in range(3):
            w = wt[sec % 2]
            rw = r_wt[sec % 2]
            for k in range(8):
                P.dma('pool', w[:, k, :], self.ada_w[l, k * 128:(k + 1) * 128, sec * 2048:(sec + 1) * 2048], W=[rw])
            for fc in range(16):
                j = sec * 16 + fc
                for k in range(8):
                    P.op('pe', lambda e, w=w, k=k, fc=fc, j=j: e.matmul(
                        ps[:, j, :], w[:, k, fc * 128:(fc + 1) * 128], sT[:, k, :],
                        start=(k == 0), stop=(k == 7)), R=[rw, r_sT], W=[r_ps])
        mod = self.mod[l]
        rm = self.r_mod[l]
        for c in range(2):
            P.op('dve', lambda e, c=c: e.tensor_tensor(mod[:, :, c], ps[:, :, c], bT, ALU.add),
                 R=[r_ps, r_b], W=[rm])
        for c in range(2):
            P.op('dve', lambda e, c=c: e.scalar_tensor_tensor(
                self.A1[l][:, :, c], mod[:, 8:16, c], 1.0, g1, ALU.add, ALU.mult), R=[rm, r_g], W=[rm])
            P.op('dve', lambda e, c=c: e.scalar_tensor_tensor(
                self.A2[l][:, :, c], mod[:, 32:40, c], 1.0, g2, ALU.add, ALU.mult), R=[rm, r_g], W=[rm])
        P.barrier()

    def norm_block(self, src, t0, T, c, A, B, hdst, r_hdst, bufs, i):
        P, M = self.P, self.M
        xb, sq, rstd, tmp, rx, rsq, rrs, rtmp, psb, rps = bufs[i % 2]
        P.dma('sp', xb[:, :, 0:T], src[:, t0:t0 + T].rearrange('(k p) t -> p k t', p=128), W=[rx])
        P.op('act', lambda e: e.activation(sq[:, :, 0:T], xb[:, :, 0:T], AF.Square), R=[rx], W=[rsq])
        for k in range(8):
            P.op('pe', lambda e, k=k: e.matmul(psb[:, 0:T], self.ones_bf, sq[:, k, 0:T],
                                               start=(k == 0), stop=(k == 7)),
                 R=[rsq, self.r_ones], W=[rps])
        P.op('act', lambda e: e.activation(rstd[:, 0:T], psb[:, 0:T], AF.Sqrt, bias=EPS, scale=1.0 / D),
             R=[rps], W=[rrs])
        P.op('dve', lambda e: e.reciprocal(rstd[:, 0:T], rstd[:, 0:T]), R=[rrs], W=[rrs])
        for k in range(8):
            P.op('dve', lambda e, k=k: e.scalar_tensor_tensor(
                tmp[:, k, 0:T], xb[:, k, 0:T], A[:, k, c:c + 1], rstd[:, 0:T], ALU.mult, ALU.mult),
                R=[rx, rrs, self.r_mod[0], self.r_mod[1]], W=[rtmp[k]])
            P.op('act', lambda e, k=k: e.activation(hdst[:, k, :], tmp[:, k, 0:T], AF.Identity,
                                                    bias=B[:, k, c:c + 1]),
                 R=[rtmp[k], self.r_mod[0], self.r_mod[1]], W=[r_hdst])

    def norm_bufs(self, psbanks):
        M = self.M
        bufs = []
        for i in range(2):
            bufs.append((M.tile([8, 512], F32), M.tile([8, 512], BF16), M.tile([512], F32),
                         M.tile([8, 512], F32), Res(), Res(), Res(), [Res() for _ in range(8)],
                         M.bank(psbanks[i]), Res()))
        return bufs

    def phase_norm_proj(self, l, src, which, w_dram, F, zdst, blocks=None):
        P, M = self.P, self.M
        M.reset()
        blocks = blocks or token_blocks()
        A = self.A1[l] if which == 1 else self.A2[l]
        B = self.mod[l][:, 0:8, :] if which == 1 else self.mod[l][:, 24:32, :]
        hT = M.tile([8, NT], BF16)
        r_h = [Res() for _ in blocks]
        nfc = F // 128
        wt = M.tile([8, F], BF16)
        r_w = [Res() for _ in range(nfc)]
        for g in range(0, F, 2048):
            gw = min(2048, F - g)
            for k in range(8):
                P.dma('pool', wt[:, k, g:g + gw], w_dram[k * 128:(k + 1) * 128, g:g + gw],
                      W=[r_w[j] for j in range(g // 128, (g + gw) // 128)])
        bufs = self.norm_bufs([6, 7])
        for i, (t0, T, c) in enumerate(blocks):
            self.norm_block(src, t0, T, c, A, B, hT[:, :, t0:t0 + T], r_h[i], bufs, i)
        stg = [M.tile([512], F32) for _ in range(4)]
        r_stg = [Res() for _ in range(4)]
        r_ps = [Res() for _ in range(4)]
        n = 0
        for fc in range(nfc):
            for i, (t0, T, c) in enumerate(blocks):
                b = n % 4
                ps = M.bank(b)
                for k in range(8):
                    P.op('pe', lambda e, k=k, fc=fc, t0=t0, T=T, ps=ps: e.matmul(
                        ps[:, 0:T], wt[:, k, fc * 128:(fc + 1) * 128], hT[:, k, t0:t0 + T],
                        start=(k == 0), stop=(k == 7)), R=[r_w[fc], r_h[i]], W=[r_ps[b]])
                ev = 'act' if n % 2 == 0 else 'dve'
                if ev == 'act':
                    P.op('act', lambda e, b=b, T=T, ps=ps: e.copy(stg[b][:, 0:T], ps[:, 0:T]),
                         R=[r_ps[b]], W=[r_stg[b]])
                else:
                    P.op('dve', lambda e, b=b, T=T, ps=ps: e.tensor_copy(stg[b][:, 0:T], ps[:, 0:T]),
                         R=[r_ps[b]], W=[r_stg[b]])
                P.dma('sp', zdst[fc * 128:(fc + 1) * 128, t0:t0 + T], stg[b][:, 0:T], R=[r_stg[b]])
                n += 1
        P.barrier()


def rev(ap, n):
    pat = [list(p) for p in ap.ap]
    assert len(pat) == 2 and pat[1][1] == n
    return bass.AP(ap.tensor, ap.offset + (n - 1) * pat[1][0], [pat[0], [-pat[1][0], n]])


def _phase_rg(self):
    P, M = self.P, self.M
    M.reset()
    f = lambda: M.tile([NT], F32)
    zx, gate, xc, hf, hb, ga = [f() for _ in range(6)]
    rgs, igs, aas, bbs = [[f(), f()] for _ in range(4)]
    xcb = M.tile([NT], BF16)
    ob = M.tile([NT], BF16)
    cw = M.tile([4], F32)
    cb = M.tile([1], F32)
    bias4 = M.tile([4], F32)
    lam = M.tile([2], F32)
    cl = M.tile([4], F32)
    wbd = [M.tile([128], BF16) for _ in range(4)]
    segs = [(0, S), (S, CT)]
    blocks = token_blocks()
    for cc in range(4):
        R = {k: Res() for k in ['zx', 'gate', 'xc', 'xcb', 'r0', 'i0', 'a0', 'b0', 'r1', 'i1', 'a1', 'b1', 'hf', 'hb', 'small', 'cl', 'ob', 'ga']}
        rw = [Res() for _ in range(4)]
        rps = [Res() for _ in range(4)]
        fs = slice(cc * 128, (cc + 1) * 128)
        P.dma('sp', zx, self.z0[cc * 128:(cc + 1) * 128, :], W=[R['zx']])
        P.dma('sp', gate, self.z0[512 + cc * 128:512 + (cc + 1) * 128, :], W=[R['gate']])
        P.dma('sp', cw, self.rg_conv_w[:, fs].rearrange('j p -> p j'), W=[R['small']], allow_slow_non_contiguous=True)
        P.dma('sp', cb, self.rg_conv_b[fs].rearrange('(p o) -> p o', o=1), W=[R['small']])
        for d in range(2):
            P.dma('sp', bias4[:, 2 * d:2 * d + 1], self.rg_ba[d, fs].rearrange('(p o) -> p o', o=1), W=[R['small']])
            P.dma('sp', bias4[:, 2 * d + 1:2 * d + 2], self.rg_bx[d, fs].rearrange('(p o) -> p o', o=1), W=[R['small']])
            P.dma('sp', lam[:, d:d + 1], self.rg_lambda[d, fs].rearrange('(p o) -> p o', o=1), W=[R['small']])
        for d in range(2):
            for ty, wsrc in enumerate([self.rg_wa, self.rg_wx]):
                w = wbd[2 * d + ty]
                r_ = rw[2 * d + ty]
                P.op('dve', lambda e, w=w: e.memset(w, 0.0), W=[r_])
                for hh in range(2):
                    P.dma('pool', w[hh * 64:(hh + 1) * 64, hh * 64:(hh + 1) * 64], wsrc[d, 2 * cc + hh], W=[r_])
        P.op('act', lambda e: e.activation(cl[:, 0:2], lam, AF.Exp, scale=-1.0), R=[R['small']], W=[R['cl']])
        P.op('act', lambda e: e.activation(cl[:, 0:2], cl[:, 0:2], AF.Ln, bias=1.0), R=[R['cl']], W=[R['cl']])
        P.op('dve', lambda e: e.tensor_scalar(cl[:, 2:4], cl[:, 0:2], -16.0, None, ALU.mult), R=[R['cl']], W=[R['cl']])
        P.op('dve', lambda e: e.tensor_scalar(cl[:, 0:2], cl[:, 0:2], -8.0, None, ALU.mult), R=[R['cl']], W=[R['cl']])
        P.op('act', lambda e: e.activation(xc, zx, AF.Identity, bias=cb[:, 0:1], scale=cw[:, 2:3]),
             R=[R['zx'], R['small']], W=[R['xc']])
        for j in (0, 1, 3):
            off = j - 2
            for (s0, n) in segs:
                lo = max(0, -off)
                hi = n - max(0, off)
                P.op('dve', lambda e, j=j, off=off, s0=s0, lo=lo, hi=hi: e.scalar_tensor_tensor(
                    xc[:, s0 + lo:s0 + hi], zx[:, s0 + lo + off:s0 + hi + off], cw[:, j:j + 1],
                    xc[:, s0 + lo:s0 + hi], ALU.mult, ALU.add), R=[R['zx'], R['small'], R['xc']], W=[R['xc']])
        P.op('act', lambda e: e.copy(xcb, xc), R=[R['xc']], W=[R['xcb']])
        P.op('act', lambda e: e.activation(ga, gate, AF.Square), R=[R['gate']], W=[R['ga']])
        P.op('pool', lambda e: e.tensor_scalar(ga, ga, 0.044715, 1.0, ALU.mult, ALU.add), R=[R['ga']], W=[R['ga']])
        P.op('pool', lambda e: e.tensor_tensor(ga, ga, gate, ALU.mult), R=[R['ga'], R['gate']], W=[R['ga']])
        P.op('act', lambda e: e.activation(ga, ga, AF.Sigmoid, scale=1.5957691216), R=[R['ga']], W=[R['ga']])
        P.op('pool', lambda e: e.tensor_tensor(ga, ga, gate, ALU.mult), R=[R['ga'], R['gate']], W=[R['ga']])
        for d in range(2):
            rg, ig, aa, bb = rgs[d], igs[d], aas[d], bbs[d]
            kr_, ki_, ka_, kb_ = 'r%d' % d, 'i%d' % d, 'a%d' % d, 'b%d' % d
            for ty, dst, rk in ((0, rg, kr_), (1, ig, ki_)):
                for bi, (t0, T, c) in enumerate(blocks):
                    bk = (bi + ty) % 4
                    ps = M.bank(bk)
                    P.op('pe', lambda e, ps=ps, d=d, ty=ty, t0=t0, T=T: e.matmul(
                        ps[:, 0:T], wbd[2 * d + ty], xcb[:, t0:t0 + T], start=True, stop=True),
                        R=[rw[2 * d + ty], R['xcb']], W=[rps[bk]])
                    P.op('act', lambda e, ps=ps, dst=dst, d=d, ty=ty, t0=t0, T=T: e.activation(
                        dst[:, t0:t0 + T], ps[:, 0:T], AF.Sigmoid, bias=bias4[:, 2 * d + ty:2 * d + ty + 1]),
                        R=[rps[bk], R['small']], W=[R[rk]])
            P.op('act', lambda e, d=d, aa=aa, rg=rg: e.activation(aa, rg, AF.Exp, scale=cl[:, d:d + 1]), R=[R[kr_], R['cl']], W=[R[ka_]])
            P.op('act', lambda e, d=d, bb=bb, rg=rg: e.activation(bb, rg, AF.Exp, scale=cl[:, 2 + d:3 + d]), R=[R[kr_], R['cl']], W=[R[kb_]])
            P.op('act', lambda e, bb=bb: e.activation(bb, bb, AF.Sqrt, bias=1.0, scale=-1.0), R=[R[kb_]], W=[R[kb_]])
            P.op('pool', lambda e, bb=bb, ig=ig: e.tensor_tensor(bb, bb, ig, ALU.mult), R=[R[kb_], R[ki_]], W=[R[kb_]])
            P.op('pool', lambda e, bb=bb: e.tensor_tensor(bb, bb, xc, ALU.mult), R=[R[kb_], R['xc']], W=[R[kb_]])
            h = hf if d == 0 else hb
            hk = 'hf' if d == 0 else 'hb'
            if d == 0:
                P.op('dve', lambda e, h=h, aa=aa, bb=bb: e.tensor_tensor_scan(h[:, S:NT], aa[:, S:NT], bb[:, S:NT], 0.0, ALU.mult, ALU.add),
                     R=[R[ka_], R[kb_]], W=[R[hk]])
                P.op('dve', lambda e, h=h, aa=aa, bb=bb: e.tensor_tensor_scan(h[:, 0:S], aa[:, 0:S], bb[:, 0:S], h[:, NT - 1:NT], ALU.mult, ALU.add),
                     R=[R[ka_], R[kb_], R[hk]], W=[R[hk]])
            else:
                P.op('dve', lambda e, h=h, aa=aa, bb=bb: e.tensor_tensor_scan(rev(h[:, S:NT], CT), rev(aa[:, S:NT], CT), rev(bb[:, S:NT], CT),
                                                                               0.0, ALU.mult, ALU.add), R=[R[ka_], R[kb_]], W=[R[hk]])
                P.op('dve', lambda e, h=h, aa=aa, bb=bb: e.tensor_tensor_scan(rev(h[:, 0:S], S), rev(aa[:, 0:S], S), rev(bb[:, 0:S], S),
                                                                               h[:, S:S + 1], ALU.mult, ALU.add),
                     R=[R[ka_], R[kb_], R[hk]], W=[R[hk]])
        P.op('pool', lambda e: e.tensor_tensor(hf, hf, hb, ALU.add), R=[R['hf'], R['hb']], W=[R['hf']])
        P.op('dve', lambda e: e.tensor_tensor(ob, ga, hf, ALU.mult), R=[R['ga'], R['hf']], W=[R['ob']])
        P.dma('sp', self.mixT[cc * 128:(cc + 1) * 128, :], ob, R=[R['ob']])
        P.barrier()


Builder.phase_rg = _phase_rg


def bcast_rows(ap_row, nparts=128):
    pat = [list(p) for p in ap_row.ap]
    return bass.AP(ap_row.tensor, ap_row.offset, [[0, nparts]] + pat)


def _phase_hy_filters(self, n, zT_d, dec_d, decb_d, Ccol, Scol, wk_d, Kf_d):
    P, M = self.P, self.M
    M.reset()
    nj = n // 128
    w1 = M.tile([64], F32)
    w2 = M.tile([64], F32)
    w3 = M.tile([2048], F32)
    sm = M.tile([3], F32)
    zT = M.tile([n], F32)
    h1 = M.tile([n], F32)
    h2 = M.tile([n], F32)
    tm = M.tile([512], F32)
    r_w, r_z, r_h1, r_h2, r_tm = Res(), Res(), Res(), Res(), Res()
    P.dma('sp', w1[0:33, :], self.f_w1, W=[r_w])
    P.dma('sp', w2[0:64, :], self.f_w2, W=[r_w])
    P.dma('sp', w3[0:64, :], self.f_w3, W=[r_w])
    P.dma('sp', sm[0:64, 0:1], self.f_b1.rearrange('(p o) -> p o', o=1), W=[r_w])
    P.dma('sp', sm[0:64, 1:2], self.f_freq.rearrange('(p o) -> p o', o=1), W=[r_w])
    P.dma('sp', sm[0:64, 2:3], self.f_b2.rearrange('(p o) -> p o', o=1), W=[r_w])
    P.dma('sp', zT[0:33, :], zT_d, W=[r_z])
    rps = [Res() for _ in range(8)]
    PI = math.pi

    def sin_layer(lhsT, kk, src, rsrc, bcol, dst, rdst):
        for bi, t0 in enumerate(range(0, n, 512)):
            T = min(512, n - t0)
            ps = M.bank(bi % 2)
            P.op('pe', lambda e, ps=ps, t0=t0, T=T: e.matmul(ps[0:64, 0:T], lhsT, src[0:kk, t0:t0 + T], start=True, stop=True),
                 R=[r_w, rsrc], W=[rps[bi % 2]])
            P.op('dve', lambda e, ps=ps, T=T: e.tensor_scalar(tm[0:64, 0:T], ps[0:64, 0:T], sm[0:64, bcol:bcol + 1],
                                                           sm[0:64, 1:2], ALU.add, ALU.mult), R=[rps[bi % 2], r_w, r_tm], W=[r_tm])
            for thr, op, mul in ((PI, ALU.is_gt, -2 * PI), (-PI, ALU.is_lt, 2 * PI)):
                P.op('dve', lambda e, T=T, t0=t0, thr=thr, op=op, mul=mul: e.tensor_scalar(
                    dst[0:64, t0:t0 + T], tm[0:64, 0:T], thr, mul, op, ALU.mult), R=[r_tm], W=[rdst])
                P.op('dve', lambda e, T=T, t0=t0: e.tensor_tensor(tm[0:64, 0:T], tm[0:64, 0:T], dst[0:64, t0:t0 + T], ALU.add),
                     R=[r_tm, rdst], W=[r_tm])
            P.op('act', lambda e, T=T, t0=t0: e.activation(dst[0:64, t0:t0 + T], tm[0:64, 0:T], AF.Sin), R=[r_tm], W=[rdst])

    sin_layer(w1[0:33, :], 33, zT, r_z, 0, h1, r_h1)
    sin_layer(w2[0:64, :], 64, h1, r_h1, 2, h2, r_h2)
    ks = M.tile([nj, 1024], BF16)
    kd = M.tile([nj, 1024], BF16)
    dec = [M.tile([512], F32) for _ in range(2)]
    decb = [M.tile([512], F32) for _ in range(2)]
    kf = [M.tile([512], F32) for _ in range(2)]
    kb = [M.tile([512], F32) for _ in range(2)]
    ab = [M.tile([512], BF16) for _ in range(2)]
    r_dec = [Res(), Res()]
    r_kf = [Res(), Res()]
    r_kb = [Res(), Res()]
    r_ab = [Res(), Res()]
    r_ks = Res()
    r_nrm = Res()
    cnt = 0
    for j in range(nj):
        P.dma('sp', dec[j % 2], dec_d[j * 128:(j + 1) * 128, :], W=[r_dec[j % 2]])
        P.dma('sp', decb[j % 2], decb_d[j * 128:(j + 1) * 128, :], W=[r_dec[j % 2]])
        for o in range(2):
            q = cnt % 2
            cnt += 1
            for dr in range(2):
                bk = 2 + 2 * q + dr
                ps = M.bank(bk)
                c0 = o * 1024 + dr * 512
                P.op('pe', lambda e, ps=ps, j=j, c0=c0: e.matmul(ps, h2[0:64, j * 128:(j + 1) * 128], w3[0:64, c0:c0 + 512],
                                                             start=True, stop=True), R=[r_h2, r_w], W=[rps[bk]])
                dst, rd, dc = (kf[q], r_kf[q], dec[j % 2]) if dr == 0 else (kb[q], r_kb[q], decb[j % 2])
                P.op('dve', lambda e, ps=ps, dst=dst, dc=dc: e.tensor_tensor(dst, ps, dc, ALU.mult),
                     R=[rps[bk], r_dec[j % 2]], W=[rd])
                P.op('act', lambda e, dst=dst, q=q: e.activation(ab[q], dst, AF.Abs), R=[rd], W=[r_ab[q]])
                first = (j == 0 and dr == 0)
                last = (j == nj - 1 and dr == 1)
                P.op('pe', lambda e, o=o, q=q, first=first, last=last: e.matmul(M.bank(6 + o), self.ones_bf, ab[q], start=first, stop=last),
                     R=[r_ab[q], self.r_ones], W=[r_nrm])
            P.op('dve', lambda e, q=q, j=j, o=o: e.tensor_tensor(ks[:, j, o * 512:(o + 1) * 512], kf[q], kb[q], ALU.add),
                 R=[r_kf[q], r_kb[q]], W=[r_ks])
            P.op('dve', lambda e, q=q, j=j, o=o: e.tensor_tensor(kd[:, j, o * 512:(o + 1) * 512], kb[q], kf[q], ALU.subtract),
                 R=[r_kf[q], r_kb[q]], W=[r_ks])
    rn = M.tile([1024], F32)
    r_rn = Res()
    for o in range(2):
        P.op('dve', lambda e, o=o: e.reciprocal(rn[:, o * 512:(o + 1) * 512], M.bank(6 + o)), R=[r_nrm], W=[r_rn])
    wk = M.tile([nj], F32)
    P.dma('sp', wk, wk_d, W=[r_rn])
    cs = [[M.tile([nj, 128], BF16) for _ in range(2)] for _ in range(2)]
    r_cs = [Res(), Res()]
    ot = [M.tile([512], F32) for _ in range(4)]
    r_ot = [Res() for _ in range(4)]
    m = 0
    for f in range(nj):
        q = f % 2
        P.dma('sp', cs[q][0], Ccol[f], W=[r_cs[q]])
        P.dma('sp', cs[q][1], Scol[f], W=[r_cs[q]])
        for ri in range(2):
            src = ks if ri == 0 else kd
            for o in range(2):
                bk = m % 4
                ps = M.bank(bk)
                for j in range(nj):
                    P.op('pe', lambda e, ps=ps, q=q, ri=ri, j=j, o=o, src=src: e.matmul(
                        ps, cs[q][ri][:, j, :], src[:, j, o * 512:(o + 1) * 512], start=(j == 0), stop=(j == nj - 1)),
                        R=[r_cs[q], r_ks], W=[rps[bk]])
                P.op('dve', lambda e, ps=ps, bk=bk, f=f, o=o: e.scalar_tensor_tensor(
                    ot[bk], ps, wk[:, f:f + 1], rn[:, o * 512:(o + 1) * 512], ALU.mult, ALU.mult),
                    R=[rps[bk], r_rn], W=[r_ot[bk]])
                P.dma('sp', Kf_d[o, ri, f * 128:(f + 1) * 128, :], ot[bk], R=[r_ot[bk]])
                m += 1
    P.barrier()


def _phase_hy_prep(self):
    P, M = self.P, self.M
    M.reset()
    zz = [M.tile([NT], F32) for _ in range(2)]
    zc = [M.tile([NT], F32) for _ in range(2)]
    cw = [M.tile([4], F32) for _ in range(2)]
    stg = [M.tile([4, 128], F32) for _ in range(2)]
    r_zz = [Res(), Res()]
    r_zc = [Res(), Res()]
    r_cw = [Res(), Res()]
    r_stg = [Res(), Res()]
    r_ps = [Res(), Res()]
    segs = [(0, S), (S, CT)]
    m = 0
    for ch in range(12):
        q = ch % 2
        g, cc = ch // 4, ch % 4
        fs = slice(ch * 128, (ch + 1) * 128)
        P.dma('sp', zz[q], self.z0[1024 + ch * 128:1024 + (ch + 1) * 128, :], W=[r_zz[q]])
        P.dma('sp', cw[q][:, 0:3], self.hy_conv_w[:, fs].rearrange('j p -> p j'), W=[r_cw[q]], allow_slow_non_contiguous=True)
        P.dma('sp', cw[q][:, 3:4], self.hy_conv_b[fs].rearrange('(p o) -> p o', o=1), W=[r_cw[q]])
        P.op('act', lambda e, q=q: e.activation(zc[q], zz[q], AF.Identity, bias=cw[q][:, 3:4], scale=cw[q][:, 1:2]),
             R=[r_zz[q], r_cw[q]], W=[r_zc[q]])
        for j in (0, 2):
            off = j - 1
            for (s0, n) in segs:
                lo = max(0, -off)
                hi = n - max(0, off)
                P.op('dve', lambda e, q=q, j=j, off=off, s0=s0, lo=lo, hi=hi: e.scalar_tensor_tensor(
                    zc[q][:, s0 + lo:s0 + hi], zz[q][:, s0 + lo + off:s0 + hi + off], cw[q][:, j:j + 1],
                    zc[q][:, s0 + lo:s0 + hi], ALU.mult, ALU.add), R=[r_zz[q], r_cw[q], r_zc[q]], W=[r_zc[q]])
        for tg in range(0, NT // 128, 4):
            ntl = min(4, NT // 128 - tg)
            b = m % 2
            m += 1
            ps = M.bank(b)
            for i in range(ntl):
                P.op('pe', lambda e, ps=ps, q=q, tg=tg, i=i: e.transpose(
                    ps[:, i * 128:(i + 1) * 128], zc[q][:, (tg + i) * 128:(tg + i + 1) * 128], self.id32),
                    R=[r_zc[q], self.r_id], W=[r_ps[b]])
            P.op('act' if b == 0 else 'dve',
                 (lambda e, ps=ps, b=b, ntl=ntl: e.copy(stg[b][:, 0:ntl, :], ps[:, 0:ntl * 128].rearrange('p (a c) -> p a c', c=128)))
                 if b == 0 else
                 (lambda e, ps=ps, b=b, ntl=ntl: e.tensor_copy(stg[b][:, 0:ntl, :], ps[:, 0:ntl * 128].rearrange('p (a c) -> p a c', c=128))),
                 R=[r_ps[b]], W=[r_stg[b]])
            P.dma('sp', self.vtok[g, tg * 128:(tg + ntl) * 128, cc * 128:(cc + 1) * 128].rearrange('(a p) c -> p a c', p=128),
                  stg[b][:, 0:ntl, :], R=[r_stg[b]])
    P.barrier()


def _phase_hy_conv(self, n, tok0, Ccol, Scol, Kf_d):
    P, M = self.P, self.M
    M.reset()
    nj = n // 128
    u = M.tile([nj, 512], BF16)
    Y = M.tile([nj, 2, 512], BF16)
    r_u = [Res() for _ in range(nj)]
    r_Y = [Res() for _ in range(nj)]
    cs = [[M.tile([nj, 128], BF16) for _ in range(2)] for _ in range(3)]
    r_cs = [Res(), Res(), Res()]
    kk = [[M.tile([512], F32) for _ in range(2)] for _ in range(3)]
    r_kk = [Res(), Res(), Res()]
    t1 = [M.tile([512], F32) for _ in range(2)]
    t2 = [M.tile([512], F32) for _ in range(2)]
    t3 = [M.tile([512], F32) for _ in range(2)]
    t4 = [M.tile([512], F32) for _ in range(2)]
    r_t1 = [Res(), Res()]
    r_t2 = [Res(), Res()]
    r_t3 = [Res(), Res()]
    r_t4 = [Res(), Res()]
    biasb = M.tile([2, 512], F32)
    r_bias = Res()
    for o in range(2):
        P.dma('sp', biasb[:, o, :], bcast_rows(self.hy_bias[o]), W=[r_bias])
    uf = [M.tile([512], F32) for _ in range(3)]
    xg = [M.tile([512], F32) for _ in range(3)]
    un = [M.tile([512], F32) for _ in range(2)]
    ub = [M.tile([512], BF16) for _ in range(2)]
    ot = [M.tile([512], BF16) for _ in range(2)]
    r_uf = [Res(), Res(), Res()]
    r_xg = [Res(), Res(), Res()]
    r_un = [Res(), Res()]
    r_ub = [Res(), Res()]
    r_ot = [Res(), Res()]
    rps = [Res() for _ in range(8)]
    for j in range(nj):
        P.dma('pool', u[:, j, :], self.vtok[0, tok0 + j * 128:tok0 + (j + 1) * 128, :], W=[r_u[j]])
    for o in range(2):
        for f in range(nj):
            q3 = f % 3
            q = f % 2
            P.dma('sp', cs[q3][0], Ccol[f], W=[r_cs[q3]])
            P.dma('sp', cs[q3][1], Scol[f], W=[r_cs[q3]])
            for ri in range(2):
                P.dma('sp', kk[q3][ri], Kf_d[o, ri, f * 128:(f + 1) * 128, :], W=[r_kk[q3]])
            pr, pi = M.bank(2 * q), M.bank(2 * q + 1)
            for ri, ps in ((0, pr), (1, pi)):
                for j in range(nj):
                    P.op('pe', lambda e, ps=ps, q3=q3, ri=ri, j=j: e.matmul(ps, cs[q3][ri][:, j, :], u[:, j, :],
                                                                          start=(j == 0), stop=(j == nj - 1)),
                         R=[r_cs[q3], r_u[j]], W=[rps[2 * q + ri]])
            R_ = [rps[2 * q], rps[2 * q + 1], r_kk[q3]]
            P.op('dve', lambda e, q=q, q3=q3, pr=pr: e.tensor_tensor(t1[q], pr, kk[q3][0], ALU.mult), R=R_ + [r_t1[q]], W=[r_t1[q]])
            P.op('dve', lambda e, q=q, q3=q3, pi=pi: e.tensor_tensor(t2[q], pi, kk[q3][1], ALU.mult), R=R_ + [r_t2[q]], W=[r_t2[q]])
            P.op('dve', lambda e, q=q, q3=q3, pi=pi: e.tensor_tensor(t3[q], pi, kk[q3][0], ALU.mult), R=R_ + [r_t3[q]], W=[r_t3[q]])
            P.op('dve', lambda e, q=q, q3=q3, pr=pr: e.tensor_tensor(t4[q], pr, kk[q3][1], ALU.mult), R=R_ + [r_t4[q]], W=[r_t4[q]])
            P.op('pool', lambda e, q=q, f=f: e.tensor_tensor(Y[:, f, 0, :], t1[q], t2[q], ALU.add), R=[r_t1[q], r_t2[q]], W=[r_Y[f]])
            P.op('pool', lambda e, q=q, f=f: e.tensor_tensor(Y[:, f, 1, :], t3[q], t4[q], ALU.subtract), R=[r_t3[q], r_t4[q]], W=[r_Y[f]])
        for t in range(nj):
            q = t % 2
            q3 = t % 3
            P.dma('sp', cs[q3][0], Ccol[t], W=[r_cs[q3]])
            P.dma('sp', cs[q3][1], Scol[t], W=[r_cs[q3]])
            rows = slice(tok0 + t * 128, tok0 + (t + 1) * 128)
            if o == 0:
                P.dma('sp', uf[q3], self.vtok[0, rows, :], W=[r_uf[q3]])
            else:
                P.dma('sp', uf[q3], self.u2tok[rows, :], W=[r_uf[q3]])
            P.dma('sp', xg[q3], self.vtok[1 + o, rows, :], W=[r_xg[q3]])
            ps = M.bank(4 + q)
            for ri in range(2):
                for j in range(nj):
                    P.op('pe', lambda e, ps=ps, q3=q3, ri=ri, j=j: e.matmul(ps, cs[q3][ri][:, j, :], Y[:, j, ri, :],
                                                                          start=(ri == 0 and j == 0), stop=(ri == 1 and j == nj - 1)),
                         R=[r_cs[q3], r_Y[j]], W=[rps[4 + q]])
            P.op('pool', lambda e, q=q, q3=q3, o=o: e.tensor_tensor(un[q], uf[q3], biasb[:, o, :], ALU.mult), R=[r_uf[q3], r_bias], W=[r_un[q]])
            P.op('dve', lambda e, q=q, ps=ps: e.tensor_tensor(un[q], un[q], ps, ALU.add), R=[r_un[q], rps[4 + q]], W=[r_un[q]])
            if o == 0:
                P.op('dve', lambda e, q=q, q3=q3: e.tensor_tensor(un[q], un[q], xg[q3], ALU.mult), R=[r_un[q], r_xg[q3]], W=[r_un[q]])
                P.dma('sp', self.u2tok[rows, :], un[q], R=[r_un[q]])
                P.op('act', lambda e, q=q, t=t: e.copy(u[:, t, :], un[q]), R=[r_un[q]], W=[r_u[t]])
            else:
                P.op('dve', lambda e, q=q, q3=q3: e.tensor_tensor(ub[q], un[q], xg[q3], ALU.mult), R=[r_un[q], r_xg[q3]], W=[r_ub[q]])
                pt = M.bank(6 + q, 512, BF16)
                for cc in range(4):
                    P.op('pe', lambda e, pt=pt, q=q, cc=cc: e.transpose(pt[:, cc * 128:(cc + 1) * 128], ub[q][:, cc * 128:(cc + 1) * 128], self.idbf),
                         R=[r_ub[q], self.r_idbf], W=[rps[6 + q]])
                P.op('act', lambda e, pt=pt, q=q: e.copy(ot[q], pt), R=[rps[6 + q]], W=[r_ot[q]])
                P.dma('sp', self.mixT[512:1024, tok0 + t * 128:tok0 + (t + 1) * 128].rearrange('(a p) t -> p a t', p=128),
                      ot[q].rearrange('p (a t) -> p a t', a=4), R=[r_ot[q]])
    P.barrier()


Builder.phase_hy_filters = _phase_hy_filters
Builder.phase_hy_prep = _phase_hy_prep
Builder.phase_hy_conv = _phase_hy_conv


def hyena_consts():
    out = {}
    f32 = np.float32
    max_decay = math.log(1e-2) / 0.3
    min_decay = math.log(1e-2) / 1.5
    deltas = np.linspace(min_decay, max_decay, 512, dtype=f32)
    for n in (2048, 256):
        pos = np.arange(n, dtype=f32)
        t = np.linspace(0.0, 1.0, n, dtype=f32)[:, None]
        w = (f32(2.0 * math.pi) * pos / f32(n)).astype(f32)
        fr = np.linspace(1e-4, 15, 16, dtype=f32)
        ang = (w[:, None] * fr[None, :]).astype(f32)
        z = np.concatenate([t, np.cos(ang), -np.sin(ang)], axis=-1).astype(f32)
        out['zT%d' % n] = np.ascontiguousarray(z.T)
        dec = np.exp(-t * np.abs(deltas)[None, :]).astype(f32)
        out['dec%d' % n] = dec
        decb = dec.copy()
        decb[0] = 0.0
        out['decb%d' % n] = decb
        N = 2 * n - 1
        sk = (np.arange(n, dtype=np.int64)[:, None] * np.arange(n, dtype=np.int64)[None, :]) % N
        ang = 2.0 * np.pi * sk.astype(np.float64) / N
        nj = n // 128
        for nm, mat in (('C', np.cos(ang)), ('S', np.sin(ang))):
            m4 = mat.reshape(nj, 128, nj, 128).transpose(2, 1, 0, 3)
            out['%scol%d' % (nm, n)] = np.ascontiguousarray(m4).astype(ml_dtypes.bfloat16)
        wk = np.full(n, 2.0 / N, dtype=f32)
        wk[0] = 1.0 / N
        out['wk%d' % n] = np.ascontiguousarray(wk.reshape(nj, 128).T)
    return out


def _phase_out_proj(self, l, src_bf, w_dram, xsrc, xdst, blocks):
    P, M = self.P, self.M
    M.reset()
    ntok = max(t0 + T for t0, T, c in blocks)
    mx = M.tile([8, ntok], BF16)
    wt = M.tile([8, D], BF16)
    r_mx, r_w = Res(), Res()
    for k in range(8):
        P.dma('sp', mx[:, k, :], src_bf[k * 128:(k + 1) * 128, 0:ntok], W=[r_mx])
        P.dma('pool', wt[:, k, :], w_dram[k * 128:(k + 1) * 128, :], W=[r_w])
    xb = [M.tile([512], F32) for _ in range(3)]
    ob = [M.tile([512], F32) for _ in range(3)]
    r_xb = [Res() for _ in range(3)]
    r_ob = [Res() for _ in range(3)]
    rps = [Res() for _ in range(4)]
    n = 0
    mod = self.mod[l]
    for dc in range(8):
        for (t0, T, c) in blocks:
            b = n % 4
            q = n % 3
            n += 1
            ps = M.bank(b)
            P.dma('sp', xb[q][:, 0:T], xsrc[dc * 128:(dc + 1) * 128, t0:t0 + T], W=[r_xb[q]])
            for k in range(8):
                P.op('pe', lambda e, ps=ps, k=k, dc=dc, t0=t0, T=T: e.matmul(
                    ps[:, 0:T], wt[:, k, dc * 128:(dc + 1) * 128], mx[:, k, t0:t0 + T], start=(k == 0), stop=(k == 7)),
                    R=[r_w, r_mx], W=[rps[b]])
            P.op('dve', lambda e, ps=ps, q=q, dc=dc, c=c, T=T: e.scalar_tensor_tensor(
                ob[q][:, 0:T], ps[:, 0:T], mod[:, 16 + dc, c:c + 1], xb[q][:, 0:T], ALU.mult, ALU.add),
                R=[rps[b], r_xb[q], self.r_mod[l]], W=[r_ob[q]])
            P.dma('sp', xdst[dc * 128:(dc + 1) * 128, t0:t0 + T], ob[q][:, 0:T], R=[r_ob[q]])
    P.barrier()


def _phase_ffn(self, l, xsrc, xdst, segs, final_out=None):
    P, M = self.P, self.M
    M.reset()
    TB = 256
    NP = 22
    wu = M.tile([8, 5632], BF16)
    wd = M.tile([NP, D], BF16)
    r_wu = [Res() for _ in range(44)]
    r_wd = Res()
    for g in range(0, 5632, 2048):
        gw = min(2048, 5632 - g)
        for k in range(8):
            P.dma('pool', wu[:, k, g:g + gw], self.ffn_w_up[l, k * 128:(k + 1) * 128, g:g + gw],
                  W=[r_wu[j] for j in range(g // 128, (g + gw) // 128)])
    for p in range(NP):
        P.dma('pool', wd[:, p, :], self.ffn_w_down[l, p * 128:(p + 1) * 128, :], W=[r_wd])
    cw = M.tile([44, 4], F32)
    r_cw = Res()
    for j in range(3):
        P.dma('sp', cw[:, :, j], self.ffn_conv_w[l, j].rearrange('(c p) -> p c', p=128), W=[r_cw], allow_slow_non_contiguous=True)
    P.dma('sp', cw[:, :, 3], self.ffn_conv_b[l].rearrange('(c p) -> p c', p=128), W=[r_cw], allow_slow_non_contiguous=True)
    NC = TB + 2
    xb = M.tile([8, NC], F32)
    sq = M.tile([8, NC], BF16)
    h2 = M.tile([8, NC], BF16)
    rstd = M.tile([NC], F32)
    tmp = [M.tile([NC], F32) for _ in range(2)]
    mm = M.tile([NP, TB], BF16)
    cg = [M.tile([TB], F32) for _ in range(3)]
    cv = [M.tile([TB], F32) for _ in range(3)]
    xn = M.tile([8, TB], F32)
    r_xb, r_sq, r_h2, r_rs, r_xn = Res(), Res(), Res(), Res(), Res()
    r_tmp = [Res(), Res()]
    r_mm = [Res() for _ in range(NP)]
    r_cg = [Res(), Res(), Res()]
    r_cv = [Res(), Res(), Res()]
    rps = [Res() for _ in range(8)]
    A = self.A2[l]
    mod = self.mod[l]
    if final_out is not None:
        fg = M.tile([8], F32)
        r_fg = Res()
        P.dma('sp', fg, self.final_g.rearrange('(k p) -> p k', p=128), W=[r_fg], allow_slow_non_contiguous=True)
        sq2 = M.tile([8, TB], BF16)
        r_sq2 = Res()
        fo = M.tile([8, TB], F32)
        r_fo = Res()
    RM = [self.r_mod[l]]
    nblk = 0
    for (s0, s1, c) in segs:
        for t0 in range(s0, s1, TB):
            T = min(TB, s1 - t0)
            lo = max(s0, t0 - 1)
            hi = min(s1, t0 + T + 1)
            nc_ = hi - lo
            c0 = t0 - lo
            P.dma('sp', xb[:, :, 0:nc_], xsrc[:, lo:hi].rearrange('(k p) t -> p k t', p=128), W=[r_xb])
            P.op('act', lambda e, nc_=nc_: e.activation(sq[:, :, 0:nc_], xb[:, :, 0:nc_], AF.Square), R=[r_xb], W=[r_sq])
            psn = M.bank(6)
            for k in range(8):
                P.op('pe', lambda e, k=k, nc_=nc_, psn=psn: e.matmul(psn[:, 0:nc_], self.ones_bf, sq[:, k, 0:nc_], start=(k == 0), stop=(k == 7)),
                     R=[r_sq, self.r_ones], W=[rps[6]])
            P.op('act', lambda e, nc_=nc_, psn=psn: e.activation(rstd[:, 0:nc_], psn[:, 0:nc_], AF.Sqrt, bias=EPS, scale=1.0 / D),
                 R=[rps[6]], W=[r_rs])
            P.op('dve', lambda e, nc_=nc_: e.reciprocal(rstd[:, 0:nc_], rstd[:, 0:nc_]), R=[r_rs], W=[r_rs])
            for k in range(8):
                q = k % 2
                P.op('pool', lambda e, k=k, q=q, nc_=nc_: e.tensor_tensor(
                    tmp[q][:, 0:nc_], xb[:, k, 0:nc_], rstd[:, 0:nc_], ALU.mult), R=[r_xb, r_rs], W=[r_tmp[q]])
                P.op('act', lambda e, k=k, q=q, nc_=nc_, c=c: e.activation(
                    h2[:, k, 0:nc_], tmp[q][:, 0:nc_], AF.Identity, bias=mod[:, 24 + k, c:c + 1], scale=A[:, k, c:c + 1]),
                    R=[r_tmp[q]] + RM, W=[r_h2])
            a = 1 if t0 == s0 else 0
            bnd = 1 if t0 + T == s1 else 0

            def st_pe(p):
                q = p % 3
                for ch, bk in ((p, 2 * q), (NP + p, 2 * q + 1)):
                    ps = M.bank(bk)
                    for k in range(8):
                        P.op('pe', lambda e, ps=ps, k=k, ch=ch, nc_=nc_: e.matmul(
                            ps[:, 0:nc_], wu[:, k, ch * 128:(ch + 1) * 128], h2[:, k, 0:nc_], start=(k == 0), stop=(k == 7)),
                            R=[r_wu[ch], r_h2], W=[rps[bk]])

            def st_conv(p):
                q = p % 3
                pairs = ((p, M.bank(2 * q), 2 * q, cg[q], r_cg[q]), (NP + p, M.bank(2 * q + 1), 2 * q + 1, cv[q], r_cv[q]))
                for ch, ps, bk, dst, rd in pairs:
                    P.op('act', lambda e, ps=ps, ch=ch, dst=dst, T=T, c0=c0: e.activation(
                        dst[:, 0:T], ps[:, c0:c0 + T], AF.Identity, bias=cw[:, ch, 3:4], scale=cw[:, ch, 1:2]),
                        R=[rps[bk], r_cw], W=[rd])
                for ch, ps, bk, dst, rd in pairs:
                    P.op('dve', lambda e, ps=ps, ch=ch, dst=dst, T=T, c0=c0, a=a: e.scalar_tensor_tensor(
                        dst[:, a:T], ps[:, c0 - 1 + a:c0 - 1 + T], cw[:, ch, 0:1], dst[:, a:T], ALU.mult, ALU.add),
                        R=[rps[bk], r_cw, rd], W=[rd])
                for ch, ps, bk, dst, rd in pairs:
                    P.op('dve', lambda e, ps=ps, ch=ch, dst=dst, T=T, c0=c0, bnd=bnd: e.scalar_tensor_tensor(
                        dst[:, 0:T - bnd], ps[:, c0 + 1:c0 + 1 + T - bnd], cw[:, ch, 2:3], dst[:, 0:T - bnd], ALU.mult, ALU.add),
                        R=[rps[bk], r_cw, rd], W=[rd])

            def st_gate(p):
                q = p % 3
                P.op('act', lambda e, q=q, T=T: e.activation(cg[q][:, 0:T], cg[q][:, 0:T], AF.Silu), R=[r_cg[q]], W=[r_cg[q]])
                P.op('pool', lambda e, q=q, p=p, T=T: e.tensor_tensor(mm[:, p, 0:T], cg[q][:, 0:T], cv[q][:, 0:T], ALU.mult),
                     R=[r_cg[q], r_cv[q]], W=[r_mm[p]])

            for i in range(NP + 2):
                if i < NP:
                    st_pe(i)
                if 1 <= i <= NP:
                    st_conv(i - 1)
                if 2 <= i <= NP + 1:
                    st_gate(i - 2)
            for dc in range(8):
                bk = 6 + dc % 2
                ps = M.bank(bk)
                for p in range(NP):
                    P.op('pe', lambda e, ps=ps, p=p, dc=dc, T=T: e.matmul(
                        ps[:, 0:T], wd[:, p, dc * 128:(dc + 1) * 128], mm[:, p, 0:T], start=(p == 0), stop=(p == NP - 1)),
                        R=[r_wd, r_mm[p]], W=[rps[bk]])
                P.op('dve', lambda e, ps=ps, dc=dc, T=T, c0=c0, c=c: e.scalar_tensor_tensor(
                    xn[:, dc, 0:T], ps[:, 0:T], mod[:, 40 + dc, c:c + 1], xb[:, dc, c0:c0 + T], ALU.mult, ALU.add),
                    R=[rps[bk], r_xb] + RM, W=[r_xn])
            if final_out is None:
                P.dma('sp', xdst[:, t0:t0 + T].rearrange('(k p) t -> p k t', p=128), xn[:, :, 0:T], R=[r_xn])
            else:
                P.op('act', lambda e, T=T: e.activation(sq2[:, :, 0:T], xn[:, :, 0:T], AF.Square), R=[r_xn], W=[r_sq2])
                psn = M.bank(7)
                for k in range(8):
                    P.op('pe', lambda e, k=k, T=T, psn=psn: e.matmul(psn[:, 0:T], self.ones_bf, sq2[:, k, 0:T], start=(k == 0), stop=(k == 7)),
                         R=[r_sq2, self.r_ones], W=[rps[7]])
                P.op('act', lambda e, T=T, psn=psn: e.activation(rstd[:, 0:T], psn[:, 0:T], AF.Sqrt, bias=EPS, scale=1.0 / D),
                     R=[rps[7], r_rs], W=[r_rs])
                P.op('dve', lambda e, T=T: e.reciprocal(rstd[:, 0:T], rstd[:, 0:T]), R=[r_rs], W=[r_rs])
                for k in range(8):
                    P.op('dve', lambda e, k=k, T=T: e.scalar_tensor_tensor(
                        fo[:, k, 0:T], xn[:, k, 0:T], fg[:, k:k + 1], rstd[:, 0:T], ALU.mult, ALU.mult),
                        R=[r_xn, r_rs, r_fg], W=[r_fo])
                P.dma('sp', final_out[:, t0:t0 + T].rearrange('(k p) t -> p k t', p=128), fo[:, :, 0:T], R=[r_fo])
            nblk += 1
    P.barrier()


Builder.phase_out_proj = _phase_out_proj
Builder.phase_ffn = _phase_ffn


def _rms_feat(self, zt, nk, T, g, dst, rz, rg_, rdst, tmp, r_tmp, sq, r_sq, rstd, r_rs, psb, rps):
    P = self.P
    P.op('act', lambda e: e.activation(sq[:, 0:nk, 0:T], zt[:, :, 0:T], AF.Square), R=[rz], W=[r_sq])
    for k in range(nk):
        P.op('pe', lambda e, k=k: e.matmul(psb[:, 0:T], self.ones_bf, sq[:, k, 0:T], start=(k == 0), stop=(k == nk - 1)),
             R=[r_sq, self.r_ones], W=[rps])
    P.op('act', lambda e: e.activation(rstd[:, 0:T], psb[:, 0:T], AF.Sqrt, bias=EPS, scale=1.0 / (nk * 128)), R=[rps], W=[r_rs])
    P.op('dve', lambda e: e.reciprocal(rstd[:, 0:T], rstd[:, 0:T]), R=[r_rs], W=[r_rs])
    for k in range(nk):
        P.op('dve', lambda e, k=k: e.scalar_tensor_tensor(dst[:, k, :], zt[:, k, 0:T], g[:, k:k + 1], rstd[:, 0:T], ALU.mult, ALU.mult),
             R=[rz, r_rs, rg_], W=[rdst])


def _phase_mla_prep(self):
    P, M = self.P, self.M
    M.reset()
    z1 = self.z1
    wq = M.tile([4, 1536], BF16)
    wkv = M.tile([2, 2048], BF16)
    r_w = Res()
    for k in range(4):
        P.dma('pool', wq[:, k, :], self.od_w_uq[k * 128:(k + 1) * 128, :], W=[r_w])
    for k in range(2):
        for g in range(2):
            P.dma('pool', wkv[:, k, g * 1024:(g + 1) * 1024], self.od_w_ukv[k * 128:(k + 1) * 128, g * 1024:(g + 1) * 1024], W=[r_w])
    gq = M.tile([4], F32)
    gkv = M.tile([2], F32)
    rotm = M.tile([64], F32)
    cos2 = M.tile([S], F32)
    sin2 = M.tile([S], F32)
    r_c = Res()
    P.dma('sp', gq, self.od_q_norm_g.rearrange('(k p) -> p k', p=128), W=[r_c], allow_slow_non_contiguous=True)
    P.dma('sp', gkv, self.od_kv_norm_g.rearrange('(k p) -> p k', p=128), W=[r_c], allow_slow_non_contiguous=True)
    P.dma('sp', rotm[0:64, :], self.rotm, W=[r_c])
    P.dma('sp', cos2[0:64, :], self.cos2, W=[r_c])
    P.dma('sp', sin2[0:64, :], self.sin2, W=[r_c])
    qn = M.tile([4, S], BF16)
    ckv = M.tile([2, NT], BF16)
    r_qn = [Res() for _ in range(4)]
    r_ckv = [Res() for _ in range(5)]
    zt = [M.tile([4, 512], F32) for _ in range(2)]
    r_zt = [Res(), Res()]
    sq = M.tile([4, 512], BF16)
    rstd = M.tile([512], F32)
    tmp = None
    r_sq, r_rs = Res(), Res()
    rps = [Res() for _ in range(8)]
    blocks = token_blocks()
    for i, (t0, T, c) in enumerate(blocks[:4]):
        q = i % 2
        P.dma('sp', zt[q][:, :, 0:T], z1[0:512, t0:t0 + T].rearrange('(k p) t -> p k t', p=128), W=[r_zt[q]])
        self.rms_feat(zt[q], 4, T, gq, qn[:, :, t0:t0 + T], r_zt[q], r_c, r_qn[i], None, None, sq, r_sq, rstd, r_rs, M.bank(7), rps[7])
    for i, (t0, T, c) in enumerate(blocks):
        q = i % 2
        P.dma('sp', zt[q][:, 0:2, 0:T], z1[512:768, t0:t0 + T].rearrange('(k p) t -> p k t', p=128), W=[r_zt[q]])
        self.rms_feat(zt[q][:, 0:2, :], 2, T, gkv, ckv[:, :, t0:t0 + T], r_zt[q], r_c, r_ckv[i], None, None, sq, r_sq, rstd, r_rs, M.bank(7), rps[7])
    stb = [M.tile([512], BF16) for _ in range(4)]
    r_stb = [Res() for _ in range(4)]
    xr = [M.tile([512], F32) for _ in range(2)]
    r_xr = [Res(), Res()]
    tt_ = [M.tile([512], F32) for _ in range(2)]
    r_tt = [Res(), Res()]
    n = 0

    def rope_out(src_sb, r_src, t0, T, dst_dram, latent, q):
        nonlocal n
        b = n % 4
        n += 1
        if latent:
            pr = M.bank(4 + q)
            P.op('pe', lambda e: e.matmul(pr[0:64, 0:T], rotm[0:64, :], src_sb[0:64, 0:T], start=True, stop=True),
                 R=[r_src, r_c], W=[rps[4 + q]])
            P.op('dve', lambda e: e.tensor_tensor(tt_[q][0:64, 0:T], pr[0:64, 0:T], sin2[0:64, t0:t0 + T], ALU.mult),
                 R=[rps[4 + q], r_c], W=[r_tt[q]])
            P.op('dve', lambda e: e.tensor_tensor(src_sb[0:64, 0:T], src_sb[0:64, 0:T], cos2[0:64, t0:t0 + T], ALU.mult),
                 R=[r_src, r_c], W=[r_src])
            P.op('dve', lambda e: e.tensor_tensor(stb[b][0:64, 0:T], src_sb[0:64, 0:T], tt_[q][0:64, 0:T], ALU.add),
                 R=[r_src, r_tt[q]], W=[r_stb[b]])
        else:
            P.op('dve', lambda e: e.tensor_copy(stb[b][0:64, 0:T], src_sb[0:64, 0:T]), R=[r_src], W=[r_stb[b]])
        P.dma('sp', dst_dram, stb[b][0:64, 0:T], R=[r_stb[b]])

    for h in range(8):
        for i, (t0, T, c) in enumerate(blocks[:4]):
            b = n % 4
            n += 1
            ps = M.bank(b)
            for k in range(4):
                P.op('pe', lambda e, ps=ps, k=k, h=h, t0=t0, T=T: e.matmul(
                    ps[:, 0:T], wq[:, k, h * 192:h * 192 + 128], qn[:, k, t0:t0 + T], start=(k == 0), stop=(k == 3)),
                    R=[r_w, r_qn[i]], W=[rps[b]])
            P.op('act', lambda e, ps=ps, b=b, T=T: e.copy(stb[b][:, 0:T], ps[:, 0:T]), R=[rps[b]], W=[r_stb[b]])
            P.dma('sp', self.qT[h, 0:128, t0:t0 + T], stb[b][:, 0:T], R=[r_stb[b]])
            q = i % 2
            b2 = n % 4
            n += 1
            ps2 = M.bank(b2)
            for k in range(4):
                P.op('pe', lambda e, ps2=ps2, k=k, h=h, t0=t0, T=T: e.matmul(
                    ps2[0:64, 0:T], wq[:, k, h * 192 + 128:h * 192 + 192], qn[:, k, t0:t0 + T], start=(k == 0), stop=(k == 3)),
                    R=[r_w, r_qn[i]], W=[rps[b2]])
            P.op('act', lambda e, ps2=ps2, q=q, T=T: e.copy(xr[q][0:64, 0:T], ps2[0:64, 0:T]), R=[rps[b2]], W=[r_xr[q]])
            rope_out(xr[q], r_xr[q], t0, T, self.qT[h, 128:192, t0:t0 + T], True, q)
    for h in range(8):
        for i, (t0, T, c) in enumerate(blocks):
            b = n % 4
            n += 1
            ps = M.bank(b)
            for k in range(2):
                P.op('pe', lambda e, ps=ps, k=k, h=h, t0=t0, T=T: e.matmul(
                    ps[:, 0:T], wkv[:, k, h * 128:(h + 1) * 128], ckv[:, k, t0:t0 + T], start=(k == 0), stop=(k == 1)),
                    R=[r_w, r_ckv[i]], W=[rps[b]])
            P.op('act', lambda e, ps=ps, b=b, T=T: e.copy(stb[b][:, 0:T], ps[:, 0:T]), R=[rps[b]], W=[r_stb[b]])
            P.dma('sp', self.kT[h, :, t0:t0 + T], stb[b][:, 0:T], R=[r_stb[b]])
    for tt in range(NT // 128):
        i = min(tt // 4, 4)
        for g in range(2):
            b = n % 4
            n += 1
            ps = M.bank(b)
            for k in range(2):
                P.op('pe', lambda e, ps=ps, k=k, g=g, tt=tt: e.matmul(
                    ps, ckv[:, k, tt * 128:(tt + 1) * 128], wkv[:, k, 1024 + g * 512:1024 + (g + 1) * 512], start=(k == 0), stop=(k == 1)),
                    R=[r_w, r_ckv[i]], W=[rps[b]])
            P.op('dve', lambda e, ps=ps, b=b: e.tensor_copy(stb[b], ps), R=[rps[b]], W=[r_stb[b]])
            P.dma('sp', self.Vtok[tt * 128:(tt + 1) * 128, g * 512:(g + 1) * 512], stb[b], R=[r_stb[b]])
    for i, (t0, T, c) in enumerate(blocks):
        q = i % 2
        P.dma('sp', xr[q][0:64, 0:T], z1[768:832, t0:t0 + T], W=[r_xr[q]])
        rope_out(xr[q], r_xr[q], t0, T, self.krT[:, t0:t0 + T], c == 0, q)
    P.barrier()


def _phase_attn(self):
    P, M = self.P, self.M
    M.reset()
    SC = 192.0 ** -0.5
    V = M.tile([18, 1024], BF16)
    kra = M.tile([NT], BF16)
    colsel = M.tile([128], BF16)
    r_v, r_kra, r_cs = Res(), Res(), Res()
    for tt in range(18):
        P.dma('sp', V[:, tt, :], self.Vtok[tt * 128:(tt + 1) * 128, :], W=[r_v])
    P.op('dve', lambda e: e.memset(kra[64:128, :], 0.0), W=[r_kra])
    P.op('dve', lambda e: e.memset(kra[64:65, :], 1.0), W=[r_kra])
    P.dma('sp', kra[0:64, :], self.krT, W=[r_kra])
    P.op('dve', lambda e: e.memset(colsel, 0.0), W=[r_cs])
    P.op('dve', lambda e: e.memset(colsel[:, 64:65], 1.0), W=[r_cs])
    qn = [M.tile([S], BF16) for _ in range(2)]
    qrz = [M.tile([S], BF16) for _ in range(2)]
    kn = [M.tile([NT], BF16) for _ in range(2)]
    r_hd = [Res(), Res()]
    for i in range(2):
        P.op('dve', lambda e, i=i: e.memset(qrz[i][64:128, :], 0.0), W=[r_hd[i]])
    NB = 3
    qra = [M.tile([512], BF16) for _ in range(NB)]
    r_qra = [Res() for _ in range(NB)]
    mx = [M.tile([8], F32) for _ in range(4)]
    r_mx = [Res() for _ in range(4)]
    dg = [M.tile([128], BF16) for _ in range(2)]
    r_dg = [Res(), Res()]
    PTt = [M.tile([512], BF16) for _ in range(4)]
    r_PT = [Res() for _ in range(4)]
    rs = [M.tile([512], F32) for _ in range(2)]
    r_rs = [Res(), Res()]
    ob = [M.tile([512], BF16) for _ in range(2)]
    r_ob = [Res(), Res()]
    rps = [Res() for _ in range(8)]
    kblocks = token_blocks()
    cnt = {'sc': 0, 'mx': 0, 'dg': 0, 'p2': 0, 'pt': 0}
    if getattr(self, 'attn_hook', None):
        self.attn_hook()

    def load_head(h):
        i = h % 2
        P.dma('sp', qn[i], self.qT[h, 0:128, :], W=[r_hd[i]])
        P.dma('sp', qrz[i][0:64, :], self.qT[h, 128:192, :], W=[r_hd[i]])
        P.dma('sp', kn[i], self.kT[h], W=[r_hd[i]])

    def pass1(h, qb, bi):
        i = h % 2
        u = bi % NB
        P.dma('sp', qra[u][0:64, :], self.qT[h, 128:192, qb * 512:(qb + 1) * 512], W=[r_qra[u]])
        for qt in range(4):
            qs = slice(qb * 512 + qt * 128, qb * 512 + (qt + 1) * 128)
            m_ = cnt['mx'] % 4
            cnt['mx'] += 1
            for j, (k0, T, c) in enumerate(kblocks):
                b = cnt['sc'] % 3
                cnt['sc'] += 1
                ps = M.bank(b)
                P.op('pe', lambda e, ps=ps, i=i, qs=qs, k0=k0, T=T: e.matmul(ps[:, 0:T], qn[i][:, qs], kn[i][:, k0:k0 + T], start=True, stop=False),
                     R=[r_hd[i]], W=[rps[b]])
                P.op('pe', lambda e, ps=ps, i=i, qs=qs, k0=k0, T=T: e.matmul(ps[:, 0:T], qrz[i][:, qs], kra[:, k0:k0 + T], start=False, stop=True),
                     R=[r_hd[i], r_kra], W=[rps[b]])
                P.op('dve', lambda e, ps=ps, m_=m_, j=j, T=T: e.tensor_reduce(mx[m_][:, j:j + 1], ps[:, 0:T], AX.X, ALU.max),
                     R=[rps[b]], W=[r_mx[m_]])
                yield
            P.op('dve', lambda e, m_=m_: e.tensor_reduce(mx[m_][:, 5:6], mx[m_][:, 0:5], AX.X, ALU.max, negate=True),
                 R=[r_mx[m_]], W=[r_mx[m_]])
            g = cnt['dg'] % 2
            cnt['dg'] += 1
            P.op('dve', lambda e, g=g, m_=m_: e.tensor_scalar(dg[g], self.idbf, mx[m_][:, 5:6], None, ALU.mult),
                 R=[r_mx[m_], self.r_idbf], W=[r_dg[g]])
            P.op('pe', lambda e, g=g, qt=qt: e.matmul(M.bank(3)[:, qt * 128:(qt + 1) * 128], colsel, dg[g], start=True, stop=True),
                 R=[r_cs, r_dg[g]], W=[rps[3]])
        P.op('act', lambda e, u=u: e.copy(qra[u][64:128, :], M.bank(3)[64:128, :]), R=[rps[3]], W=[r_qra[u]])

    def pass2(h, qb, bi):
        i = h % 2
        u = bi % NB
        qcols = slice(qb * 512, (qb + 1) * 512)
        def emit_s(kt):
            b = 4 + kt % 2
            ps = M.bank(b)
            ks = slice(kt * 128, (kt + 1) * 128)
            P.op('pe', lambda e, ps=ps, i=i, ks=ks, qcols=qcols: e.matmul(ps, kn[i][:, ks], qn[i][:, qcols], start=True, stop=False),
                 R=[r_hd[i]], W=[rps[b]])
            P.op('pe', lambda e, ps=ps, ks=ks, u=u: e.matmul(ps, kra[:, ks], qra[u], start=False, stop=True),
                 R=[r_kra, r_qra[u]], W=[rps[b]])

        emit_s(0)
        for kt in range(18):
            if kt + 1 < 18:
                emit_s(kt + 1)
            b = 4 + kt % 2
            ps = M.bank(b)
            t = cnt['pt'] % 4
            cnt['pt'] += 1
            P.op('act', lambda e, ps=ps, t=t: e.activation(PTt[t], ps, AF.Exp, scale=SC), R=[rps[b]], W=[r_PT[t]])
            P.op('pe', lambda e, t=t, kt=kt: e.matmul(M.bank(6), self.ones_bf, PTt[t], start=(kt == 0), stop=(kt == 17)),
                 R=[r_PT[t], self.r_ones], W=[rps[6]])
            P.op('pe', lambda e, t=t, kt=kt, h=h: e.matmul(M.bank(7), V[:, kt, h * 128:(h + 1) * 128], PTt[t], start=(kt == 0), stop=(kt == 17)),
                 R=[r_PT[t], r_v], W=[rps[7]])
            yield
        o = bi % 2
        P.op('dve', lambda e, o=o: e.reciprocal(rs[o], M.bank(6)), R=[rps[6]], W=[r_rs[o]])
        P.op('dve', lambda e, o=o: e.tensor_tensor(ob[o], M.bank(7), rs[o], ALU.mult), R=[rps[7], r_rs[o]], W=[r_ob[o]])
        P.dma('sp', self.oT[h * 128:(h + 1) * 128, qcols], ob[o], R=[r_ob[o]])

    blocks = [(h, qb) for h in range(8) for qb in range(4)]
    def run_both(g2, g1):
        a, b_ = True, True
        while a or b_:
            if a:
                try:
                    next(g2)
                except StopIteration:
                    a = False
            if b_:
                try:
                    next(g1)
                except StopIteration:
                    b_ = False

    load_head(0)
    for _ in pass1(0, 0, 0):
        pass
    for bi, (h, qb) in enumerate(blocks):
        g2 = pass2(h, qb, bi)
        if bi + 1 < len(blocks):
            h1, qb1 = blocks[bi + 1]
            if qb1 == 0:
                pass
            g1 = pass1(h1, qb1, bi + 1)
        else:
            g1 = iter(())
        run_both(g2, g1)
        if qb == 0 and h + 1 < 8:
            load_head(h + 1)
    P.barrier()


Builder.rms_feat = _rms_feat
Builder.phase_mla_prep = _phase_mla_prep
Builder.phase_attn = _phase_attn


def mla_consts():
    f32 = np.float32
    n = S
    row = np.repeat(np.arange(n // 64, dtype=f32), 64)
    col = np.tile(np.arange(64, dtype=f32), n // 64)
    inv = (f32(10000.0) ** (-np.arange(16, dtype=f32) / f32(16))).astype(f32)
    ang = np.concatenate([row[:, None] * inv[None, :], col[:, None] * inv[None, :]], axis=-1).astype(f32)
    cos2 = np.repeat(np.cos(ang).astype(f32), 2, axis=1).T
    sin2 = np.repeat(np.sin(ang).astype(f32), 2, axis=1).T
    rotm = np.zeros((64, 64), f32)
    for i in range(32):
        rotm[2 * i + 1, 2 * i] = -1.0
        rotm[2 * i, 2 * i + 1] = 1.0
    return {'cos2': np.ascontiguousarray(cos2), 'sin2': np.ascontiguousarray(sin2), 'rotm': rotm}

def make_inputs(inputs, b):
    x = np.asarray(inputs['x'][b], np.float32)
    ctx = np.asarray(inputs['ctx'][b], np.float32)
    m = {}
    m['xT'] = np.ascontiguousarray(np.concatenate([x.T, ctx.T], axis=1))
    m['cvec'] = np.ascontiguousarray(np.stack([inputs['c'][b], inputs['c_ctx']]).astype(np.float32))
    m['ada_w'] = np.ascontiguousarray(inputs['ada_w'], np.float32)
    m['ada_b'] = np.ascontiguousarray(inputs['ada_b'], np.float32)
    m['norm1_g'] = np.ascontiguousarray(inputs['norm1_g'], np.float32)
    m['norm2_g'] = np.ascontiguousarray(inputs['norm2_g'], np.float32)
    m['ev_w_in'] = np.ascontiguousarray(inputs['ev_w_in'][0], np.float32)
    m['ident'] = np.eye(128, dtype=np.float32)
    f = lambda k: np.ascontiguousarray(inputs[k][0], np.float32)
    m['rg_conv_w'] = f('ev_rg_conv_w'); m['rg_conv_b'] = f('ev_rg_conv_b')
    m['rg_wa'] = f('ev_rg_wa'); m['rg_ba'] = f('ev_rg_ba'); m['rg_wx'] = f('ev_rg_wx'); m['rg_bx'] = f('ev_rg_bx')
    m['rg_lambda'] = f('ev_rg_lambda')
    m['hy_conv_w'] = f('ev_hy_conv_w'); m['hy_conv_b'] = f('ev_hy_conv_b')
    m['f_w1'] = f('ev_hy_f_w1'); m['f_b1'] = f('ev_hy_f_b1'); m['f_freq'] = f('ev_hy_f_freq')
    m['f_w2'] = f('ev_hy_f_w2'); m['f_b2'] = f('ev_hy_f_b2'); m['f_w3'] = f('ev_hy_f_w3')
    m['hy_bias'] = f('ev_hy_bias')
    m.update(HC())
    m['ev_w_out'] = f('ev_w_out')
    wi = np.zeros((D, 896), np.float32)
    wi[:, :832] = inputs['od_w_in'][0]
    m['od_w_in'] = wi
    m['od_q_norm_g'] = f('od_q_norm_g'); m['od_kv_norm_g'] = f('od_kv_norm_g')
    m['od_w_uq'] = f('od_w_uq'); m['od_w_o'] = f('od_w_o')
    wkv = np.asarray(inputs['od_w_ukv'][0], np.float32).reshape(256, 8, 256)
    m['od_w_ukv'] = np.ascontiguousarray(np.concatenate([wkv[:, :, :128].reshape(256, 1024), wkv[:, :, 128:].reshape(256, 1024)], axis=1))
    m.update(MC())
    for k in ('ffn_w_up', 'ffn_conv_w', 'ffn_conv_b', 'ffn_w_down', 'final_g'):
        m[k] = np.ascontiguousarray(inputs[k], np.float32)
    return m


_HC = {}


def HC():
    if not _HC:
        _HC.update(hyena_consts())
    return _HC


_MC = {}


def MC():
    if not _MC:
        _MC.update(mla_consts())
    return _MC


def kernel(**inputs):
    inputs = {k: np.asarray(v) for k, v in inputs.items()}
    nb = inputs['x'].shape[0]
    B = Builder(dbg=True)
    nc = B.build()
    shared = make_inputs(inputs, 0)
    in_maps = []
    for b in range(nb):
        m = dict(shared)
        x = np.asarray(inputs['x'][b], np.float32)
        ctx = np.asarray(inputs['ctx'][b], np.float32)
        m['xT'] = np.ascontiguousarray(np.concatenate([x.T, ctx.T], axis=1))
        m['cvec'] = np.ascontiguousarray(np.stack([inputs['c'][b], inputs['c_ctx']]).astype(np.float32))
        in_maps.append({k: v for k, v in m.items() if k in B.din})
    res = run_bass_kernel_spmd(nc, in_maps, core_ids=list(range(nb)))
    out = np.stack([np.asarray(res.results[b]['outT'], np.float32).T for b in range(nb)], axis=0)
    return np.ascontiguousarray(out)
```

```python
import contextlib
import math
import numpy as np
import ml_dtypes
import concourse.bass as bass
import concourse.mybir as mybir
from concourse.bass_utils import run_bass_kernel_spmd

dt = mybir.dt
F32 = dt.float32
BF16 = dt.bfloat16
AF = mybir.ActivationFunctionType
ALU = mybir.AluOpType
AX = mybir.AxisListType

D = 1024
S = 2048
CT = 256
NT = S + CT
EPS = 1e-6
ENG = ['pe', 'dve', 'act', 'pool', 'sp']
SAME_SYNC = True


class Res:
    __slots__ = ('wl', 'r')

    def __init__(self):
        self.wl = []
        self.r = {}


class Prog:
    def __init__(self, nc, stack, n_dma=40):
        self.nc = nc
        self.ops = {e: [] for e in ENG}
        self.seq = {e: 0 for e in ENG}
        self.known = {e: {} for e in ENG}
        self.esem = {e: stack.enter_context(nc.semaphore('s_' + e)) for e in ENG}
        self.dsem = [stack.enter_context(nc.semaphore('d%d' % i)) for i in range(n_dma)]
        self.dtgt = [0] * n_dma
        self.drr = 0
        self.n_fg = n_dma - 6
        self.bgrr = 0

    def _sem(self, key):
        return self.esem[key[1]] if key[0] == 'e' else self.dsem[key[1]]

    @staticmethod
    def _joinable(w):
        return bool(w.wl) and not w.r and all(ev[0][0] == 'd' for ev in w.wl)

    def _collect(self, eng, R, W, dma_write=False):
        need = {}

        def add(key, val):
            if val > need.get(key, 0):
                need[key] = val
        for r in R:
            for ev in r.wl:
                add(*ev)
        for w in W:
            if not (dma_write and self._joinable(w)):
                for ev in w.wl:
                    add(*ev)
            for k, v in w.r.items():
                add(k, v)
        waits = []
        kn = self.known[eng]
        for key, val in need.items():
            if key == ('e', eng) and (eng == 'pe' or not SAME_SYNC):
                continue
            if kn.get(key, 0) >= val:
                continue
            kn[key] = val
            waits.append((self._sem(key), val))
        return waits

    def _mark(self, ev, R, W, dma_write=False):
        for r in R:
            if ev[1] > r.r.get(ev[0], 0):
                r.r[ev[0]] = ev[1]
        for w in W:
            if dma_write and self._joinable(w):
                w.wl.append(ev)
            else:
                w.wl = [ev]
                w.r = {}

    def op(self, eng, fn, R=(), W=()):
        waits = self._collect(eng, R, W)
        self.seq[eng] += 1
        ev = (('e', eng), self.seq[eng])
        es = self.esem[eng]

        def emit(e):
            for s, v in waits:
                e.wait_ge(s, v)
            fn(e).then_inc(es, 1)
        self.ops[eng].append(emit)
        self._mark(ev, R, W)

    def dma(self, q, out, in_, R=(), W=(), bg=False, **kw):
        waits = self._collect(q, R, W, dma_write=True)
        if bg:
            si = self.n_fg + self.bgrr
            self.bgrr = (self.bgrr + 1) % (len(self.dsem) - self.n_fg)
        else:
            si = self.drr
            self.drr = (self.drr + 1) % self.n_fg
        prev = self.dtgt[si]
        key = ('d', si)
        if prev > 0 and self.known[q].get(key, 0) < prev:
            self.known[q][key] = prev
            waits.append((self.dsem[si], prev))
        tgt = prev + 16
        self.dtgt[si] = tgt
        ds = self.dsem[si]

        def emit(e):
            for s, v in waits:
                e.wait_ge(s, v)
            e.dma_start(out=out, in_=in_, **kw).then_inc(ds, 16)
        self.ops[q].append(emit)
        self._mark((key, tgt), R, W, dma_write=True)

    def barrier(self, final=False):
        for e in ENG:
            waits = []
            kn = self.known[e]
            for f in ENG:
                if f == e and e == 'pe':
                    continue
                v = self.seq[f]
                key = ('e', f)
                if v > kn.get(key, 0):
                    kn[key] = v
                    waits.append((self.esem[f], v))
            for si, t in enumerate(self.dtgt):
                if si >= self.n_fg and not final:
                    continue
                key = ('d', si)
                if t > kn.get(key, 0):
                    kn[key] = t
                    waits.append((self.dsem[si], t))

            def emit(eng, waits=waits):
                for s, v in waits:
                    eng.wait_ge(s, v)
            self.ops[e].append(emit)

    def play(self):
        with self.nc.Block() as blk:
            @blk.tensor
            def _(e):
                for f in self.ops['pe']:
                    f(e)

            @blk.vector
            def _(e):
                for f in self.ops['dve']:
                    f(e)

            @blk.scalar
            def _(e):
                for f in self.ops['act']:
                    f(e)

            @blk.gpsimd
            def _(e):
                for f in self.ops['pool']:
                    f(e)

            @blk.sync
            def _(e):
                for f in self.ops['sp']:
                    f(e)


class Mem:
    def __init__(self, nc, stack, words=49152):
        self.arena = stack.enter_context(nc.sbuf_tensor('arena', [128, words], F32))
        self.psum = stack.enter_context(nc.psum_tensor('psum', [128, 4096], F32))
        self.words = words
        self.ptr = 0
        self.mark = 0
        self.limit = words * 4
        self.v32 = self.arena
        self.v16 = self.arena.bitcast(BF16)
        self.p16 = self.psum.bitcast(BF16)

    def tile(self, shape, dtype=F32):
        shape = list(shape) if isinstance(shape, (list, tuple)) else [shape]
        n = int(np.prod(shape))
        esz = 4 if dtype == F32 else 2
        self.ptr = (self.ptr + 63) // 64 * 64
        off = self.ptr // esz
        self.ptr += n * esz
        assert self.ptr <= self.limit, 'SBUF arena overflow %d > %d' % (self.ptr, self.limit)
        base = self.v32 if dtype == F32 else self.v16
        ap = base[:, off:off + n]
        if len(shape) == 2:
            ap = ap.rearrange('p (a b) -> p a b', a=shape[0])
        elif len(shape) == 3:
            ap = ap.rearrange('p (a b c) -> p a b c', a=shape[0], b=shape[1])
        return ap

    def persist(self):
        self.mark = self.ptr

    def reserve_top(self, nbytes):
        self.limit = self.words * 4 - nbytes
        assert self.ptr <= self.limit
        off = self.limit // 2
        return self.v16[:, off:off + nbytes // 2]

    def release_top(self):
        self.limit = self.words * 4

    def reset(self):
        self.ptr = self.mark

    def bank(self, i, n=512, dtype=F32, off=0):
        if dtype == F32:
            return self.psum[:, i * 512 + off:i * 512 + off + n]
        return self.p16[:, i * 1024 + off:i * 1024 + off + n]


def token_blocks():
    return [(0, 512, 0), (512, 512, 0), (1024, 512, 0), (1536, 512, 0), (2048, 256, 1)]


class Builder:
    def __init__(self, dbg=False, upto=99, mode=None, feed=()):
        self.dbg = dbg
        self.upto = upto
        self.mode = mode
        self.feed = set(feed)
        self.nc = bass.Bass('TRN2', target_bir_lowering=False)
        self.stack = contextlib.ExitStack()
        self.din = {}
        self.dout = {}

    def inp(self, name, shape, dtype=F32):
        t = self.nc.dram_tensor(name, list(shape), dtype, kind='ExternalInput').ap()
        self.din[name] = t
        return t

    def scratch(self, name, shape, dtype=F32, out=False):
        kind = 'ExternalOutput' if (out or self.dbg) else 'Internal'
        if name in self.feed:
            kind = 'ExternalInput'
        t = self.nc.dram_tensor(name, list(shape), dtype, kind=kind).ap()
        if kind == 'ExternalInput':
            self.din[name] = t
        if kind == 'ExternalOutput':
            self.dout[name] = t
        return t

    def build(self):
        nc = self.nc
        st = self.stack
        P = self.P = Prog(nc, st)
        M = self.M = Mem(nc, st)
        I = self.inp
        self.xT = I('xT', [D, NT])
        self.cvec = I('cvec', [2, D])
        self.ada_w = I('ada_w', [2, D, 6 * D])
        self.ada_b = I('ada_b', [2, 6 * D])
        self.norm1_g = I('norm1_g', [2, D])
        self.norm2_g = I('norm2_g', [2, D])
        self.ev_w_in = I('ev_w_in', [D, 2560])
        self.ident = I('ident', [128, 128])
        self.rg_conv_w = I('rg_conv_w', [4, 512])
        self.rg_conv_b = I('rg_conv_b', [512])
        self.rg_wa = I('rg_wa', [2, 8, 64, 64])
        self.rg_ba = I('rg_ba', [2, 512])
        self.rg_wx = I('rg_wx', [2, 8, 64, 64])
        self.rg_bx = I('rg_bx', [2, 512])
        self.rg_lambda = I('rg_lambda', [2, 512])
        self.hy_conv_w = I('hy_conv_w', [3, 1536])
        self.hy_conv_b = I('hy_conv_b', [1536])
        self.f_w1 = I('f_w1', [33, 64]); self.f_b1 = I('f_b1', [64]); self.f_freq = I('f_freq', [64])
        self.f_w2 = I('f_w2', [64, 64]); self.f_b2 = I('f_b2', [64]); self.f_w3 = I('f_w3', [64, 2048])
        self.hy_bias = I('hy_bias', [2, 512])
        self.ev_w_out = I('ev_w_out', [D, D])
        self.ffn_w_up = I('ffn_w_up', [2, D, 5632])
        self.ffn_conv_w = I('ffn_conv_w', [2, 3, 5632])
        self.ffn_conv_b = I('ffn_conv_b', [2, 5632])
        self.ffn_cw = I('ffn_cw', [2, 128, 44, 4])
        self.ffn_w_down = I('ffn_w_down', [2, 2816, D])
        self.final_g = I('final_g', [D])
        self.xa0 = self.scratch('xa0', [D, NT])
        self.xb0 = self.scratch('xb0', [D, NT])
        self.od_w_in = I('od_w_in', [D, 896])
        self.od_q_norm_g = I('od_q_norm_g', [512]); self.od_kv_norm_g = I('od_kv_norm_g', [256])
        self.od_w_uq = I('od_w_uq', [512, 1536]); self.od_w_ukv = I('od_w_ukv', [256, 2048])
        self.od_w_o = I('od_w_o', [D, D])
        self.cos2 = I('cos2', [64, S]); self.sin2 = I('sin2', [64, S]); self.rotm = I('rotm', [64, 64])
        self.z1 = self.scratch('z1', [896, NT])
        self.qT = self.scratch('qT', [8, 192, S], BF16)
        self.kT = self.scratch('kT', [8, 128, NT], BF16)
        self.krT = self.scratch('krT', [64, NT], BF16)
        self.Vtok = self.scratch('Vtok', [NT, D], BF16)
        self.oT = self.scratch('oT', [D, S], BF16)
        self.xa1 = self.scratch('xa1', [D, S])
        self.outT = self.scratch('outT', [D, S], out=True)
        self.hc = {}
        for n in (2048, 256):
            nj = n // 128
            self.hc[n] = dict(zT=I('zT%d' % n, [33, n]), dec=I('dec%d' % n, [n, 512]), decb=I('decb%d' % n, [n, 512]),
                              C=I('Ccol%d' % n, [nj, 128, nj, 128], BF16), S=I('Scol%d' % n, [nj, 128, nj, 128], BF16),
                              wk=I('wk%d' % n, [128, nj]), Kf=self.scratch('Kf%d' % n, [2, 2, n, 512]))
        self.vtok = self.scratch('vtok', [3, NT, 512])
        self.u2tok = self.scratch('u2tok', [NT, 512])
        self.z0 = self.scratch('z0', [2560, NT])
        self.mixT = self.scratch('mixT', [D, NT], BF16)
        self.consts()
        if self.mode == 'attn':
            self.phase_attn()
            P.barrier()
            P.play()
            return nc
        if self.upto >= 1:
            for l in range(2):
                self.phase_mod(l)
        if self.upto >= 2:
            self.phase_norm_proj(0, self.xT, 1, self.ev_w_in, 2560, self.z0)
        if self.upto >= 3:
            self.phase_rg()
        if self.upto >= 4:
            for n in (2048, 256):
                h = self.hc[n]
                self.phase_hy_filters(n, h['zT'], h['dec'], h['decb'], h['C'], h['S'], h['wk'], h['Kf'])
            self.phase_hy_prep()
            for n, tok0 in ((2048, 0), (256, S)):
                h = self.hc[n]
                self.phase_hy_conv(n, tok0, h['C'], h['S'], h['Kf'])
        if self.upto >= 5:
            self.phase_out_proj(0, self.mixT, self.ev_w_out, self.xT, self.xa0, token_blocks(), prefetch=0)
        if self.upto >= 6:
            self.phase_ffn(0, self.xa0, self.xb0, [(0, S, 0), (S, NT, 1)])
        lat = token_blocks()[:4]
        if self.upto >= 7:
            self.phase_norm_proj(1, self.xb0, 1, self.od_w_in, 896, self.z1)
            self.phase_mla_prep()
        if self.upto >= 8:
            self.attn_hook = lambda: self.prefetch_wup(1)
            self.phase_attn()
            self.phase_out_proj(1, self.oT, self.od_w_o, self.xb0, self.xa1, lat)
        if self.upto >= 9:
            self.phase_ffn(1, self.xa1, None, [(0, S, 0)], final_out=self.outT)
        P.barrier(final=True)
        P.play()
        return nc

    def consts(self):
        P, M = self.P, self.M
        self.ones_bf = M.tile([128], BF16)
        self.r_ones = Res()
        P.op('dve', lambda e: e.memset(self.ones_bf, 1.0), W=[self.r_ones])
        self.id32 = M.tile([128], F32)
        self.idbf = M.tile([128], BF16)
        self.r_id = Res()
        P.dma('sp', self.id32, self.ident, W=[self.r_id])
        self.r_idbf = Res()
        P.op('dve', lambda e: e.tensor_copy(self.idbf, self.id32), R=[self.r_id], W=[self.r_idbf])
        self.mod = [M.tile([48, 2], F32) for _ in range(2)]
        self.A1 = [M.tile([8, 2], F32) for _ in range(2)]
        self.A2 = [M.tile([8, 2], F32) for _ in range(2)]
        self.r_mod = [Res() for _ in range(2)]
        M.persist()

    def phase_mod(self, l):
        P, M = self.P, self.M
        M.reset()
        cv = M.tile([8, 2], F32)
        sT = M.tile([8, 2], BF16)
        bT = M.tile([48], F32)
        g1 = M.tile([8], F32)
        g2 = M.tile([8], F32)
        r_cv, r_sT, r_b, r_g = Res(), Res(), Res(), Res()
        for j in range(2):
            P.dma('sp', cv[:, :, j], self.cvec[j].rearrange('(k p) -> p k', p=128), W=[r_cv],
                  allow_slow_non_contiguous=True)
        P.dma('sp', bT, self.ada_b[l].rearrange('(j p) -> p j', p=128), W=[r_b], allow_slow_non_contiguous=True)
        P.dma('sp', g1, self.norm1_g[l].rearrange('(k p) -> p k', p=128), W=[r_g], allow_slow_non_contiguous=True)
        P.dma('sp', g2, self.norm2_g[l].rearrange('(k p) -> p k', p=128), W=[r_g], allow_slow_non_contiguous=True)
        P.op('act', lambda e: e.activation(sT, cv, AF.Silu), R=[r_cv], W=[r_sT])
        wt = [M.tile([8, 2048], BF16) for _ in range(2)]
        r_wt = [Res(), Res()]
        ps = M.bank(0, 96).rearrange('p (j c) -> p j c', c=2)
        r_ps = Res()
        for sec in range(3):
            w = wt[sec % 2]
            rw = r_wt[sec % 2]
            for k in range(8):
                P.dma('pool', w[:, k, :], self.ada_w[l, k * 128:(k + 1) * 128, sec * 2048:(sec + 1) * 2048], W=[rw])
            for fc in range(16):
                j = sec * 16 + fc
                for k in range(8):
                    P.op('pe', lambda e, w=w, k=k, fc=fc, j=j: e.matmul(
                        ps[:, j, :], w[:, k, fc * 128:(fc + 1) * 128], sT[:, k, :],
                        start=(k == 0), stop=(k == 7)), R=[rw, r_sT], W=[r_ps])
        mod = self.mod[l]
        rm = self.r_mod[l]
        for c in range(2):
            P.op('dve', lambda e, c=c: e.tensor_tensor(mod[:, :, c], ps[:, :, c], bT, ALU.add),
                 R=[r_ps, r_b], W=[rm])
        for c in range(2):
            P.op('dve', lambda e, c=c: e.scalar_tensor_tensor(
                self.A1[l][:, :, c], mod[:, 8:16, c], 1.0, g1, ALU.add, ALU.mult), R=[rm, r_g], W=[rm])
            P.op('dve', lambda e, c=c: e.scalar_tensor_tensor(
                self.A2[l][:, :, c], mod[:, 32:40, c], 1.0, g2, ALU.add, ALU.mult), R=[rm, r_g], W=[rm])
        P.barrier()

    def norm_block(self, src, t0, T, c, A, B, hdst, r_hdst, bufs, i):
        P, M = self.P, self.M
        xb, sq, rstd, tmp, rx, rsq, rrs, rtmp, psb, rps = bufs[i % 2]
        P.dma('sp', xb[:, :, 0:T], src[:, t0:t0 + T].rearrange('(k p) t -> p k t', p=128), W=[rx])
        P.op('act', lambda e: e.activation(sq[:, :, 0:T], xb[:, :, 0:T], AF.Square), R=[rx], W=[rsq])
        for k in range(8):
            P.op('pe', lambda e, k=k: e.matmul(psb[:, 0:T], self.ones_bf, sq[:, k, 0:T],
                                               start=(k == 0), stop=(k == 7)),
                 R=[rsq, self.r_ones], W=[rps])
        P.op('act', lambda e: e.activation(rstd[:, 0:T], psb[:, 0:T], AF.Sqrt, bias=EPS, scale=1.0 / D),
             R=[rps], W=[rrs])
        P.op('dve', lambda e: e.reciprocal(rstd[:, 0:T], rstd[:, 0:T]), R=[rrs], W=[rrs])
        for k in range(8):
            P.op('dve', lambda e, k=k: e.scalar_tensor_tensor(
                tmp[:, k, 0:T], xb[:, k, 0:T], A[:, k, c:c + 1], rstd[:, 0:T], ALU.mult, ALU.mult),
                R=[rx, rrs, self.r_mod[0], self.r_mod[1]], W=[rtmp[k]])
            P.op('act', lambda e, k=k: e.activation(hdst[:, k, :], tmp[:, k, 0:T], AF.Identity,
                                                    bias=B[:, k, c:c + 1]),
                 R=[rtmp[k], self.r_mod[0], self.r_mod[1]], W=[r_hdst])

    def norm_bufs(self, psbanks):
        M = self.M
        bufs = []
        for i in range(2):
            bufs.append((M.tile([8, 512], F32), M.tile([8, 512], BF16), M.tile([512], F32),
                         M.tile([8, 512], F32), Res(), Res(), Res(), [Res() for _ in range(8)],
                         M.bank(psbanks[i]), Res()))
        return bufs

    def phase_norm_proj(self, l, src, which, w_dram, F, zdst, blocks=None):
        P, M = self.P, self.M
        M.reset()
        blocks = blocks or token_blocks()
        A = self.A1[l] if which == 1 else self.A2[l]
        B = self.mod[l][:, 0:8, :] if which == 1 else self.mod[l][:, 24:32, :]
        hT = M.tile([8, NT], BF16)
        r_h = [Res() for _ in blocks]
        nfc = F // 128
        wt = M.tile([8, F], BF16)
        r_w = [Res() for _ in range(nfc)]
        for g in range(0, F, 2048):
            gw = min(2048, F - g)
            for k in range(8):
                P.dma('pool', wt[:, k, g:g + gw], w_dram[k * 128:(k + 1) * 128, g:g + gw],
                      W=[r_w[j] for j in range(g // 128, (g + gw) // 128)])
        bufs = self.norm_bufs([6, 7])
        for i, (t0, T, c) in enumerate(blocks):
            self.norm_block(src, t0, T, c, A, B, hT[:, :, t0:t0 + T], r_h[i], bufs, i)
        stg = [M.tile([512], F32) for _ in range(4)]
        r_stg = [Res() for _ in range(4)]
        r_ps = [Res() for _ in range(4)]
        n = 0
        for fc in range(nfc):
            for i, (t0, T, c) in enumerate(blocks):
                b = n % 4
                ps = M.bank(b)
                for k in range(8):
                    P.op('pe', lambda e, k=k, fc=fc, t0=t0, T=T, ps=ps: e.matmul(
                        ps[:, 0:T], wt[:, k, fc * 128:(fc + 1) * 128], hT[:, k, t0:t0 + T],
                        start=(k == 0), stop=(k == 7)), R=[r_w[fc], r_h[i]], W=[r_ps[b]])
                ev = 'act' if n % 2 == 0 else 'dve'
                if ev == 'act':
                    P.op('act', lambda e, b=b, T=T, ps=ps: e.copy(stg[b][:, 0:T], ps[:, 0:T]),
                         R=[r_ps[b]], W=[r_stg[b]])
                else:
                    P.op('dve', lambda e, b=b, T=T, ps=ps: e.tensor_copy(stg[b][:, 0:T], ps[:, 0:T]),
                         R=[r_ps[b]], W=[r_stg[b]])
                P.dma('sp', zdst[fc * 128:(fc + 1) * 128, t0:t0 + T], stg[b][:, 0:T], R=[r_stg[b]])
                n += 1
        P.barrier()


def rev(ap, n):
    pat = [list(p) for p in ap.ap]
    assert len(pat) == 2 and pat[1][1] == n
    return bass.AP(ap.tensor, ap.offset + (n - 1) * pat[1][0], [pat[0], [-pat[1][0], n]])


def _phase_rg(self):
    P, M = self.P, self.M
    M.reset()
    f = lambda: M.tile([NT], F32)
    zx, gate, xc, hf, hb, ga = [f() for _ in range(6)]
    rgs, igs, aas, bbs = [[f(), f()] for _ in range(4)]
    xcb = M.tile([NT], BF16)
    ob = M.tile([NT], BF16)
    cw = M.tile([4], F32)
    cb = M.tile([1], F32)
    bias4 = M.tile([4], F32)
    lam = M.tile([2], F32)
    cl = M.tile([4], F32)
    wbd = [M.tile([128], BF16) for _ in range(4)]
    segs = [(0, S), (S, CT)]
    blocks = token_blocks()
    for cc in range(4):
        R = {k: Res() for k in ['zx', 'gate', 'xc', 'xcb', 'r0', 'i0', 'a0', 'b0', 'r1', 'i1', 'a1', 'b1', 'hf', 'hb', 'small', 'cl', 'ob', 'ga']}
        rw = [Res() for _ in range(4)]
        rps = [Res() for _ in range(4)]
        fs = slice(cc * 128, (cc + 1) * 128)
        P.dma('sp', zx, self.z0[cc * 128:(cc + 1) * 128, :], W=[R['zx']])
        P.dma('sp', gate, self.z0[512 + cc * 128:512 + (cc + 1) * 128, :], W=[R['gate']])
        P.dma('sp', cw, self.rg_conv_w[:, fs].rearrange('j p -> p j'), W=[R['small']], allow_slow_non_contiguous=True)
        P.dma('sp', cb, self.rg_conv_b[fs].rearrange('(p o) -> p o', o=1), W=[R['small']])
        for d in range(2):
            P.dma('sp', bias4[:, 2 * d:2 * d + 1], self.rg_ba[d, fs].rearrange('(p o) -> p o', o=1), W=[R['small']])
            P.dma('sp', bias4[:, 2 * d + 1:2 * d + 2], self.rg_bx[d, fs].rearrange('(p o) -> p o', o=1), W=[R['small']])
            P.dma('sp', lam[:, d:d + 1], self.rg_lambda[d, fs].rearrange('(p o) -> p o', o=1), W=[R['small']])
        for d in range(2):
            for ty, wsrc in enumerate([self.rg_wa, self.rg_wx]):
                w = wbd[2 * d + ty]
                r_ = rw[2 * d + ty]
                P.op('dve', lambda e, w=w: e.memset(w, 0.0), W=[r_])
                for hh in range(2):
                    P.dma('pool', w[hh * 64:(hh + 1) * 64, hh * 64:(hh + 1) * 64], wsrc[d, 2 * cc + hh], W=[r_])
        P.op('act', lambda e: e.activation(cl[:, 0:2], lam, AF.Exp, scale=-1.0), R=[R['small']], W=[R['cl']])
        P.op('act', lambda e: e.activation(cl[:, 0:2], cl[:, 0:2], AF.Ln, bias=1.0), R=[R['cl']], W=[R['cl']])
        P.op('dve', lambda e: e.tensor_scalar(cl[:, 2:4], cl[:, 0:2], -16.0, None, ALU.mult), R=[R['cl']], W=[R['cl']])
        P.op('dve', lambda e: e.tensor_scalar(cl[:, 0:2], cl[:, 0:2], -8.0, None, ALU.mult), R=[R['cl']], W=[R['cl']])
        P.op('act', lambda e: e.activation(xc, zx, AF.Identity, bias=cb[:, 0:1], scale=cw[:, 2:3]),
             R=[R['zx'], R['small']], W=[R['xc']])
        for j in (0, 1, 3):
            off = j - 2
            for (s0, n) in segs:
                lo = max(0, -off)
                hi = n - max(0, off)
                P.op('dve', lambda e, j=j, off=off, s0=s0, lo=lo, hi=hi: e.scalar_tensor_tensor(
                    xc[:, s0 + lo:s0 + hi], zx[:, s0 + lo + off:s0 + hi + off], cw[:, j:j + 1],
                    xc[:, s0 + lo:s0 + hi], ALU.mult, ALU.add), R=[R['zx'], R['small'], R['xc']], W=[R['xc']])
        P.op('act', lambda e: e.copy(xcb, xc), R=[R['xc']], W=[R['xcb']])
        P.op('act', lambda e: e.activation(ga, gate, AF.Square), R=[R['gate']], W=[R['ga']])
        P.op('pool', lambda e: e.tensor_scalar(ga, ga, 0.044715, 1.0, ALU.mult, ALU.add), R=[R['ga']], W=[R['ga']])
        P.op('pool', lambda e: e.tensor_tensor(ga, ga, gate, ALU.mult), R=[R['ga'], R['gate']], W=[R['ga']])
        P.op('act', lambda e: e.activation(ga, ga, AF.Sigmoid, scale=1.5957691216), R=[R['ga']], W=[R['ga']])
        P.op('pool', lambda e: e.tensor_tensor(ga, ga, gate, ALU.mult), R=[R['ga'], R['gate']], W=[R['ga']])
        for d in range(2):
            rg, ig, aa, bb = rgs[d], igs[d], aas[d], bbs[d]
            kr_, ki_, ka_, kb_ = 'r%d' % d, 'i%d' % d, 'a%d' % d, 'b%d' % d
            for ty, dst, rk in ((0, rg, kr_), (1, ig, ki_)):
                for bi, (t0, T, c) in enumerate(blocks):
                    bk = (bi + ty) % 4
                    ps = M.bank(bk)
                    P.op('pe', lambda e, ps=ps, d=d, ty=ty, t0=t0, T=T: e.matmul(
                        ps[:, 0:T], wbd[2 * d + ty], xcb[:, t0:t0 + T], start=True, stop=True),
                        R=[rw[2 * d + ty], R['xcb']], W=[rps[bk]])
                    P.op('act', lambda e, ps=ps, dst=dst, d=d, ty=ty, t0=t0, T=T: e.activation(
                        dst[:, t0:t0 + T], ps[:, 0:T], AF.Sigmoid, bias=bias4[:, 2 * d + ty:2 * d + ty + 1]),
                        R=[rps[bk], R['small']], W=[R[rk]])
            P.op('act', lambda e, d=d, aa=aa, rg=rg: e.activation(aa, rg, AF.Exp, scale=cl[:, d:d + 1]), R=[R[kr_], R['cl']], W=[R[ka_]])
            P.op('act', lambda e, d=d, bb=bb, rg=rg: e.activation(bb, rg, AF.Exp, scale=cl[:, 2 + d:3 + d]), R=[R[kr_], R['cl']], W=[R[kb_]])
            P.op('act', lambda e, bb=bb: e.activation(bb, bb, AF.Sqrt, bias=1.0, scale=-1.0), R=[R[kb_]], W=[R[kb_]])
            P.op('pool', lambda e, bb=bb, ig=ig: e.tensor_tensor(bb, bb, ig, ALU.mult), R=[R[kb_], R[ki_]], W=[R[kb_]])
            P.op('pool', lambda e, bb=bb: e.tensor_tensor(bb, bb, xc, ALU.mult), R=[R[kb_], R['xc']], W=[R[kb_]])
            h = hf if d == 0 else hb
            hk = 'hf' if d == 0 else 'hb'
            if d == 0:
                P.op('dve', lambda e, h=h, aa=aa, bb=bb: e.tensor_tensor_scan(h[:, S:NT], aa[:, S:NT], bb[:, S:NT], 0.0, ALU.mult, ALU.add),
                     R=[R[ka_], R[kb_]], W=[R[hk]])
                P.op('dve', lambda e, h=h, aa=aa, bb=bb: e.tensor_tensor_scan(h[:, 0:S], aa[:, 0:S], bb[:, 0:S], h[:, NT - 1:NT], ALU.mult, ALU.add),
                     R=[R[ka_], R[kb_], R[hk]], W=[R[hk]])
            else:
                P.op('dve', lambda e, h=h, aa=aa, bb=bb: e.tensor_tensor_scan(rev(h[:, S:NT], CT), rev(aa[:, S:NT], CT), rev(bb[:, S:NT], CT),
                                                                               0.0, ALU.mult, ALU.add), R=[R[ka_], R[kb_]], W=[R[hk]])
                P.op('dve', lambda e, h=h, aa=aa, bb=bb: e.tensor_tensor_scan(rev(h[:, 0:S], S), rev(aa[:, 0:S], S), rev(bb[:, 0:S], S),
                                                                               h[:, S:S + 1], ALU.mult, ALU.add),
                     R=[R[ka_], R[kb_], R[hk]], W=[R[hk]])
        P.op('pool', lambda e: e.tensor_tensor(hf, hf, hb, ALU.add), R=[R['hf'], R['hb']], W=[R['hf']])
        P.op('dve', lambda e: e.tensor_tensor(ob, ga, hf, ALU.mult), R=[R['ga'], R['hf']], W=[R['ob']])
        P.dma('sp', self.mixT[cc * 128:(cc + 1) * 128, :], ob, R=[R['ob']])
        P.barrier()


Builder.phase_rg = _phase_rg


def bcast_rows(ap_row, nparts=128):
    pat = [list(p) for p in ap_row.ap]
    return bass.AP(ap_row.tensor, ap_row.offset, [[0, nparts]] + pat)


def _phase_hy_filters(self, n, zT_d, dec_d, decb_d, Ccol, Scol, wk_d, Kf_d):
    P, M = self.P, self.M
    M.reset()
    nj = n // 128
    w1 = M.tile([64], F32)
    w2 = M.tile([64], F32)
    w3 = M.tile([2048], F32)
    sm = M.tile([3], F32)
    zT = M.tile([n], F32)
    h1 = M.tile([n], F32)
    h2 = M.tile([n], F32)
    tm = M.tile([512], F32)
    r_w, r_z, r_h1, r_h2, r_tm = Res(), Res(), Res(), Res(), Res()
    P.dma('sp', w1[0:33, :], self.f_w1, W=[r_w])
    P.dma('sp', w2[0:64, :], self.f_w2, W=[r_w])
    P.dma('sp', w3[0:64, :], self.f_w3, W=[r_w])
    P.dma('sp', sm[0:64, 0:1], self.f_b1.rearrange('(p o) -> p o', o=1), W=[r_w])
    P.dma('sp', sm[0:64, 1:2], self.f_freq.rearrange('(p o) -> p o', o=1), W=[r_w])
    P.dma('sp', sm[0:64, 2:3], self.f_b2.rearrange('(p o) -> p o', o=1), W=[r_w])
    P.dma('sp', zT[0:33, :], zT_d, W=[r_z])
    rps = [Res() for _ in range(8)]
    PI = math.pi

    def sin_layer(lhsT, kk, src, rsrc, bcol, dst, rdst):
        for bi, t0 in enumerate(range(0, n, 512)):
            T = min(512, n - t0)
            ps = M.bank(bi % 2)
            P.op('pe', lambda e, ps=ps, t0=t0, T=T: e.matmul(ps[0:64, 0:T], lhsT, src[0:kk, t0:t0 + T], start=True, stop=True),
                 R=[r_w, rsrc], W=[rps[bi % 2]])
            P.op('dve', lambda e, ps=ps, T=T: e.tensor_scalar(tm[0:64, 0:T], ps[0:64, 0:T], sm[0:64, bcol:bcol + 1],
                                                           sm[0:64, 1:2], ALU.add, ALU.mult), R=[rps[bi % 2], r_w, r_tm], W=[r_tm])
            for thr, op, mul in ((PI, ALU.is_gt, -2 * PI), (-PI, ALU.is_lt, 2 * PI)):
                P.op('dve', lambda e, T=T, t0=t0, thr=thr, op=op, mul=mul: e.tensor_scalar(
                    dst[0:64, t0:t0 + T], tm[0:64, 0:T], thr, mul, op, ALU.mult), R=[r_tm], W=[rdst])
                P.op('dve', lambda e, T=T, t0=t0: e.tensor_tensor(tm[0:64, 0:T], tm[0:64, 0:T], dst[0:64, t0:t0 + T], ALU.add),
                     R=[r_tm, rdst], W=[r_tm])
            P.op('act', lambda e, T=T, t0=t0: e.activation(dst[0:64, t0:t0 + T], tm[0:64, 0:T], AF.Sin), R=[r_tm], W=[rdst])

    sin_layer(w1[0:33, :], 33, zT, r_z, 0, h1, r_h1)
    sin_layer(w2[0:64, :], 64, h1, r_h1, 2, h2, r_h2)
    ks = M.tile([nj, 1024], BF16)
    kd = M.tile([nj, 1024], BF16)
    dec = [M.tile([512], F32) for _ in range(2)]
    decb = [M.tile([512], F32) for _ in range(2)]
    kf = [M.tile([512], F32) for _ in range(2)]
    kb = [M.tile([512], F32) for _ in range(2)]
    ab = [M.tile([512], BF16) for _ in range(2)]
    r_dec = [Res(), Res()]
    r_kf = [Res(), Res()]
    r_kb = [Res(), Res()]
    r_ab = [Res(), Res()]
    r_ks = Res()
    r_nrm = Res()
    cnt = 0
    for j in range(nj):
        P.dma('sp', dec[j % 2], dec_d[j * 128:(j + 1) * 128, :], W=[r_dec[j % 2]])
        P.dma('sp', decb[j % 2], decb_d[j * 128:(j + 1) * 128, :], W=[r_dec[j % 2]])
        for o in range(2):
            q = cnt % 2
            cnt += 1
            for dr in range(2):
                bk = 2 + 2 * q + dr
                ps = M.bank(bk)
                c0 = o * 1024 + dr * 512
                P.op('pe', lambda e, ps=ps, j=j, c0=c0: e.matmul(ps, h2[0:64, j * 128:(j + 1) * 128], w3[0:64, c0:c0 + 512],
                                                             start=True, stop=True), R=[r_h2, r_w], W=[rps[bk]])
                dst, rd, dc = (kf[q], r_kf[q], dec[j % 2]) if dr == 0 else (kb[q], r_kb[q], decb[j % 2])
                P.op('dve', lambda e, ps=ps, dst=dst, dc=dc: e.tensor_tensor(dst, ps, dc, ALU.mult),
                     R=[rps[bk], r_dec[j % 2]], W=[rd])
                P.op('act', lambda e, dst=dst, q=q: e.activation(ab[q], dst, AF.Abs), R=[rd], W=[r_ab[q]])
                first = (j == 0 and dr == 0)
                last = (j == nj - 1 and dr == 1)
                P.op('pe', lambda e, o=o, q=q, first=first, last=last: e.matmul(M.bank(6 + o), self.ones_bf, ab[q], start=first, stop=last),
                     R=[r_ab[q], self.r_ones], W=[r_nrm])
            P.op('dve', lambda e, q=q, j=j, o=o: e.tensor_tensor(ks[:, j, o * 512:(o + 1) * 512], kf[q], kb[q], ALU.add),
                 R=[r_kf[q], r_kb[q]], W=[r_ks])
            P.op('dve', lambda e, q=q, j=j, o=o: e.tensor_tensor(kd[:, j, o * 512:(o + 1) * 512], kb[q], kf[q], ALU.subtract),
                 R=[r_kf[q], r_kb[q]], W=[r_ks])
    rn = M.tile([1024], F32)
    r_rn = Res()
    for o in range(2):
        P.op('dve', lambda e, o=o: e.reciprocal(rn[:, o * 512:(o + 1) * 512], M.bank(6 + o)), R=[r_nrm], W=[r_rn])
    wk = M.tile([nj], F32)
    P.dma('sp', wk, wk_d, W=[r_rn])
    cs = [[M.tile([nj, 128], BF16) for _ in range(2)] for _ in range(2)]
    r_cs = [Res(), Res()]
    ot = [M.tile([512], F32) for _ in range(4)]
    r_ot = [Res() for _ in range(4)]
    m = 0
    for f in range(nj):
        q = f % 2
        P.dma('sp', cs[q][0], Ccol[f], W=[r_cs[q]])
        P.dma('sp', cs[q][1], Scol[f], W=[r_cs[q]])
        for ri in range(2):
            src = ks if ri == 0 else kd
            for o in range(2):
                bk = m % 4
                ps = M.bank(bk)
                for j in range(nj):
                    P.op('pe', lambda e, ps=ps, q=q, ri=ri, j=j, o=o, src=src: e.matmul(
                        ps, cs[q][ri][:, j, :], src[:, j, o * 512:(o + 1) * 512], start=(j == 0), stop=(j == nj - 1)),
                        R=[r_cs[q], r_ks], W=[rps[bk]])
                P.op('dve', lambda e, ps=ps, bk=bk, f=f, o=o: e.scalar_tensor_tensor(
                    ot[bk], ps, wk[:, f:f + 1], rn[:, o * 512:(o + 1) * 512], ALU.mult, ALU.mult),
                    R=[rps[bk], r_rn], W=[r_ot[bk]])
                P.dma('sp', Kf_d[o, ri, f * 128:(f + 1) * 128, :], ot[bk], R=[r_ot[bk]])
                m += 1
    P.barrier()


def _phase_hy_prep(self):
    P, M = self.P, self.M
    M.reset()
    zz = [M.tile([NT], F32) for _ in range(2)]
    zc = [M.tile([NT], F32) for _ in range(2)]
    cw = [M.tile([4], F32) for _ in range(2)]
    stg = [M.tile([4, 128], F32) for _ in range(2)]
    r_zz = [Res(), Res()]
    r_zc = [Res(), Res()]
    r_cw = [Res(), Res()]
    r_stg = [Res(), Res()]
    r_ps = [Res(), Res()]
    segs = [(0, S), (S, CT)]
    m = 0
    for ch in range(12):
        q = ch % 2
        g, cc = ch // 4, ch % 4
        fs = slice(ch * 128, (ch + 1) * 128)
        P.dma('sp', zz[q], self.z0[1024 + ch * 128:1024 + (ch + 1) * 128, :], W=[r_zz[q]])
        P.dma('sp', cw[q][:, 0:3], self.hy_conv_w[:, fs].rearrange('j p -> p j'), W=[r_cw[q]], allow_slow_non_contiguous=True)
        P.dma('sp', cw[q][:, 3:4], self.hy_conv_b[fs].rearrange('(p o) -> p o', o=1), W=[r_cw[q]])
        P.op('act', lambda e, q=q: e.activation(zc[q], zz[q], AF.Identity, bias=cw[q][:, 3:4], scale=cw[q][:, 1:2]),
             R=[r_zz[q], r_cw[q]], W=[r_zc[q]])
        for j in (0, 2):
            off = j - 1
            for (s0, n) in segs:
                lo = max(0, -off)
                hi = n - max(0, off)
                P.op('dve', lambda e, q=q, j=j, off=off, s0=s0, lo=lo, hi=hi: e.scalar_tensor_tensor(
                    zc[q][:, s0 + lo:s0 + hi], zz[q][:, s0 + lo + off:s0 + hi + off], cw[q][:, j:j + 1],
                    zc[q][:, s0 + lo:s0 + hi], ALU.mult, ALU.add), R=[r_zz[q], r_cw[q], r_zc[q]], W=[r_zc[q]])
        for tg in range(0, NT // 128, 4):
            ntl = min(4, NT // 128 - tg)
            b = m % 2
            m += 1
            ps = M.bank(b)
            for i in range(ntl):
                P.op('pe', lambda e, ps=ps, q=q, tg=tg, i=i: e.transpose(
                    ps[:, i * 128:(i + 1) * 128], zc[q][:, (tg + i) * 128:(tg + i + 1) * 128], self.id32),
                    R=[r_zc[q], self.r_id], W=[r_ps[b]])
            P.op('act' if b == 0 else 'dve',
                 (lambda e, ps=ps, b=b, ntl=ntl: e.copy(stg[b][:, 0:ntl, :], ps[:, 0:ntl * 128].rearrange('p (a c) -> p a c', c=128)))
                 if b == 0 else
                 (lambda e, ps=ps, b=b, ntl=ntl: e.tensor_copy(stg[b][:, 0:ntl, :], ps[:, 0:ntl * 128].rearrange('p (a c) -> p a c', c=128))),
                 R=[r_ps[b]], W=[r_stg[b]])
            P.dma('sp', self.vtok[g, tg * 128:(tg + ntl) * 128, cc * 128:(cc + 1) * 128].rearrange('(a p) c -> p a c', p=128),
                  stg[b][:, 0:ntl, :], R=[r_stg[b]])
    P.barrier()


def _phase_hy_conv(self, n, tok0, Ccol, Scol, Kf_d):
    P, M = self.P, self.M
    M.reset()
    nj = n // 128
    u = M.tile([nj, 512], BF16)
    Y = M.tile([nj, 2, 512], BF16)
    r_u = [Res() for _ in range(nj)]
    r_u2 = [Res() for _ in range(nj)]
    r_Y = [Res() for _ in range(nj)]
    cs = [[M.tile([nj, 128], BF16) for _ in range(2)] for _ in range(3)]
    r_cs = [Res(), Res(), Res()]
    kk = [[M.tile([512], F32) for _ in range(2)] for _ in range(3)]
    r_kk = [Res(), Res(), Res()]
    t1 = [M.tile([512], F32) for _ in range(2)]
    t2 = [M.tile([512], F32) for _ in range(2)]
    t3 = [M.tile([512], F32) for _ in range(2)]
    t4 = [M.tile([512], F32) for _ in range(2)]
    r_t1 = [Res(), Res()]
    r_t2 = [Res(), Res()]
    r_t3 = [Res(), Res()]
    r_t4 = [Res(), Res()]
    biasb = M.tile([2, 512], F32)
    r_bias = Res()
    for o in range(2):
        P.dma('sp', biasb[:, o, :], bcast_rows(self.hy_bias[o]), W=[r_bias])
    uf = [M.tile([512], F32) for _ in range(3)]
    xg = [M.tile([512], F32) for _ in range(3)]
    un = [M.tile([512], F32) for _ in range(2)]
    ub = [M.tile([512], BF16) for _ in range(2)]
    ot = [M.tile([512], BF16) for _ in range(2)]
    r_uf = [Res(), Res(), Res()]
    r_xg = [Res(), Res(), Res()]
    r_un = [Res(), Res()]
    r_ub = [Res(), Res()]
    r_ot = [Res(), Res()]
    rps = [Res() for _ in range(8)]
    for j in range(nj):
        P.dma('pool', u[:, j, :], self.vtok[0, tok0 + j * 128:tok0 + (j + 1) * 128, :], W=[r_u[j]])
    for o in range(2):
        for f in range(nj):
            q3 = f % 3
            q = f % 2
            P.dma('sp', cs[q3][0], Ccol[f], W=[r_cs[q3]])
            P.dma('sp', cs[q3][1], Scol[f], W=[r_cs[q3]])
            for ri in range(2):
                P.dma('sp', kk[q3][ri], Kf_d[o, ri, f * 128:(f + 1) * 128, :], W=[r_kk[q3]])
            pr, pi = M.bank(2 * q), M.bank(2 * q + 1)
            for ri, ps in ((0, pr), (1, pi)):
                for j in range(nj):
                    P.op('pe', lambda e, ps=ps, q3=q3, ri=ri, j=j: e.matmul(ps, cs[q3][ri][:, j, :], u[:, j, :],
                                                                          start=(j == 0), stop=(j == nj - 1)),
                         R=[r_cs[q3], r_u[j]], W=[rps[2 * q + ri]])
            R_ = [rps[2 * q], rps[2 * q + 1], r_kk[q3]]
            P.op('dve', lambda e, q=q, q3=q3, pr=pr: e.tensor_tensor(t1[q], pr, kk[q3][0], ALU.mult), R=R_ + [r_t1[q]], W=[r_t1[q]])
            P.op('dve', lambda e, q=q, q3=q3, pi=pi: e.tensor_tensor(t2[q], pi, kk[q3][1], ALU.mult), R=R_ + [r_t2[q]], W=[r_t2[q]])
            P.op('dve', lambda e, q=q, q3=q3, pi=pi: e.tensor_tensor(t3[q], pi, kk[q3][0], ALU.mult), R=R_ + [r_t3[q]], W=[r_t3[q]])
            P.op('dve', lambda e, q=q, q3=q3, pr=pr: e.tensor_tensor(t4[q], pr, kk[q3][1], ALU.mult), R=R_ + [r_t4[q]], W=[r_t4[q]])
            P.op('pool', lambda e, q=q, f=f: e.tensor_tensor(Y[:, f, 0, :], t1[q], t2[q], ALU.add), R=[r_t1[q], r_t2[q]], W=[r_Y[f]])
            P.op('pool', lambda e, q=q, f=f: e.tensor_tensor(Y[:, f, 1, :], t3[q], t4[q], ALU.subtract), R=[r_t3[q], r_t4[q]], W=[r_Y[f]])
        for t in range(nj):
            q = t % 2
            q3 = t % 3
            P.dma('sp', cs[q3][0], Ccol[t], W=[r_cs[q3]])
            P.dma('sp', cs[q3][1], Scol[t], W=[r_cs[q3]])
            rows = slice(tok0 + t * 128, tok0 + (t + 1) * 128)
            if o == 0:
                P.dma('sp', uf[q3], self.vtok[0, rows, :], W=[r_uf[q3]])
            else:
                P.dma('sp', uf[q3], self.u2tok[rows, :], R=[r_u2[t]], W=[r_uf[q3]])
            P.dma('sp', xg[q3], self.vtok[1 + o, rows, :], W=[r_xg[q3]])
            ps = M.bank(4 + q)
            for ri in range(2):
                for j in range(nj):
                    P.op('pe', lambda e, ps=ps, q3=q3, ri=ri, j=j: e.matmul(ps, cs[q3][ri][:, j, :], Y[:, j, ri, :],
                                                                          start=(ri == 0 and j == 0), stop=(ri == 1 and j == nj - 1)),
                         R=[r_cs[q3], r_Y[j]], W=[rps[4 + q]])
            P.op('pool', lambda e, q=q, q3=q3, o=o: e.tensor_tensor(un[q], uf[q3], biasb[:, o, :], ALU.mult), R=[r_uf[q3], r_bias], W=[r_un[q]])
            P.op('dve', lambda e, q=q, ps=ps: e.tensor_tensor(un[q], un[q], ps, ALU.add), R=[r_un[q], rps[4 + q]], W=[r_un[q]])
            if o == 0:
                P.op('dve', lambda e, q=q, q3=q3: e.tensor_tensor(un[q], un[q], xg[q3], ALU.mult), R=[r_un[q], r_xg[q3]], W=[r_un[q]])
                P.dma('sp', self.u2tok[rows, :], un[q], R=[r_un[q]], W=[r_u2[t]])
                P.op('act', lambda e, q=q, t=t: e.copy(u[:, t, :], un[q]), R=[r_un[q]], W=[r_u[t]])
            else:
                P.op('dve', lambda e, q=q, q3=q3: e.tensor_tensor(ub[q], un[q], xg[q3], ALU.mult), R=[r_un[q], r_xg[q3]], W=[r_ub[q]])
                pt = M.bank(6 + q, 512, BF16)
                for cc in range(4):
                    P.op('pe', lambda e, pt=pt, q=q, cc=cc: e.transpose(pt[:, cc * 128:(cc + 1) * 128], ub[q][:, cc * 128:(cc + 1) * 128], self.idbf),
                         R=[r_ub[q], self.r_idbf], W=[rps[6 + q]])
                P.op('act', lambda e, pt=pt, q=q: e.copy(ot[q], pt), R=[rps[6 + q]], W=[r_ot[q]])
                P.dma('sp', self.mixT[512:1024, tok0 + t * 128:tok0 + (t + 1) * 128].rearrange('(a p) t -> p a t', p=128),
                      ot[q].rearrange('p (a t) -> p a t', a=4), R=[r_ot[q]])
    P.barrier()


Builder.phase_hy_filters = _phase_hy_filters
Builder.phase_hy_prep = _phase_hy_prep
Builder.phase_hy_conv = _phase_hy_conv


def hyena_consts():
    out = {}
    f32 = np.float32
    max_decay = math.log(1e-2) / 0.3
    min_decay = math.log(1e-2) / 1.5
    deltas = np.linspace(min_decay, max_decay, 512, dtype=f32)
    for n in (2048, 256):
        pos = np.arange(n, dtype=f32)
        t = np.linspace(0.0, 1.0, n, dtype=f32)[:, None]
        w = (f32(2.0 * math.pi) * pos / f32(n)).astype(f32)
        fr = np.linspace(1e-4, 15, 16, dtype=f32)
        ang = (w[:, None] * fr[None, :]).astype(f32)
        z = np.concatenate([t, np.cos(ang), -np.sin(ang)], axis=-1).astype(f32)
        out['zT%d' % n] = np.ascontiguousarray(z.T)
        dec = np.exp(-t * np.abs(deltas)[None, :]).astype(f32)
        out['dec%d' % n] = dec
        decb = dec.copy()
        decb[0] = 0.0
        out['decb%d' % n] = decb
        N = 2 * n - 1
        sk = (np.arange(n, dtype=np.int64)[:, None] * np.arange(n, dtype=np.int64)[None, :]) % N
        ang = 2.0 * np.pi * sk.astype(np.float64) / N
        nj = n // 128
        for nm, mat in (('C', np.cos(ang)), ('S', np.sin(ang))):
            m4 = mat.reshape(nj, 128, nj, 128).transpose(2, 1, 0, 3)
            out['%scol%d' % (nm, n)] = np.ascontiguousarray(m4).astype(ml_dtypes.bfloat16)
        wk = np.full(n, 2.0 / N, dtype=f32)
        wk[0] = 1.0 / N
        out['wk%d' % n] = np.ascontiguousarray(wk.reshape(nj, 128).T)
    return out


def _phase_out_proj(self, l, src_bf, w_dram, xsrc, xdst, blocks, prefetch=None):
    P, M = self.P, self.M
    M.reset()
    ntok = max(t0 + T for t0, T, c in blocks)
    mx = M.tile([8, ntok], BF16)
    wt = M.tile([8, D], BF16)
    r_mx, r_w = Res(), Res()
    for k in range(8):
        P.dma('sp', mx[:, k, :], src_bf[k * 128:(k + 1) * 128, 0:ntok], W=[r_mx])
        P.dma('pool', wt[:, k, :], w_dram[k * 128:(k + 1) * 128, :], W=[r_w])
    if prefetch is not None:
        self.prefetch_wup(prefetch)
    xb = [M.tile([512], F32) for _ in range(3)]
    ob = [M.tile([512], F32) for _ in range(3)]
    r_xb = [Res() for _ in range(3)]
    r_ob = [Res() for _ in range(3)]
    rps = [Res() for _ in range(4)]
    n = 0
    mod = self.mod[l]
    for dc in range(8):
        for (t0, T, c) in blocks:
            b = n % 4
            q = n % 3
            n += 1
            ps = M.bank(b)
            P.dma('sp', xb[q][:, 0:T], xsrc[dc * 128:(dc + 1) * 128, t0:t0 + T], W=[r_xb[q]])
            for k in range(8):
                P.op('pe', lambda e, ps=ps, k=k, dc=dc, t0=t0, T=T: e.matmul(
                    ps[:, 0:T], wt[:, k, dc * 128:(dc + 1) * 128], mx[:, k, t0:t0 + T], start=(k == 0), stop=(k == 7)),
                    R=[r_w, r_mx], W=[rps[b]])
            P.op('dve', lambda e, ps=ps, q=q, dc=dc, c=c, T=T: e.scalar_tensor_tensor(
                ob[q][:, 0:T], ps[:, 0:T], mod[:, 16 + dc, c:c + 1], xb[q][:, 0:T], ALU.mult, ALU.add),
                R=[rps[b], r_xb[q], self.r_mod[l]], W=[r_ob[q]])
            P.dma('sp', xdst[dc * 128:(dc + 1) * 128, t0:t0 + T], ob[q][:, 0:T], R=[r_ob[q]])
    P.barrier()


def _phase_ffn(self, l, xsrc, xdst, segs, final_out=None):
    P, M = self.P, self.M
    M.reset()
    TB = 256
    NP = 22
    pre = getattr(self, 'wup', None)
    if pre is not None and pre[0] == l:
        wu, r_wu = pre[1], pre[2]
    else:
        wu = M.tile([8, 5632], BF16)
        r_wu = [Res() for _ in range(44)]
        for g in range(0, 5632, 2048):
            gw = min(2048, 5632 - g)
            for k in range(8):
                P.dma('pool', wu[:, k, g:g + gw], self.ffn_w_up[l, k * 128:(k + 1) * 128, g:g + gw],
                      W=[r_wu[j] for j in range(g // 128, (g + gw) // 128)])
    wd = M.tile([NP, D], BF16)
    r_wd = Res()
    for p in range(NP):
        P.dma('pool', wd[:, p, :], self.ffn_w_down[l, p * 128:(p + 1) * 128, :], W=[r_wd])
    cw = M.tile([44, 4], F32)
    r_cw = Res()
    P.dma('sp', cw, self.ffn_cw[l], W=[r_cw])
    NC = TB + 2
    xbs = [M.tile([8, NC], F32) for _ in range(2)]
    sq = M.tile([8, NC], BF16)
    h2 = M.tile([8, NC], BF16)
    rstd = M.tile([NC], F32)
    tmp = [M.tile([NC], F32) for _ in range(2)]
    mm = M.tile([NP, TB], BF16)
    cg = [M.tile([TB], F32) for _ in range(3)]
    cv = [M.tile([TB], F32) for _ in range(3)]
    xn = M.tile([8, TB], F32)
    r_xbs = [Res(), Res()]
    r_sq, r_h2, r_rs, r_xn = Res(), Res(), Res(), Res()
    r_tmp = [Res(), Res()]
    r_mm = [Res() for _ in range(NP)]
    r_cg = [Res(), Res(), Res()]
    r_cv = [Res(), Res(), Res()]
    rps = [Res() for _ in range(8)]
    A = self.A2[l]
    mod = self.mod[l]
    if final_out is not None:
        fg = M.tile([8], F32)
        r_fg = Res()
        P.dma('sp', fg, self.final_g.rearrange('(k p) -> p k', p=128), W=[r_fg], allow_slow_non_contiguous=True)
        sq2, r_sq2 = sq, r_sq
        fo, r_fo = xn, r_xn
    RM = [self.r_mod[l]]
    blist = []
    for (s0, s1, c) in segs:
        for t0 in range(s0, s1, TB):
            T = min(TB, s1 - t0)
            lo = max(s0, t0 - 1)
            hi = min(s1, t0 + T + 1)
            blist.append((s0, s1, c, t0, T, lo, hi))

    def load_x(i):
        s0, s1, c, t0, T, lo, hi = blist[i]
        P.dma('sp', xbs[i % 2][:, :, 0:hi - lo], xsrc[:, lo:hi].rearrange('(k p) t -> p k t', p=128), W=[r_xbs[i % 2]])

    load_x(0)
    for nblk, (s0, s1, c, t0, T, lo, hi) in enumerate(blist):
        if True:
            nc_ = hi - lo
            c0 = t0 - lo
            xb, r_xb = xbs[nblk % 2], r_xbs[nblk % 2]
            P.op('act', lambda e, nc_=nc_, xb=xb: e.activation(sq[:, :, 0:nc_], xb[:, :, 0:nc_], AF.Square), R=[r_xb], W=[r_sq])
            psn = M.bank(6)
            for k in range(8):
                P.op('pe', lambda e, k=k, nc_=nc_, psn=psn: e.matmul(psn[:, 0:nc_], self.ones_bf, sq[:, k, 0:nc_], start=(k == 0), stop=(k == 7)),
                     R=[r_sq, self.r_ones], W=[rps[6]])
            P.op('act', lambda e, nc_=nc_, psn=psn: e.activation(rstd[:, 0:nc_], psn[:, 0:nc_], AF.Sqrt, bias=EPS, scale=1.0 / D),
                 R=[rps[6]], W=[r_rs])
            P.op('dve', lambda e, nc_=nc_: e.reciprocal(rstd[:, 0:nc_], rstd[:, 0:nc_]), R=[r_rs], W=[r_rs])
            for k in range(8):
                q = k % 2
                P.op('pool', lambda e, k=k, q=q, nc_=nc_, xb=xb: e.tensor_tensor(
                    tmp[q][:, 0:nc_], xb[:, k, 0:nc_], rstd[:, 0:nc_], ALU.mult), R=[r_xb, r_rs], W=[r_tmp[q]])
                P.op('act', lambda e, k=k, q=q, nc_=nc_, c=c: e.activation(
                    h2[:, k, 0:nc_], tmp[q][:, 0:nc_], AF.Identity, bias=mod[:, 24 + k, c:c + 1], scale=A[:, k, c:c + 1]),
                    R=[r_tmp[q]] + RM, W=[r_h2])
            if nblk + 1 < len(blist):
                load_x(nblk + 1)
            a = 1 if t0 == s0 else 0
            bnd = 1 if t0 + T == s1 else 0

            def st_pe(p):
                q = p % 3
                for ch, bk in ((p, 2 * q), (NP + p, 2 * q + 1)):
                    ps = M.bank(bk)
                    for k in range(8):
                        P.op('pe', lambda e, ps=ps, k=k, ch=ch, nc_=nc_: e.matmul(
                            ps[:, 0:nc_], wu[:, k, ch * 128:(ch + 1) * 128], h2[:, k, 0:nc_], start=(k == 0), stop=(k == 7)),
                            R=[r_wu[ch], r_h2], W=[rps[bk]])

            def st_conv(p):
                q = p % 3
                pairs = ((p, M.bank(2 * q), 2 * q, cg[q], r_cg[q]), (NP + p, M.bank(2 * q + 1), 2 * q + 1, cv[q], r_cv[q]))
                for ch, ps, bk, dst, rd in pairs:
                    P.op('act', lambda e, ps=ps, ch=ch, dst=dst, T=T, c0=c0: e.activation(
                        dst[:, 0:T], ps[:, c0:c0 + T], AF.Identity, bias=cw[:, ch, 3:4], scale=cw[:, ch, 1:2]),
                        R=[rps[bk], r_cw], W=[rd])
                for ch, ps, bk, dst, rd in pairs:
                    P.op('dve', lambda e, ps=ps, ch=ch, dst=dst, T=T, c0=c0, a=a: e.scalar_tensor_tensor(
                        dst[:, a:T], ps[:, c0 - 1 + a:c0 - 1 + T], cw[:, ch, 0:1], dst[:, a:T], ALU.mult, ALU.add),
                        R=[rps[bk], r_cw, rd], W=[rd])
                for ch, ps, bk, dst, rd in pairs:
                    P.op('dve', lambda e, ps=ps, ch=ch, dst=dst, T=T, c0=c0, bnd=bnd: e.scalar_tensor_tensor(
                        dst[:, 0:T - bnd], ps[:, c0 + 1:c0 + 1 + T - bnd], cw[:, ch, 2:3], dst[:, 0:T - bnd], ALU.mult, ALU.add),
                        R=[rps[bk], r_cw, rd], W=[rd])

            def st_gate(p):
                q = p % 3
                P.op('act', lambda e, q=q, T=T: e.activation(cg[q][:, 0:T], cg[q][:, 0:T], AF.Silu), R=[r_cg[q]], W=[r_cg[q]])
                P.op('pool', lambda e, q=q, p=p, T=T: e.tensor_tensor(mm[:, p, 0:T], cg[q][:, 0:T], cv[q][:, 0:T], ALU.mult),
                     R=[r_cg[q], r_cv[q]], W=[r_mm[p]])

            for i in range(NP + 2):
                if i < NP:
                    st_pe(i)
                if 1 <= i <= NP:
                    st_conv(i - 1)
                if 2 <= i <= NP + 1:
                    st_gate(i - 2)
            for dc in range(8):
                bk = 6 + dc % 2
                ps = M.bank(bk)
                for p in range(NP):
                    P.op('pe', lambda e, ps=ps, p=p, dc=dc, T=T: e.matmul(
                        ps[:, 0:T], wd[:, p, dc * 128:(dc + 1) * 128], mm[:, p, 0:T], start=(p == 0), stop=(p == NP - 1)),
                        R=[r_wd, r_mm[p]], W=[rps[bk]])
                P.op('dve', lambda e, ps=ps, dc=dc, T=T, c0=c0, c=c, xb=xb: e.scalar_tensor_tensor(
                    xn[:, dc, 0:T], ps[:, 0:T], mod[:, 40 + dc, c:c + 1], xb[:, dc, c0:c0 + T], ALU.mult, ALU.add),
                    R=[rps[bk], r_xb] + RM, W=[r_xn])
            if final_out is None:
                P.dma('sp', xdst[:, t0:t0 + T].rearrange('(k p) t -> p k t', p=128), xn[:, :, 0:T], R=[r_xn])
            else:
                P.op('act', lambda e, T=T: e.activation(sq2[:, :, 0:T], xn[:, :, 0:T], AF.Square), R=[r_xn], W=[r_sq2])
                psn = M.bank(7)
                for k in range(8):
                    P.op('pe', lambda e, k=k, T=T, psn=psn: e.matmul(psn[:, 0:T], self.ones_bf, sq2[:, k, 0:T], start=(k == 0), stop=(k == 7)),
                         R=[r_sq2, self.r_ones], W=[rps[7]])
                P.op('act', lambda e, T=T, psn=psn: e.activation(rstd[:, 0:T], psn[:, 0:T], AF.Sqrt, bias=EPS, scale=1.0 / D),
                     R=[rps[7], r_rs], W=[r_rs])
                P.op('dve', lambda e, T=T: e.reciprocal(rstd[:, 0:T], rstd[:, 0:T]), R=[r_rs], W=[r_rs])
                for k in range(8):
                    P.op('dve', lambda e, k=k, T=T: e.scalar_tensor_tensor(
                        fo[:, k, 0:T], xn[:, k, 0:T], fg[:, k:k + 1], rstd[:, 0:T], ALU.mult, ALU.mult),
                        R=[r_xn, r_rs, r_fg], W=[r_fo])
                P.dma('sp', final_out[:, t0:t0 + T].rearrange('(k p) t -> p k t', p=128), fo[:, :, 0:T], R=[r_fo])
    P.barrier()
    if pre is not None and pre[0] == l:
        M.release_top()
        self.wup = None


def _prefetch_wup(self, l):
    P, M = self.P, self.M
    flat = M.reserve_top(8 * 5632 * 2)
    wu = flat.rearrange('p (k f) -> p k f', k=8)
    r_wu = [Res() for _ in range(44)]
    for g in range(0, 5632, 2048):
        gw = min(2048, 5632 - g)
        for k in range(8):
            P.dma('pool', wu[:, k, g:g + gw], self.ffn_w_up[l, k * 128:(k + 1) * 128, g:g + gw],
                  W=[r_wu[j] for j in range(g // 128, (g + gw) // 128)], bg=True)
    self.wup = (l, wu, r_wu)


Builder.prefetch_wup = _prefetch_wup
Builder.phase_out_proj = _phase_out_proj
Builder.phase_ffn = _phase_ffn


def _rms_feat(self, zt, nk, T, g, dst, rz, rg_, rdst, tmp, r_tmp, sq, r_sq, rstd, r_rs, psb, rps):
    P = self.P
    P.op('act', lambda e: e.activation(sq[:, 0:nk, 0:T], zt[:, :, 0:T], AF.Square), R=[rz], W=[r_sq])
    for k in range(nk):
        P.op('pe', lambda e, k=k: e.matmul(psb[:, 0:T], self.ones_bf, sq[:, k, 0:T], start=(k == 0), stop=(k == nk - 1)),
             R=[r_sq, self.r_ones], W=[rps])
    P.op('act', lambda e: e.activation(rstd[:, 0:T], psb[:, 0:T], AF.Sqrt, bias=EPS, scale=1.0 / (nk * 128)), R=[rps], W=[r_rs])
    P.op('dve', lambda e: e.reciprocal(rstd[:, 0:T], rstd[:, 0:T]), R=[r_rs], W=[r_rs])
    for k in range(nk):
        P.op('dve', lambda e, k=k: e.scalar_tensor_tensor(dst[:, k, :], zt[:, k, 0:T], g[:, k:k + 1], rstd[:, 0:T], ALU.mult, ALU.mult),
             R=[rz, r_rs, rg_], W=[rdst])


def _phase_mla_prep(self):
    P, M = self.P, self.M
    M.reset()
    z1 = self.z1
    wq = M.tile([4, 1536], BF16)
    wkv = M.tile([2, 2048], BF16)
    r_w = Res()
    for k in range(4):
        P.dma('pool', wq[:, k, :], self.od_w_uq[k * 128:(k + 1) * 128, :], W=[r_w])
    for k in range(2):
        for g in range(2):
            P.dma('pool', wkv[:, k, g * 1024:(g + 1) * 1024], self.od_w_ukv[k * 128:(k + 1) * 128, g * 1024:(g + 1) * 1024], W=[r_w])
    gq = M.tile([4], F32)
    gkv = M.tile([2], F32)
    rotm = M.tile([64], F32)
    cos2 = M.tile([S], F32)
    sin2 = M.tile([S], F32)
    r_c = Res()
    P.dma('sp', gq, self.od_q_norm_g.rearrange('(k p) -> p k', p=128), W=[r_c], allow_slow_non_contiguous=True)
    P.dma('sp', gkv, self.od_kv_norm_g.rearrange('(k p) -> p k', p=128), W=[r_c], allow_slow_non_contiguous=True)
    P.dma('sp', rotm[0:64, :], self.rotm, W=[r_c])
    P.dma('sp', cos2[0:64, :], self.cos2, W=[r_c])
    P.dma('sp', sin2[0:64, :], self.sin2, W=[r_c])
    qn = M.tile([4, S], BF16)
    ckv = M.tile([2, NT], BF16)
    r_qn = [Res() for _ in range(4)]
    r_ckv = [Res() for _ in range(5)]
    zt = [M.tile([4, 512], F32) for _ in range(2)]
    r_zt = [Res(), Res()]
    sq = M.tile([4, 512], BF16)
    rstd = M.tile([512], F32)
    tmp = None
    r_sq, r_rs = Res(), Res()
    rps = [Res() for _ in range(8)]
    blocks = token_blocks()
    for i, (t0, T, c) in enumerate(blocks[:4]):
        q = i % 2
        P.dma('sp', zt[q][:, :, 0:T], z1[0:512, t0:t0 + T].rearrange('(k p) t -> p k t', p=128), W=[r_zt[q]])
        self.rms_feat(zt[q], 4, T, gq, qn[:, :, t0:t0 + T], r_zt[q], r_c, r_qn[i], None, None, sq, r_sq, rstd, r_rs, M.bank(7), rps[7])
    for i, (t0, T, c) in enumerate(blocks):
        q = i % 2
        P.dma('sp', zt[q][:, 0:2, 0:T], z1[512:768, t0:t0 + T].rearrange('(k p) t -> p k t', p=128), W=[r_zt[q]])
        self.rms_feat(zt[q][:, 0:2, :], 2, T, gkv, ckv[:, :, t0:t0 + T], r_zt[q], r_c, r_ckv[i], None, None, sq, r_sq, rstd, r_rs, M.bank(7), rps[7])
    stb = [M.tile([512], BF16) for _ in range(4)]
    r_stb = [Res() for _ in range(4)]
    xr = [M.tile([512], F32) for _ in range(2)]
    r_xr = [Res(), Res()]
    tt_ = [M.tile([512], F32) for _ in range(2)]
    r_tt = [Res(), Res()]
    n = 0

    def rope_out(src_sb, r_src, t0, T, dst_dram, latent, q):
        nonlocal n
        b = n % 4
        n += 1
        if latent:
            pr = M.bank(4 + q)
            P.op('pe', lambda e: e.matmul(pr[0:64, 0:T], rotm[0:64, :], src_sb[0:64, 0:T], start=True, stop=True),
                 R=[r_src, r_c], W=[rps[4 + q]])
            P.op('dve', lambda e: e.tensor_tensor(tt_[q][0:64, 0:T], pr[0:64, 0:T], sin2[0:64, t0:t0 + T], ALU.mult),
                 R=[rps[4 + q], r_c], W=[r_tt[q]])
            P.op('dve', lambda e: e.tensor_tensor(src_sb[0:64, 0:T], src_sb[0:64, 0:T], cos2[0:64, t0:t0 + T], ALU.mult),
                 R=[r_src, r_c], W=[r_src])
            P.op('dve', lambda e: e.tensor_tensor(stb[b][0:64, 0:T], src_sb[0:64, 0:T], tt_[q][0:64, 0:T], ALU.add),
                 R=[r_src, r_tt[q]], W=[r_stb[b]])
        else:
            P.op('dve', lambda e: e.tensor_copy(stb[b][0:64, 0:T], src_sb[0:64, 0:T]), R=[r_src], W=[r_stb[b]])
        P.dma('sp', dst_dram, stb[b][0:64, 0:T], R=[r_stb[b]])

    for h in range(8):
        for i, (t0, T, c) in enumerate(blocks[:4]):
            b = n % 4
            n += 1
            ps = M.bank(b)
            for k in range(4):
                P.op('pe', lambda e, ps=ps, k=k, h=h, t0=t0, T=T: e.matmul(
                    ps[:, 0:T], wq[:, k, h * 192:h * 192 + 128], qn[:, k, t0:t0 + T], start=(k == 0), stop=(k == 3)),
                    R=[r_w, r_qn[i]], W=[rps[b]])
            P.op('act', lambda e, ps=ps, b=b, T=T: e.copy(stb[b][:, 0:T], ps[:, 0:T]), R=[rps[b]], W=[r_stb[b]])
            P.dma('sp', self.qT[h, 0:128, t0:t0 + T], stb[b][:, 0:T], R=[r_stb[b]])
            q = i % 2
            b2 = n % 4
            n += 1
            ps2 = M.bank(b2)
            for k in range(4):
                P.op('pe', lambda e, ps2=ps2, k=k, h=h, t0=t0, T=T: e.matmul(
                    ps2[0:64, 0:T], wq[:, k, h * 192 + 128:h * 192 + 192], qn[:, k, t0:t0 + T], start=(k == 0), stop=(k == 3)),
                    R=[r_w, r_qn[i]], W=[rps[b2]])
            P.op('act', lambda e, ps2=ps2, q=q, T=T: e.copy(xr[q][0:64, 0:T], ps2[0:64, 0:T]), R=[rps[b2]], W=[r_xr[q]])
            rope_out(xr[q], r_xr[q], t0, T, self.qT[h, 128:192, t0:t0 + T], True, q)
    for h in range(8):
        for i, (t0, T, c) in enumerate(blocks):
            b = n % 4
            n += 1
            ps = M.bank(b)
            for k in range(2):
                P.op('pe', lambda e, ps=ps, k=k, h=h, t0=t0, T=T: e.matmul(
                    ps[:, 0:T], wkv[:, k, h * 128:(h + 1) * 128], ckv[:, k, t0:t0 + T], start=(k == 0), stop=(k == 1)),
                    R=[r_w, r_ckv[i]], W=[rps[b]])
            P.op('act', lambda e, ps=ps, b=b, T=T: e.copy(stb[b][:, 0:T], ps[:, 0:T]), R=[rps[b]], W=[r_stb[b]])
            P.dma('sp', self.kT[h, :, t0:t0 + T], stb[b][:, 0:T], R=[r_stb[b]])
    for tt in range(NT // 128):
        i = min(tt // 4, 4)
        for g in range(2):
            b = n % 4
            n += 1
            ps = M.bank(b)
            for k in range(2):
                P.op('pe', lambda e, ps=ps, k=k, g=g, tt=tt: e.matmul(
                    ps, ckv[:, k, tt * 128:(tt + 1) * 128], wkv[:, k, 1024 + g * 512:1024 + (g + 1) * 512], start=(k == 0), stop=(k == 1)),
                    R=[r_w, r_ckv[i]], W=[rps[b]])
            P.op('dve', lambda e, ps=ps, b=b: e.tensor_copy(stb[b], ps), R=[rps[b]], W=[r_stb[b]])
            P.dma('sp', self.Vtok[tt * 128:(tt + 1) * 128, g * 512:(g + 1) * 512], stb[b], R=[r_stb[b]])
    for i, (t0, T, c) in enumerate(blocks):
        q = i % 2
        P.dma('sp', xr[q][0:64, 0:T], z1[768:832, t0:t0 + T], W=[r_xr[q]])
        rope_out(xr[q], r_xr[q], t0, T, self.krT[:, t0:t0 + T], c == 0, q)
    P.barrier()


def _phase_attn(self):
    P, M = self.P, self.M
    M.reset()
    SC = 192.0 ** -0.5
    V = M.tile([18, 1024], BF16)
    kra = M.tile([NT], BF16)
    colsel = M.tile([128], BF16)
    r_v, r_kra, r_cs = Res(), Res(), Res()
    for tt in range(18):
        P.dma('sp', V[:, tt, :], self.Vtok[tt * 128:(tt + 1) * 128, :], W=[r_v])
    P.op('dve', lambda e: e.memset(kra[64:128, :], 0.0), W=[r_kra])
    P.op('dve', lambda e: e.memset(kra[64:65, :], 1.0), W=[r_kra])
    P.dma('sp', kra[0:64, :], self.krT, W=[r_kra])
    P.op('dve', lambda e: e.memset(colsel, 0.0), W=[r_cs])
    P.op('dve', lambda e: e.memset(colsel[:, 64:65], 1.0), W=[r_cs])
    qn = [M.tile([S], BF16) for _ in range(2)]
    qrz = [M.tile([S], BF16) for _ in range(2)]
    kn = [M.tile([NT], BF16) for _ in range(2)]
    r_hd = [Res(), Res()]
    for i in range(2):
        P.op('dve', lambda e, i=i: e.memset(qrz[i][64:128, :], 0.0), W=[r_hd[i]])
    NB = 3
    qra = [M.tile([512], BF16) for _ in range(NB)]
    r_qra = [Res() for _ in range(NB)]
    mx = [M.tile([8], F32) for _ in range(4)]
    r_mx = [Res() for _ in range(4)]
    dg = [M.tile([128], BF16) for _ in range(2)]
    r_dg = [Res(), Res()]
    PTt = [M.tile([512], BF16) for _ in range(4)]
    r_PT = [Res() for _ in range(4)]
    rs = [M.tile([512], F32) for _ in range(2)]
    r_rs = [Res(), Res()]
    ob = [M.tile([512], BF16) for _ in range(2)]
    r_ob = [Res(), Res()]
    rps = [Res() for _ in range(8)]
    kblocks = token_blocks()
    cnt = {'sc': 0, 'mx': 0, 'dg': 0, 'p2': 0, 'pt': 0}
    if getattr(self, 'attn_hook', None):
        self.attn_hook()

    def load_head(h):
        i = h % 2
        P.dma('sp', qn[i], self.qT[h, 0:128, :], W=[r_hd[i]])
        P.dma('sp', qrz[i][0:64, :], self.qT[h, 128:192, :], W=[r_hd[i]])
        P.dma('sp', kn[i], self.kT[h], W=[r_hd[i]])

    def pass1(h, qb, bi):
        i = h % 2
        u = bi % NB
        P.dma('sp', qra[u][0:64, :], self.qT[h, 128:192, qb * 512:(qb + 1) * 512], W=[r_qra[u]])
        for qt in range(4):
            qs = slice(qb * 512 + qt * 128, qb * 512 + (qt + 1) * 128)
            m_ = cnt['mx'] % 4
            cnt['mx'] += 1
            for j, (k0, T, c) in enumerate(kblocks):
                b = cnt['sc'] % 3
                cnt['sc'] += 1
                ps = M.bank(b)
                P.op('pe', lambda e, ps=ps, i=i, qs=qs, k0=k0, T=T: e.matmul(ps[:, 0:T], qn[i][:, qs], kn[i][:, k0:k0 + T], start=True, stop=False),
                     R=[r_hd[i]], W=[rps[b]])
                P.op('pe', lambda e, ps=ps, i=i, qs=qs, k0=k0, T=T: e.matmul(ps[:, 0:T], qrz[i][:, qs], kra[:, k0:k0 + T], start=False, stop=True),
                     R=[r_hd[i], r_kra], W=[rps[b]])
                P.op('dve', lambda e, ps=ps, m_=m_, j=j, T=T: e.tensor_reduce(mx[m_][:, j:j + 1], ps[:, 0:T], AX.X, ALU.max),
                     R=[rps[b]], W=[r_mx[m_]])
                yield
            P.op('dve', lambda e, m_=m_: e.tensor_reduce(mx[m_][:, 5:6], mx[m_][:, 0:5], AX.X, ALU.max, negate=True),
                 R=[r_mx[m_]], W=[r_mx[m_]])
            g = cnt['dg'] % 2
            cnt['dg'] += 1
            P.op('dve', lambda e, g=g, m_=m_: e.tensor_scalar(dg[g], self.idbf, mx[m_][:, 5:6], None, ALU.mult),
                 R=[r_mx[m_], self.r_idbf], W=[r_dg[g]])
            P.op('pe', lambda e, g=g, qt=qt: e.matmul(M.bank(3)[:, qt * 128:(qt + 1) * 128], colsel, dg[g], start=True, stop=True),
                 R=[r_cs, r_dg[g]], W=[rps[3]])
        P.op('act', lambda e, u=u: e.copy(qra[u][64:128, :], M.bank(3)[64:128, :]), R=[rps[3]], W=[r_qra[u]])

    def pass2(h, qb, bi):
        i = h % 2
        u = bi % NB
        qcols = slice(qb * 512, (qb + 1) * 512)
        def emit_s(kt):
            b = 4 + kt % 2
            ps = M.bank(b)
            ks = slice(kt * 128, (kt + 1) * 128)
            P.op('pe', lambda e, ps=ps, i=i, ks=ks, qcols=qcols: e.matmul(ps, kn[i][:, ks], qn[i][:, qcols], start=True, stop=False),
                 R=[r_hd[i]], W=[rps[b]])
            P.op('pe', lambda e, ps=ps, ks=ks, u=u: e.matmul(ps, kra[:, ks], qra[u], start=False, stop=True),
                 R=[r_kra, r_qra[u]], W=[rps[b]])

        emit_s(0)
        for kt in range(18):
            if kt + 1 < 18:
                emit_s(kt + 1)
            b = 4 + kt % 2
            ps = M.bank(b)
            t = cnt['pt'] % 4
            cnt['pt'] += 1
            P.op('act', lambda e, ps=ps, t=t: e.activation(PTt[t], ps, AF.Exp, scale=SC), R=[rps[b]], W=[r_PT[t]])
            P.op('pe', lambda e, t=t, kt=kt: e.matmul(M.bank(6), self.ones_bf, PTt[t], start=(kt == 0), stop=(kt == 17)),
                 R=[r_PT[t], self.r_ones], W=[rps[6]])
            P.op('pe', lambda e, t=t, kt=kt, h=h: e.matmul(M.bank(7), V[:, kt, h * 128:(h + 1) * 128], PTt[t], start=(kt == 0), stop=(kt == 17)),
                 R=[r_PT[t], r_v], W=[rps[7]])
            yield
        o = bi % 2
        P.op('dve', lambda e, o=o: e.reciprocal(rs[o], M.bank(6)), R=[rps[6]], W=[r_rs[o]])
        P.op('dve', lambda e, o=o: e.tensor_tensor(ob[o], M.bank(7), rs[o], ALU.mult), R=[rps[7], r_rs[o]], W=[r_ob[o]])
        P.dma('sp', self.oT[h * 128:(h + 1) * 128, qcols], ob[o], R=[r_ob[o]])

    blocks = [(h, qb) for h in range(8) for qb in range(4)]
    def run_both(g2, g1):
        a, b_ = True, True
        while a or b_:
            if a:
                try:
                    next(g2)
                except StopIteration:
                    a = False
            if b_:
                try:
                    next(g1)
                except StopIteration:
                    b_ = False

    load_head(0)
    for _ in pass1(0, 0, 0):
        pass
    for bi, (h, qb) in enumerate(blocks):
        g2 = pass2(h, qb, bi)
        if bi + 1 < len(blocks):
            h1, qb1 = blocks[bi + 1]
            if qb1 == 0:
                pass
            g1 = pass1(h1, qb1, bi + 1)
        else:
            g1 = iter(())
        run_both(g2, g1)
        if qb == 0 and h + 1 < 8:
            load_head(h + 1)
    P.barrier()


Builder.rms_feat = _rms_feat
Builder.phase_mla_prep = _phase_mla_prep
Builder.phase_attn = _phase_attn


def mla_consts():
    f32 = np.float32
    n = S
    row = np.repeat(np.arange(n // 64, dtype=f32), 64)
    col = np.tile(np.arange(64, dtype=f32), n // 64)
    inv = (f32(10000.0) ** (-np.arange(16, dtype=f32) / f32(16))).astype(f32)
    ang = np.concatenate([row[:, None] * inv[None, :], col[:, None] * inv[None, :]], axis=-1).astype(f32)
    cos2 = np.repeat(np.cos(ang).astype(f32), 2, axis=1).T
    sin2 = np.repeat(np.sin(ang).astype(f32), 2, axis=1).T
    rotm = np.zeros((64, 64), f32)
    for i in range(32):
        rotm[2 * i + 1, 2 * i] = -1.0
        rotm[2 * i, 2 * i + 1] = 1.0
    return {'cos2': np.ascontiguousarray(cos2), 'sin2': np.ascontiguousarray(sin2), 'rotm': rotm}

def make_inputs(inputs, b):
    x = np.asarray(inputs['x'][b], np.float32)
    ctx = np.asarray(inputs['ctx'][b], np.float32)
    m = {}
    m['xT'] = np.ascontiguousarray(np.concatenate([x.T, ctx.T], axis=1))
    m['cvec'] = np.ascontiguousarray(np.stack([inputs['c'][b], inputs['c_ctx']]).astype(np.float32))
    m['ada_w'] = np.ascontiguousarray(inputs['ada_w'], np.float32)
    m['ada_b'] = np.ascontiguousarray(inputs['ada_b'], np.float32)
    m['norm1_g'] = np.ascontiguousarray(inputs['norm1_g'], np.float32)
    m['norm2_g'] = np.ascontiguousarray(inputs['norm2_g'], np.float32)
    m['ev_w_in'] = np.ascontiguousarray(inputs['ev_w_in'][0], np.float32)
    m['ident'] = np.eye(128, dtype=np.float32)
    f = lambda k: np.ascontiguousarray(inputs[k][0], np.float32)
    m['rg_conv_w'] = f('ev_rg_conv_w'); m['rg_conv_b'] = f('ev_rg_conv_b')
    m['rg_wa'] = f('ev_rg_wa'); m['rg_ba'] = f('ev_rg_ba'); m['rg_wx'] = f('ev_rg_wx'); m['rg_bx'] = f('ev_rg_bx')
    m['rg_lambda'] = f('ev_rg_lambda')
    m['hy_conv_w'] = f('ev_hy_conv_w'); m['hy_conv_b'] = f('ev_hy_conv_b')
    m['f_w1'] = f('ev_hy_f_w1'); m['f_b1'] = f('ev_hy_f_b1'); m['f_freq'] = f('ev_hy_f_freq')
    m['f_w2'] = f('ev_hy_f_w2'); m['f_b2'] = f('ev_hy_f_b2'); m['f_w3'] = f('ev_hy_f_w3')
    m['hy_bias'] = f('ev_hy_bias')
    m.update(HC())
    m['ev_w_out'] = f('ev_w_out')
    wi = np.zeros((D, 896), np.float32)
    wi[:, :832] = inputs['od_w_in'][0]
    m['od_w_in'] = wi
    m['od_q_norm_g'] = f('od_q_norm_g'); m['od_kv_norm_g'] = f('od_kv_norm_g')
    m['od_w_uq'] = f('od_w_uq'); m['od_w_o'] = f('od_w_o')
    wkv = np.asarray(inputs['od_w_ukv'][0], np.float32).reshape(256, 8, 256)
    m['od_w_ukv'] = np.ascontiguousarray(np.concatenate([wkv[:, :, :128].reshape(256, 1024), wkv[:, :, 128:].reshape(256, 1024)], axis=1))
    m.update(MC())
    for k in ('ffn_w_up', 'ffn_conv_w', 'ffn_conv_b', 'ffn_w_down', 'final_g'):
        m[k] = np.ascontiguousarray(inputs[k], np.float32)
    cwb = np.concatenate([np.asarray(inputs['ffn_conv_w'], np.float32), np.asarray(inputs['ffn_conv_b'], np.float32)[:, None, :]], axis=1)
    m['ffn_cw'] = np.ascontiguousarray(cwb.reshape(2, 4, 44, 128).transpose(0, 3, 2, 1))
    return m


_HC = {}


def HC():
    if not _HC:
        _HC.update(hyena_consts())
    return _HC


_MC = {}


def MC():
    if not _MC:
        _MC.update(mla_consts())
    return _MC


def kernel(**inputs):
    inputs = {k: np.asarray(v) for k, v in inputs.items()}
    nb = inputs['x'].shape[0]
    B = Builder(dbg=True)
    nc = B.build()
    shared = make_inputs(inputs, 0)
    in_maps = []
    for b in range(nb):
        m = dict(shared)
        x = np.asarray(inputs['x'][b], np.float32)
        ctx = np.asarray(inputs['ctx'][b], np.float32)
        m['xT'] = np.ascontiguousarray(np.concatenate([x.T, ctx.T], axis=1))
        m['cvec'] = np.ascontiguousarray(np.stack([inputs['c'][b], inputs['c_ctx']]).astype(np.float32))
        in_maps.append({k: v for k, v in m.items() if k in B.din})
    res = run_bass_kernel_spmd(nc, in_maps, core_ids=list(range(nb)))
    out = np.stack([np.asarray(res.results[b]['outT'], np.float32).T for b in range(nb)], axis=0)
    return np.ascontiguousarray(out)
```

```python
import contextlib
import math
import numpy as np
import ml_dtypes
import concourse.bass as bass
import concourse.mybir as mybir
from concourse.bass_utils import run_bass_kernel_spmd

dt = mybir.dt
F32 = dt.float32
BF16 = dt.bfloat16
AF = mybir.ActivationFunctionType
ALU = mybir.AluOpType
AX = mybir.AxisListType

D = 1024
S = 2048
CT = 256
NT = S + CT
EPS = 1e-6
ENG = ['pe', 'dve', 'act', 'pool', 'sp']
SAME_SYNC = True


class Res:
    __slots__ = ('wl', 'r')

    def __init__(self):
        self.wl = []
        self.r = {}


class Prog:
    def __init__(self, nc, stack, n_dma=40):
        self.nc = nc
        self.ops = {e: [] for e in ENG}
        self.seq = {e: 0 for e in ENG}
        self.known = {e: {} for e in ENG}
        self.esem = {e: stack.enter_context(nc.semaphore('s_' + e)) for e in ENG}
        self.dsem = [stack.enter_context(nc.semaphore('d%d' % i)) for i in range(n_dma)]
        self.dtgt = [0] * n_dma
        self.drr = 0
        self.n_fg = n_dma - 6
        self.bgrr = 0

    def _sem(self, key):
        return self.esem[key[1]] if key[0] == 'e' else self.dsem[key[1]]

    @staticmethod
    def _joinable(w):
        return bool(w.wl) and not w.r and all(ev[0][0] == 'd' for ev in w.wl)

    def _collect(self, eng, R, W, dma_write=False):
        need = {}

        def add(key, val):
            if val > need.get(key, 0):
                need[key] = val
        for r in R:
            for ev in r.wl:
                add(*ev)
        for w in W:
            if not (dma_write and self._joinable(w)):
                for ev in w.wl:
                    add(*ev)
            for k, v in w.r.items():
                add(k, v)
        waits = []
        kn = self.known[eng]
        for key, val in need.items():
            if key == ('e', eng) and (eng == 'pe' or not SAME_SYNC):
                continue
            if kn.get(key, 0) >= val:
                continue
            kn[key] = val
            waits.append((self._sem(key), val))
        return waits

    def _mark(self, ev, R, W, dma_write=False):
        for r in R:
            if ev[1] > r.r.get(ev[0], 0):
                r.r[ev[0]] = ev[1]
        for w in W:
            if dma_write and self._joinable(w):
                w.wl.append(ev)
            else:
                w.wl = [ev]
                w.r = {}

    def op(self, eng, fn, R=(), W=()):
        waits = self._collect(eng, R, W)
        self.seq[eng] += 1
        ev = (('e', eng), self.seq[eng])
        es = self.esem[eng]

        def emit(e):
            for s, v in waits:
                e.wait_ge(s, v)
            fn(e).then_inc(es, 1)
        self.ops[eng].append(emit)
        self._mark(ev, R, W)

    def dma(self, q, out, in_, R=(), W=(), bg=False, **kw):
        waits = self._collect(q, R, W, dma_write=True)
        if bg:
            si = self.n_fg + self.bgrr
            self.bgrr = (self.bgrr + 1) % (len(self.dsem) - self.n_fg)
        else:
            si = self.drr
            self.drr = (self.drr + 1) % self.n_fg
        prev = self.dtgt[si]
        key = ('d', si)
        if prev > 0 and self.known[q].get(key, 0) < prev:
            self.known[q][key] = prev
            waits.append((self.dsem[si], prev))
        tgt = prev + 16
        self.dtgt[si] = tgt
        ds = self.dsem[si]

        def emit(e):
            for s, v in waits:
                e.wait_ge(s, v)
            e.dma_start(out=out, in_=in_, **kw).then_inc(ds, 16)
        self.ops[q].append(emit)
        self._mark((key, tgt), R, W, dma_write=True)

    def barrier(self, final=False):
        for e in ENG:
            waits = []
            kn = self.known[e]
            for f in ENG:
                if f == e and e == 'pe':
                    continue
                v = self.seq[f]
                key = ('e', f)
                if v > kn.get(key, 0):
                    kn[key] = v
                    waits.append((self.esem[f], v))
            for si, t in enumerate(self.dtgt):
                if si >= self.n_fg and not final:
                    continue
                key = ('d', si)
                if t > kn.get(key, 0):
                    kn[key] = t
                    waits.append((self.dsem[si], t))

            def emit(eng, waits=waits):
                for s, v in waits:
                    eng.wait_ge(s, v)
            self.ops[e].append(emit)

    def play(self):
        with self.nc.Block() as blk:
            @blk.tensor
            def _(e):
                for f in self.ops['pe']:
                    f(e)

            @blk.vector
            def _(e):
                for f in self.ops['dve']:
                    f(e)

            @blk.scalar
            def _(e):
                for f in self.ops['act']:
                    f(e)

            @blk.gpsimd
            def _(e):
                for f in self.ops['pool']:
                    f(e)

            @blk.sync
            def _(e):
                for f in self.ops['sp']:
                    f(e)


class Mem:
    def __init__(self, nc, stack, words=49152):
        self.arena = stack.enter_context(nc.sbuf_tensor('arena', [128, words], F32))
        self.psum = stack.enter_context(nc.psum_tensor('psum', [128, 4096], F32))
        self.words = words
        self.ptr = 0
        self.mark = 0
        self.limit = words * 4
        self.v32 = self.arena
        self.v16 = self.arena.bitcast(BF16)
        self.p16 = self.psum.bitcast(BF16)

    def tile(self, shape, dtype=F32):
        shape = list(shape) if isinstance(shape, (list, tuple)) else [shape]
        n = int(np.prod(shape))
        esz = 4 if dtype == F32 else 2
        self.ptr = (self.ptr + 63) // 64 * 64
        off = self.ptr // esz
        self.ptr += n * esz
        assert self.ptr <= self.limit, 'SBUF arena overflow %d > %d' % (self.ptr, self.limit)
        base = self.v32 if dtype == F32 else self.v16
        ap = base[:, off:off + n]
        if len(shape) == 2:
            ap = ap.rearrange('p (a b) -> p a b', a=shape[0])
        elif len(shape) == 3:
            ap = ap.rearrange('p (a b c) -> p a b c', a=shape[0], b=shape[1])
        return ap

    def persist(self):
        self.mark = self.ptr

    def reserve_top(self, nbytes):
        self.limit = self.words * 4 - nbytes
        assert self.ptr <= self.limit
        off = self.limit // 2
        return self.v16[:, off:off + nbytes // 2]

    def release_top(self):
        self.limit = self.words * 4

    def reset(self):
        self.ptr = self.mark

    def bank(self, i, n=512, dtype=F32, off=0):
        if dtype == F32:
            return self.psum[:, i * 512 + off:i * 512 + off + n]
        return self.p16[:, i * 1024 + off:i * 1024 + off + n]


def token_blocks():
    return [(0, 512, 0), (512, 512, 0), (1024, 512, 0), (1536, 512, 0), (2048, 256, 1)]


class Builder:
    def __init__(self, dbg=False, upto=99, mode=None, feed=()):
        self.dbg = dbg
        self.upto = upto
        self.mode = mode
        self.feed = set(feed)
        self.nc = bass.Bass('TRN2', target_bir_lowering=False)
        self.stack = contextlib.ExitStack()
        self.din = {}
        self.dout = {}

    def inp(self, name, shape, dtype=F32):
        t = self.nc.dram_tensor(name, list(shape), dtype, kind='ExternalInput').ap()
        self.din[name] = t
        return t

    def scratch(self, name, shape, dtype=F32, out=False):
        kind = 'ExternalOutput' if (out or self.dbg) else 'Internal'
        if name in self.feed:
            kind = 'ExternalInput'
        t = self.nc.dram_tensor(name, list(shape), dtype, kind=kind).ap()
        if kind == 'ExternalInput':
            self.din[name] = t
        if kind == 'ExternalOutput':
            self.dout[name] = t
        return t

    def build(self):
        nc = self.nc
        st = self.stack
        P = self.P = Prog(nc, st)
        M = self.M = Mem(nc, st)
        I = self.inp
        self.xT = I('xT', [D, NT])
        self.cvec = I('cvec', [2, D])
        self.ada_w = I('ada_w', [2, D, 6 * D])
        self.ada_b = I('ada_b', [2, 6 * D])
        self.norm1_g = I('norm1_g', [2, D])
        self.norm2_g = I('norm2_g', [2, D])
        self.ev_w_in = I('ev_w_in', [D, 2560])
        self.ident = I('ident', [128, 128])
        self.rg_conv_w = I('rg_conv_w', [4, 512])
        self.rg_conv_b = I('rg_conv_b', [512])
        self.rg_wa = I('rg_wa', [2, 8, 64, 64])
        self.rg_ba = I('rg_ba', [2, 512])
        self.rg_wx = I('rg_wx', [2, 8, 64, 64])
        self.rg_bx = I('rg_bx', [2, 512])
        self.rg_lambda = I('rg_lambda', [2, 512])
        self.hy_conv_w = I('hy_conv_w', [3, 1536])
        self.hy_conv_b = I('hy_conv_b', [1536])
        self.f_w1 = I('f_w1', [33, 64]); self.f_b1 = I('f_b1', [64]); self.f_freq = I('f_freq', [64])
        self.f_w2 = I('f_w2', [64, 64]); self.f_b2 = I('f_b2', [64]); self.f_w3 = I('f_w3', [64, 2048])
        self.hy_bias = I('hy_bias', [2, 512])
        self.ev_w_out = I('ev_w_out', [D, D])
        self.ffn_w_up = I('ffn_w_up', [2, D, 5632])
        self.ffn_conv_w = I('ffn_conv_w', [2, 3, 5632])
        self.ffn_conv_b = I('ffn_conv_b', [2, 5632])
        self.ffn_cw = I('ffn_cw', [2, 128, 44, 4])
        self.ffn_w_down = I('ffn_w_down', [2, 2816, D])
        self.final_g = I('final_g', [D])
        self.xa0 = self.scratch('xa0', [D, NT])
        self.xb0 = self.scratch('xb0', [D, NT])
        self.od_w_in = I('od_w_in', [D, 896])
        self.od_q_norm_g = I('od_q_norm_g', [512]); self.od_kv_norm_g = I('od_kv_norm_g', [256])
        self.od_w_uq = I('od_w_uq', [512, 1536]); self.od_w_ukv = I('od_w_ukv', [256, 2048])
        self.od_w_o = I('od_w_o', [D, D])
        self.cos2 = I('cos2', [64, S]); self.sin2 = I('sin2', [64, S]); self.rotm = I('rotm', [64, 64])
        self.z1 = self.scratch('z1', [896, NT])
        self.qT = self.scratch('qT', [8, 192, S], BF16)
        self.kT = self.scratch('kT', [8, 128, NT], BF16)
        self.krT = self.scratch('krT', [64, NT], BF16)
        self.Vtok = self.scratch('Vtok', [NT, D], BF16)
        self.oT = self.scratch('oT', [D, S], BF16)
        self.xa1 = self.scratch('xa1', [D, S])
        self.outT = self.scratch('outT', [D, S], out=True)
        self.hc = {}
        for n in (2048, 256):
            nj = n // 128
            self.hc[n] = dict(zT=I('zT%d' % n, [33, n]), dec=I('dec%d' % n, [n, 512]), decb=I('decb%d' % n, [n, 512]),
                              C=I('Ccol%d' % n, [nj, 128, nj, 128], BF16), S=I('Scol%d' % n, [nj, 128, nj, 128], BF16),
                              wk=I('wk%d' % n, [128, nj]), Kf=self.scratch('Kf%d' % n, [2, 2, n, 512]))
        self.vtok = self.scratch('vtok', [3, NT, 512])
        self.u2tok = self.scratch('u2tok', [NT, 512])
        self.z0 = self.scratch('z0', [2560, NT])
        self.mixT = self.scratch('mixT', [D, NT], BF16)
        self.consts()
        if self.mode == 'attn':
            self.phase_attn()
            P.barrier()
            P.play()
            return nc
        if self.upto >= 1:
            for l in range(2):
                self.phase_mod(l)
        if self.upto >= 2:
            self.phase_norm_proj(0, self.xT, 1, self.ev_w_in, 2560, self.z0)
        if self.upto >= 3:
            self.phase_rg()
        if self.upto >= 4:
            for n in (2048, 256):
                h = self.hc[n]
                self.phase_hy_filters(n, h['zT'], h['dec'], h['decb'], h['C'], h['S'], h['wk'], h['Kf'])
            self.phase_hy_prep()
            for n, tok0 in ((2048, 0), (256, S)):
                h = self.hc[n]
                self.phase_hy_conv(n, tok0, h['C'], h['S'], h['Kf'])
        if self.upto >= 5:
            self.phase_out_proj(0, self.mixT, self.ev_w_out, self.xT, self.xa0, token_blocks(), prefetch=0)
        if self.upto >= 6:
            self.phase_ffn(0, self.xa0, self.xb0, [(0, S, 0), (S, NT, 1)])
        lat = token_blocks()[:4]
        if self.upto >= 7:
            self.phase_norm_proj(1, self.xb0, 1, self.od_w_in, 896, self.z1)
            self.phase_mla_prep()
        if self.upto >= 8:
            self.attn_hook = lambda: self.prefetch_wup(1)
            self.phase_attn()
            self.phase_out_proj(1, self.oT, self.od_w_o, self.xb0, self.xa1, lat)
        if self.upto >= 9:
            self.phase_ffn(1, self.xa1, None, [(0, S, 0)], final_out=self.outT)
        P.barrier(final=True)
        P.play()
        return nc

    def consts(self):
        P, M = self.P, self.M
        self.ones_bf = M.tile([128], BF16)
        self.r_ones = Res()
        P.op('dve', lambda e: e.memset(self.ones_bf, 1.0), W=[self.r_ones])
        self.id32 = M.tile([128], F32)
        self.idbf = M.tile([128], BF16)
        self.r_id = Res()
        P.dma('sp', self.id32, self.ident, W=[self.r_id])
        self.r_idbf = Res()
        P.op('dve', lambda e: e.tensor_copy(self.idbf, self.id32), R=[self.r_id], W=[self.r_idbf])
        self.mod = [M.tile([48, 2], F32) for _ in range(2)]
        self.A1 = [M.tile([8, 2], F32) for _ in range(2)]
        self.A2 = [M.tile([8, 2], F32) for _ in range(2)]
        self.r_mod = [Res() for _ in range(2)]
        M.persist()

    def phase_mod(self, l):
        P, M = self.P, self.M
        M.reset()
        cv = M.tile([8, 2], F32)
        sT = M.tile([8, 2], BF16)
        bT = M.tile([48], F32)
        g1 = M.tile([8], F32)
        g2 = M.tile([8], F32)
        r_cv, r_sT, r_b, r_g = Res(), Res(), Res(), Res()
        for j in range(2):
            P.dma('sp', cv[:, :, j], self.cvec[j].rearrange('(k p) -> p k', p=128), W=[r_cv],
                  allow_slow_non_contiguous=True)
        P.dma('sp', bT, self.ada_b[l].rearrange('(j p) -> p j', p=128), W=[r_b], allow_slow_non_contiguous=True)
        P.dma('sp', g1, self.norm1_g[l].rearrange('(k p) -> p k', p=128), W=[r_g], allow_slow_non_contiguous=True)
        P.dma('sp', g2, self.norm2_g[l].rearrange('(k p) -> p k', p=128), W=[r_g], allow_slow_non_contiguous=True)
        P.op('act', lambda e: e.activation(sT, cv, AF.Silu), R=[r_cv], W=[r_sT])
        wt = [M.tile([8, 2048], BF16) for _ in range(2)]
        r_wt = [Res(), Res()]
        ps = M.bank(0, 96).rearrange('p (j c) -> p j c', c=2)
        r_ps = Res()
        for sec in range(3):
            w = wt[sec % 2]
            rw = r_wt[sec % 2]
            for k in range(8):
                P.dma('pool', w[:, k, :], self.ada_w[l, k * 128:(k + 1) * 128, sec * 2048:(sec + 1) * 2048], W=[rw])
            for fc in range(16):
                j = sec * 16 + fc
                for k in range(8):
                    P.op('pe', lambda e, w=w, k=k, fc=fc, j=j: e.matmul(
                        ps[:, j, :], w[:, k, fc * 128:(fc + 1) * 128], sT[:, k, :],
                        start=(k == 0), stop=(k == 7)), R=[rw, r_sT], W=[r_ps])
        mod = self.mod[l]
        rm = self.r_mod[l]
        for c in range(2):
            P.op('dve', lambda e, c=c: e.tensor_tensor(mod[:, :, c], ps[:, :, c], bT, ALU.add),
                 R=[r_ps, r_b], W=[rm])
        for c in range(2):
            P.op('dve', lambda e, c=c: e.scalar_tensor_tensor(
                self.A1[l][:, :, c], mod[:, 8:16, c], 1.0, g1, ALU.add, ALU.mult), R=[rm, r_g], W=[rm])
            P.op('dve', lambda e, c=c: e.scalar_tensor_tensor(
                self.A2[l][:, :, c], mod[:, 32:40, c], 1.0, g2, ALU.add, ALU.mult), R=[rm, r_g], W=[rm])
        P.barrier()

    def norm_block(self, src, t0, T, c, A, B, hdst, r_hdst, bufs, i):
        P, M = self.P, self.M
        xb, sq, rstd, tmp, rx, rsq, rrs, rtmp, psb, rps = bufs[i % 2]
        P.dma('sp', xb[:, :, 0:T], src[:, t0:t0 + T].rearrange('(k p) t -> p k t', p=128), W=[rx])
        P.op('act', lambda e: e.activation(sq[:, :, 0:T], xb[:, :, 0:T], AF.Square), R=[rx], W=[rsq])
        for k in range(8):
            P.op('pe', lambda e, k=k: e.matmul(psb[:, 0:T], self.ones_bf, sq[:, k, 0:T],
                                               start=(k == 0), stop=(k == 7)),
                 R=[rsq, self.r_ones], W=[rps])
        P.op('act', lambda e: e.activation(rstd[:, 0:T], psb[:, 0:T], AF.Sqrt, bias=EPS, scale=1.0 / D),
             R=[rps], W=[rrs])
        P.op('dve', lambda e: e.reciprocal(rstd[:, 0:T], rstd[:, 0:T]), R=[rrs], W=[rrs])
        for k in range(8):
            P.op('dve', lambda e, k=k: e.scalar_tensor_tensor(
                tmp[:, k, 0:T], xb[:, k, 0:T], A[:, k, c:c + 1], rstd[:, 0:T], ALU.mult, ALU.mult),
                R=[rx, rrs, self.r_mod[0], self.r_mod[1]], W=[rtmp[k]])
            P.op('act', lambda e, k=k: e.activation(hdst[:, k, :], tmp[:, k, 0:T], AF.Identity,
                                                    bias=B[:, k, c:c + 1]),
                 R=[rtmp[k], self.r_mod[0], self.r_mod[1]], W=[r_hdst])

    def norm_bufs(self, psbanks):
        M = self.M
        bufs = []
        for i in range(2):
            bufs.append((M.tile([8, 512], F32), M.tile([8, 512], BF16), M.tile([512], F32),
                         M.tile([8, 512], F32), Res(), Res(), Res(), [Res() for _ in range(8)],
                         M.bank(psbanks[i]), Res()))
        return bufs

    def phase_norm_proj(self, l, src, which, w_dram, F, zdst, blocks=None):
        P, M = self.P, self.M
        M.reset()
        blocks = blocks or token_blocks()
        A = self.A1[l] if which == 1 else self.A2[l]
        B = self.mod[l][:, 0:8, :] if which == 1 else self.mod[l][:, 24:32, :]
        hT = M.tile([8, NT], BF16)
        r_h = [Res() for _ in blocks]
        nfc = F // 128
        wt = M.tile([8, F], BF16)
        r_w = [Res() for _ in range(nfc)]
        for g in range(0, F, 2048):
            gw = min(2048, F - g)
            for k in range(8):
                P.dma('pool', wt[:, k, g:g + gw], w_dram[k * 128:(k + 1) * 128, g:g + gw],
                      W=[r_w[j] for j in range(g // 128, (g + gw) // 128)])
        bufs = self.norm_bufs([6, 7])
        for i, (t0, T, c) in enumerate(blocks):
            self.norm_block(src, t0, T, c, A, B, hT[:, :, t0:t0 + T], r_h[i], bufs, i)
        stg = [M.tile([512], F32) for _ in range(4)]
        r_stg = [Res() for _ in range(4)]
        r_ps = [Res() for _ in range(4)]
        n = 0
        for fc in range(nfc):
            for i, (t0, T, c) in enumerate(blocks):
                b = n % 4
                ps = M.bank(b)
                for k in range(8):
                    P.op('pe', lambda e, k=k, fc=fc, t0=t0, T=T, ps=ps: e.matmul(
                        ps[:, 0:T], wt[:, k, fc * 128:(fc + 1) * 128], hT[:, k, t0:t0 + T],
                        start=(k == 0), stop=(k == 7)), R=[r_w[fc], r_h[i]], W=[r_ps[b]])
                ev = 'act' if n % 2 == 0 else 'dve'
                if ev == 'act':
                    P.op('act', lambda e, b=b, T=T, ps=ps: e.copy(stg[b][:, 0:T], ps[:, 0:T]),
                         R=[r_ps[b]], W=[r_stg[b]])
                else:
                    P.op('dve', lambda e, b=b, T=T, ps=ps: e.tensor_copy(stg[b][:, 0:T], ps[:, 0:T]),
                         R=[r_ps[b]], W=[r_stg[b]])
                P.dma('act', zdst[fc * 128:(fc + 1) * 128, t0:t0 + T], stg[b][:, 0:T], R=[r_stg[b]])
                n += 1
        P.barrier()


def rev(ap, n):
    pat = [list(p) for p in ap.ap]
    assert len(pat) == 2 and pat[1][1] == n
    return bass.AP(ap.tensor, ap.offset + (n - 1) * pat[1][0], [pat[0], [-pat[1][0], n]])


def _phase_rg(self):
    P, M = self.P, self.M
    M.reset()
    f = lambda: M.tile([NT], F32)
    zx, gate, xc, hf, hb, ga = [f() for _ in range(6)]
    rgs, igs, aas, bbs = [[f(), f()] for _ in range(4)]
    xcb = M.tile([NT], BF16)
    ob = M.tile([NT], BF16)
    cw = M.tile([4], F32)
    cb = M.tile([1], F32)
    bias4 = M.tile([4], F32)
    lam = M.tile([2], F32)
    cl = M.tile([4], F32)
    wbd = [M.tile([128], BF16) for _ in range(4)]
    segs = [(0, S), (S, CT)]
    blocks = token_blocks()
    for cc in range(4):
        R = {k: Res() for k in ['zx', 'gate', 'xc', 'xcb', 'r0', 'i0', 'a0', 'b0', 'r1', 'i1', 'a1', 'b1', 'hf', 'hb', 'small', 'cl', 'ob', 'ga']}
        rw = [Res() for _ in range(4)]
        rps = [Res() for _ in range(4)]
        fs = slice(cc * 128, (cc + 1) * 128)
        P.dma('sp', zx, self.z0[cc * 128:(cc + 1) * 128, :], W=[R['zx']])
        P.dma('sp', gate, self.z0[512 + cc * 128:512 + (cc + 1) * 128, :], W=[R['gate']])
        P.dma('sp', cw, self.rg_conv_w[:, fs].rearrange('j p -> p j'), W=[R['small']], allow_slow_non_contiguous=True)
        P.dma('sp', cb, self.rg_conv_b[fs].rearrange('(p o) -> p o', o=1), W=[R['small']])
        for d in range(2):
            P.dma('sp', bias4[:, 2 * d:2 * d + 1], self.rg_ba[d, fs].rearrange('(p o) -> p o', o=1), W=[R['small']])
            P.dma('sp', bias4[:, 2 * d + 1:2 * d + 2], self.rg_bx[d, fs].rearrange('(p o) -> p o', o=1), W=[R['small']])
            P.dma('sp', lam[:, d:d + 1], self.rg_lambda[d, fs].rearrange('(p o) -> p o', o=1), W=[R['small']])
        for d in range(2):
            for ty, wsrc in enumerate([self.rg_wa, self.rg_wx]):
                w = wbd[2 * d + ty]
                r_ = rw[2 * d + ty]
                P.op('dve', lambda e, w=w: e.memset(w, 0.0), W=[r_])
                for hh in range(2):
                    P.dma('pool', w[hh * 64:(hh + 1) * 64, hh * 64:(hh + 1) * 64], wsrc[d, 2 * cc + hh], W=[r_])
        P.op('act', lambda e: e.activation(cl[:, 0:2], lam, AF.Exp, scale=-1.0), R=[R['small']], W=[R['cl']])
        P.op('act', lambda e: e.activation(cl[:, 0:2], cl[:, 0:2], AF.Ln, bias=1.0), R=[R['cl']], W=[R['cl']])
        P.op('dve', lambda e: e.tensor_scalar(cl[:, 2:4], cl[:, 0:2], -16.0, None, ALU.mult), R=[R['cl']], W=[R['cl']])
        P.op('dve', lambda e: e.tensor_scalar(cl[:, 0:2], cl[:, 0:2], -8.0, None, ALU.mult), R=[R['cl']], W=[R['cl']])
        P.op('act', lambda e: e.activation(xc, zx, AF.Identity, bias=cb[:, 0:1], scale=cw[:, 2:3]),
             R=[R['zx'], R['small']], W=[R['xc']])
        for j in (0, 1, 3):
            off = j - 2
            for (s0, n) in segs:
                lo = max(0, -off)
                hi = n - max(0, off)
                P.op('dve', lambda e, j=j, off=off, s0=s0, lo=lo, hi=hi: e.scalar_tensor_tensor(
                    xc[:, s0 + lo:s0 + hi], zx[:, s0 + lo + off:s0 + hi + off], cw[:, j:j + 1],
                    xc[:, s0 + lo:s0 + hi], ALU.mult, ALU.add), R=[R['zx'], R['small'], R['xc']], W=[R['xc']])
        P.op('act', lambda e: e.copy(xcb, xc), R=[R['xc']], W=[R['xcb']])
        P.op('act', lambda e: e.activation(ga, gate, AF.Square), R=[R['gate']], W=[R['ga']])
        P.op('pool', lambda e: e.tensor_scalar(ga, ga, 0.044715, 1.0, ALU.mult, ALU.add), R=[R['ga']], W=[R['ga']])
        P.op('pool', lambda e: e.tensor_tensor(ga, ga, gate, ALU.mult), R=[R['ga'], R['gate']], W=[R['ga']])
        P.op('act', lambda e: e.activation(ga, ga, AF.Sigmoid, scale=1.5957691216), R=[R['ga']], W=[R['ga']])
        P.op('pool', lambda e: e.tensor_tensor(ga, ga, gate, ALU.mult), R=[R['ga'], R['gate']], W=[R['ga']])
        for d in range(2):
            rg, ig, aa, bb = rgs[d], igs[d], aas[d], bbs[d]
            kr_, ki_, ka_, kb_ = 'r%d' % d, 'i%d' % d, 'a%d' % d, 'b%d' % d
            for ty, dst, rk in ((0, rg, kr_), (1, ig, ki_)):
                for bi, (t0, T, c) in enumerate(blocks):
                    bk = (bi + ty) % 4
                    ps = M.bank(bk)
                    P.op('pe', lambda e, ps=ps, d=d, ty=ty, t0=t0, T=T: e.matmul(
                        ps[:, 0:T], wbd[2 * d + ty], xcb[:, t0:t0 + T], start=True, stop=True),
                        R=[rw[2 * d + ty], R['xcb']], W=[rps[bk]])
                    P.op('act', lambda e, ps=ps, dst=dst, d=d, ty=ty, t0=t0, T=T: e.activation(
                        dst[:, t0:t0 + T], ps[:, 0:T], AF.Sigmoid, bias=bias4[:, 2 * d + ty:2 * d + ty + 1]),
                        R=[rps[bk], R['small']], W=[R[rk]])
            P.op('act', lambda e, d=d, aa=aa, rg=rg: e.activation(aa, rg, AF.Exp, scale=cl[:, d:d + 1]), R=[R[kr_], R['cl']], W=[R[ka_]])
            P.op('act', lambda e, d=d, bb=bb, rg=rg: e.activation(bb, rg, AF.Exp, scale=cl[:, 2 + d:3 + d]), R=[R[kr_], R['cl']], W=[R[kb_]])
            P.op('act', lambda e, bb=bb: e.activation(bb, bb, AF.Sqrt, bias=1.0, scale=-1.0), R=[R[kb_]], W=[R[kb_]])
            P.op('pool', lambda e, bb=bb, ig=ig: e.tensor_tensor(bb, bb, ig, ALU.mult), R=[R[kb_], R[ki_]], W=[R[kb_]])
            P.op('pool', lambda e, bb=bb: e.tensor_tensor(bb, bb, xc, ALU.mult), R=[R[kb_], R['xc']], W=[R[kb_]])
            h = hf if d == 0 else hb
            hk = 'hf' if d == 0 else 'hb'
            if d == 0:
                P.op('dve', lambda e, h=h, aa=aa, bb=bb: e.tensor_tensor_scan(h[:, S:NT], aa[:, S:NT], bb[:, S:NT], 0.0, ALU.mult, ALU.add),
                     R=[R[ka_], R[kb_]], W=[R[hk]])
                P.op('dve', lambda e, h=h, aa=aa, bb=bb: e.tensor_tensor_scan(h[:, 0:S], aa[:, 0:S], bb[:, 0:S], h[:, NT - 1:NT], ALU.mult, ALU.add),
                     R=[R[ka_], R[kb_], R[hk]], W=[R[hk]])
            else:
                P.op('dve', lambda e, h=h, aa=aa, bb=bb: e.tensor_tensor_scan(rev(h[:, S:NT], CT), rev(aa[:, S:NT], CT), rev(bb[:, S:NT], CT),
                                                                               0.0, ALU.mult, ALU.add), R=[R[ka_], R[kb_]], W=[R[hk]])
                P.op('dve', lambda e, h=h, aa=aa, bb=bb: e.tensor_tensor_scan(rev(h[:, 0:S], S), rev(aa[:, 0:S], S), rev(bb[:, 0:S], S),
                                                                               h[:, S:S + 1], ALU.mult, ALU.add),
                     R=[R[ka_], R[kb_], R[hk]], W=[R[hk]])
        P.op('pool', lambda e: e.tensor_tensor(hf, hf, hb, ALU.add), R=[R['hf'], R['hb']], W=[R['hf']])
        P.op('dve', lambda e: e.tensor_tensor(ob, ga, hf, ALU.mult), R=[R['ga'], R['hf']], W=[R['ob']])
        P.dma('act', self.mixT[cc * 128:(cc + 1) * 128, :], ob, R=[R['ob']])
        P.barrier()


Builder.phase_rg = _phase_rg


def bcast_rows(ap_row, nparts=128):
    pat = [list(p) for p in ap_row.ap]
    return bass.AP(ap_row.tensor, ap_row.offset, [[0, nparts]] + pat)


def _phase_hy_filters(self, n, zT_d, dec_d, decb_d, Ccol, Scol, wk_d, Kf_d):
    P, M = self.P, self.M
    M.reset()
    nj = n // 128
    w1 = M.tile([64], F32)
    w2 = M.tile([64], F32)
    w3 = M.tile([2048], F32)
    sm = M.tile([3], F32)
    zT = M.tile([n], F32)
    h1 = M.tile([n], F32)
    h2 = M.tile([n], F32)
    tm = M.tile([512], F32)
    r_w, r_z, r_h1, r_h2, r_tm = Res(), Res(), Res(), Res(), Res()
    P.dma('sp', w1[0:33, :], self.f_w1, W=[r_w])
    P.dma('sp', w2[0:64, :], self.f_w2, W=[r_w])
    P.dma('sp', w3[0:64, :], self.f_w3, W=[r_w])
    P.dma('sp', sm[0:64, 0:1], self.f_b1.rearrange('(p o) -> p o', o=1), W=[r_w])
    P.dma('sp', sm[0:64, 1:2], self.f_freq.rearrange('(p o) -> p o', o=1), W=[r_w])
    P.dma('sp', sm[0:64, 2:3], self.f_b2.rearrange('(p o) -> p o', o=1), W=[r_w])
    P.dma('sp', zT[0:33, :], zT_d, W=[r_z])
    rps = [Res() for _ in range(8)]
    PI = math.pi

    def sin_layer(lhsT, kk, src, rsrc, bcol, dst, rdst):
        for bi, t0 in enumerate(range(0, n, 512)):
            T = min(512, n - t0)
            ps = M.bank(bi % 2)
            P.op('pe', lambda e, ps=ps, t0=t0, T=T: e.matmul(ps[0:64, 0:T], lhsT, src[0:kk, t0:t0 + T], start=True, stop=True),
                 R=[r_w, rsrc], W=[rps[bi % 2]])
            P.op('dve', lambda e, ps=ps, T=T: e.tensor_scalar(tm[0:64, 0:T], ps[0:64, 0:T], sm[0:64, bcol:bcol + 1],
                                                           sm[0:64, 1:2], ALU.add, ALU.mult), R=[rps[bi % 2], r_w, r_tm], W=[r_tm])
            for thr, op, mul in ((PI, ALU.is_gt, -2 * PI), (-PI, ALU.is_lt, 2 * PI)):
                P.op('dve', lambda e, T=T, t0=t0, thr=thr, op=op, mul=mul: e.tensor_scalar(
                    dst[0:64, t0:t0 + T], tm[0:64, 0:T], thr, mul, op, ALU.mult), R=[r_tm], W=[rdst])
                P.op('dve', lambda e, T=T, t0=t0: e.tensor_tensor(tm[0:64, 0:T], tm[0:64, 0:T], dst[0:64, t0:t0 + T], ALU.add),
                     R=[r_tm, rdst], W=[r_tm])
            P.op('act', lambda e, T=T, t0=t0: e.activation(dst[0:64, t0:t0 + T], tm[0:64, 0:T], AF.Sin), R=[r_tm], W=[rdst])

    sin_layer(w1[0:33, :], 33, zT, r_z, 0, h1, r_h1)
    sin_layer(w2[0:64, :], 64, h1, r_h1, 2, h2, r_h2)
    ks = M.tile([nj, 1024], BF16)
    kd = M.tile([nj, 1024], BF16)
    dec = [M.tile([512], F32) for _ in range(2)]
    decb = [M.tile([512], F32) for _ in range(2)]
    kf = [M.tile([512], F32) for _ in range(2)]
    kb = [M.tile([512], F32) for _ in range(2)]
    ab = [[M.tile([512], BF16) for _ in range(2)] for _ in range(2)]
    r_dec = [Res(), Res()]
    r_kf = [Res(), Res()]
    r_kb = [Res(), Res()]
    r_ab = [[Res(), Res()], [Res(), Res()]]
    r_ks = Res()
    r_nrm = Res()
    cnt = 0
    pending = []

    def flush():
        for fn in pending:
            fn()
        del pending[:]

    for j in range(nj):
        P.dma('sp', dec[j % 2], dec_d[j * 128:(j + 1) * 128, :], W=[r_dec[j % 2]])
        P.dma('sp', decb[j % 2], decb_d[j * 128:(j + 1) * 128, :], W=[r_dec[j % 2]])
        for o in range(2):
            q = cnt % 2
            cnt += 1
            todo = []
            for dr in range(2):
                bk = 2 + 2 * q + dr
                ps = M.bank(bk)
                c0 = o * 1024 + dr * 512
                P.op('pe', lambda e, ps=ps, j=j, c0=c0: e.matmul(ps, h2[0:64, j * 128:(j + 1) * 128], w3[0:64, c0:c0 + 512],
                                                             start=True, stop=True), R=[r_h2, r_w], W=[rps[bk]])
                dst, rd, dc = (kf[q], r_kf[q], dec[j % 2]) if dr == 0 else (kb[q], r_kb[q], decb[j % 2])
                P.op('dve', lambda e, ps=ps, dst=dst, dc=dc: e.tensor_tensor(dst, ps, dc, ALU.mult),
                     R=[rps[bk], r_dec[j % 2]], W=[rd])
                P.op('act', lambda e, dst=dst, q=q, dr=dr: e.activation(ab[q][dr], dst, AF.Abs), R=[rd], W=[r_ab[q][dr]])
                first = (j == 0 and dr == 0)
                last = (j == nj - 1 and dr == 1)

                def nm(o=o, q=q, dr=dr, first=first, last=last):
                    P.op('pe', lambda e: e.matmul(M.bank(6 + o), self.ones_bf, ab[q][dr], start=first, stop=last),
                         R=[r_ab[q][dr], self.r_ones], W=[r_nrm])
                todo.append(nm)
            flush()
            pending.extend(todo)
            P.op('dve', lambda e, q=q, j=j, o=o: e.tensor_tensor(ks[:, j, o * 512:(o + 1) * 512], kf[q], kb[q], ALU.add),
                 R=[r_kf[q], r_kb[q]], W=[r_ks])
            P.op('dve', lambda e, q=q, j=j, o=o: e.tensor_tensor(kd[:, j, o * 512:(o + 1) * 512], kb[q], kf[q], ALU.subtract),
                 R=[r_kf[q], r_kb[q]], W=[r_ks])
    flush()
    rn = M.tile([1024], F32)
    r_rn = Res()
    for o in range(2):
        P.op('dve', lambda e, o=o: e.reciprocal(rn[:, o * 512:(o + 1) * 512], M.bank(6 + o)), R=[r_nrm], W=[r_rn])
    wk = M.tile([nj], F32)
    P.dma('sp', wk, wk_d, W=[r_rn])
    cs = [[M.tile([nj, 128], BF16) for _ in range(2)] for _ in range(2)]
    r_cs = [Res(), Res()]
    ot = [M.tile([512], F32) for _ in range(4)]
    r_ot = [Res() for _ in range(4)]
    m = 0
    for f in range(nj):
        q = f % 2
        P.dma('sp', cs[q][0], Ccol[f], W=[r_cs[q]])
        P.dma('sp', cs[q][1], Scol[f], W=[r_cs[q]])
        for ri in range(2):
            src = ks if ri == 0 else kd
            for o in range(2):
                bk = m % 4
                ps = M.bank(bk)
                for j in range(nj):
                    P.op('pe', lambda e, ps=ps, q=q, ri=ri, j=j, o=o, src=src: e.matmul(
                        ps, cs[q][ri][:, j, :], src[:, j, o * 512:(o + 1) * 512], start=(j == 0), stop=(j == nj - 1)),
                        R=[r_cs[q], r_ks], W=[rps[bk]])
                P.op('dve', lambda e, ps=ps, bk=bk, f=f, o=o: e.scalar_tensor_tensor(
                    ot[bk], ps, wk[:, f:f + 1], rn[:, o * 512:(o + 1) * 512], ALU.mult, ALU.mult),
                    R=[rps[bk], r_rn], W=[r_ot[bk]])
                P.dma('act', Kf_d[o, ri, f * 128:(f + 1) * 128, :], ot[bk], R=[r_ot[bk]])
                m += 1
    P.barrier()


def _phase_hy_prep(self):
    P, M = self.P, self.M
    M.reset()
    zz = [M.tile([NT], F32) for _ in range(2)]
    zc = [M.tile([NT], F32) for _ in range(2)]
    cw = [M.tile([4], F32) for _ in range(2)]
    stg = [M.tile([4, 128], F32) for _ in range(2)]
    r_zz = [Res(), Res()]
    r_zc = [Res(), Res()]
    r_cw = [Res(), Res()]
    r_stg = [Res(), Res()]
    r_ps = [Res(), Res()]
    segs = [(0, S), (S, CT)]
    m = 0
    for ch in range(12):
        q = ch % 2
        g, cc = ch // 4, ch % 4
        fs = slice(ch * 128, (ch + 1) * 128)
        P.dma('sp', zz[q], self.z0[1024 + ch * 128:1024 + (ch + 1) * 128, :], W=[r_zz[q]])
        P.dma('sp', cw[q][:, 0:3], self.hy_conv_w[:, fs].rearrange('j p -> p j'), W=[r_cw[q]], allow_slow_non_contiguous=True)
        P.dma('sp', cw[q][:, 3:4], self.hy_conv_b[fs].rearrange('(p o) -> p o', o=1), W=[r_cw[q]])
        P.op('act', lambda e, q=q: e.activation(zc[q], zz[q], AF.Identity, bias=cw[q][:, 3:4], scale=cw[q][:, 1:2]),
             R=[r_zz[q], r_cw[q]], W=[r_zc[q]])
        for j in (0, 2):
            off = j - 1
            for (s0, n) in segs:
                lo = max(0, -off)
                hi = n - max(0, off)
                P.op('dve', lambda e, q=q, j=j, off=off, s0=s0, lo=lo, hi=hi: e.scalar_tensor_tensor(
                    zc[q][:, s0 + lo:s0 + hi], zz[q][:, s0 + lo + off:s0 + hi + off], cw[q][:, j:j + 1],
                    zc[q][:, s0 + lo:s0 + hi], ALU.mult, ALU.add), R=[r_zz[q], r_cw[q], r_zc[q]], W=[r_zc[q]])
        for tg in range(0, NT // 128, 4):
            ntl = min(4, NT // 128 - tg)
            b = m % 2
            m += 1
            ps = M.bank(b)
            for i in range(ntl):
                P.op('pe', lambda e, ps=ps, q=q, tg=tg, i=i: e.transpose(
                    ps[:, i * 128:(i + 1) * 128], zc[q][:, (tg + i) * 128:(tg + i + 1) * 128], self.id32),
                    R=[r_zc[q], self.r_id], W=[r_ps[b]])
            P.op('act' if b == 0 else 'dve',
                 (lambda e, ps=ps, b=b, ntl=ntl: e.copy(stg[b][:, 0:ntl, :], ps[:, 0:ntl * 128].rearrange('p (a c) -> p a c', c=128)))
                 if b == 0 else
                 (lambda e, ps=ps, b=b, ntl=ntl: e.tensor_copy(stg[b][:, 0:ntl, :], ps[:, 0:ntl * 128].rearrange('p (a c) -> p a c', c=128))),
                 R=[r_ps[b]], W=[r_stg[b]])
            P.dma('act', self.vtok[g, tg * 128:(tg + ntl) * 128, cc * 128:(cc + 1) * 128].rearrange('(a p) c -> p a c', p=128),
                  stg[b][:, 0:ntl, :], R=[r_stg[b]])
    P.barrier()


def _phase_hy_conv(self, n, tok0, Ccol, Scol, Kf_d):
    P, M = self.P, self.M
    M.reset()
    nj = n // 128
    u = M.tile([nj, 512], BF16)
    Y = M.tile([nj, 2, 512], BF16)
    r_u = [Res() for _ in range(nj)]
    r_u2 = [Res() for _ in range(nj)]
    r_Y = [Res() for _ in range(nj)]
    cs = [[M.tile([nj, 128], BF16) for _ in range(2)] for _ in range(3)]
    r_cs = [Res(), Res(), Res()]
    kk = [[M.tile([512], F32) for _ in range(2)] for _ in range(3)]
    r_kk = [Res(), Res(), Res()]
    t1 = [M.tile([512], F32) for _ in range(2)]
    t2 = [M.tile([512], F32) for _ in range(2)]
    t3 = [M.tile([512], F32) for _ in range(2)]
    t4 = [M.tile([512], F32) for _ in range(2)]
    r_t1 = [Res(), Res()]
    r_t2 = [Res(), Res()]
    r_t3 = [Res(), Res()]
    r_t4 = [Res(), Res()]
    biasb = M.tile([2, 512], F32)
    r_bias = Res()
    for o in range(2):
        P.dma('sp', biasb[:, o, :], bcast_rows(self.hy_bias[o]), W=[r_bias])
    uf = [M.tile([512], F32) for _ in range(3)]
    xg = [M.tile([512], F32) for _ in range(3)]
    un = [M.tile([512], F32) for _ in range(2)]
    ub = [M.tile([512], BF16) for _ in range(2)]
    ot = [M.tile([512], BF16) for _ in range(2)]
    r_uf = [Res(), Res(), Res()]
    r_xg = [Res(), Res(), Res()]
    r_un = [Res(), Res()]
    r_ub = [Res(), Res()]
    r_ot = [Res(), Res()]
    rps = [Res() for _ in range(8)]
    for j in range(nj):
        P.dma('pool', u[:, j, :], self.vtok[0, tok0 + j * 128:tok0 + (j + 1) * 128, :], W=[r_u[j]])
    for o in range(2):
        for f in range(nj):
            q3 = f % 3
            q = f % 2
            P.dma('sp', cs[q3][0], Ccol[f], W=[r_cs[q3]])
            P.dma('sp', cs[q3][1], Scol[f], W=[r_cs[q3]])
            for ri in range(2):
                P.dma('sp', kk[q3][ri], Kf_d[o, ri, f * 128:(f + 1) * 128, :], W=[r_kk[q3]])
            pr, pi = M.bank(2 * q), M.bank(2 * q + 1)
            for ri, ps in ((0, pr), (1, pi)):
                for j in range(nj):
                    P.op('pe', lambda e, ps=ps, q3=q3, ri=ri, j=j: e.matmul(ps, cs[q3][ri][:, j, :], u[:, j, :],
                                                                          start=(j == 0), stop=(j == nj - 1)),
                         R=[r_cs[q3], r_u[j]], W=[rps[2 * q + ri]])
            R_ = [rps[2 * q], rps[2 * q + 1], r_kk[q3]]
            P.op('dve', lambda e, q=q, q3=q3, pr=pr: e.tensor_tensor(t1[q], pr, kk[q3][0], ALU.mult), R=R_ + [r_t1[q]], W=[r_t1[q]])
            P.op('dve', lambda e, q=q, q3=q3, pi=pi: e.tensor_tensor(t2[q], pi, kk[q3][1], ALU.mult), R=R_ + [r_t2[q]], W=[r_t2[q]])
            P.op('dve', lambda e, q=q, q3=q3, pi=pi: e.tensor_tensor(t3[q], pi, kk[q3][0], ALU.mult), R=R_ + [r_t3[q]], W=[r_t3[q]])
            P.op('dve', lambda e, q=q, q3=q3, pr=pr: e.tensor_tensor(t4[q], pr, kk[q3][1], ALU.mult), R=R_ + [r_t4[q]], W=[r_t4[q]])
            P.op('pool', lambda e, q=q, f=f: e.tensor_tensor(Y[:, f, 0, :], t1[q], t2[q], ALU.add), R=[r_t1[q], r_t2[q]], W=[r_Y[f]])
            P.op('pool', lambda e, q=q, f=f: e.tensor_tensor(Y[:, f, 1, :], t3[q], t4[q], ALU.subtract), R=[r_t3[q], r_t4[q]], W=[r_Y[f]])
        for t in range(nj):
            q = t % 2
            q3 = t % 3
            P.dma('sp', cs[q3][0], Ccol[t], W=[r_cs[q3]])
            P.dma('sp', cs[q3][1], Scol[t], W=[r_cs[q3]])
            rows = slice(tok0 + t * 128, tok0 + (t + 1) * 128)
            if o == 0:
                P.dma('sp', uf[q3], self.vtok[0, rows, :], W=[r_uf[q3]])
            else:
                P.dma('sp', uf[q3], self.u2tok[rows, :], R=[r_u2[t]], W=[r_uf[q3]])
            P.dma('sp', xg[q3], self.vtok[1 + o, rows, :], W=[r_xg[q3]])
            ps = M.bank(4 + q)
            for ri in range(2):
                for j in range(nj):
                    P.op('pe', lambda e, ps=ps, q3=q3, ri=ri, j=j: e.matmul(ps, cs[q3][ri][:, j, :], Y[:, j, ri, :],
                                                                          start=(ri == 0 and j == 0), stop=(ri == 1 and j == nj - 1)),
                         R=[r_cs[q3], r_Y[j]], W=[rps[4 + q]])
            P.op('pool', lambda e, q=q, q3=q3, o=o: e.tensor_tensor(un[q], uf[q3], biasb[:, o, :], ALU.mult), R=[r_uf[q3], r_bias], W=[r_un[q]])
            P.op('dve', lambda e, q=q, ps=ps: e.tensor_tensor(un[q], un[q], ps, ALU.add), R=[r_un[q], rps[4 + q]], W=[r_un[q]])
            if o == 0:
                P.op('dve', lambda e, q=q, q3=q3: e.tensor_tensor(un[q], un[q], xg[q3], ALU.mult), R=[r_un[q], r_xg[q3]], W=[r_un[q]])
                P.dma('act', self.u2tok[rows, :], un[q], R=[r_un[q]], W=[r_u2[t]])
                P.op('act', lambda e, q=q, t=t: e.copy(u[:, t, :], un[q]), R=[r_un[q]], W=[r_u[t]])
            else:
                P.op('dve', lambda e, q=q, q3=q3: e.tensor_tensor(ub[q], un[q], xg[q3], ALU.mult), R=[r_un[q], r_xg[q3]], W=[r_ub[q]])
                pt = M.bank(6 + q, 512, BF16)
                for cc in range(4):
                    P.op('pe', lambda e, pt=pt, q=q, cc=cc: e.transpose(pt[:, cc * 128:(cc + 1) * 128], ub[q][:, cc * 128:(cc + 1) * 128], self.idbf),
                         R=[r_ub[q], self.r_idbf], W=[rps[6 + q]])
                P.op('act', lambda e, pt=pt, q=q: e.copy(ot[q], pt), R=[rps[6 + q]], W=[r_ot[q]])
                P.dma('act', self.mixT[512:1024, tok0 + t * 128:tok0 + (t + 1) * 128].rearrange('(a p) t -> p a t', p=128),
                      ot[q].rearrange('p (a t) -> p a t', a=4), R=[r_ot[q]])
    P.barrier()


Builder.phase_hy_filters = _phase_hy_filters
Builder.phase_hy_prep = _phase_hy_prep
Builder.phase_hy_conv = _phase_hy_conv


def hyena_consts():
    out = {}
    f32 = np.float32
    max_decay = math.log(1e-2) / 0.3
    min_decay = math.log(1e-2) / 1.5
    deltas = np.linspace(min_decay, max_decay, 512, dtype=f32)
    for n in (2048, 256):
        pos = np.arange(n, dtype=f32)
        t = np.linspace(0.0, 1.0, n, dtype=f32)[:, None]
        w = (f32(2.0 * math.pi) * pos / f32(n)).astype(f32)
        fr = np.linspace(1e-4, 15, 16, dtype=f32)
        ang = (w[:, None] * fr[None, :]).astype(f32)
        z = np.concatenate([t, np.cos(ang), -np.sin(ang)], axis=-1).astype(f32)
        out['zT%d' % n] = np.ascontiguousarray(z.T)
        dec = np.exp(-t * np.abs(deltas)[None, :]).astype(f32)
        out['dec%d' % n] = dec
        decb = dec.copy()
        decb[0] = 0.0
        out['decb%d' % n] = decb
        N = 2 * n - 1
        sk = (np.arange(n, dtype=np.int64)[:, None] * np.arange(n, dtype=np.int64)[None, :]) % N
        ang = 2.0 * np.pi * sk.astype(np.float64) / N
        nj = n // 128
        for nm, mat in (('C', np.cos(ang)), ('S', np.sin(ang))):
            m4 = mat.reshape(nj, 128, nj, 128).transpose(2, 1, 0, 3)
            out['%scol%d' % (nm, n)] = np.ascontiguousarray(m4).astype(ml_dtypes.bfloat16)
        wk = np.full(n, 2.0 / N, dtype=f32)
        wk[0] = 1.0 / N
        out['wk%d' % n] = np.ascontiguousarray(wk.reshape(nj, 128).T)
    return out


def _phase_out_proj(self, l, src_bf, w_dram, xsrc, xdst, blocks, prefetch=None):
    P, M = self.P, self.M
    M.reset()
    ntok = max(t0 + T for t0, T, c in blocks)
    mx = M.tile([8, ntok], BF16)
    wt = M.tile([8, D], BF16)
    r_mx, r_w = Res(), Res()
    for k in range(8):
        P.dma('sp', mx[:, k, :], src_bf[k * 128:(k + 1) * 128, 0:ntok], W=[r_mx])
        P.dma('pool', wt[:, k, :], w_dram[k * 128:(k + 1) * 128, :], W=[r_w])
    if prefetch is not None:
        self.prefetch_wup(prefetch)
    xb = [M.tile([512], F32) for _ in range(3)]
    ob = [M.tile([512], F32) for _ in range(3)]
    r_xb = [Res() for _ in range(3)]
    r_ob = [Res() for _ in range(3)]
    rps = [Res() for _ in range(4)]
    n = 0
    mod = self.mod[l]
    for dc in range(8):
        for (t0, T, c) in blocks:
            b = n % 4
            q = n % 3
            n += 1
            ps = M.bank(b)
            P.dma('sp', xb[q][:, 0:T], xsrc[dc * 128:(dc + 1) * 128, t0:t0 + T], W=[r_xb[q]])
            for k in range(8):
                P.op('pe', lambda e, ps=ps, k=k, dc=dc, t0=t0, T=T: e.matmul(
                    ps[:, 0:T], wt[:, k, dc * 128:(dc + 1) * 128], mx[:, k, t0:t0 + T], start=(k == 0), stop=(k == 7)),
                    R=[r_w, r_mx], W=[rps[b]])
            P.op('dve', lambda e, ps=ps, q=q, dc=dc, c=c, T=T: e.scalar_tensor_tensor(
                ob[q][:, 0:T], ps[:, 0:T], mod[:, 16 + dc, c:c + 1], xb[q][:, 0:T], ALU.mult, ALU.add),
                R=[rps[b], r_xb[q], self.r_mod[l]], W=[r_ob[q]])
            P.dma('act', xdst[dc * 128:(dc + 1) * 128, t0:t0 + T], ob[q][:, 0:T], R=[r_ob[q]])
    P.barrier()


def _phase_ffn(self, l, xsrc, xdst, segs, final_out=None):
    P, M = self.P, self.M
    M.reset()
    TB = 256
    NP = 22
    pre = getattr(self, 'wup', None)
    if pre is not None and pre[0] == l:
        wu, r_wu = pre[1], pre[2]
    else:
        wu = M.tile([8, 5632], BF16)
        r_wu = [Res() for _ in range(44)]
        for g in range(0, 5632, 2048):
            gw = min(2048, 5632 - g)
            for k in range(8):
                P.dma('pool', wu[:, k, g:g + gw], self.ffn_w_up[l, k * 128:(k + 1) * 128, g:g + gw],
                      W=[r_wu[j] for j in range(g // 128, (g + gw) // 128)])
    wd = M.tile([NP, D], BF16)
    r_wd = Res()
    for p in range(NP):
        P.dma('pool', wd[:, p, :], self.ffn_w_down[l, p * 128:(p + 1) * 128, :], W=[r_wd])
    cw = M.tile([44, 4], F32)
    r_cw = Res()
    P.dma('sp', cw, self.ffn_cw[l], W=[r_cw])
    NC = TB + 2
    xbs = [M.tile([8, NC], F32) for _ in range(2)]
    sq = M.tile([8, NC], BF16)
    h2 = M.tile([8, NC], BF16)
    rstd = M.tile([NC], F32)
    tmp = [M.tile([NC], F32) for _ in range(2)]
    mm = M.tile([NP, TB], BF16)
    cg = [M.tile([TB], F32) for _ in range(3)]
    cv = [M.tile([TB], F32) for _ in range(3)]
    xn = M.tile([8, TB], F32)
    r_xbs = [Res(), Res()]
    r_sq, r_h2, r_rs, r_xn = Res(), Res(), Res(), Res()
    r_tmp = [Res(), Res()]
    r_mm = [Res() for _ in range(NP)]
    r_cg = [Res(), Res(), Res()]
    r_cv = [Res(), Res(), Res()]
    rps = [Res() for _ in range(8)]
    A = self.A2[l]
    mod = self.mod[l]
    if final_out is not None:
        fg = M.tile([8], F32)
        r_fg = Res()
        P.dma('sp', fg, self.final_g.rearrange('(k p) -> p k', p=128), W=[r_fg], allow_slow_non_contiguous=True)
        sq2, r_sq2 = sq, r_sq
        fo, r_fo = xn, r_xn
    RM = [self.r_mod[l]]
    blist = []
    for (s0, s1, c) in segs:
        for t0 in range(s0, s1, TB):
            T = min(TB, s1 - t0)
            lo = max(s0, t0 - 1)
            hi = min(s1, t0 + T + 1)
            blist.append((s0, s1, c, t0, T, lo, hi))

    def load_x(i):
        s0, s1, c, t0, T, lo, hi = blist[i]
        P.dma('sp', xbs[i % 2][:, :, 0:hi - lo], xsrc[:, lo:hi].rearrange('(k p) t -> p k t', p=128), W=[r_xbs[i % 2]])

    load_x(0)
    for nblk, (s0, s1, c, t0, T, lo, hi) in enumerate(blist):
        if True:
            nc_ = hi - lo
            c0 = t0 - lo
            xb, r_xb = xbs[nblk % 2], r_xbs[nblk % 2]
            P.op('act', lambda e, nc_=nc_, xb=xb: e.activation(sq[:, :, 0:nc_], xb[:, :, 0:nc_], AF.Square), R=[r_xb], W=[r_sq])
            psn = M.bank(6)
            for k in range(8):
                P.op('pe', lambda e, k=k, nc_=nc_, psn=psn: e.matmul(psn[:, 0:nc_], self.ones_bf, sq[:, k, 0:nc_], start=(k == 0), stop=(k == 7)),
                     R=[r_sq, self.r_ones], W=[rps[6]])
            P.op('act', lambda e, nc_=nc_, psn=psn: e.activation(rstd[:, 0:nc_], psn[:, 0:nc_], AF.Sqrt, bias=EPS, scale=1.0 / D),
                 R=[rps[6]], W=[r_rs])
            P.op('dve', lambda e, nc_=nc_: e.reciprocal(rstd[:, 0:nc_], rstd[:, 0:nc_]), R=[r_rs], W=[r_rs])
            for k in range(8):
                q = k % 2
                P.op('pool', lambda e, k=k, q=q, nc_=nc_, xb=xb: e.tensor_tensor(
                    tmp[q][:, 0:nc_], xb[:, k, 0:nc_], rstd[:, 0:nc_], ALU.mult), R=[r_xb, r_rs], W=[r_tmp[q]])
                P.op('act', lambda e, k=k, q=q, nc_=nc_, c=c: e.activation(
                    h2[:, k, 0:nc_], tmp[q][:, 0:nc_], AF.Identity, bias=mod[:, 24 + k, c:c + 1], scale=A[:, k, c:c + 1]),
                    R=[r_tmp[q]] + RM, W=[r_h2])
            if nblk + 1 < len(blist):
                load_x(nblk + 1)
            a = 1 if t0 == s0 else 0
            bnd = 1 if t0 + T == s1 else 0

            def st_pe(p):
                q = p % 3
                for ch, bk in ((p, 2 * q), (NP + p, 2 * q + 1)):
                    ps = M.bank(bk)
                    for k in range(8):
                        P.op('pe', lambda e, ps=ps, k=k, ch=ch, nc_=nc_: e.matmul(
                            ps[:, 0:nc_], wu[:, k, ch * 128:(ch + 1) * 128], h2[:, k, 0:nc_], start=(k == 0), stop=(k == 7)),
                            R=[r_wu[ch], r_h2], W=[rps[bk]])

            def st_conv(p):
                q = p % 3
                pairs = ((p, M.bank(2 * q), 2 * q, cg[q], r_cg[q]), (NP + p, M.bank(2 * q + 1), 2 * q + 1, cv[q], r_cv[q]))
                for ch, ps, bk, dst, rd in pairs:
                    P.op('act', lambda e, ps=ps, ch=ch, dst=dst, T=T, c0=c0: e.activation(
                        dst[:, 0:T], ps[:, c0:c0 + T], AF.Identity, bias=cw[:, ch, 3:4], scale=cw[:, ch, 1:2]),
                        R=[rps[bk], r_cw], W=[rd])
                for ch, ps, bk, dst, rd in pairs:
                    P.op('dve', lambda e, ps=ps, ch=ch, dst=dst, T=T, c0=c0, a=a: e.scalar_tensor_tensor(
                        dst[:, a:T], ps[:, c0 - 1 + a:c0 - 1 + T], cw[:, ch, 0:1], dst[:, a:T], ALU.mult, ALU.add),
                        R=[rps[bk], r_cw, rd], W=[rd])
                for ch, ps, bk, dst, rd in pairs:
                    P.op('dve', lambda e, ps=ps, ch=ch, dst=dst, T=T, c0=c0, bnd=bnd: e.scalar_tensor_tensor(
                        dst[:, 0:T - bnd], ps[:, c0 + 1:c0 + 1 + T - bnd], cw[:, ch, 2:3], dst[:, 0:T - bnd], ALU.mult, ALU.add),
                        R=[rps[bk], r_cw, rd], W=[rd])

            def st_gate(p):
                q = p % 3
                P.op('act', lambda e, q=q, T=T: e.activation(cg[q][:, 0:T], cg[q][:, 0:T], AF.Silu), R=[r_cg[q]], W=[r_cg[q]])
                P.op('pool', lambda e, q=q, p=p, T=T: e.tensor_tensor(mm[:, p, 0:T], cg[q][:, 0:T], cv[q][:, 0:T], ALU.mult),
                     R=[r_cg[q], r_cv[q]], W=[r_mm[p]])

            for i in range(NP + 2):
                if i < NP:
                    st_pe(i)
                if 1 <= i <= NP:
                    st_conv(i - 1)
                if 2 <= i <= NP + 1:
                    st_gate(i - 2)
            for dc in range(8):
                bk = 6 + dc % 2
                ps = M.bank(bk)
                for p in range(NP):
                    P.op('pe', lambda e, ps=ps, p=p, dc=dc, T=T: e.matmul(
                        ps[:, 0:T], wd[:, p, dc * 128:(dc + 1) * 128], mm[:, p, 0:T], start=(p == 0), stop=(p == NP - 1)),
                        R=[r_wd, r_mm[p]], W=[rps[bk]])
                P.op('dve', lambda e, ps=ps, dc=dc, T=T, c0=c0, c=c, xb=xb: e.scalar_tensor_tensor(
                    xn[:, dc, 0:T], ps[:, 0:T], mod[:, 40 + dc, c:c + 1], xb[:, dc, c0:c0 + T], ALU.mult, ALU.add),
                    R=[rps[bk], r_xb] + RM, W=[r_xn])
            if final_out is None:
                P.dma('sp', xdst[:, t0:t0 + T].rearrange('(k p) t -> p k t', p=128), xn[:, :, 0:T], R=[r_xn])
            else:
                P.op('act', lambda e, T=T: e.activation(sq2[:, :, 0:T], xn[:, :, 0:T], AF.Square), R=[r_xn], W=[r_sq2])
                psn = M.bank(7)
                for k in range(8):
                    P.op('pe', lambda e, k=k, T=T, psn=psn: e.matmul(psn[:, 0:T], self.ones_bf, sq2[:, k, 0:T], start=(k == 0), stop=(k == 7)),
                         R=[r_sq2, self.r_ones], W=[rps[7]])
                P.op('act', lambda e, T=T, psn=psn: e.activation(rstd[:, 0:T], psn[:, 0:T], AF.Sqrt, bias=EPS, scale=1.0 / D),
                     R=[rps[7], r_rs], W=[r_rs])
                P.op('dve', lambda e, T=T: e.reciprocal(rstd[:, 0:T], rstd[:, 0:T]), R=[r_rs], W=[r_rs])
                for k in range(8):
                    P.op('dve', lambda e, k=k, T=T: e.scalar_tensor_tensor(
                        fo[:, k, 0:T], xn[:, k, 0:T], fg[:, k:k + 1], rstd[:, 0:T], ALU.mult, ALU.mult),
                        R=[r_xn, r_rs, r_fg], W=[r_fo])
                P.dma('sp', final_out[:, t0:t0 + T].rearrange('(k p) t -> p k t', p=128), fo[:, :, 0:T], R=[r_fo])
    P.barrier()
    if pre is not None and pre[0] == l:
        M.release_top()
        self.wup = None


def _prefetch_wup(self, l):
    P, M = self.P, self.M
    flat = M.reserve_top(8 * 5632 * 2)
    wu = flat.rearrange('p (k f) -> p k f', k=8)
    r_wu = [Res() for _ in range(44)]
    for g in range(0, 5632, 2048):
        gw = min(2048, 5632 - g)
        for k in range(8):
            P.dma('pool', wu[:, k, g:g + gw], self.ffn_w_up[l, k * 128:(k + 1) * 128, g:g + gw],
                  W=[r_wu[j] for j in range(g // 128, (g + gw) // 128)], bg=True)
    self.wup = (l, wu, r_wu)


Builder.prefetch_wup = _prefetch_wup
Builder.phase_out_proj = _phase_out_proj
Builder.phase_ffn = _phase_ffn


def _rms_feat(self, zt, nk, T, g, dst, rz, rg_, rdst, tmp, r_tmp, sq, r_sq, rstd, r_rs, psb, rps):
    P = self.P
    P.op('act', lambda e: e.activation(sq[:, 0:nk, 0:T], zt[:, :, 0:T], AF.Square), R=[rz], W=[r_sq])
    for k in range(nk):
        P.op('pe', lambda e, k=k: e.matmul(psb[:, 0:T], self.ones_bf, sq[:, k, 0:T], start=(k == 0), stop=(k == nk - 1)),
             R=[r_sq, self.r_ones], W=[rps])
    P.op('act', lambda e: e.activation(rstd[:, 0:T], psb[:, 0:T], AF.Sqrt, bias=EPS, scale=1.0 / (nk * 128)), R=[rps], W=[r_rs])
    P.op('dve', lambda e: e.reciprocal(rstd[:, 0:T], rstd[:, 0:T]), R=[r_rs], W=[r_rs])
    for k in range(nk):
        P.op('dve', lambda e, k=k: e.scalar_tensor_tensor(dst[:, k, :], zt[:, k, 0:T], g[:, k:k + 1], rstd[:, 0:T], ALU.mult, ALU.mult),
             R=[rz, r_rs, rg_], W=[rdst])


def _phase_mla_prep(self):
    P, M = self.P, self.M
    M.reset()
    z1 = self.z1
    wq = M.tile([4, 1536], BF16)
    wkv = M.tile([2, 2048], BF16)
    r_w = Res()
    for k in range(4):
        P.dma('pool', wq[:, k, :], self.od_w_uq[k * 128:(k + 1) * 128, :], W=[r_w])
    for k in range(2):
        for g in range(2):
            P.dma('pool', wkv[:, k, g * 1024:(g + 1) * 1024], self.od_w_ukv[k * 128:(k + 1) * 128, g * 1024:(g + 1) * 1024], W=[r_w])
    gq = M.tile([4], F32)
    gkv = M.tile([2], F32)
    rotm = M.tile([64], F32)
    cos2 = M.tile([S], F32)
    sin2 = M.tile([S], F32)
    r_c = Res()
    P.dma('sp', gq, self.od_q_norm_g.rearrange('(k p) -> p k', p=128), W=[r_c], allow_slow_non_contiguous=True)
    P.dma('sp', gkv, self.od_kv_norm_g.rearrange('(k p) -> p k', p=128), W=[r_c], allow_slow_non_contiguous=True)
    P.dma('sp', rotm[0:64, :], self.rotm, W=[r_c])
    P.dma('sp', cos2[0:64, :], self.cos2, W=[r_c])
    P.dma('sp', sin2[0:64, :], self.sin2, W=[r_c])
    qn = M.tile([4, S], BF16)
    ckv = M.tile([2, NT], BF16)
    r_qn = [Res() for _ in range(4)]
    r_ckv = [Res() for _ in range(5)]
    zt = [M.tile([4, 512], F32) for _ in range(2)]
    r_zt = [Res(), Res()]
    sq = M.tile([4, 512], BF16)
    rstd = M.tile([512], F32)
    tmp = None
    r_sq, r_rs = Res(), Res()
    rps = [Res() for _ in range(8)]
    blocks = token_blocks()
    for i, (t0, T, c) in enumerate(blocks[:4]):
        q = i % 2
        P.dma('sp', zt[q][:, :, 0:T], z1[0:512, t0:t0 + T].rearrange('(k p) t -> p k t', p=128), W=[r_zt[q]])
        self.rms_feat(zt[q], 4, T, gq, qn[:, :, t0:t0 + T], r_zt[q], r_c, r_qn[i], None, None, sq, r_sq, rstd, r_rs, M.bank(7), rps[7])
    for i, (t0, T, c) in enumerate(blocks):
        q = i % 2
        P.dma('sp', zt[q][:, 0:2, 0:T], z1[512:768, t0:t0 + T].rearrange('(k p) t -> p k t', p=128), W=[r_zt[q]])
        self.rms_feat(zt[q][:, 0:2, :], 2, T, gkv, ckv[:, :, t0:t0 + T], r_zt[q], r_c, r_ckv[i], None, None, sq, r_sq, rstd, r_rs, M.bank(7), rps[7])
    stb = [M.tile([512], BF16) for _ in range(4)]
    r_stb = [Res() for _ in range(4)]
    xr = [M.tile([512], F32) for _ in range(2)]
    r_xr = [Res(), Res()]
    tt_ = [M.tile([512], F32) for _ in range(2)]
    r_tt = [Res(), Res()]
    n = 0

    def rope_out(src_sb, r_src, t0, T, dst_dram, latent, q):
        nonlocal n
        b = n % 4
        n += 1
        if latent:
            pr = M.bank(4 + q)
            P.op('pe', lambda e: e.matmul(pr[0:64, 0:T], rotm[0:64, :], src_sb[0:64, 0:T], start=True, stop=True),
                 R=[r_src, r_c], W=[rps[4 + q]])
            P.op('dve', lambda e: e.tensor_tensor(tt_[q][0:64, 0:T], pr[0:64, 0:T], sin2[0:64, t0:t0 + T], ALU.mult),
                 R=[rps[4 + q], r_c], W=[r_tt[q]])
            P.op('dve', lambda e: e.tensor_tensor(src_sb[0:64, 0:T], src_sb[0:64, 0:T], cos2[0:64, t0:t0 + T], ALU.mult),
                 R=[r_src, r_c], W=[r_src])
            P.op('dve', lambda e: e.tensor_tensor(stb[b][0:64, 0:T], src_sb[0:64, 0:T], tt_[q][0:64, 0:T], ALU.add),
                 R=[r_src, r_tt[q]], W=[r_stb[b]])
        else:
            P.op('dve', lambda e: e.tensor_copy(stb[b][0:64, 0:T], src_sb[0:64, 0:T]), R=[r_src], W=[r_stb[b]])
        P.dma('act', dst_dram, stb[b][0:64, 0:T], R=[r_stb[b]])

    for h in range(8):
        for i, (t0, T, c) in enumerate(blocks[:4]):
            b = n % 4
            n += 1
            ps = M.bank(b)
            for k in range(4):
                P.op('pe', lambda e, ps=ps, k=k, h=h, t0=t0, T=T: e.matmul(
                    ps[:, 0:T], wq[:, k, h * 192:h * 192 + 128], qn[:, k, t0:t0 + T], start=(k == 0), stop=(k == 3)),
                    R=[r_w, r_qn[i]], W=[rps[b]])
            P.op('act', lambda e, ps=ps, b=b, T=T: e.copy(stb[b][:, 0:T], ps[:, 0:T]), R=[rps[b]], W=[r_stb[b]])
            P.dma('act', self.qT[h, 0:128, t0:t0 + T], stb[b][:, 0:T], R=[r_stb[b]])
            q = i % 2
            b2 = n % 4
            n += 1
            ps2 = M.bank(b2)
            for k in range(4):
                P.op('pe', lambda e, ps2=ps2, k=k, h=h, t0=t0, T=T: e.matmul(
                    ps2[0:64, 0:T], wq[:, k, h * 192 + 128:h * 192 + 192], qn[:, k, t0:t0 + T], start=(k == 0), stop=(k == 3)),
                    R=[r_w, r_qn[i]], W=[rps[b2]])
            P.op('act', lambda e, ps2=ps2, q=q, T=T: e.copy(xr[q][0:64, 0:T], ps2[0:64, 0:T]), R=[rps[b2]], W=[r_xr[q]])
            rope_out(xr[q], r_xr[q], t0, T, self.qT[h, 128:192, t0:t0 + T], True, q)
    for h in range(8):
        for i, (t0, T, c) in enumerate(blocks):
            b = n % 4
            n += 1
            ps = M.bank(b)
            for k in range(2):
                P.op('pe', lambda e, ps=ps, k=k, h=h, t0=t0, T=T: e.matmul(
                    ps[:, 0:T], wkv[:, k, h * 128:(h + 1) * 128], ckv[:, k, t0:t0 + T], start=(k == 0), stop=(k == 1)),
                    R=[r_w, r_ckv[i]], W=[rps[b]])
            P.op('act', lambda e, ps=ps, b=b, T=T: e.copy(stb[b][:, 0:T], ps[:, 0:T]), R=[rps[b]], W=[r_stb[b]])
            P.dma('act', self.kT[h, :, t0:t0 + T], stb[b][:, 0:T], R=[r_stb[b]])
    for tt in range(NT // 128):
        i = min(tt // 4, 4)
        for g in range(2):
            b = n % 4
            n += 1
            ps = M.bank(b)
            for k in range(2):
                P.op('pe', lambda e, ps=ps, k=k, g=g, tt=tt: e.matmul(
                    ps, ckv[:, k, tt * 128:(tt + 1) * 128], wkv[:, k, 1024 + g * 512:1024 + (g + 1) * 512], start=(k == 0), stop=(k == 1)),
                    R=[r_w, r_ckv[i]], W=[rps[b]])
            P.op('dve', lambda e, ps=ps, b=b: e.tensor_copy(stb[b], ps), R=[rps[b]], W=[r_stb[b]])
            P.dma('act', self.Vtok[tt * 128:(tt + 1) * 128, g * 512:(g + 1) * 512], stb[b], R=[r_stb[b]])
    for i, (t0, T, c) in enumerate(blocks):
        q = i % 2
        P.dma('sp', xr[q][0:64, 0:T], z1[768:832, t0:t0 + T], W=[r_xr[q]])
        rope_out(xr[q], r_xr[q], t0, T, self.krT[:, t0:t0 + T], c == 0, q)
    P.barrier()


def _phase_attn(self):
    P, M = self.P, self.M
    M.reset()
    SC = 192.0 ** -0.5
    V = M.tile([18, 1024], BF16)
    kra = M.tile([NT], BF16)
    colsel = M.tile([128], BF16)
    r_v, r_kra, r_cs = Res(), Res(), Res()
    for tt in range(18):
        P.dma('sp', V[:, tt, :], self.Vtok[tt * 128:(tt + 1) * 128, :], W=[r_v])
    P.op('dve', lambda e: e.memset(kra[64:128, :], 0.0), W=[r_kra])
    P.op('dve', lambda e: e.memset(kra[64:65, :], 1.0), W=[r_kra])
    P.dma('sp', kra[0:64, :], self.krT, W=[r_kra])
    P.op('dve', lambda e: e.memset(colsel, 0.0), W=[r_cs])
    P.op('dve', lambda e: e.memset(colsel[:, 64:65], 1.0), W=[r_cs])
    qn = [M.tile([S], BF16) for _ in range(2)]
    qrz = [M.tile([S], BF16) for _ in range(2)]
    kn = [M.tile([NT], BF16) for _ in range(2)]
    r_hd = [Res(), Res()]
    for i in range(2):
        P.op('dve', lambda e, i=i: e.memset(qrz[i][64:128, :], 0.0), W=[r_hd[i]])
    NB = 3
    qra = [M.tile([512], BF16) for _ in range(NB)]
    r_qra = [Res() for _ in range(NB)]
    mx = [M.tile([8], F32) for _ in range(4)]
    r_mx = [Res() for _ in range(4)]
    dg = [M.tile([128], BF16) for _ in range(2)]
    r_dg = [Res(), Res()]
    PTt = [M.tile([512], BF16) for _ in range(4)]
    r_PT = [Res() for _ in range(4)]
    rs = [M.tile([512], F32) for _ in range(2)]
    r_rs = [Res(), Res()]
    ob = [M.tile([512], BF16) for _ in range(2)]
    r_ob = [Res(), Res()]
    rps = [Res() for _ in range(8)]
    kblocks = token_blocks()
    cnt = {'sc': 0, 'mx': 0, 'dg': 0, 'p2': 0, 'pt': 0}
    if getattr(self, 'attn_hook', None):
        self.attn_hook()

    def load_head(h):
        i = h % 2
        P.dma('sp', qn[i], self.qT[h, 0:128, :], W=[r_hd[i]])
        P.dma('sp', qrz[i][0:64, :], self.qT[h, 128:192, :], W=[r_hd[i]])
        P.dma('sp', kn[i], self.kT[h], W=[r_hd[i]])

    def pass1(h, qb, bi):
        i = h % 2
        u = bi % NB
        P.dma('sp', qra[u][0:64, :], self.qT[h, 128:192, qb * 512:(qb + 1) * 512], W=[r_qra[u]])
        for qt in range(4):
            qs = slice(qb * 512 + qt * 128, qb * 512 + (qt + 1) * 128)
            m_ = cnt['mx'] % 4
            cnt['mx'] += 1
            for j, (k0, T, c) in enumerate(kblocks):
                b = cnt['sc'] % 3
                cnt['sc'] += 1
                ps = M.bank(b)
                P.op('pe', lambda e, ps=ps, i=i, qs=qs, k0=k0, T=T: e.matmul(ps[:, 0:T], qn[i][:, qs], kn[i][:, k0:k0 + T], start=True, stop=False),
                     R=[r_hd[i]], W=[rps[b]])
                P.op('pe', lambda e, ps=ps, i=i, qs=qs, k0=k0, T=T: e.matmul(ps[:, 0:T], qrz[i][:, qs], kra[:, k0:k0 + T], start=False, stop=True),
                     R=[r_hd[i], r_kra], W=[rps[b]])
                P.op('dve', lambda e, ps=ps, m_=m_, j=j, T=T: e.tensor_reduce(mx[m_][:, j:j + 1], ps[:, 0:T], AX.X, ALU.max),
                     R=[rps[b]], W=[r_mx[m_]])
                yield
            P.op('dve', lambda e, m_=m_: e.tensor_reduce(mx[m_][:, 5:6], mx[m_][:, 0:5], AX.X, ALU.max, negate=True),
                 R=[r_mx[m_]], W=[r_mx[m_]])
            g = cnt['dg'] % 2
            cnt['dg'] += 1
            P.op('dve', lambda e, g=g, m_=m_: e.tensor_scalar(dg[g], self.idbf, mx[m_][:, 5:6], None, ALU.mult),
                 R=[r_mx[m_], self.r_idbf], W=[r_dg[g]])
            P.op('pe', lambda e, g=g, qt=qt: e.matmul(M.bank(3)[:, qt * 128:(qt + 1) * 128], colsel, dg[g], start=True, stop=True),
                 R=[r_cs, r_dg[g]], W=[rps[3]])
        P.op('act', lambda e, u=u: e.copy(qra[u][64:128, :], M.bank(3)[64:128, :]), R=[rps[3]], W=[r_qra[u]])

    def pass2(h, qb, bi):
        i = h % 2
        u = bi % NB
        qcols = slice(qb * 512, (qb + 1) * 512)
        def emit_s(kt):
            b = 4 + kt % 2
            ps = M.bank(b)
            ks = slice(kt * 128, (kt + 1) * 128)
            P.op('pe', lambda e, ps=ps, i=i, ks=ks, qcols=qcols: e.matmul(ps, kn[i][:, ks], qn[i][:, qcols], start=True, stop=False),
                 R=[r_hd[i]], W=[rps[b]])
            P.op('pe', lambda e, ps=ps, ks=ks, u=u: e.matmul(ps, kra[:, ks], qra[u], start=False, stop=True),
                 R=[r_kra, r_qra[u]], W=[rps[b]])

        emit_s(0)
        for kt in range(18):
            if kt + 1 < 18:
                emit_s(kt + 1)
            b = 4 + kt % 2
            ps = M.bank(b)
            t = cnt['pt'] % 4
            cnt['pt'] += 1
            P.op('act', lambda e, ps=ps, t=t: e.activation(PTt[t], ps, AF.Exp, scale=SC), R=[rps[b]], W=[r_PT[t]])
            P.op('pe', lambda e, t=t, kt=kt: e.matmul(M.bank(6), self.ones_bf, PTt[t], start=(kt == 0), stop=(kt == 17)),
                 R=[r_PT[t], self.r_ones], W=[rps[6]])
            P.op('pe', lambda e, t=t, kt=kt, h=h: e.matmul(M.bank(7), V[:, kt, h * 128:(h + 1) * 128], PTt[t], start=(kt == 0), stop=(kt == 17)),
                 R=[r_PT[t], r_v], W=[rps[7]])
            yield
        o = bi % 2
        P.op('dve', lambda e, o=o: e.reciprocal(rs[o], M.bank(6)), R=[rps[6]], W=[r_rs[o]])
        P.op('dve', lambda e, o=o: e.tensor_tensor(ob[o], M.bank(7), rs[o], ALU.mult), R=[rps[7], r_rs[o]], W=[r_ob[o]])
        P.dma('sp', self.oT[h * 128:(h + 1) * 128, qcols], ob[o], R=[r_ob[o]])

    blocks = [(h, qb) for h in range(8) for qb in range(4)]
    def run_both(g2, g1):
        a, b_ = True, True
        while a or b_:
            if a:
                try:
                    next(g2)
                except StopIteration:
                    a = False
            if b_:
                try:
                    next(g1)
                except StopIteration:
                    b_ = False

    load_head(0)
    for _ in pass1(0, 0, 0):
        pass
    for bi, (h, qb) in enumerate(blocks):
        g2 = pass2(h, qb, bi)
        if bi + 1 < len(blocks):
            h1, qb1 = blocks[bi + 1]
            if qb1 == 0:
                pass
            g1 = pass1(h1, qb1, bi + 1)
        else:
            g1 = iter(())
        run_both(g2, g1)
        if qb == 0 and h + 1 < 8:
            load_head(h + 1)
    P.barrier()


Builder.rms_feat = _rms_feat
Builder.phase_mla_prep = _phase_mla_prep
Builder.phase_attn = _phase_attn


def mla_consts():
    f32 = np.float32
    n = S
    row = np.repeat(np.arange(n // 64, dtype=f32), 64)
    col = np.tile(np.arange(64, dtype=f32), n // 64)
    inv = (f32(10000.0) ** (-np.arange(16, dtype=f32) / f32(16))).astype(f32)
    ang = np.concatenate([row[:, None] * inv[None, :], col[:, None] * inv[None, :]], axis=-1).astype(f32)
    cos2 = np.repeat(np.cos(ang).astype(f32), 2, axis=1).T
    sin2 = np.repeat(np.sin(ang).astype(f32), 2, axis=1).T
    rotm = np.zeros((64, 64), f32)
    for i in range(32):
        rotm[2 * i + 1, 2 * i] = -1.0
        rotm[2 * i, 2 * i + 1] = 1.0
    return {'cos2': np.ascontiguousarray(cos2), 'sin2': np.ascontiguousarray(sin2), 'rotm': rotm}

def make_inputs(inputs, b):
    x = np.asarray(inputs['x'][b], np.float32)
    ctx = np.asarray(inputs['ctx'][b], np.float32)
    m = {}
    m['xT'] = np.ascontiguousarray(np.concatenate([x.T, ctx.T], axis=1))
    m['cvec'] = np.ascontiguousarray(np.stack([inputs['c'][b], inputs['c_ctx']]).astype(np.float32))
    m['ada_w'] = np.ascontiguousarray(inputs['ada_w'], np.float32)
    m['ada_b'] = np.ascontiguousarray(inputs['ada_b'], np.float32)
    m['norm1_g'] = np.ascontiguousarray(inputs['norm1_g'], np.float32)
    m['norm2_g'] = np.ascontiguousarray(inputs['norm2_g'], np.float32)
    m['ev_w_in'] = np.ascontiguousarray(inputs['ev_w_in'][0], np.float32)
    m['ident'] = np.eye(128, dtype=np.float32)
    f = lambda k: np.ascontiguousarray(inputs[k][0], np.float32)
    m['rg_conv_w'] = f('ev_rg_conv_w'); m['rg_conv_b'] = f('ev_rg_conv_b')
    m['rg_wa'] = f('ev_rg_wa'); m['rg_ba'] = f('ev_rg_ba'); m['rg_wx'] = f('ev_rg_wx'); m['rg_bx'] = f('ev_rg_bx')
    m['rg_lambda'] = f('ev_rg_lambda')
    m['hy_conv_w'] = f('ev_hy_conv_w'); m['hy_conv_b'] = f('ev_hy_conv_b')
    m['f_w1'] = f('ev_hy_f_w1'); m['f_b1'] = f('ev_hy_f_b1'); m['f_freq'] = f('ev_hy_f_freq')
    m['f_w2'] = f('ev_hy_f_w2'); m['f_b2'] = f('ev_hy_f_b2'); m['f_w3'] = f('ev_hy_f_w3')
    m['hy_bias'] = f('ev_hy_bias')
    m.update(HC())
    m['ev_w_out'] = f('ev_w_out')
    wi = np.zeros((D, 896), np.float32)
    wi[:, :832] = inputs['od_w_in'][0]
    m['od_w_in'] = wi
    m['od_q_norm_g'] = f('od_q_norm_g'); m['od_kv_norm_g'] = f('od_kv_norm_g')
    m['od_w_uq'] = f('od_w_uq'); m['od_w_o'] = f('od_w_o')
    wkv = np.asarray(inputs['od_w_ukv'][0], np.float32).reshape(256, 8, 256)
    m['od_w_ukv'] = np.ascontiguousarray(np.concatenate([wkv[:, :, :128].reshape(256, 1024), wkv[:, :, 128:].reshape(256, 1024)], axis=1))
    m.update(MC())
    for k in ('ffn_w_up', 'ffn_conv_w', 'ffn_conv_b', 'ffn_w_down', 'final_g'):
        m[k] = np.ascontiguousarray(inputs[k], np.float32)
    cwb = np.concatenate([np.asarray(inputs['ffn_conv_w'], np.float32), np.asarray(inputs['ffn_conv_b'], np.float32)[:, None, :]], axis=1)
    m['ffn_cw'] = np.ascontiguousarray(cwb.reshape(2, 4, 44, 128).transpose(0, 3, 2, 1))
    return m


_HC = {}


def HC():
    if not _HC:
        _HC.update(hyena_consts())
    return _HC


_MC = {}


def MC():
    if not _MC:
        _MC.update(mla_consts())
    return _MC


def kernel(**inputs):
    inputs = {k: np.asarray(v) for k, v in inputs.items()}
    nb = inputs['x'].shape[0]
    B = Builder(dbg=True)
    nc = B.build()
    shared = make_inputs(inputs, 0)
    in_maps = []
    for b in range(nb):
        m = dict(shared)
        x = np.asarray(inputs['x'][b], np.float32)
        ctx = np.asarray(inputs['ctx'][b], np.float32)
        m['xT'] = np.ascontiguousarray(np.concatenate([x.T, ctx.T], axis=1))
        m['cvec'] = np.ascontiguousarray(np.stack([inputs['c'][b], inputs['c_ctx']]).astype(np.float32))
        in_maps.append({k: v for k, v in m.items() if k in B.din})
    res = run_bass_kernel_spmd(nc, in_maps, core_ids=list(range(nb)))
    out = np.stack([np.asarray(res.results[b]['outT'], np.float32).T for b in range(nb)], axis=0)
    return np.ascontiguousarray(out)
```

```python
import contextlib
import math
import numpy as np
import ml_dtypes
import concourse.bass as bass
import concourse.mybir as mybir
from concourse.bass_utils import run_bass_kernel_spmd

dt = mybir.dt
F32 = dt.float32
BF16 = dt.bfloat16
AF = mybir.ActivationFunctionType
ALU = mybir.AluOpType
AX = mybir.AxisListType

D = 1024
S = 2048
CT = 256
NT = S + CT
EPS = 1e-6
ENG = ['pe', 'dve', 'act', 'pool', 'sp']
SAME_SYNC = True


class Res:
    __slots__ = ('wl', 'r')

    def __init__(self):
        self.wl = []
        self.r = {}


class Prog:
    def __init__(self, nc, stack, n_dma=40):
        self.nc = nc
        self.ops = {e: [] for e in ENG}
        self.seq = {e: 0 for e in ENG}
        self.known = {e: {} for e in ENG}
        self.esem = {e: stack.enter_context(nc.semaphore('s_' + e)) for e in ENG}
        self.dsem = [stack.enter_context(nc.semaphore('d%d' % i)) for i in range(n_dma)]
        self.dtgt = [0] * n_dma
        self.drr = 0
        self.n_fg = n_dma - 6
        self.bgrr = 0

    def _sem(self, key):
        return self.esem[key[1]] if key[0] == 'e' else self.dsem[key[1]]

    @staticmethod
    def _joinable(w):
        return bool(w.wl) and not w.r and all(ev[0][0] == 'd' for ev in w.wl)

    def _collect(self, eng, R, W, dma_write=False):
        need = {}

        def add(key, val):
            if val > need.get(key, 0):
                need[key] = val
        for r in R:
            for ev in r.wl:
                add(*ev)
        for w in W:
            if not (dma_write and self._joinable(w)):
                for ev in w.wl:
                    add(*ev)
            for k, v in w.r.items():
                add(k, v)
        waits = []
        kn = self.known[eng]
        for key, val in need.items():
            if key == ('e', eng) and (eng == 'pe' or not SAME_SYNC):
                continue
            if kn.get(key, 0) >= val:
                continue
            kn[key] = val
            waits.append((self._sem(key), val))
        return waits

    def _mark(self, ev, R, W, dma_write=False):
        for r in R:
            if ev[1] > r.r.get(ev[0], 0):
                r.r[ev[0]] = ev[1]
        for w in W:
            if dma_write and self._joinable(w):
                w.wl.append(ev)
            else:
                w.wl = [ev]
                w.r = {}

    def op(self, eng, fn, R=(), W=()):
        waits = self._collect(eng, R, W)
        self.seq[eng] += 1
        ev = (('e', eng), self.seq[eng])
        es = self.esem[eng]

        def emit(e):
            for s, v in waits:
                e.wait_ge(s, v)
            fn(e).then_inc(es, 1)
        self.ops[eng].append(emit)
        self._mark(ev, R, W)

    def dma(self, q, out, in_, R=(), W=(), bg=False, **kw):
        waits = self._collect(q, R, W, dma_write=True)
        if bg:
            si = self.n_fg + self.bgrr
            self.bgrr = (self.bgrr + 1) % (len(self.dsem) - self.n_fg)
        else:
            si = self.drr
            self.drr = (self.drr + 1) % self.n_fg
        prev = self.dtgt[si]
        key = ('d', si)
        if prev > 0 and self.known[q].get(key, 0) < prev:
            self.known[q][key] = prev
            waits.append((self.dsem[si], prev))
        tgt = prev + 16
        self.dtgt[si] = tgt
        ds = self.dsem[si]

        def emit(e):
            for s, v in waits:
                e.wait_ge(s, v)
            e.dma_start(out=out, in_=in_, **kw).then_inc(ds, 16)
        self.ops[q].append(emit)
        self._mark((key, tgt), R, W, dma_write=True)

    def barrier(self, final=False):
        for e in ENG:
            waits = []
            kn = self.known[e]
            for f in ENG:
                if f == e and e == 'pe':
                    continue
                v = self.seq[f]
                key = ('e', f)
                if v > kn.get(key, 0):
                    kn[key] = v
                    waits.append((self.esem[f], v))
            for si, t in enumerate(self.dtgt):
                if si >= self.n_fg and not final:
                    continue
                key = ('d', si)
                if t > kn.get(key, 0):
                    kn[key] = t
                    waits.append((self.dsem[si], t))

            def emit(eng, waits=waits):
                for s, v in waits:
                    eng.wait_ge(s, v)
            self.ops[e].append(emit)

    def play(self):
        with self.nc.Block() as blk:
            @blk.tensor
            def _(e):
                for f in self.ops['pe']:
                    f(e)

            @blk.vector
            def _(e):
                for f in self.ops['dve']:
                    f(e)

            @blk.scalar
            def _(e):
                for f in self.ops['act']:
                    f(e)

            @blk.gpsimd
            def _(e):
                for f in self.ops['pool']:
                    f(e)

            @blk.sync
            def _(e):
                for f in self.ops['sp']:
                    f(e)


class Mem:
    def __init__(self, nc, stack, words=49152):
        self.arena = stack.enter_context(nc.sbuf_tensor('arena', [128, words], F32))
        self.psum = stack.enter_context(nc.psum_tensor('psum', [128, 4096], F32))
        self.words = words
        self.ptr = 0
        self.mark = 0
        self.limit = words * 4
        self.v32 = self.arena
        self.v16 = self.arena.bitcast(BF16)
        self.p16 = self.psum.bitcast(BF16)

    def tile(self, shape, dtype=F32):
        shape = list(shape) if isinstance(shape, (list, tuple)) else [shape]
        n = int(np.prod(shape))
        esz = 4 if dtype == F32 else 2
        self.ptr = (self.ptr + 63) // 64 * 64
        off = self.ptr // esz
        self.ptr += n * esz
        assert self.ptr <= self.limit, 'SBUF arena overflow %d > %d' % (self.ptr, self.limit)
        base = self.v32 if dtype == F32 else self.v16
        ap = base[:, off:off + n]
        if len(shape) == 2:
            ap = ap.rearrange('p (a b) -> p a b', a=shape[0])
        elif len(shape) == 3:
            ap = ap.rearrange('p (a b c) -> p a b c', a=shape[0], b=shape[1])
        return ap

    def persist(self):
        self.mark = self.ptr

    def reserve_top(self, nbytes):
        self.limit = self.words * 4 - nbytes
        assert self.ptr <= self.limit
        off = self.limit // 2
        return self.v16[:, off:off + nbytes // 2]

    def release_top(self):
        self.limit = self.words * 4

    def reset(self):
        self.ptr = self.mark

    def bank(self, i, n=512, dtype=F32, off=0):
        if dtype == F32:
            return self.psum[:, i * 512 + off:i * 512 + off + n]
        return self.p16[:, i * 1024 + off:i * 1024 + off + n]


def token_blocks():
    return [(0, 512, 0), (512, 512, 0), (1024, 512, 0), (1536, 512, 0), (2048, 256, 1)]


class Builder:
    def __init__(self, dbg=False, upto=99, mode=None, feed=()):
        self.dbg = dbg
        self.upto = upto
        self.mode = mode
        self.feed = set(feed)
        self.nc = bass.Bass('TRN2', target_bir_lowering=False)
        self.stack = contextlib.ExitStack()
        self.din = {}
        self.dout = {}

    def inp(self, name, shape, dtype=F32):
        t = self.nc.dram_tensor(name, list(shape), dtype, kind='ExternalInput').ap()
        self.din[name] = t
        return t

    def scratch(self, name, shape, dtype=F32, out=False):
        kind = 'ExternalOutput' if (out or self.dbg) else 'Internal'
        if name in self.feed:
            kind = 'ExternalInput'
        t = self.nc.dram_tensor(name, list(shape), dtype, kind=kind).ap()
        if kind == 'ExternalInput':
            self.din[name] = t
        if kind == 'ExternalOutput':
            self.dout[name] = t
        return t

    def build(self):
        nc = self.nc
        st = self.stack
        P = self.P = Prog(nc, st)
        M = self.M = Mem(nc, st)
        I = self.inp
        self.xT = I('xT', [D, NT])
        self.cvec = I('cvec', [2, D])
        self.ada_w = I('ada_w', [2, D, 6 * D])
        self.ada_b = I('ada_b', [2, 6 * D])
        self.norm1_g = I('norm1_g', [2, D])
        self.norm2_g = I('norm2_g', [2, D])
        self.ev_w_in = I('ev_w_in', [D, 2560])
        self.ident = I('ident', [128, 128])
        self.rg_conv_w = I('rg_conv_w', [4, 512])
        self.rg_conv_b = I('rg_conv_b', [512])
        self.rg_wa = I('rg_wa', [2, 8, 64, 64])
        self.rg_ba = I('rg_ba', [2, 512])
        self.rg_wx = I('rg_wx', [2, 8, 64, 64])
        self.rg_bx = I('rg_bx', [2, 512])
        self.rg_lambda = I('rg_lambda', [2, 512])
        self.hy_conv_w = I('hy_conv_w', [3, 1536])
        self.hy_conv_b = I('hy_conv_b', [1536])
        self.f_w1 = I('f_w1', [33, 64]); self.f_b1 = I('f_b1', [64]); self.f_freq = I('f_freq', [64])
        self.f_w2 = I('f_w2', [64, 64]); self.f_b2 = I('f_b2', [64]); self.f_w3 = I('f_w3', [64, 2048])
        self.hy_bias = I('hy_bias', [2, 512])
        self.ev_w_out = I('ev_w_out', [D, D])
        self.ffn_w_up = I('ffn_w_up', [2, D, 5632])
        self.ffn_conv_w = I('ffn_conv_w', [2, 3, 5632])
        self.ffn_conv_b = I('ffn_conv_b', [2, 5632])
        self.ffn_cw = I('ffn_cw', [2, 128, 44, 4])
        self.ffn_w_down = I('ffn_w_down', [2, 2816, D])
        self.final_g = I('final_g', [D])
        self.xa0 = self.scratch('xa0', [D, NT])
        self.xb0 = self.scratch('xb0', [D, NT])
        self.od_w_in = I('od_w_in', [D, 896])
        self.od_q_norm_g = I('od_q_norm_g', [512]); self.od_kv_norm_g = I('od_kv_norm_g', [256])
        self.od_w_uq = I('od_w_uq', [512, 1536]); self.od_w_ukv = I('od_w_ukv', [256, 2048])
        self.od_w_o = I('od_w_o', [D, D])
        self.cos2 = I('cos2', [64, S]); self.sin2 = I('sin2', [64, S]); self.rotm = I('rotm', [64, 64])
        self.z1 = self.scratch('z1', [896, NT])
        self.qT = self.scratch('qT', [8, 192, S], BF16)
        self.kT = self.scratch('kT', [8, 128, NT], BF16)
        self.krT = self.scratch('krT', [64, NT], BF16)
        self.Vtok = self.scratch('Vtok', [NT, D], BF16)
        self.oT = self.scratch('oT', [D, S], BF16)
        self.xa1 = self.scratch('xa1', [D, S])
        self.outT = self.scratch('outT', [D, S], out=True)
        self.hc = {}
        for n in (2048, 256):
            nj = n // 128
            self.hc[n] = dict(zT=I('zT%d' % n, [33, n]), dec=I('dec%d' % n, [n, 512]), decb=I('decb%d' % n, [n, 512]),
                              C=I('Ccol%d' % n, [nj, 128, nj, 128], BF16), S=I('Scol%d' % n, [nj, 128, nj, 128], BF16),
                              wk=I('wk%d' % n, [128, nj]), Kf=self.scratch('Kf%d' % n, [2, 2, n, 512]))
        self.vtok = self.scratch('vtok', [3, NT, 512])
        self.u2tok = self.scratch('u2tok', [NT, 512])
        self.z0 = self.scratch('z0', [2560, NT])
        self.mixT = self.scratch('mixT', [D, NT], BF16)
        self.consts()
        if self.mode == 'attn':
            self.phase_attn()
            P.barrier()
            P.play()
            return nc
        if self.upto >= 1:
            for l in range(2):
                self.phase_mod(l)
        if self.upto >= 2:
            self.phase_norm_proj(0, self.xT, 1, self.ev_w_in, 2560, self.z0)
        if self.upto >= 3:
            self.phase_rg()
        if self.upto >= 4:
            for n in (2048, 256):
                h = self.hc[n]
                self.phase_hy_filters(n, h['zT'], h['dec'], h['decb'], h['C'], h['S'], h['wk'], h['Kf'])
            self.phase_hy_prep()
            for n, tok0 in ((2048, 0), (256, S)):
                h = self.hc[n]
                self.phase_hy_conv(n, tok0, h['C'], h['S'], h['Kf'])
        if self.upto >= 5:
            self.phase_out_proj(0, self.mixT, self.ev_w_out, self.xT, self.xa0, token_blocks(), prefetch=0)
        if self.upto >= 6:
            self.phase_ffn(0, self.xa0, self.xb0, [(0, S, 0), (S, NT, 1)])
        lat = token_blocks()[:4]
        if self.upto >= 7:
            self.phase_norm_proj(1, self.xb0, 1, self.od_w_in, 896, self.z1)
            self.phase_mla_prep()
        if self.upto >= 8:
            self.attn_hook = lambda: self.prefetch_wup(1)
            self.phase_attn()
            self.phase_out_proj(1, self.oT, self.od_w_o, self.xb0, self.xa1, lat)
        if self.upto >= 9:
            self.phase_ffn(1, self.xa1, None, [(0, S, 0)], final_out=self.outT)
        P.barrier(final=True)
        P.play()
        return nc

    def consts(self):
        P, M = self.P, self.M
        self.ones_bf = M.tile([128], BF16)
        self.r_ones = Res()
        P.op('dve', lambda e: e.memset(self.ones_bf, 1.0), W=[self.r_ones])
        self.id32 = M.tile([128], F32)
        self.idbf = M.tile([128], BF16)
        self.r_id = Res()
        P.dma('sp', self.id32, self.ident, W=[self.r_id])
        self.r_idbf = Res()
        P.op('dve', lambda e: e.tensor_copy(self.idbf, self.id32), R=[self.r_id], W=[self.r_idbf])
        self.mod = [M.tile([48, 2], F32) for _ in range(2)]
        self.A1 = [M.tile([8, 2], F32) for _ in range(2)]
        self.A2 = [M.tile([8, 2], F32) for _ in range(2)]
        self.r_mod = [Res() for _ in range(2)]
        M.persist()

    def phase_mod(self, l):
        P, M = self.P, self.M
        M.reset()
        cv = M.tile([8, 2], F32)
        sT = M.tile([8, 2], BF16)
        bT = M.tile([48], F32)
        g1 = M.tile([8], F32)
        g2 = M.tile([8], F32)
        r_cv, r_sT, r_b, r_g = Res(), Res(), Res(), Res()
        for j in range(2):
            P.dma('sp', cv[:, :, j], self.cvec[j].rearrange('(k p) -> p k', p=128), W=[r_cv],
                  allow_slow_non_contiguous=True)
        P.dma('sp', bT, self.ada_b[l].rearrange('(j p) -> p j', p=128), W=[r_b], allow_slow_non_contiguous=True)
        P.dma('sp', g1, self.norm1_g[l].rearrange('(k p) -> p k', p=128), W=[r_g], allow_slow_non_contiguous=True)
        P.dma('sp', g2, self.norm2_g[l].rearrange('(k p) -> p k', p=128), W=[r_g], allow_slow_non_contiguous=True)
        P.op('act', lambda e: e.activation(sT, cv, AF.Silu), R=[r_cv], W=[r_sT])
        wt = [M.tile([8, 2048], BF16) for _ in range(2)]
        r_wt = [Res(), Res()]
        ps = M.bank(0, 96).rearrange('p (j c) -> p j c', c=2)
        r_ps = Res()
        for sec in range(3):
            w = wt[sec % 2]
            rw = r_wt[sec % 2]
            for k in range(8):
                P.dma('pool', w[:, k, :], self.ada_w[l, k * 128:(k + 1) * 128, sec * 2048:(sec + 1) * 2048], W=[rw])
            for fc in range(16):
                j = sec * 16 + fc
                for k in range(8):
                    P.op('pe', lambda e, w=w, k=k, fc=fc, j=j: e.matmul(
                        ps[:, j, :], w[:, k, fc * 128:(fc + 1) * 128], sT[:, k, :],
                        start=(k == 0), stop=(k == 7)), R=[rw, r_sT], W=[r_ps])
        mod = self.mod[l]
        rm = self.r_mod[l]
        for c in range(2):
            P.op('dve', lambda e, c=c: e.tensor_tensor(mod[:, :, c], ps[:, :, c], bT, ALU.add),
                 R=[r_ps, r_b], W=[rm])
        for c in range(2):
            P.op('dve', lambda e, c=c: e.scalar_tensor_tensor(
                self.A1[l][:, :, c], mod[:, 8:16, c], 1.0, g1, ALU.add, ALU.mult), R=[rm, r_g], W=[rm])
            P.op('dve', lambda e, c=c: e.scalar_tensor_tensor(
                self.A2[l][:, :, c], mod[:, 32:40, c], 1.0, g2, ALU.add, ALU.mult), R=[rm, r_g], W=[rm])
        P.barrier()

    def norm_block(self, src, t0, T, c, A, B, hdst, r_hdst, bufs, i):
        P, M = self.P, self.M
        xb, sq, rstd, tmp, rx, rsq, rrs, rtmp, psb, rps = bufs[i % 2]
        P.dma('sp', xb[:, :, 0:T], src[:, t0:t0 + T].rearrange('(k p) t -> p k t', p=128), W=[rx])
        P.op('act', lambda e: e.activation(sq[:, :, 0:T], xb[:, :, 0:T], AF.Square), R=[rx], W=[rsq])
        for k in range(8):
            P.op('pe', lambda e, k=k: e.matmul(psb[:, 0:T], self.ones_bf, sq[:, k, 0:T],
                                               start=(k == 0), stop=(k == 7)),
                 R=[rsq, self.r_ones], W=[rps])
        P.op('act', lambda e: e.activation(rstd[:, 0:T], psb[:, 0:T], AF.Sqrt, bias=EPS, scale=1.0 / D),
             R=[rps], W=[rrs])
        P.op('dve', lambda e: e.reciprocal(rstd[:, 0:T], rstd[:, 0:T]), R=[rrs], W=[rrs])
        for k in range(8):
            P.op('dve', lambda e, k=k: e.scalar_tensor_tensor(
                tmp[:, k, 0:T], xb[:, k, 0:T], A[:, k, c:c + 1], rstd[:, 0:T], ALU.mult, ALU.mult),
                R=[rx, rrs, self.r_mod[0], self.r_mod[1]], W=[rtmp[k]])
            P.op('act', lambda e, k=k: e.activation(hdst[:, k, :], tmp[:, k, 0:T], AF.Identity,
                                                    bias=B[:, k, c:c + 1]),
                 R=[rtmp[k], self.r_mod[0], self.r_mod[1]], W=[r_hdst])

    def norm_bufs(self, psbanks):
        M = self.M
        bufs = []
        for i in range(2):
            bufs.append((M.tile([8, 512], F32), M.tile([8, 512], BF16), M.tile([512], F32),
                         M.tile([8, 512], F32), Res(), Res(), Res(), [Res() for _ in range(8)],
                         M.bank(psbanks[i]), Res()))
        return bufs

    def phase_norm_proj(self, l, src, which, w_dram, F, zdst, blocks=None):
        P, M = self.P, self.M
        M.reset()
        blocks = blocks or token_blocks()
        A = self.A1[l] if which == 1 else self.A2[l]
        B = self.mod[l][:, 0:8, :] if which == 1 else self.mod[l][:, 24:32, :]
        hT = M.tile([8, NT], BF16)
        r_h = [Res() for _ in blocks]
        nfc = F // 128
        wt = M.tile([8, F], BF16)
        r_w = [Res() for _ in range(nfc)]
        for g in range(0, F, 2048):
            gw = min(2048, F - g)
            for k in range(8):
                P.dma('pool', wt[:, k, g:g + gw], w_dram[k * 128:(k + 1) * 128, g:g + gw],
                      W=[r_w[j] for j in range(g // 128, (g + gw) // 128)])
        bufs = self.norm_bufs([6, 7])
        for i, (t0, T, c) in enumerate(blocks):
            self.norm_block(src, t0, T, c, A, B, hT[:, :, t0:t0 + T], r_h[i], bufs, i)
        stg = [M.tile([512], F32) for _ in range(4)]
        r_stg = [Res() for _ in range(4)]
        r_ps = [Res() for _ in range(4)]
        n = 0
        for fc in range(nfc):
            for i, (t0, T, c) in enumerate(blocks):
                b = n % 4
                ps = M.bank(b)
                for k in range(8):
                    P.op('pe', lambda e, k=k, fc=fc, t0=t0, T=T, ps=ps: e.matmul(
                        ps[:, 0:T], wt[:, k, fc * 128:(fc + 1) * 128], hT[:, k, t0:t0 + T],
                        start=(k == 0), stop=(k == 7)), R=[r_w[fc], r_h[i]], W=[r_ps[b]])
                ev = 'act' if n % 2 == 0 else 'dve'
                if ev == 'act':
                    P.op('act', lambda e, b=b, T=T, ps=ps: e.copy(stg[b][:, 0:T], ps[:, 0:T]),
                         R=[r_ps[b]], W=[r_stg[b]])
                else:
                    P.op('dve', lambda e, b=b, T=T, ps=ps: e.tensor_copy(stg[b][:, 0:T], ps[:, 0:T]),
                         R=[r_ps[b]], W=[r_stg[b]])
                P.dma('act', zdst[fc * 128:(fc + 1) * 128, t0:t0 + T], stg[b][:, 0:T], R=[r_stg[b]])
                n += 1
        P.barrier()


def rev(ap, n):
    pat = [list(p) for p in ap.ap]
    assert len(pat) == 2 and pat[1][1] == n
    return bass.AP(ap.tensor, ap.offset + (n - 1) * pat[1][0], [pat[0], [-pat[1][0], n]])


def _phase_rg(self):
    P, M = self.P, self.M
    M.reset()
    f = lambda: M.tile([NT], F32)
    zx, gate, xc, hf, hb, ga = [f() for _ in range(6)]
    rgs, igs, aas, bbs = [[f(), f()] for _ in range(4)]
    xcb = M.tile([NT], BF16)
    ob = M.tile([NT], BF16)
    cw = M.tile([4], F32)
    cb = M.tile([1], F32)
    bias4 = M.tile([4], F32)
    lam = M.tile([2], F32)
    cl = M.tile([4], F32)
    wbd = [M.tile([128], BF16) for _ in range(4)]
    segs = [(0, S), (S, CT)]
    blocks = token_blocks()
    for cc in range(4):
        R = {k: Res() for k in ['zx', 'gate', 'xc', 'xcb', 'r0', 'i0', 'a0', 'b0', 'r1', 'i1', 'a1', 'b1', 'hf', 'hb', 'small', 'cl', 'ob', 'ga']}
        rw = [Res() for _ in range(4)]
        rps = [Res() for _ in range(4)]
        fs = slice(cc * 128, (cc + 1) * 128)
        P.dma('sp', zx, self.z0[cc * 128:(cc + 1) * 128, :], W=[R['zx']])
        P.dma('sp', gate, self.z0[512 + cc * 128:512 + (cc + 1) * 128, :], W=[R['gate']])
        P.dma('sp', cw, self.rg_conv_w[:, fs].rearrange('j p -> p j'), W=[R['small']], allow_slow_non_contiguous=True)
        P.dma('sp', cb, self.rg_conv_b[fs].rearrange('(p o) -> p o', o=1), W=[R['small']])
        for d in range(2):
            P.dma('sp', bias4[:, 2 * d:2 * d + 1], self.rg_ba[d, fs].rearrange('(p o) -> p o', o=1), W=[R['small']])
            P.dma('sp', bias4[:, 2 * d + 1:2 * d + 2], self.rg_bx[d, fs].rearrange('(p o) -> p o', o=1), W=[R['small']])
            P.dma('sp', lam[:, d:d + 1], self.rg_lambda[d, fs].rearrange('(p o) -> p o', o=1), W=[R['small']])
        for d in range(2):
            for ty, wsrc in enumerate([self.rg_wa, self.rg_wx]):
                w = wbd[2 * d + ty]
                r_ = rw[2 * d + ty]
                P.op('dve', lambda e, w=w: e.memset(w, 0.0), W=[r_])
                for hh in range(2):
                    P.dma('pool', w[hh * 64:(hh + 1) * 64, hh * 64:(hh + 1) * 64], wsrc[d, 2 * cc + hh], W=[r_])
        P.op('act', lambda e: e.activation(cl[:, 0:2], lam, AF.Exp, scale=-1.0), R=[R['small']], W=[R['cl']])
        P.op('act', lambda e: e.activation(cl[:, 0:2], cl[:, 0:2], AF.Ln, bias=1.0), R=[R['cl']], W=[R['cl']])
        P.op('dve', lambda e: e.tensor_scalar(cl[:, 2:4], cl[:, 0:2], -16.0, None, ALU.mult), R=[R['cl']], W=[R['cl']])
        P.op('dve', lambda e: e.tensor_scalar(cl[:, 0:2], cl[:, 0:2], -8.0, None, ALU.mult), R=[R['cl']], W=[R['cl']])
        P.op('act', lambda e: e.activation(xc, zx, AF.Identity, bias=cb[:, 0:1], scale=cw[:, 2:3]),
             R=[R['zx'], R['small']], W=[R['xc']])
        for j in (0, 1, 3):
            off = j - 2
            for (s0, n) in segs:
                lo = max(0, -off)
                hi = n - max(0, off)
                P.op('dve', lambda e, j=j, off=off, s0=s0, lo=lo, hi=hi: e.scalar_tensor_tensor(
                    xc[:, s0 + lo:s0 + hi], zx[:, s0 + lo + off:s0 + hi + off], cw[:, j:j + 1],
                    xc[:, s0 + lo:s0 + hi], ALU.mult, ALU.add), R=[R['zx'], R['small'], R['xc']], W=[R['xc']])
        P.op('act', lambda e: e.copy(xcb, xc), R=[R['xc']], W=[R['xcb']])
        P.op('act', lambda e: e.activation(ga, gate, AF.Square), R=[R['gate']], W=[R['ga']])
        P.op('pool', lambda e: e.tensor_scalar(ga, ga, 0.044715, 1.0, ALU.mult, ALU.add), R=[R['ga']], W=[R['ga']])
        P.op('pool', lambda e: e.tensor_tensor(ga, ga, gate, ALU.mult), R=[R['ga'], R['gate']], W=[R['ga']])
        P.op('act', lambda e: e.activation(ga, ga, AF.Sigmoid, scale=1.5957691216), R=[R['ga']], W=[R['ga']])
        P.op('pool', lambda e: e.tensor_tensor(ga, ga, gate, ALU.mult), R=[R['ga'], R['gate']], W=[R['ga']])
        for d in range(2):
            rg, ig, aa, bb = rgs[d], igs[d], aas[d], bbs[d]
            kr_, ki_, ka_, kb_ = 'r%d' % d, 'i%d' % d, 'a%d' % d, 'b%d' % d
            for ty, dst, rk in ((0, rg, kr_), (1, ig, ki_)):
                for bi, (t0, T, c) in enumerate(blocks):
                    bk = (bi + ty) % 4
                    ps = M.bank(bk)
                    P.op('pe', lambda e, ps=ps, d=d, ty=ty, t0=t0, T=T: e.matmul(
                        ps[:, 0:T], wbd[2 * d + ty], xcb[:, t0:t0 + T], start=True, stop=True),
                        R=[rw[2 * d + ty], R['xcb']], W=[rps[bk]])
                    P.op('act', lambda e, ps=ps, dst=dst, d=d, ty=ty, t0=t0, T=T: e.activation(
                        dst[:, t0:t0 + T], ps[:, 0:T], AF.Sigmoid, bias=bias4[:, 2 * d + ty:2 * d + ty + 1]),
                        R=[rps[bk], R['small']], W=[R[rk]])
            P.op('act', lambda e, d=d, aa=aa, rg=rg: e.activation(aa, rg, AF.Exp, scale=cl[:, d:d + 1]), R=[R[kr_], R['cl']], W=[R[ka_]])
            P.op('act', lambda e, d=d, bb=bb, rg=rg: e.activation(bb, rg, AF.Exp, scale=cl[:, 2 + d:3 + d]), R=[R[kr_], R['cl']], W=[R[kb_]])
            P.op('act', lambda e, bb=bb: e.activation(bb, bb, AF.Sqrt, bias=1.0, scale=-1.0), R=[R[kb_]], W=[R[kb_]])
            P.op('pool', lambda e, bb=bb, ig=ig: e.tensor_tensor(bb, bb, ig, ALU.mult), R=[R[kb_], R[ki_]], W=[R[kb_]])
            P.op('pool', lambda e, bb=bb: e.tensor_tensor(bb, bb, xc, ALU.mult), R=[R[kb_], R['xc']], W=[R[kb_]])
            h = hf if d == 0 else hb
            hk = 'hf' if d == 0 else 'hb'
            if d == 0:
                P.op('dve', lambda e, h=h, aa=aa, bb=bb: e.tensor_tensor_scan(h[:, S:NT], aa[:, S:NT], bb[:, S:NT], 0.0, ALU.mult, ALU.add),
                     R=[R[ka_], R[kb_]], W=[R[hk]])
                P.op('dve', lambda e, h=h, aa=aa, bb=bb: e.tensor_tensor_scan(h[:, 0:S], aa[:, 0:S], bb[:, 0:S], h[:, NT - 1:NT], ALU.mult, ALU.add),
                     R=[R[ka_], R[kb_], R[hk]], W=[R[hk]])
            else:
                P.op('dve', lambda e, h=h, aa=aa, bb=bb: e.tensor_tensor_scan(rev(h[:, S:NT], CT), rev(aa[:, S:NT], CT), rev(bb[:, S:NT], CT),
                                                                               0.0, ALU.mult, ALU.add), R=[R[ka_], R[kb_]], W=[R[hk]])
                P.op('dve', lambda e, h=h, aa=aa, bb=bb: e.tensor_tensor_scan(rev(h[:, 0:S], S), rev(aa[:, 0:S], S), rev(bb[:, 0:S], S),
                                                                               h[:, S:S + 1], ALU.mult, ALU.add),
                     R=[R[ka_], R[kb_], R[hk]], W=[R[hk]])
        P.op('pool', lambda e: e.tensor_tensor(hf, hf, hb, ALU.add), R=[R['hf'], R['hb']], W=[R['hf']])
        P.op('dve', lambda e: e.tensor_tensor(ob, ga, hf, ALU.mult), R=[R['ga'], R['hf']], W=[R['ob']])
        P.dma('act', self.mixT[cc * 128:(cc + 1) * 128, :], ob, R=[R['ob']])
        P.barrier()


Builder.phase_rg = _phase_rg


def bcast_rows(ap_row, nparts=128):
    pat = [list(p) for p in ap_row.ap]
    return bass.AP(ap_row.tensor, ap_row.offset, [[0, nparts]] + pat)


def _phase_hy_filters(self, n, zT_d, dec_d, decb_d, Ccol, Scol, wk_d, Kf_d):
    P, M = self.P, self.M
    M.reset()
    nj = n // 128
    w1 = M.tile([64], F32)
    w2 = M.tile([64], F32)
    w3 = M.tile([2048], F32)
    sm = M.tile([3], F32)
    zT = M.tile([n], F32)
    h1 = M.tile([n], F32)
    h2 = M.tile([n], F32)
    tm = M.tile([512], F32)
    r_w, r_z, r_h1, r_h2, r_tm = Res(), Res(), Res(), Res(), Res()
    P.dma('sp', w1[0:33, :], self.f_w1, W=[r_w])
    P.dma('sp', w2[0:64, :], self.f_w2, W=[r_w])
    P.dma('sp', w3[0:64, :], self.f_w3, W=[r_w])
    P.dma('sp', sm[0:64, 0:1], self.f_b1.rearrange('(p o) -> p o', o=1), W=[r_w])
    P.dma('sp', sm[0:64, 1:2], self.f_freq.rearrange('(p o) -> p o', o=1), W=[r_w])
    P.dma('sp', sm[0:64, 2:3], self.f_b2.rearrange('(p o) -> p o', o=1), W=[r_w])
    P.dma('sp', zT[0:33, :], zT_d, W=[r_z])
    rps = [Res() for _ in range(8)]
    PI = math.pi

    def sin_layer(lhsT, kk, src, rsrc, bcol, dst, rdst):
        for bi, t0 in enumerate(range(0, n, 512)):
            T = min(512, n - t0)
            ps = M.bank(bi % 2)
            P.op('pe', lambda e, ps=ps, t0=t0, T=T: e.matmul(ps[0:64, 0:T], lhsT, src[0:kk, t0:t0 + T], start=True, stop=True),
                 R=[r_w, rsrc], W=[rps[bi % 2]])
            P.op('dve', lambda e, ps=ps, T=T: e.tensor_scalar(tm[0:64, 0:T], ps[0:64, 0:T], sm[0:64, bcol:bcol + 1],
                                                           sm[0:64, 1:2], ALU.add, ALU.mult), R=[rps[bi % 2], r_w, r_tm], W=[r_tm])
            for thr, op, mul in ((PI, ALU.is_gt, -2 * PI), (-PI, ALU.is_lt, 2 * PI)):
                P.op('dve', lambda e, T=T, t0=t0, thr=thr, op=op, mul=mul: e.tensor_scalar(
                    dst[0:64, t0:t0 + T], tm[0:64, 0:T], thr, mul, op, ALU.mult), R=[r_tm], W=[rdst])
                P.op('dve', lambda e, T=T, t0=t0: e.tensor_tensor(tm[0:64, 0:T], tm[0:64, 0:T], dst[0:64, t0:t0 + T], ALU.add),
                     R=[r_tm, rdst], W=[r_tm])
            P.op('act', lambda e, T=T, t0=t0: e.activation(dst[0:64, t0:t0 + T], tm[0:64, 0:T], AF.Sin), R=[r_tm], W=[rdst])

    sin_layer(w1[0:33, :], 33, zT, r_z, 0, h1, r_h1)
    sin_layer(w2[0:64, :], 64, h1, r_h1, 2, h2, r_h2)
    ks = M.tile([nj, 1024], BF16)
    kd = M.tile([nj, 1024], BF16)
    dec = [M.tile([512], F32) for _ in range(2)]
    decb = [M.tile([512], F32) for _ in range(2)]
    kf = [M.tile([512], F32) for _ in range(2)]
    kb = [M.tile([512], F32) for _ in range(2)]
    ab = [[M.tile([512], BF16) for _ in range(2)] for _ in range(2)]
    r_dec = [Res(), Res()]
    r_kf = [Res(), Res()]
    r_kb = [Res(), Res()]
    r_ab = [[Res(), Res()], [Res(), Res()]]
    r_ks = Res()
    r_nrm = Res()
    cnt = 0
    pending = []

    def flush():
        for fn in pending:
            fn()
        del pending[:]

    for j in range(nj):
        P.dma('sp', dec[j % 2], dec_d[j * 128:(j + 1) * 128, :], W=[r_dec[j % 2]])
        P.dma('sp', decb[j % 2], decb_d[j * 128:(j + 1) * 128, :], W=[r_dec[j % 2]])
        for o in range(2):
            q = cnt % 2
            cnt += 1
            todo = []
            for dr in range(2):
                bk = 2 + 2 * q + dr
                ps = M.bank(bk)
                c0 = o * 1024 + dr * 512
                P.op('pe', lambda e, ps=ps, j=j, c0=c0: e.matmul(ps, h2[0:64, j * 128:(j + 1) * 128], w3[0:64, c0:c0 + 512],
                                                             start=True, stop=True), R=[r_h2, r_w], W=[rps[bk]])
                dst, rd, dc = (kf[q], r_kf[q], dec[j % 2]) if dr == 0 else (kb[q], r_kb[q], decb[j % 2])
                P.op('dve', lambda e, ps=ps, dst=dst, dc=dc: e.tensor_tensor(dst, ps, dc, ALU.mult),
                     R=[rps[bk], r_dec[j % 2]], W=[rd])
                P.op('act', lambda e, dst=dst, q=q, dr=dr: e.activation(ab[q][dr], dst, AF.Abs), R=[rd], W=[r_ab[q][dr]])
                first = (j == 0 and dr == 0)
                last = (j == nj - 1 and dr == 1)

                def nm(o=o, q=q, dr=dr, first=first, last=last):
                    P.op('pe', lambda e: e.matmul(M.bank(6 + o), self.ones_bf, ab[q][dr], start=first, stop=last),
                         R=[r_ab[q][dr], self.r_ones], W=[r_nrm])
                todo.append(nm)
            flush()
            pending.extend(todo)
            P.op('dve', lambda e, q=q, j=j, o=o: e.tensor_tensor(ks[:, j, o * 512:(o + 1) * 512], kf[q], kb[q], ALU.add),
                 R=[r_kf[q], r_kb[q]], W=[r_ks])
            P.op('dve', lambda e, q=q, j=j, o=o: e.tensor_tensor(kd[:, j, o * 512:(o + 1) * 512], kb[q], kf[q], ALU.subtract),
                 R=[r_kf[q], r_kb[q]], W=[r_ks])
    flush()
    rn = M.tile([1024], F32)
    r_rn = Res()
    for o in range(2):
        P.op('dve', lambda e, o=o: e.reciprocal(rn[:, o * 512:(o + 1) * 512], M.bank(6 + o)), R=[r_nrm], W=[r_rn])
    wk = M.tile([nj], F32)
    P.dma('sp', wk, wk_d, W=[r_rn])
    cs = [[M.tile([nj, 128], BF16) for _ in range(2)] for _ in range(2)]
    r_cs = [Res(), Res()]
    ot = [M.tile([512], F32) for _ in range(4)]
    r_ot = [Res() for _ in range(4)]
    m = 0
    for f in range(nj):
        q = f % 2
        P.dma('sp', cs[q][0], Ccol[f], W=[r_cs[q]])
        P.dma('sp', cs[q][1], Scol[f], W=[r_cs[q]])
        for ri in range(2):
            src = ks if ri == 0 else kd
            for o in range(2):
                bk = m % 4
                ps = M.bank(bk)
                for j in range(nj):
                    P.op('pe', lambda e, ps=ps, q=q, ri=ri, j=j, o=o, src=src: e.matmul(
                        ps, cs[q][ri][:, j, :], src[:, j, o * 512:(o + 1) * 512], start=(j == 0), stop=(j == nj - 1)),
                        R=[r_cs[q], r_ks], W=[rps[bk]])
                P.op('dve', lambda e, ps=ps, bk=bk, f=f, o=o: e.scalar_tensor_tensor(
                    ot[bk], ps, wk[:, f:f + 1], rn[:, o * 512:(o + 1) * 512], ALU.mult, ALU.mult),
                    R=[rps[bk], r_rn], W=[r_ot[bk]])
                P.dma('act', Kf_d[o, ri, f * 128:(f + 1) * 128, :], ot[bk], R=[r_ot[bk]])
                m += 1
    P.barrier()


def _phase_hy_prep(self):
    P, M = self.P, self.M
    M.reset()
    zz = [M.tile([NT], F32) for _ in range(2)]
    zc = [M.tile([NT], F32) for _ in range(2)]
    cw = [M.tile([4], F32) for _ in range(2)]
    stg = [M.tile([4, 128], F32) for _ in range(2)]
    r_zz = [Res(), Res()]
    r_zc = [Res(), Res()]
    r_cw = [Res(), Res()]
    r_stg = [Res(), Res()]
    r_ps = [Res(), Res()]
    segs = [(0, S), (S, CT)]
    m = 0
    for ch in range(12):
        q = ch % 2
        g, cc = ch // 4, ch % 4
        fs = slice(ch * 128, (ch + 1) * 128)
        P.dma('sp', zz[q], self.z0[1024 + ch * 128:1024 + (ch + 1) * 128, :], W=[r_zz[q]])
        P.dma('sp', cw[q][:, 0:3], self.hy_conv_w[:, fs].rearrange('j p -> p j'), W=[r_cw[q]], allow_slow_non_contiguous=True)
        P.dma('sp', cw[q][:, 3:4], self.hy_conv_b[fs].rearrange('(p o) -> p o', o=1), W=[r_cw[q]])
        P.op('act', lambda e, q=q: e.activation(zc[q], zz[q], AF.Identity, bias=cw[q][:, 3:4], scale=cw[q][:, 1:2]),
             R=[r_zz[q], r_cw[q]], W=[r_zc[q]])
        for j in (0, 2):
            off = j - 1
            for (s0, n) in segs:
                lo = max(0, -off)
                hi = n - max(0, off)
                P.op('dve', lambda e, q=q, j=j, off=off, s0=s0, lo=lo, hi=hi: e.scalar_tensor_tensor(
                    zc[q][:, s0 + lo:s0 + hi], zz[q][:, s0 + lo + off:s0 + hi + off], cw[q][:, j:j + 1],
                    zc[q][:, s0 + lo:s0 + hi], ALU.mult, ALU.add), R=[r_zz[q], r_cw[q], r_zc[q]], W=[r_zc[q]])
        for tg in range(0, NT // 128, 4):
            ntl = min(4, NT // 128 - tg)
            b = m % 2
            m += 1
            ps = M.bank(b)
            for i in range(ntl):
                P.op('pe', lambda e, ps=ps, q=q, tg=tg, i=i: e.transpose(
                    ps[:, i * 128:(i + 1) * 128], zc[q][:, (tg + i) * 128:(tg + i + 1) * 128], self.id32),
                    R=[r_zc[q], self.r_id], W=[r_ps[b]])
            P.op('act' if b == 0 else 'dve',
                 (lambda e, ps=ps, b=b, ntl=ntl: e.copy(stg[b][:, 0:ntl, :], ps[:, 0:ntl * 128].rearrange('p (a c) -> p a c', c=128)))
                 if b == 0 else
                 (lambda e, ps=ps, b=b, ntl=ntl: e.tensor_copy(stg[b][:, 0:ntl, :], ps[:, 0:ntl * 128].rearrange('p (a c) -> p a c', c=128))),
                 R=[r_ps[b]], W=[r_stg[b]])
            P.dma('act', self.vtok[g, tg * 128:(tg + ntl) * 128, cc * 128:(cc + 1) * 128].rearrange('(a p) c -> p a c', p=128),
                  stg[b][:, 0:ntl, :], R=[r_stg[b]])
    P.barrier()


def _phase_hy_conv(self, n, tok0, Ccol, Scol, Kf_d):
    P, M = self.P, self.M
    M.reset()
    nj = n // 128
    u = M.tile([nj, 512], BF16)
    Y = M.tile([nj, 2, 512], BF16)
    r_u = [Res() for _ in range(nj)]
    r_u2 = [Res() for _ in range(nj)]
    r_Y = [Res() for _ in range(nj)]
    cs = [[M.tile([nj, 128], BF16) for _ in range(2)] for _ in range(3)]
    r_cs = [Res(), Res(), Res()]
    kk = [[M.tile([512], F32) for _ in range(2)] for _ in range(3)]
    r_kk = [Res(), Res(), Res()]
    t1 = [M.tile([512], F32) for _ in range(2)]
    t2 = [M.tile([512], F32) for _ in range(2)]
    t3 = [M.tile([512], F32) for _ in range(2)]
    t4 = [M.tile([512], F32) for _ in range(2)]
    r_t1 = [Res(), Res()]
    r_t2 = [Res(), Res()]
    r_t3 = [Res(), Res()]
    r_t4 = [Res(), Res()]
    biasb = M.tile([2, 512], F32)
    r_bias = Res()
    for o in range(2):
        P.dma('sp', biasb[:, o, :], bcast_rows(self.hy_bias[o]), W=[r_bias])
    uf = [M.tile([512], F32) for _ in range(3)]
    xg = [M.tile([512], F32) for _ in range(3)]
    un = [M.tile([512], F32) for _ in range(2)]
    ub = [M.tile([512], BF16) for _ in range(2)]
    ot = [M.tile([512], BF16) for _ in range(2)]
    r_uf = [Res(), Res(), Res()]
    r_xg = [Res(), Res(), Res()]
    r_un = [Res(), Res()]
    r_ub = [Res(), Res()]
    r_ot = [Res(), Res()]
    rps = [Res() for _ in range(8)]
    for j in range(nj):
        P.dma('pool', u[:, j, :], self.vtok[0, tok0 + j * 128:tok0 + (j + 1) * 128, :], W=[r_u[j]])
    for o in range(2):
        for f in range(nj):
            q3 = f % 3
            q = f % 2
            P.dma('sp', cs[q3][0], Ccol[f], W=[r_cs[q3]])
            P.dma('sp', cs[q3][1], Scol[f], W=[r_cs[q3]])
            for ri in range(2):
                P.dma('sp', kk[q3][ri], Kf_d[o, ri, f * 128:(f + 1) * 128, :], W=[r_kk[q3]])
            pr, pi = M.bank(2 * q), M.bank(2 * q + 1)
            for ri, ps in ((0, pr), (1, pi)):
                for j in range(nj):
                    P.op('pe', lambda e, ps=ps, q3=q3, ri=ri, j=j: e.matmul(ps, cs[q3][ri][:, j, :], u[:, j, :],
                                                                          start=(j == 0), stop=(j == nj - 1)),
                         R=[r_cs[q3], r_u[j]], W=[rps[2 * q + ri]])
            R_ = [rps[2 * q], rps[2 * q + 1], r_kk[q3]]
            P.op('dve', lambda e, q=q, q3=q3, pr=pr: e.tensor_tensor(t1[q], pr, kk[q3][0], ALU.mult), R=R_ + [r_t1[q]], W=[r_t1[q]])
            P.op('dve', lambda e, q=q, q3=q3, pi=pi: e.tensor_tensor(t2[q], pi, kk[q3][1], ALU.mult), R=R_ + [r_t2[q]], W=[r_t2[q]])
            P.op('dve', lambda e, q=q, q3=q3, pi=pi: e.tensor_tensor(t3[q], pi, kk[q3][0], ALU.mult), R=R_ + [r_t3[q]], W=[r_t3[q]])
            P.op('dve', lambda e, q=q, q3=q3, pr=pr: e.tensor_tensor(t4[q], pr, kk[q3][1], ALU.mult), R=R_ + [r_t4[q]], W=[r_t4[q]])
            P.op('pool', lambda e, q=q, f=f: e.tensor_tensor(Y[:, f, 0, :], t1[q], t2[q], ALU.add), R=[r_t1[q], r_t2[q]], W=[r_Y[f]])
            P.op('pool', lambda e, q=q, f=f: e.tensor_tensor(Y[:, f, 1, :], t3[q], t4[q], ALU.subtract), R=[r_t3[q], r_t4[q]], W=[r_Y[f]])
        for t in range(nj):
            q = t % 2
            q3 = t % 3
            P.dma('sp', cs[q3][0], Ccol[t], W=[r_cs[q3]])
            P.dma('sp', cs[q3][1], Scol[t], W=[r_cs[q3]])
            rows = slice(tok0 + t * 128, tok0 + (t + 1) * 128)
            if o == 0:
                P.dma('sp', uf[q3], self.vtok[0, rows, :], W=[r_uf[q3]])
            else:
                P.dma('sp', uf[q3], self.u2tok[rows, :], R=[r_u2[t]], W=[r_uf[q3]])
            P.dma('sp', xg[q3], self.vtok[1 + o, rows, :], W=[r_xg[q3]])
            ps = M.bank(4 + q)
            for ri in range(2):
                for j in range(nj):
                    P.op('pe', lambda e, ps=ps, q3=q3, ri=ri, j=j: e.matmul(ps, cs[q3][ri][:, j, :], Y[:, j, ri, :],
                                                                          start=(ri == 0 and j == 0), stop=(ri == 1 and j == nj - 1)),
                         R=[r_cs[q3], r_Y[j]], W=[rps[4 + q]])
            P.op('pool', lambda e, q=q, q3=q3, o=o: e.tensor_tensor(un[q], uf[q3], biasb[:, o, :], ALU.mult), R=[r_uf[q3], r_bias], W=[r_un[q]])
            P.op('dve', lambda e, q=q, ps=ps: e.tensor_tensor(un[q], un[q], ps, ALU.add), R=[r_un[q], rps[4 + q]], W=[r_un[q]])
            if o == 0:
                P.op('dve', lambda e, q=q, q3=q3: e.tensor_tensor(un[q], un[q], xg[q3], ALU.mult), R=[r_un[q], r_xg[q3]], W=[r_un[q]])
                P.dma('act', self.u2tok[rows, :], un[q], R=[r_un[q]], W=[r_u2[t]])
                P.op('act', lambda e, q=q, t=t: e.copy(u[:, t, :], un[q]), R=[r_un[q]], W=[r_u[t]])
            else:
                P.op('dve', lambda e, q=q, q3=q3: e.tensor_tensor(ub[q], un[q], xg[q3], ALU.mult), R=[r_un[q], r_xg[q3]], W=[r_ub[q]])
                pt = M.bank(6 + q, 512, BF16)
                for cc in range(4):
                    P.op('pe', lambda e, pt=pt, q=q, cc=cc: e.transpose(pt[:, cc * 128:(cc + 1) * 128], ub[q][:, cc * 128:(cc + 1) * 128], self.idbf),
                         R=[r_ub[q], self.r_idbf], W=[rps[6 + q]])
                P.op('act', lambda e, pt=pt, q=q: e.copy(ot[q], pt), R=[rps[6 + q]], W=[r_ot[q]])
                P.dma('act', self.mixT[512:1024, tok0 + t * 128:tok0 + (t + 1) * 128].rearrange('(a p) t -> p a t', p=128),
                      ot[q].rearrange('p (a t) -> p a t', a=4), R=[r_ot[q]])
    P.barrier()


Builder.phase_hy_filters = _phase_hy_filters
Builder.phase_hy_prep = _phase_hy_prep
Builder.phase_hy_conv = _phase_hy_conv


def hyena_consts():
    out = {}
    f32 = np.float32
    max_decay = math.log(1e-2) / 0.3
    min_decay = math.log(1e-2) / 1.5
    deltas = np.linspace(min_decay, max_decay, 512, dtype=f32)
    for n in (2048, 256):
        pos = np.arange(n, dtype=f32)
        t = np.linspace(0.0, 1.0, n, dtype=f32)[:, None]
        w = (f32(2.0 * math.pi) * pos / f32(n)).astype(f32)
        fr = np.linspace(1e-4, 15, 16, dtype=f32)
        ang = (w[:, None] * fr[None, :]).astype(f32)
        z = np.concatenate([t, np.cos(ang), -np.sin(ang)], axis=-1).astype(f32)
        out['zT%d' % n] = np.ascontiguousarray(z.T)
        dec = np.exp(-t * np.abs(deltas)[None, :]).astype(f32)
        out['dec%d' % n] = dec
        decb = dec.copy()
        decb[0] = 0.0
        out['decb%d' % n] = decb
        N = 2 * n - 1
        sk = (np.arange(n, dtype=np.int64)[:, None] * np.arange(n, dtype=np.int64)[None, :]) % N
        ang = 2.0 * np.pi * sk.astype(np.float64) / N
        nj = n // 128
        for nm, mat in (('C', np.cos(ang)), ('S', np.sin(ang))):
            m4 = mat.reshape(nj, 128, nj, 128).transpose(2, 1, 0, 3)
            out['%scol%d' % (nm, n)] = np.ascontiguousarray(m4).astype(ml_dtypes.bfloat16)
        wk = np.full(n, 2.0 / N, dtype=f32)
        wk[0] = 1.0 / N
        out['wk%d' % n] = np.ascontiguousarray(wk.reshape(nj, 128).T)
    return out


def _phase_out_proj(self, l, src_bf, w_dram, xsrc, xdst, blocks, prefetch=None):
    P, M = self.P, self.M
    M.reset()
    ntok = max(t0 + T for t0, T, c in blocks)
    mx = M.tile([8, ntok], BF16)
    wt = M.tile([8, D], BF16)
    r_mx, r_w = Res(), Res()
    for k in range(8):
        P.dma('sp', mx[:, k, :], src_bf[k * 128:(k + 1) * 128, 0:ntok], W=[r_mx])
        P.dma('pool', wt[:, k, :], w_dram[k * 128:(k + 1) * 128, :], W=[r_w])
    if prefetch is not None:
        self.prefetch_wup(prefetch)
    xb = [M.tile([512], F32) for _ in range(3)]
    ob = [M.tile([512], F32) for _ in range(3)]
    r_xb = [Res() for _ in range(3)]
    r_ob = [Res() for _ in range(3)]
    rps = [Res() for _ in range(4)]
    n = 0
    mod = self.mod[l]
    for dc in range(8):
        for (t0, T, c) in blocks:
            b = n % 4
            q = n % 3
            n += 1
            ps = M.bank(b)
            P.dma('sp', xb[q][:, 0:T], xsrc[dc * 128:(dc + 1) * 128, t0:t0 + T], W=[r_xb[q]])
            for k in range(8):
                P.op('pe', lambda e, ps=ps, k=k, dc=dc, t0=t0, T=T: e.matmul(
                    ps[:, 0:T], wt[:, k, dc * 128:(dc + 1) * 128], mx[:, k, t0:t0 + T], start=(k == 0), stop=(k == 7)),
                    R=[r_w, r_mx], W=[rps[b]])
            P.op('dve', lambda e, ps=ps, q=q, dc=dc, c=c, T=T: e.scalar_tensor_tensor(
                ob[q][:, 0:T], ps[:, 0:T], mod[:, 16 + dc, c:c + 1], xb[q][:, 0:T], ALU.mult, ALU.add),
                R=[rps[b], r_xb[q], self.r_mod[l]], W=[r_ob[q]])
            P.dma('act', xdst[dc * 128:(dc + 1) * 128, t0:t0 + T], ob[q][:, 0:T], R=[r_ob[q]])
    P.barrier()


def _phase_ffn(self, l, xsrc, xdst, segs, final_out=None):
    P, M = self.P, self.M
    M.reset()
    TB = 256
    NP = 22
    pre = getattr(self, 'wup', None)
    if pre is not None and pre[0] == l:
        wu, r_wu = pre[1], pre[2]
    else:
        wu = M.tile([8, 5632], BF16)
        r_wu = [Res() for _ in range(44)]
        for g in range(0, 5632, 2048):
            gw = min(2048, 5632 - g)
            for k in range(8):
                P.dma('pool', wu[:, k, g:g + gw], self.ffn_w_up[l, k * 128:(k + 1) * 128, g:g + gw],
                      W=[r_wu[j] for j in range(g // 128, (g + gw) // 128)])
    wd = M.tile([NP, D], BF16)
    r_wd = Res()
    for p in range(NP):
        P.dma('pool', wd[:, p, :], self.ffn_w_down[l, p * 128:(p + 1) * 128, :], W=[r_wd])
    cw = M.tile([44, 4], F32)
    r_cw = Res()
    P.dma('sp', cw, self.ffn_cw[l], W=[r_cw])
    NC = TB + 2
    xbs = [M.tile([8, NC], F32) for _ in range(2)]
    sq = M.tile([8, NC], BF16)
    h2 = M.tile([8, NC], BF16)
    rstd = M.tile([NC], F32)
    tmp = [M.tile([NC], F32) for _ in range(2)]
    mm = M.tile([NP, TB], BF16)
    cg = [M.tile([TB], F32) for _ in range(3)]
    cv = [M.tile([TB], F32) for _ in range(3)]
    xn = M.tile([8, TB], F32)
    r_xbs = [Res(), Res()]
    r_sq, r_h2, r_rs, r_xn = Res(), Res(), Res(), Res()
    r_tmp = [Res(), Res()]
    r_mm = [Res() for _ in range(NP)]
    r_cg = [Res(), Res(), Res()]
    r_cv = [Res(), Res(), Res()]
    rps = [Res() for _ in range(8)]
    A = self.A2[l]
    mod = self.mod[l]
    if final_out is not None:
        fg = M.tile([8], F32)
        r_fg = Res()
        P.dma('sp', fg, self.final_g.rearrange('(k p) -> p k', p=128), W=[r_fg], allow_slow_non_contiguous=True)
        sq2, r_sq2 = sq, r_sq
        fo, r_fo = xn, r_xn
    RM = [self.r_mod[l]]
    blist = []
    for (s0, s1, c) in segs:
        for t0 in range(s0, s1, TB):
            T = min(TB, s1 - t0)
            lo = max(s0, t0 - 1)
            hi = min(s1, t0 + T + 1)
            blist.append((s0, s1, c, t0, T, lo, hi))

    def load_x(i):
        s0, s1, c, t0, T, lo, hi = blist[i]
        P.dma('sp', xbs[i % 2][:, :, 0:hi - lo], xsrc[:, lo:hi].rearrange('(k p) t -> p k t', p=128), W=[r_xbs[i % 2]])

    load_x(0)
    for nblk, (s0, s1, c, t0, T, lo, hi) in enumerate(blist):
        if True:
            nc_ = hi - lo
            c0 = t0 - lo
            xb, r_xb = xbs[nblk % 2], r_xbs[nblk % 2]
            P.op('act', lambda e, nc_=nc_, xb=xb: e.activation(sq[:, :, 0:nc_], xb[:, :, 0:nc_], AF.Square), R=[r_xb], W=[r_sq])
            psn = M.bank(6)
            for k in range(8):
                P.op('pe', lambda e, k=k, nc_=nc_, psn=psn: e.matmul(psn[:, 0:nc_], self.ones_bf, sq[:, k, 0:nc_], start=(k == 0), stop=(k == 7)),
                     R=[r_sq, self.r_ones], W=[rps[6]])
            P.op('act', lambda e, nc_=nc_, psn=psn: e.activation(rstd[:, 0:nc_], psn[:, 0:nc_], AF.Sqrt, bias=EPS, scale=1.0 / D),
                 R=[rps[6]], W=[r_rs])
            P.op('dve', lambda e, nc_=nc_: e.reciprocal(rstd[:, 0:nc_], rstd[:, 0:nc_]), R=[r_rs], W=[r_rs])
            for k in range(8):
                q = k % 2
                P.op('pool', lambda e, k=k, q=q, nc_=nc_, xb=xb: e.tensor_tensor(
                    tmp[q][:, 0:nc_], xb[:, k, 0:nc_], rstd[:, 0:nc_], ALU.mult), R=[r_xb, r_rs], W=[r_tmp[q]])
                P.op('act', lambda e, k=k, q=q, nc_=nc_, c=c: e.activation(
                    h2[:, k, 0:nc_], tmp[q][:, 0:nc_], AF.Identity, bias=mod[:, 24 + k, c:c + 1], scale=A[:, k, c:c + 1]),
                    R=[r_tmp[q]] + RM, W=[r_h2])
            if nblk + 1 < len(blist):
                load_x(nblk + 1)
            a = 1 if t0 == s0 else 0
            bnd = 1 if t0 + T == s1 else 0

            def st_pe(p):
                q = p % 3
                for ch, bk in ((p, 2 * q), (NP + p, 2 * q + 1)):
                    ps = M.bank(bk)
                    for k in range(8):
                        P.op('pe', lambda e, ps=ps, k=k, ch=ch, nc_=nc_: e.matmul(
                            ps[:, 0:nc_], wu[:, k, ch * 128:(ch + 1) * 128], h2[:, k, 0:nc_], start=(k == 0), stop=(k == 7)),
                            R=[r_wu[ch], r_h2], W=[rps[bk]])

            def st_conv(p):
                q = p % 3
                pairs = ((p, M.bank(2 * q), 2 * q, cg[q], r_cg[q]), (NP + p, M.bank(2 * q + 1), 2 * q + 1, cv[q], r_cv[q]))
                for ch, ps, bk, dst, rd in pairs:
                    P.op('act', lambda e, ps=ps, ch=ch, dst=dst, T=T, c0=c0: e.activation(
                        dst[:, 0:T], ps[:, c0:c0 + T], AF.Identity, bias=cw[:, ch, 3:4], scale=cw[:, ch, 1:2]),
                        R=[rps[bk], r_cw], W=[rd])
                for ch, ps, bk, dst, rd in pairs:
                    P.op('dve', lambda e, ps=ps, ch=ch, dst=dst, T=T, c0=c0, a=a: e.scalar_tensor_tensor(
                        dst[:, a:T], ps[:, c0 - 1 + a:c0 - 1 + T], cw[:, ch, 0:1], dst[:, a:T], ALU.mult, ALU.add),
                        R=[rps[bk], r_cw, rd], W=[rd])
                for ch, ps, bk, dst, rd in pairs:
                    P.op('dve', lambda e, ps=ps, ch=ch, dst=dst, T=T, c0=c0, bnd=bnd: e.scalar_tensor_tensor(
                        dst[:, 0:T - bnd], ps[:, c0 + 1:c0 + 1 + T - bnd], cw[:, ch, 2:3], dst[:, 0:T - bnd], ALU.mult, ALU.add),
                        R=[rps[bk], r_cw, rd], W=[rd])

            def st_gate(p):
                q = p % 3
                P.op('act', lambda e, q=q, T=T: e.activation(cg[q][:, 0:T], cg[q][:, 0:T], AF.Silu), R=[r_cg[q]], W=[r_cg[q]])
                P.op('pool', lambda e, q=q, p=p, T=T: e.tensor_tensor(mm[:, p, 0:T], cg[q][:, 0:T], cv[q][:, 0:T], ALU.mult),
                     R=[r_cg[q], r_cv[q]], W=[r_mm[p]])

            for i in range(NP + 2):
                if i < NP:
                    st_pe(i)
                if 1 <= i <= NP:
                    st_conv(i - 1)
                if 2 <= i <= NP + 1:
                    st_gate(i - 2)
            for dc in range(8):
                bk = 6 + dc % 2
                ps = M.bank(bk)
                for p in range(NP):
                    P.op('pe', lambda e, ps=ps, p=p, dc=dc, T=T: e.matmul(
                        ps[:, 0:T], wd[:, p, dc * 128:(dc + 1) * 128], mm[:, p, 0:T], start=(p == 0), stop=(p == NP - 1)),
                        R=[r_wd, r_mm[p]], W=[rps[bk]])
                P.op('dve', lambda e, ps=ps, dc=dc, T=T, c0=c0, c=c, xb=xb: e.scalar_tensor_tensor(
                    xn[:, dc, 0:T], ps[:, 0:T], mod[:, 40 + dc, c:c + 1], xb[:, dc, c0:c0 + T], ALU.mult, ALU.add),
                    R=[rps[bk], r_xb] + RM, W=[r_xn])
            if final_out is None:
                P.dma('sp', xdst[:, t0:t0 + T].rearrange('(k p) t -> p k t', p=128), xn[:, :, 0:T], R=[r_xn])
            else:
                P.op('act', lambda e, T=T: e.activation(sq2[:, :, 0:T], xn[:, :, 0:T], AF.Square), R=[r_xn], W=[r_sq2])
                psn = M.bank(7)
                for k in range(8):
                    P.op('pe', lambda e, k=k, T=T, psn=psn: e.matmul(psn[:, 0:T], self.ones_bf, sq2[:, k, 0:T], start=(k == 0), stop=(k == 7)),
                         R=[r_sq2, self.r_ones], W=[rps[7]])
                P.op('act', lambda e, T=T, psn=psn: e.activation(rstd[:, 0:T], psn[:, 0:T], AF.Sqrt, bias=EPS, scale=1.0 / D),
                     R=[rps[7], r_rs], W=[r_rs])
                P.op('dve', lambda e, T=T: e.reciprocal(rstd[:, 0:T], rstd[:, 0:T]), R=[r_rs], W=[r_rs])
                for k in range(8):
                    P.op('dve', lambda e, k=k, T=T: e.scalar_tensor_tensor(
                        fo[:, k, 0:T], xn[:, k, 0:T], fg[:, k:k + 1], rstd[:, 0:T], ALU.mult, ALU.mult),
                        R=[r_xn, r_rs, r_fg], W=[r_fo])
                P.dma('sp', final_out[:, t0:t0 + T].rearrange('(k p) t -> p k t', p=128), fo[:, :, 0:T], R=[r_fo])
    P.barrier()
    if pre is not None and pre[0] == l:
        M.release_top()
        self.wup = None


def _prefetch_wup(self, l):
    P, M = self.P, self.M
    flat = M.reserve_top(8 * 5632 * 2)
    wu = flat.rearrange('p (k f) -> p k f', k=8)
    r_wu = [Res() for _ in range(44)]
    for g in range(0, 5632, 2048):
        gw = min(2048, 5632 - g)
        for k in range(8):
            P.dma('pool', wu[:, k, g:g + gw], self.ffn_w_up[l, k * 128:(k + 1) * 128, g:g + gw],
                  W=[r_wu[j] for j in range(g // 128, (g + gw) // 128)], bg=True)
    self.wup = (l, wu, r_wu)


Builder.prefetch_wup = _prefetch_wup
Builder.phase_out_proj = _phase_out_proj
Builder.phase_ffn = _phase_ffn


def _rms_feat(self, zt, nk, T, g, dst, rz, rg_, rdst, tmp, r_tmp, sq, r_sq, rstd, r_rs, psb, rps):
    P = self.P
    P.op('act', lambda e: e.activation(sq[:, 0:nk, 0:T], zt[:, :, 0:T], AF.Square), R=[rz], W=[r_sq])
    for k in range(nk):
        P.op('pe', lambda e, k=k: e.matmul(psb[:, 0:T], self.ones_bf, sq[:, k, 0:T], start=(k == 0), stop=(k == nk - 1)),
             R=[r_sq, self.r_ones], W=[rps])
    P.op('act', lambda e: e.activation(rstd[:, 0:T], psb[:, 0:T], AF.Sqrt, bias=EPS, scale=1.0 / (nk * 128)), R=[rps], W=[r_rs])
    P.op('dve', lambda e: e.reciprocal(rstd[:, 0:T], rstd[:, 0:T]), R=[r_rs], W=[r_rs])
    for k in range(nk):
        P.op('dve', lambda e, k=k: e.scalar_tensor_tensor(dst[:, k, :], zt[:, k, 0:T], g[:, k:k + 1], rstd[:, 0:T], ALU.mult, ALU.mult),
             R=[rz, r_rs, rg_], W=[rdst])


def _phase_mla_prep(self):
    P, M = self.P, self.M
    M.reset()
    z1 = self.z1
    wq = M.tile([4, 1536], BF16)
    wkv = M.tile([2, 2048], BF16)
    r_w = Res()
    for k in range(4):
        P.dma('pool', wq[:, k, :], self.od_w_uq[k * 128:(k + 1) * 128, :], W=[r_w])
    for k in range(2):
        for g in range(2):
            P.dma('pool', wkv[:, k, g * 1024:(g + 1) * 1024], self.od_w_ukv[k * 128:(k + 1) * 128, g * 1024:(g + 1) * 1024], W=[r_w])
    gq = M.tile([4], F32)
    gkv = M.tile([2], F32)
    rotm = M.tile([64], F32)
    cos2 = M.tile([S], F32)
    sin2 = M.tile([S], F32)
    r_c = Res()
    P.dma('sp', gq, self.od_q_norm_g.rearrange('(k p) -> p k', p=128), W=[r_c], allow_slow_non_contiguous=True)
    P.dma('sp', gkv, self.od_kv_norm_g.rearrange('(k p) -> p k', p=128), W=[r_c], allow_slow_non_contiguous=True)
    P.dma('sp', rotm[0:64, :], self.rotm, W=[r_c])
    P.dma('sp', cos2[0:64, :], self.cos2, W=[r_c])
    P.dma('sp', sin2[0:64, :], self.sin2, W=[r_c])
    qn = M.tile([4, S], BF16)
    ckv = M.tile([2, NT], BF16)
    r_qn = [Res() for _ in range(4)]
    r_ckv = [Res() for _ in range(5)]
    zt = [M.tile([4, 512], F32) for _ in range(2)]
    r_zt = [Res(), Res()]
    sq = M.tile([4, 512], BF16)
    rstd = M.tile([512], F32)
    tmp = None
    r_sq, r_rs = Res(), Res()
    rps = [Res() for _ in range(8)]
    blocks = token_blocks()
    for i, (t0, T, c) in enumerate(blocks[:4]):
        q = i % 2
        P.dma('sp', zt[q][:, :, 0:T], z1[0:512, t0:t0 + T].rearrange('(k p) t -> p k t', p=128), W=[r_zt[q]])
        self.rms_feat(zt[q], 4, T, gq, qn[:, :, t0:t0 + T], r_zt[q], r_c, r_qn[i], None, None, sq, r_sq, rstd, r_rs, M.bank(7), rps[7])
    for i, (t0, T, c) in enumerate(blocks):
        q = i % 2
        P.dma('sp', zt[q][:, 0:2, 0:T], z1[512:768, t0:t0 + T].rearrange('(k p) t -> p k t', p=128), W=[r_zt[q]])
        self.rms_feat(zt[q][:, 0:2, :], 2, T, gkv, ckv[:, :, t0:t0 + T], r_zt[q], r_c, r_ckv[i], None, None, sq, r_sq, rstd, r_rs, M.bank(7), rps[7])
    stb = [M.tile([512], BF16) for _ in range(4)]
    r_stb = [Res() for _ in range(4)]
    xr = [M.tile([512], F32) for _ in range(2)]
    r_xr = [Res(), Res()]
    tt_ = [M.tile([512], F32) for _ in range(2)]
    r_tt = [Res(), Res()]
    n = 0

    def rope_out(src_sb, r_src, t0, T, dst_dram, latent, q):
        nonlocal n
        b = n % 4
        n += 1
        if latent:
            pr = M.bank(4 + q)
            P.op('pe', lambda e: e.matmul(pr[0:64, 0:T], rotm[0:64, :], src_sb[0:64, 0:T], start=True, stop=True),
                 R=[r_src, r_c], W=[rps[4 + q]])
            P.op('dve', lambda e: e.tensor_tensor(tt_[q][0:64, 0:T], pr[0:64, 0:T], sin2[0:64, t0:t0 + T], ALU.mult),
                 R=[rps[4 + q], r_c], W=[r_tt[q]])
            P.op('dve', lambda e: e.tensor_tensor(src_sb[0:64, 0:T], src_sb[0:64, 0:T], cos2[0:64, t0:t0 + T], ALU.mult),
                 R=[r_src, r_c], W=[r_src])
            P.op('dve', lambda e: e.tensor_tensor(stb[b][0:64, 0:T], src_sb[0:64, 0:T], tt_[q][0:64, 0:T], ALU.add),
                 R=[r_src, r_tt[q]], W=[r_stb[b]])
        else:
            P.op('dve', lambda e: e.tensor_copy(stb[b][0:64, 0:T], src_sb[0:64, 0:T]), R=[r_src], W=[r_stb[b]])
        P.dma('act', dst_dram, stb[b][0:64, 0:T], R=[r_stb[b]])

    for h in range(8):
        for i, (t0, T, c) in enumerate(blocks[:4]):
            b = n % 4
            n += 1
            ps = M.bank(b)
            for k in range(4):
                P.op('pe', lambda e, ps=ps, k=k, h=h, t0=t0, T=T: e.matmul(
                    ps[:, 0:T], wq[:, k, h * 192:h * 192 + 128], qn[:, k, t0:t0 + T], start=(k == 0), stop=(k == 3)),
                    R=[r_w, r_qn[i]], W=[rps[b]])
            P.op('act', lambda e, ps=ps, b=b, T=T: e.copy(stb[b][:, 0:T], ps[:, 0:T]), R=[rps[b]], W=[r_stb[b]])
            P.dma('act', self.qT[h, 0:128, t0:t0 + T], stb[b][:, 0:T], R=[r_stb[b]])
            q = i % 2
            b2 = n % 4
            n += 1
            ps2 = M.bank(b2)
            for k in range(4):
                P.op('pe', lambda e, ps2=ps2, k=k, h=h, t0=t0, T=T: e.matmul(
                    ps2[0:64, 0:T], wq[:, k, h * 192 + 128:h * 192 + 192], qn[:, k, t0:t0 + T], start=(k == 0), stop=(k == 3)),
                    R=[r_w, r_qn[i]], W=[rps[b2]])
            P.op('act', lambda e, ps2=ps2, q=q, T=T: e.copy(xr[q][0:64, 0:T], ps2[0:64, 0:T]), R=[rps[b2]], W=[r_xr[q]])
            rope_out(xr[q], r_xr[q], t0, T, self.qT[h, 128:192, t0:t0 + T], True, q)
    for h in range(8):
        for i, (t0, T, c) in enumerate(blocks):
            b = n % 4
            n += 1
            ps = M.bank(b)
            for k in range(2):
                P.op('pe', lambda e, ps=ps, k=k, h=h, t0=t0, T=T: e.matmul(
                    ps[:, 0:T], wkv[:, k, h * 128:(h + 1) * 128], ckv[:, k, t0:t0 + T], start=(k == 0), stop=(k == 1)),
                    R=[r_w, r_ckv[i]], W=[rps[b]])
            P.op('act', lambda e, ps=ps, b=b, T=T: e.copy(stb[b][:, 0:T], ps[:, 0:T]), R=[rps[b]], W=[r_stb[b]])
            P.dma('act', self.kT[h, :, t0:t0 + T], stb[b][:, 0:T], R=[r_stb[b]])
    for tt in range(NT // 128):
        i = min(tt // 4, 4)
        for g in range(2):
            b = n % 4
            n += 1
            ps = M.bank(b)
            for k in range(2):
                P.op('pe', lambda e, ps=ps, k=k, g=g, tt=tt: e.matmul(
                    ps, ckv[:, k, tt * 128:(tt + 1) * 128], wkv[:, k, 1024 + g * 512:1024 + (g + 1) * 512], start=(k == 0), stop=(k == 1)),
                    R=[r_w, r_ckv[i]], W=[rps[b]])
            P.op('dve', lambda e, ps=ps, b=b: e.tensor_copy(stb[b], ps), R=[rps[b]], W=[r_stb[b]])
            P.dma('act', self.Vtok[tt * 128:(tt + 1) * 128, g * 512:(g + 1) * 512], stb[b], R=[r_stb[b]])
    for i, (t0, T, c) in enumerate(blocks):
        q = i % 2
        P.dma('sp', xr[q][0:64, 0:T], z1[768:832, t0:t0 + T], W=[r_xr[q]])
        rope_out(xr[q], r_xr[q], t0, T, self.krT[:, t0:t0 + T], c == 0, q)
    P.barrier()


def _phase_attn(self):
    P, M = self.P, self.M
    M.reset()
    SC = 192.0 ** -0.5
    V = M.tile([18, 1024], BF16)
    kra = M.tile([NT], BF16)
    colsel = M.tile([128], BF16)
    r_v, r_kra, r_cs = Res(), Res(), Res()
    for tt in range(18):
        P.dma('sp', V[:, tt, :], self.Vtok[tt * 128:(tt + 1) * 128, :], W=[r_v])
    P.op('dve', lambda e: e.memset(kra[64:128, :], 0.0), W=[r_kra])
    P.op('dve', lambda e: e.memset(kra[64:65, :], 1.0), W=[r_kra])
    P.dma('sp', kra[0:64, :], self.krT, W=[r_kra])
    P.op('dve', lambda e: e.memset(colsel, 0.0), W=[r_cs])
    P.op('dve', lambda e: e.memset(colsel[:, 64:65], 1.0), W=[r_cs])
    qn = [M.tile([S], BF16) for _ in range(2)]
    qrz = [M.tile([S], BF16) for _ in range(2)]
    kn = [M.tile([NT], BF16) for _ in range(2)]
    r_hd = [Res(), Res()]
    for i in range(2):
        P.op('dve', lambda e, i=i: e.memset(qrz[i][64:128, :], 0.0), W=[r_hd[i]])
    NB = 3
    qra = [M.tile([512], BF16) for _ in range(NB)]
    r_qra = [Res() for _ in range(NB)]
    mx = [M.tile([8], F32) for _ in range(4)]
    r_mx = [Res() for _ in range(4)]
    dg = [M.tile([128], BF16) for _ in range(2)]
    r_dg = [Res(), Res()]
    PTt = [M.tile([512], BF16) for _ in range(4)]
    r_PT = [Res() for _ in range(4)]
    rs = [M.tile([512], F32) for _ in range(2)]
    r_rs = [Res(), Res()]
    ob = [M.tile([512], BF16) for _ in range(2)]
    r_ob = [Res(), Res()]
    rps = [Res() for _ in range(8)]
    kblocks = token_blocks()
    cnt = {'sc': 0, 'mx': 0, 'dg': 0, 'p2': 0, 'pt': 0}
    if getattr(self, 'attn_hook', None):
        self.attn_hook()

    def load_head(h):
        i = h % 2
        P.dma('sp', qn[i], self.qT[h, 0:128, :], W=[r_hd[i]])
        P.dma('sp', qrz[i][0:64, :], self.qT[h, 128:192, :], W=[r_hd[i]])
        P.dma('sp', kn[i], self.kT[h], W=[r_hd[i]])

    def pass1(h, qb, bi):
        i = h % 2
        u = bi % NB
        P.dma('sp', qra[u][0:64, :], self.qT[h, 128:192, qb * 512:(qb + 1) * 512], W=[r_qra[u]])
        for qt in range(4):
            qs = slice(qb * 512 + qt * 128, qb * 512 + (qt + 1) * 128)
            m_ = cnt['mx'] % 4
            cnt['mx'] += 1
            for j, (k0, T, c) in enumerate(kblocks):
                b = cnt['sc'] % 3
                cnt['sc'] += 1
                ps = M.bank(b)
                P.op('pe', lambda e, ps=ps, i=i, qs=qs, k0=k0, T=T: e.matmul(ps[:, 0:T], qn[i][:, qs], kn[i][:, k0:k0 + T], start=True, stop=False),
                     R=[r_hd[i]], W=[rps[b]])
                P.op('pe', lambda e, ps=ps, i=i, qs=qs, k0=k0, T=T: e.matmul(ps[:, 0:T], qrz[i][:, qs], kra[:, k0:k0 + T], start=False, stop=True),
                     R=[r_hd[i], r_kra], W=[rps[b]])
                P.op('dve', lambda e, ps=ps, m_=m_, j=j, T=T: e.tensor_reduce(mx[m_][:, j:j + 1], ps[:, 0:T], AX.X, ALU.max),
                     R=[rps[b]], W=[r_mx[m_]])
                yield
            P.op('dve', lambda e, m_=m_: e.tensor_reduce(mx[m_][:, 5:6], mx[m_][:, 0:5], AX.X, ALU.max, negate=True),
                 R=[r_mx[m_]], W=[r_mx[m_]])
            g = cnt['dg'] % 2
            cnt['dg'] += 1
            P.op('dve', lambda e, g=g, m_=m_: e.tensor_scalar(dg[g], self.idbf, mx[m_][:, 5:6], None, ALU.mult),
                 R=[r_mx[m_], self.r_idbf], W=[r_dg[g]])
            P.op('pe', lambda e, g=g, qt=qt: e.matmul(M.bank(3)[:, qt * 128:(qt + 1) * 128], colsel, dg[g], start=True, stop=True),
                 R=[r_cs, r_dg[g]], W=[rps[3]])
        P.op('act', lambda e, u=u: e.copy(qra[u][64:128, :], M.bank(3)[64:128, :]), R=[rps[3]], W=[r_qra[u]])

    def pass2(h, qb, bi):
        i = h % 2
        u = bi % NB
        qcols = slice(qb * 512, (qb + 1) * 512)
        def emit_s(kt):
            b = 4 + kt % 2
            ps = M.bank(b)
            ks = slice(kt * 128, (kt + 1) * 128)
            P.op('pe', lambda e, ps=ps, i=i, ks=ks, qcols=qcols: e.matmul(ps, kn[i][:, ks], qn[i][:, qcols], start=True, stop=False),
                 R=[r_hd[i]], W=[rps[b]])
            P.op('pe', lambda e, ps=ps, ks=ks, u=u: e.matmul(ps, kra[:, ks], qra[u], start=False, stop=True),
                 R=[r_kra, r_qra[u]], W=[rps[b]])

        emit_s(0)
        for kt in range(18):
            if kt + 1 < 18:
                emit_s(kt + 1)
            yield
            b = 4 + kt % 2
            ps = M.bank(b)
            t = cnt['pt'] % 4
            cnt['pt'] += 1
            P.op('act', lambda e, ps=ps, t=t: e.activation(PTt[t], ps, AF.Exp, scale=SC), R=[rps[b]], W=[r_PT[t]])
            P.op('pe', lambda e, t=t, kt=kt: e.matmul(M.bank(6), self.ones_bf, PTt[t], start=(kt == 0), stop=(kt == 17)),
                 R=[r_PT[t], self.r_ones], W=[rps[6]])
            P.op('pe', lambda e, t=t, kt=kt, h=h: e.matmul(M.bank(7), V[:, kt, h * 128:(h + 1) * 128], PTt[t], start=(kt == 0), stop=(kt == 17)),
                 R=[r_PT[t], r_v], W=[rps[7]])
        o = bi % 2
        P.op('dve', lambda e, o=o: e.reciprocal(rs[o], M.bank(6)), R=[rps[6]], W=[r_rs[o]])
        P.op('dve', lambda e, o=o: e.tensor_tensor(ob[o], M.bank(7), rs[o], ALU.mult), R=[rps[7], r_rs[o]], W=[r_ob[o]])
        P.dma('sp', self.oT[h * 128:(h + 1) * 128, qcols], ob[o], R=[r_ob[o]])

    blocks = [(h, qb) for h in range(8) for qb in range(4)]
    def run_both(g2, g1):
        a, b_ = True, True
        while a or b_:
            if a:
                try:
                    next(g2)
                except StopIteration:
                    a = False
            if b_:
                try:
                    next(g1)
                except StopIteration:
                    b_ = False

    load_head(0)
    for _ in pass1(0, 0, 0):
        pass
    for bi, (h, qb) in enumerate(blocks):
        g2 = pass2(h, qb, bi)
        if bi + 1 < len(blocks):
            h1, qb1 = blocks[bi + 1]
            if qb1 == 0:
                pass
            g1 = pass1(h1, qb1, bi + 1)
        else:
            g1 = iter(())
        run_both(g2, g1)
        if qb == 0 and h + 1 < 8:
            load_head(h + 1)
    P.barrier()


Builder.rms_feat = _rms_feat
Builder.phase_mla_prep = _phase_mla_prep
Builder.phase_attn = _phase_attn


def mla_consts():
    f32 = np.float32
    n = S
    row = np.repeat(np.arange(n // 64, dtype=f32), 64)
    col = np.tile(np.arange(64, dtype=f32), n // 64)
    inv = (f32(10000.0) ** (-np.arange(16, dtype=f32) / f32(16))).astype(f32)
    ang = np.concatenate([row[:, None] * inv[None, :], col[:, None] * inv[None, :]], axis=-1).astype(f32)
    cos2 = np.repeat(np.cos(ang).astype(f32), 2, axis=1).T
    sin2 = np.repeat(np.sin(ang).astype(f32), 2, axis=1).T
    rotm = np.zeros((64, 64), f32)
    for i in range(32):
        rotm[2 * i + 1, 2 * i] = -1.0
        rotm[2 * i, 2 * i + 1] = 1.0
    return {'cos2': np.ascontiguousarray(cos2), 'sin2': np.ascontiguousarray(sin2), 'rotm': rotm}

def make_inputs(inputs, b):
    x = np.asarray(inputs['x'][b], np.float32)
    ctx = np.asarray(inputs['ctx'][b], np.float32)
    m = {}
    m['xT'] = np.ascontiguousarray(np.concatenate([x.T, ctx.T], axis=1))
    m['cvec'] = np.ascontiguousarray(np.stack([inputs['c'][b], inputs['c_ctx']]).astype(np.float32))
    m['ada_w'] = np.ascontiguousarray(inputs['ada_w'], np.float32)
    m['ada_b'] = np.ascontiguousarray(inputs['ada_b'], np.float32)
    m['norm1_g'] = np.ascontiguousarray(inputs['norm1_g'], np.float32)
    m['norm2_g'] = np.ascontiguousarray(inputs['norm2_g'], np.float32)
    m['ev_w_in'] = np.ascontiguousarray(inputs['ev_w_in'][0], np.float32)
    m['ident'] = np.eye(128, dtype=np.float32)
    f = lambda k: np.ascontiguousarray(inputs[k][0], np.float32)
    m['rg_conv_w'] = f('ev_rg_conv_w'); m['rg_conv_b'] = f('ev_rg_conv_b')
    m['rg_wa'] = f('ev_rg_wa'); m['rg_ba'] = f('ev_rg_ba'); m['rg_wx'] = f('ev_rg_wx'); m['rg_bx'] = f('ev_rg_bx')
    m['rg_lambda'] = f('ev_rg_lambda')
    m['hy_conv_w'] = f('ev_hy_conv_w'); m['hy_conv_b'] = f('ev_hy_conv_b')
    m['f_w1'] = f('ev_hy_f_w1'); m['f_b1'] = f('ev_hy_f_b1'); m['f_freq'] = f('ev_hy_f_freq')
    m['f_w2'] = f('ev_hy_f_w2'); m['f_b2'] = f('ev_hy_f_b2'); m['f_w3'] = f('ev_hy_f_w3')
    m['hy_bias'] = f('ev_hy_bias')
    m.update(HC())
    m['ev_w_out'] = f('ev_w_out')
    wi = np.zeros((D, 896), np.float32)
    wi[:, :832] = inputs['od_w_in'][0]
    m['od_w_in'] = wi
    m['od_q_norm_g'] = f('od_q_norm_g'); m['od_kv_norm_g'] = f('od_kv_norm_g')
    m['od_w_uq'] = f('od_w_uq'); m['od_w_o'] = f('od_w_o')
    wkv = np.asarray(inputs['od_w_ukv'][0], np.float32).reshape(256, 8, 256)
    m['od_w_ukv'] = np.ascontiguousarray(np.concatenate([wkv[:, :, :128].reshape(256, 1024), wkv[:, :, 128:].reshape(256, 1024)], axis=1))
    m.update(MC())
    for k in ('ffn_w_up', 'ffn_conv_w', 'ffn_conv_b', 'ffn_w_down', 'final_g'):
        m[k] = np.ascontiguousarray(inputs[k], np.float32)
    cwb = np.concatenate([np.asarray(inputs['ffn_conv_w'], np.float32), np.asarray(inputs['ffn_conv_b'], np.float32)[:, None, :]], axis=1)
    m['ffn_cw'] = np.ascontiguousarray(cwb.reshape(2, 4, 44, 128).transpose(0, 3, 2, 1))
    return m


_HC = {}


def HC():
    if not _HC:
        _HC.update(hyena_consts())
    return _HC


_MC = {}


def MC():
    if not _MC:
        _MC.update(mla_consts())
    return _MC


def kernel(**inputs):
    inputs = {k: np.asarray(v) for k, v in inputs.items()}
    nb = inputs['x'].shape[0]
    B = Builder(dbg=True)
    nc = B.build()
    shared = make_inputs(inputs, 0)
    in_maps = []
    for b in range(nb):
        m = dict(shared)
        x = np.asarray(inputs['x'][b], np.float32)
        ctx = np.asarray(inputs['ctx'][b], np.float32)
        m['xT'] = np.ascontiguousarray(np.concatenate([x.T, ctx.T], axis=1))
        m['cvec'] = np.ascontiguousarray(np.stack([inputs['c'][b], inputs['c_ctx']]).astype(np.float32))
        in_maps.append({k: v for k, v in m.items() if k in B.din})
    res = run_bass_kernel_spmd(nc, in_maps, core_ids=list(range(nb)))
    out = np.stack([np.asarray(res.results[b]['outT'], np.float32).T for b in range(nb)], axis=0)
    return np.ascontiguousarray(out)
```

```python
import contextlib
import math
import numpy as np
import ml_dtypes
import concourse.bass as bass
import concourse.mybir as mybir
from concourse.bass_utils import run_bass_kernel_spmd

dt = mybir.dt
F32 = dt.float32
BF16 = dt.bfloat16
AF = mybir.ActivationFunctionType
ALU = mybir.AluOpType
AX = mybir.AxisListType

D = 1024
S = 2048
CT = 256
NT = S + CT
EPS = 1e-6
ENG = ['pe', 'dve', 'act', 'pool', 'sp']
SAME_SYNC = True


class Res:
    __slots__ = ('wl', 'r')

    def __init__(self):
        self.wl = []
        self.r = {}


class Prog:
    def __init__(self, nc, stack, n_dma=40):
        self.nc = nc
        self.ops = {e: [] for e in ENG}
        self.seq = {e: 0 for e in ENG}
        self.known = {e: {} for e in ENG}
        self.esem = {e: stack.enter_context(nc.semaphore('s_' + e)) for e in ENG}
        self.dsem = [stack.enter_context(nc.semaphore('d%d' % i)) for i in range(n_dma)]
        self.dtgt = [0] * n_dma
        self.drr = 0
        self.n_fg = n_dma - 6
        self.bgrr = 0

    def _sem(self, key):
        return self.esem[key[1]] if key[0] == 'e' else self.dsem[key[1]]

    @staticmethod
    def _joinable(w):
        return bool(w.wl) and not w.r and all(ev[0][0] == 'd' for ev in w.wl)

    def _collect(self, eng, R, W, dma_write=False):
        need = {}

        def add(key, val):
            if val > need.get(key, 0):
                need[key] = val
        for r in R:
            for ev in r.wl:
                add(*ev)
        for w in W:
            if not (dma_write and self._joinable(w)):
                for ev in w.wl:
                    add(*ev)
            for k, v in w.r.items():
                add(k, v)
        waits = []
        kn = self.known[eng]
        for key, val in need.items():
            if key == ('e', eng) and (eng == 'pe' or not SAME_SYNC):
                continue
            if kn.get(key, 0) >= val:
                continue
            kn[key] = val
            waits.append((self._sem(key), val))
        return waits

    def _mark(self, ev, R, W, dma_write=False):
        for r in R:
            if ev[1] > r.r.get(ev[0], 0):
                r.r[ev[0]] = ev[1]
        for w in W:
            if dma_write and self._joinable(w):
                w.wl.append(ev)
            else:
                w.wl = [ev]
                w.r = {}

    def op(self, eng, fn, R=(), W=()):
        waits = self._collect(eng, R, W)
        self.seq[eng] += 1
        ev = (('e', eng), self.seq[eng])
        es = self.esem[eng]

        def emit(e):
            for s, v in waits:
                e.wait_ge(s, v)
            fn(e).then_inc(es, 1)
        self.ops[eng].append(emit)
        self._mark(ev, R, W)

    def dma(self, q, out, in_, R=(), W=(), bg=False, **kw):
        waits = self._collect(q, R, W, dma_write=True)
        if bg:
            si = self.n_fg + self.bgrr
            self.bgrr = (self.bgrr + 1) % (len(self.dsem) - self.n_fg)
        else:
            si = self.drr
            self.drr = (self.drr + 1) % self.n_fg
        prev = self.dtgt[si]
        key = ('d', si)
        if prev > 0 and self.known[q].get(key, 0) < prev:
            self.known[q][key] = prev
            waits.append((self.dsem[si], prev))
        tgt = prev + 16
        self.dtgt[si] = tgt
        ds = self.dsem[si]

        def emit(e):
            for s, v in waits:
                e.wait_ge(s, v)
            e.dma_start(out=out, in_=in_, **kw).then_inc(ds, 16)
        self.ops[q].append(emit)
        self._mark((key, tgt), R, W, dma_write=True)

    def barrier(self, final=False):
        for e in ENG:
            waits = []
            kn = self.known[e]
            for f in ENG:
                if f == e and e == 'pe':
                    continue
                v = self.seq[f]
                key = ('e', f)
                if v > kn.get(key, 0):
                    kn[key] = v
                    waits.append((self.esem[f], v))
            for si, t in enumerate(self.dtgt):
                if si >= self.n_fg and not final:
                    continue
                key = ('d', si)
                if t > kn.get(key, 0):
                    kn[key] = t
                    waits.append((self.dsem[si], t))

            def emit(eng, waits=waits):
                for s, v in waits:
                    eng.wait_ge(s, v)
            self.ops[e].append(emit)

    def play(self):
        with self.nc.Block() as blk:
            @blk.tensor
            def _(e):
                for f in self.ops['pe']:
                    f(e)

            @blk.vector
            def _(e):
                for f in self.ops['dve']:
                    f(e)

            @blk.scalar
            def _(e):
                for f in self.ops['act']:
                    f(e)

            @blk.gpsimd
            def _(e):
                for f in self.ops['pool']:
                    f(e)

            @blk.sync
            def _(e):
                for f in self.ops['sp']:
                    f(e)


class Mem:
    def __init__(self, nc, stack, words=49152):
        self.arena = stack.enter_context(nc.sbuf_tensor('arena', [128, words], F32))
        self.psum = stack.enter_context(nc.psum_tensor('psum', [128, 4096], F32))
        self.words = words
        self.ptr = 0
        self.mark = 0
        self.limit = words * 4
        self.v32 = self.arena
        self.v16 = self.arena.bitcast(BF16)
        self.p16 = self.psum.bitcast(BF16)

    def tile(self, shape, dtype=F32):
        shape = list(shape) if isinstance(shape, (list, tuple)) else [shape]
        n = int(np.prod(shape))
        esz = 4 if dtype == F32 else 2
        self.ptr = (self.ptr + 63) // 64 * 64
        off = self.ptr // esz
        self.ptr += n * esz
        assert self.ptr <= self.limit, 'SBUF arena overflow %d > %d' % (self.ptr, self.limit)
        base = self.v32 if dtype == F32 else self.v16
        ap = base[:, off:off + n]
        if len(shape) == 2:
            ap = ap.rearrange('p (a b) -> p a b', a=shape[0])
        elif len(shape) == 3:
            ap = ap.rearrange('p (a b c) -> p a b c', a=shape[0], b=shape[1])
        return ap

    def persist(self):
        self.mark = self.ptr

    def reserve_top(self, nbytes):
        self.limit = self.words * 4 - nbytes
        assert self.ptr <= self.limit
        off = self.limit // 2
        return self.v16[:, off:off + nbytes // 2]

    def release_top(self):
        self.limit = self.words * 4

    def reset(self):
        self.ptr = self.mark

    def bank(self, i, n=512, dtype=F32, off=0):
        if dtype == F32:
            return self.psum[:, i * 512 + off:i * 512 + off + n]
        return self.p16[:, i * 1024 + off:i * 1024 + off + n]


def token_blocks():
    return [(0, 512, 0), (512, 512, 0), (1024, 512, 0), (1536, 512, 0), (2048, 256, 1)]


class Builder:
    def __init__(self, dbg=False, upto=99, mode=None, feed=()):
        self.dbg = dbg
        self.upto = upto
        self.mode = mode
        self.feed = set(feed)
        self.nc = bass.Bass('TRN2', target_bir_lowering=False)
        self.stack = contextlib.ExitStack()
        self.din = {}
        self.dout = {}

    def inp(self, name, shape, dtype=F32):
        t = self.nc.dram_tensor(name, list(shape), dtype, kind='ExternalInput').ap()
        self.din[name] = t
        return t

    def scratch(self, name, shape, dtype=F32, out=False):
        kind = 'ExternalOutput' if (out or self.dbg) else 'Internal'
        if name in self.feed:
            kind = 'ExternalInput'
        t = self.nc.dram_tensor(name, list(shape), dtype, kind=kind).ap()
        if kind == 'ExternalInput':
            self.din[name] = t
        if kind == 'ExternalOutput':
            self.dout[name] = t
        return t

    def build(self):
        nc = self.nc
        st = self.stack
        P = self.P = Prog(nc, st)
        M = self.M = Mem(nc, st)
        I = self.inp
        self.xT = I('xT', [D, NT])
        self.cvec = I('cvec', [2, D])
        self.ada_w = I('ada_w', [2, D, 6 * D])
        self.ada_b = I('ada_b', [2, 6 * D])
        self.norm1_g = I('norm1_g', [2, D])
        self.norm2_g = I('norm2_g', [2, D])
        self.ev_w_in = I('ev_w_in', [D, 2560])
        self.ident = I('ident', [128, 128])
        self.rg_conv_w = I('rg_conv_w', [4, 512])
        self.rg_conv_b = I('rg_conv_b', [512])
        self.rg_wa = I('rg_wa', [2, 8, 64, 64])
        self.rg_ba = I('rg_ba', [2, 512])
        self.rg_wx = I('rg_wx', [2, 8, 64, 64])
        self.rg_bx = I('rg_bx', [2, 512])
        self.rg_lambda = I('rg_lambda', [2, 512])
        self.hy_conv_w = I('hy_conv_w', [3, 1536])
        self.hy_conv_b = I('hy_conv_b', [1536])
        self.f_w1 = I('f_w1', [33, 64]); self.f_b1 = I('f_b1', [64]); self.f_freq = I('f_freq', [64])
        self.f_w2 = I('f_w2', [64, 64]); self.f_b2 = I('f_b2', [64]); self.f_w3 = I('f_w3', [64, 2048])
        self.hy_bias = I('hy_bias', [2, 512])
        self.ev_w_out = I('ev_w_out', [D, D])
        self.ffn_w_up = I('ffn_w_up', [2, D, 5632])
        self.ffn_conv_w = I('ffn_conv_w', [2, 3, 5632])
        self.ffn_conv_b = I('ffn_conv_b', [2, 5632])
        self.ffn_cw = I('ffn_cw', [2, 128, 44, 4])
        self.ffn_w_down = I('ffn_w_down', [2, 2816, D])
        self.final_g = I('final_g', [D])
        self.xa0 = self.scratch('xa0', [D, NT])
        self.xb0 = self.scratch('xb0', [D, NT])
        self.od_w_in = I('od_w_in', [D, 896])
        self.od_q_norm_g = I('od_q_norm_g', [512]); self.od_kv_norm_g = I('od_kv_norm_g', [256])
        self.od_w_uq = I('od_w_uq', [512, 1536]); self.od_w_ukv = I('od_w_ukv', [256, 2048])
        self.od_w_o = I('od_w_o', [D, D])
        self.cos2 = I('cos2', [64, S]); self.sin2 = I('sin2', [64, S]); self.rotm = I('rotm', [64, 64])
        self.z1 = self.scratch('z1', [896, NT])
        self.qT = self.scratch('qT', [8, 192, S], BF16)
        self.kT = self.scratch('kT', [8, 128, NT], BF16)
        self.krT = self.scratch('krT', [64, NT], BF16)
        self.Vtok = self.scratch('Vtok', [NT, D], BF16)
        self.oT = self.scratch('oT', [D, S], BF16)
        self.xa1 = self.scratch('xa1', [D, S])
        self.outT = self.scratch('outT', [D, S], out=True)
        self.hc = {}
        for n in (2048, 256):
            nj = n // 128
            self.hc[n] = dict(zT=I('zT%d' % n, [33, n]), dec=I('dec%d' % n, [n, 512]), decb=I('decb%d' % n, [n, 512]),
                              C=I('Ccol%d' % n, [nj, 128, nj, 128], BF16), S=I('Scol%d' % n, [nj, 128, nj, 128], BF16),
                              wk=I('wk%d' % n, [128, nj]), Kf=self.scratch('Kf%d' % n, [2, 2, n, 512]))
        self.vtok = self.scratch('vtok', [3, NT, 512])
        self.u2tok = self.scratch('u2tok', [NT, 512])
        self.z0 = self.scratch('z0', [2560, NT])
        self.mixT = self.scratch('mixT', [D, NT], BF16)
        self.consts()
        if self.mode == 'attn':
            self.phase_attn()
            P.barrier()
            P.play()
            return nc
        if self.upto >= 1:
            for l in range(2):
                self.phase_mod(l)
        if self.upto >= 2:
            self.phase_norm_proj(0, self.xT, 1, self.ev_w_in, 2560, self.z0)
        if self.upto >= 3:
            self.phase_rg()
        if self.upto >= 4:
            for n in (2048, 256):
                h = self.hc[n]
                self.phase_hy_filters(n, h['zT'], h['dec'], h['decb'], h['C'], h['S'], h['wk'], h['Kf'])
            self.phase_hy_prep()
            for n, tok0 in ((2048, 0), (256, S)):
                h = self.hc[n]
                self.phase_hy_conv(n, tok0, h['C'], h['S'], h['Kf'])
        if self.upto >= 5:
            self.phase_out_proj(0, self.mixT, self.ev_w_out, self.xT, self.xa0, token_blocks(), prefetch=0)
        if self.upto >= 6:
            self.phase_ffn(0, self.xa0, self.xb0, [(0, S, 0), (S, NT, 1)])
        lat = token_blocks()[:4]
        if self.upto >= 7:
            self.phase_norm_proj(1, self.xb0, 1, self.od_w_in, 896, self.z1)
            self.phase_mla_prep()
        if self.upto >= 8:
            self.attn_hook = lambda: self.prefetch_wup(1)
            self.phase_attn()
            self.phase_out_proj(1, self.oT, self.od_w_o, self.xb0, self.xa1, lat)
        if self.upto >= 9:
            self.phase_ffn(1, self.xa1, None, [(0, S, 0)], final_out=self.outT)
        P.barrier(final=True)
        P.play()
        return nc

    def consts(self):
        P, M = self.P, self.M
        self.ones_bf = M.tile([128], BF16)
        self.r_ones = Res()
        P.op('dve', lambda e: e.memset(self.ones_bf, 1.0), W=[self.r_ones])
        self.id32 = M.tile([128], F32)
        self.idbf = M.tile([128], BF16)
        self.r_id = Res()
        P.dma('sp', self.id32, self.ident, W=[self.r_id])
        self.r_idbf = Res()
        P.op('dve', lambda e: e.tensor_copy(self.idbf, self.id32), R=[self.r_id], W=[self.r_idbf])
        self.mod = [M.tile([48, 2], F32) for _ in range(2)]
        self.A1 = [M.tile([8, 2], F32) for _ in range(2)]
        self.A2 = [M.tile([8, 2], F32) for _ in range(2)]
        self.r_mod = [Res() for _ in range(2)]
        M.persist()

    def phase_mod(self, l):
        P, M = self.P, self.M
        M.reset()
        cv = M.tile([8, 2], F32)
        sT = M.tile([8, 2], BF16)
        bT = M.tile([48], F32)
        g1 = M.tile([8], F32)
        g2 = M.tile([8], F32)
        r_cv, r_sT, r_b, r_g = Res(), Res(), Res(), Res()
        for j in range(2):
            P.dma('sp', cv[:, :, j], self.cvec[j].rearrange('(k p) -> p k', p=128), W=[r_cv],
                  allow_slow_non_contiguous=True)
        P.dma('sp', bT, self.ada_b[l].rearrange('(j p) -> p j', p=128), W=[r_b], allow_slow_non_contiguous=True)
        P.dma('sp', g1, self.norm1_g[l].rearrange('(k p) -> p k', p=128), W=[r_g], allow_slow_non_contiguous=True)
        P.dma('sp', g2, self.norm2_g[l].rearrange('(k p) -> p k', p=128), W=[r_g], allow_slow_non_contiguous=True)
        P.op('act', lambda e: e.activation(sT, cv, AF.Silu), R=[r_cv], W=[r_sT])
        wt = [M.tile([8, 2048], BF16) for _ in range(2)]
        r_wt = [Res(), Res()]
        ps = M.bank(0, 96).rearrange('p (j c) -> p j c', c=2)
        r_ps = Res()
        for sec in range(3):
            w = wt[sec % 2]
            rw = r_wt[sec % 2]
            for k in range(8):
                P.dma('pool', w[:, k, :], self.ada_w[l, k * 128:(k + 1) * 128, sec * 2048:(sec + 1) * 2048], W=[rw])
            for fc in range(16):
                j = sec * 16 + fc
                for k in range(8):
                    P.op('pe', lambda e, w=w, k=k, fc=fc, j=j: e.matmul(
                        ps[:, j, :], w[:, k, fc * 128:(fc + 1) * 128], sT[:, k, :],
                        start=(k == 0), stop=(k == 7)), R=[rw, r_sT], W=[r_ps])
        mod = self.mod[l]
        rm = self.r_mod[l]
        for c in range(2):
            P.op('dve', lambda e, c=c: e.tensor_tensor(mod[:, :, c], ps[:, :, c], bT, ALU.add),
                 R=[r_ps, r_b], W=[rm])
        for c in range(2):
            P.op('dve', lambda e, c=c: e.scalar_tensor_tensor(
                self.A1[l][:, :, c], mod[:, 8:16, c], 1.0, g1, ALU.add, ALU.mult), R=[rm, r_g], W=[rm])
            P.op('dve', lambda e, c=c: e.scalar_tensor_tensor(
                self.A2[l][:, :, c], mod[:, 32:40, c], 1.0, g2, ALU.add, ALU.mult), R=[rm, r_g], W=[rm])
        P.barrier()

    def norm_block(self, src, t0, T, c, A, B, hdst, r_hdst, bufs, i):
        P, M = self.P, self.M
        xb, sq, rstd, tmp, rx, rsq, rrs, rtmp, psb, rps = bufs[i % 2]
        P.dma('sp', xb[:, :, 0:T], src[:, t0:t0 + T].rearrange('(k p) t -> p k t', p=128), W=[rx])
        P.op('act', lambda e: e.activation(sq[:, :, 0:T], xb[:, :, 0:T], AF.Square), R=[rx], W=[rsq])
        for k in range(8):
            P.op('pe', lambda e, k=k: e.matmul(psb[:, 0:T], self.ones_bf, sq[:, k, 0:T],
                                               start=(k == 0), stop=(k == 7)),
                 R=[rsq, self.r_ones], W=[rps])
        P.op('act', lambda e: e.activation(rstd[:, 0:T], psb[:, 0:T], AF.Sqrt, bias=EPS, scale=1.0 / D),
             R=[rps], W=[rrs])
        P.op('dve', lambda e: e.reciprocal(rstd[:, 0:T], rstd[:, 0:T]), R=[rrs], W=[rrs])
        for k in range(8):
            P.op('dve', lambda e, k=k: e.scalar_tensor_tensor(
                tmp[:, k, 0:T], xb[:, k, 0:T], A[:, k, c:c + 1], rstd[:, 0:T], ALU.mult, ALU.mult),
                R=[rx, rrs, self.r_mod[0], self.r_mod[1]], W=[rtmp[k]])
            P.op('act', lambda e, k=k: e.activation(hdst[:, k, :], tmp[:, k, 0:T], AF.Identity,
                                                    bias=B[:, k, c:c + 1]),
                 R=[rtmp[k], self.r_mod[0], self.r_mod[1]], W=[r_hdst])

    def norm_bufs(self, psbanks):
        M = self.M
        bufs = []
        for i in range(2):
            bufs.append((M.tile([8, 512], F32), M.tile([8, 512], BF16), M.tile([512], F32),
                         M.tile([8, 512], F32), Res(), Res(), Res(), [Res() for _ in range(8)],
                         M.bank(psbanks[i]), Res()))
        return bufs

    def phase_norm_proj(self, l, src, which, w_dram, F, zdst, blocks=None):
        P, M = self.P, self.M
        M.reset()
        blocks = blocks or token_blocks()
        A = self.A1[l] if which == 1 else self.A2[l]
        B = self.mod[l][:, 0:8, :] if which == 1 else self.mod[l][:, 24:32, :]
        hT = M.tile([8, NT], BF16)
        r_h = [Res() for _ in blocks]
        nfc = F // 128
        wt = M.tile([8, F], BF16)
        r_w = [Res() for _ in range(nfc)]
        for g in range(0, F, 2048):
            gw = min(2048, F - g)
            for k in range(8):
                P.dma('pool', wt[:, k, g:g + gw], w_dram[k * 128:(k + 1) * 128, g:g + gw],
                      W=[r_w[j] for j in range(g // 128, (g + gw) // 128)])
        bufs = self.norm_bufs([6, 7])
        for i, (t0, T, c) in enumerate(blocks):
            self.norm_block(src, t0, T, c, A, B, hT[:, :, t0:t0 + T], r_h[i], bufs, i)
        stg = [M.tile([512], F32) for _ in range(4)]
        r_stg = [Res() for _ in range(4)]
        r_ps = [Res() for _ in range(4)]
        n = 0
        for fc in range(nfc):
            for i, (t0, T, c) in enumerate(blocks):
                b = n % 4
                ps = M.bank(b)
                for k in range(8):
                    P.op('pe', lambda e, k=k, fc=fc, t0=t0, T=T, ps=ps: e.matmul(
                        ps[:, 0:T], wt[:, k, fc * 128:(fc + 1) * 128], hT[:, k, t0:t0 + T],
                        start=(k == 0), stop=(k == 7)), R=[r_w[fc], r_h[i]], W=[r_ps[b]])
                ev = 'act' if n % 2 == 0 else 'dve'
                if ev == 'act':
                    P.op('act', lambda e, b=b, T=T, ps=ps: e.copy(stg[b][:, 0:T], ps[:, 0:T]),
                         R=[r_ps[b]], W=[r_stg[b]])
                else:
                    P.op('dve', lambda e, b=b, T=T, ps=ps: e.tensor_copy(stg[b][:, 0:T], ps[:, 0:T]),
                         R=[r_ps[b]], W=[r_stg[b]])
                P.dma('act', zdst[fc * 128:(fc + 1) * 128, t0:t0 + T], stg[b][:, 0:T], R=[r_stg[b]])
                n += 1
        P.barrier()


def rev(ap, n):
    pat = [list(p) for p in ap.ap]
    assert len(pat) == 2 and pat[1][1] == n
    return bass.AP(ap.tensor, ap.offset + (n - 1) * pat[1][0], [pat[0], [-pat[1][0], n]])


def _phase_rg(self):
    P, M = self.P, self.M
    M.reset()
    f = lambda: M.tile([NT], F32)
    zx, gate, xc, hf, hb, ga = [f() for _ in range(6)]
    rgs, igs, aas, bbs = [[f(), f()] for _ in range(4)]
    xcb = M.tile([NT], BF16)
    ob = M.tile([NT], BF16)
    cw = M.tile([4], F32)
    cb = M.tile([1], F32)
    bias4 = M.tile([4], F32)
    lam = M.tile([2], F32)
    cl = M.tile([4], F32)
    wbd = [M.tile([128], BF16) for _ in range(4)]
    segs = [(0, S), (S, CT)]
    blocks = token_blocks()
    for cc in range(4):
        R = {k: Res() for k in ['zx', 'gate', 'xc', 'xcb', 'r0', 'i0', 'a0', 'b0', 'r1', 'i1', 'a1', 'b1', 'hf', 'hb', 'small', 'cl', 'ob', 'ga']}
        rw = [Res() for _ in range(4)]
        rps = [Res() for _ in range(4)]
        fs = slice(cc * 128, (cc + 1) * 128)
        P.dma('sp', zx, self.z0[cc * 128:(cc + 1) * 128, :], W=[R['zx']])
        P.dma('sp', gate, self.z0[512 + cc * 128:512 + (cc + 1) * 128, :], W=[R['gate']])
        P.dma('sp', cw, self.rg_conv_w[:, fs].rearrange('j p -> p j'), W=[R['small']], allow_slow_non_contiguous=True)
        P.dma('sp', cb, self.rg_conv_b[fs].rearrange('(p o) -> p o', o=1), W=[R['small']])
        for d in range(2):
            P.dma('sp', bias4[:, 2 * d:2 * d + 1], self.rg_ba[d, fs].rearrange('(p o) -> p o', o=1), W=[R['small']])
            P.dma('sp', bias4[:, 2 * d + 1:2 * d + 2], self.rg_bx[d, fs].rearrange('(p o) -> p o', o=1), W=[R['small']])
            P.dma('sp', lam[:, d:d + 1], self.rg_lambda[d, fs].rearrange('(p o) -> p o', o=1), W=[R['small']])
        for d in range(2):
            for ty, wsrc in enumerate([self.rg_wa, self.rg_wx]):
                w = wbd[2 * d + ty]
                r_ = rw[2 * d + ty]
                P.op('dve', lambda e, w=w: e.memset(w, 0.0), W=[r_])
                for hh in range(2):
                    P.dma('pool', w[hh * 64:(hh + 1) * 64, hh * 64:(hh + 1) * 64], wsrc[d, 2 * cc + hh], W=[r_])
        P.op('act', lambda e: e.activation(cl[:, 0:2], lam, AF.Exp, scale=-1.0), R=[R['small']], W=[R['cl']])
        P.op('act', lambda e: e.activation(cl[:, 0:2], cl[:, 0:2], AF.Ln, bias=1.0), R=[R['cl']], W=[R['cl']])
        P.op('dve', lambda e: e.tensor_scalar(cl[:, 2:4], cl[:, 0:2], -16.0, None, ALU.mult), R=[R['cl']], W=[R['cl']])
        P.op('dve', lambda e: e.tensor_scalar(cl[:, 0:2], cl[:, 0:2], -8.0, None, ALU.mult), R=[R['cl']], W=[R['cl']])
        P.op('act', lambda e: e.activation(xc, zx, AF.Identity, bias=cb[:, 0:1], scale=cw[:, 2:3]),
             R=[R['zx'], R['small']], W=[R['xc']])
        for j in (0, 1, 3):
            off = j - 2
            for (s0, n) in segs:
                lo = max(0, -off)
                hi = n - max(0, off)
                P.op('dve', lambda e, j=j, off=off, s0=s0, lo=lo, hi=hi: e.scalar_tensor_tensor(
                    xc[:, s0 + lo:s0 + hi], zx[:, s0 + lo + off:s0 + hi + off], cw[:, j:j + 1],
                    xc[:, s0 + lo:s0 + hi], ALU.mult, ALU.add), R=[R['zx'], R['small'], R['xc']], W=[R['xc']])
        P.op('act', lambda e: e.copy(xcb, xc), R=[R['xc']], W=[R['xcb']])
        P.op('act', lambda e: e.activation(ga, gate, AF.Square), R=[R['gate']], W=[R['ga']])
        P.op('pool', lambda e: e.tensor_scalar(ga, ga, 0.044715, 1.0, ALU.mult, ALU.add), R=[R['ga']], W=[R['ga']])
        P.op('pool', lambda e: e.tensor_tensor(ga, ga, gate, ALU.mult), R=[R['ga'], R['gate']], W=[R['ga']])
        P.op('act', lambda e: e.activation(ga, ga, AF.Sigmoid, scale=1.5957691216), R=[R['ga']], W=[R['ga']])
        P.op('pool', lambda e: e.tensor_tensor(ga, ga, gate, ALU.mult), R=[R['ga'], R['gate']], W=[R['ga']])
        for d in range(2):
            rg, ig, aa, bb = rgs[d], igs[d], aas[d], bbs[d]
            kr_, ki_, ka_, kb_ = 'r%d' % d, 'i%d' % d, 'a%d' % d, 'b%d' % d
            for ty, dst, rk in ((0, rg, kr_), (1, ig, ki_)):
                for bi, (t0, T, c) in enumerate(blocks):
                    bk = (bi + ty) % 4
                    ps = M.bank(bk)
                    P.op('pe', lambda e, ps=ps, d=d, ty=ty, t0=t0, T=T: e.matmul(
                        ps[:, 0:T], wbd[2 * d + ty], xcb[:, t0:t0 + T], start=True, stop=True),
                        R=[rw[2 * d + ty], R['xcb']], W=[rps[bk]])
                    P.op('act', lambda e, ps=ps, dst=dst, d=d, ty=ty, t0=t0, T=T: e.activation(
                        dst[:, t0:t0 + T], ps[:, 0:T], AF.Sigmoid, bias=bias4[:, 2 * d + ty:2 * d + ty + 1]),
                        R=[rps[bk], R['small']], W=[R[rk]])
            P.op('act', lambda e, d=d, aa=aa, rg=rg: e.activation(aa, rg, AF.Exp, scale=cl[:, d:d + 1]), R=[R[kr_], R['cl']], W=[R[ka_]])
            P.op('act', lambda e, d=d, bb=bb, rg=rg: e.activation(bb, rg, AF.Exp, scale=cl[:, 2 + d:3 + d]), R=[R[kr_], R['cl']], W=[R[kb_]])
            P.op('act', lambda e, bb=bb: e.activation(bb, bb, AF.Sqrt, bias=1.0, scale=-1.0), R=[R[kb_]], W=[R[kb_]])
            P.op('pool', lambda e, bb=bb, ig=ig: e.tensor_tensor(bb, bb, ig, ALU.mult), R=[R[kb_], R[ki_]], W=[R[kb_]])
            P.op('pool', lambda e, bb=bb: e.tensor_tensor(bb, bb, xc, ALU.mult), R=[R[kb_], R['xc']], W=[R[kb_]])
            h = hf if d == 0 else hb
            hk = 'hf' if d == 0 else 'hb'
            if d == 0:
                P.op('dve', lambda e, h=h, aa=aa, bb=bb: e.tensor_tensor_scan(h[:, S:NT], aa[:, S:NT], bb[:, S:NT], 0.0, ALU.mult, ALU.add),
                     R=[R[ka_], R[kb_]], W=[R[hk]])
                P.op('dve', lambda e, h=h, aa=aa, bb=bb: e.tensor_tensor_scan(h[:, 0:S], aa[:, 0:S], bb[:, 0:S], h[:, NT - 1:NT], ALU.mult, ALU.add),
                     R=[R[ka_], R[kb_], R[hk]], W=[R[hk]])
            else:
                P.op('dve', lambda e, h=h, aa=aa, bb=bb: e.tensor_tensor_scan(rev(h[:, S:NT], CT), rev(aa[:, S:NT], CT), rev(bb[:, S:NT], CT),
                                                                               0.0, ALU.mult, ALU.add), R=[R[ka_], R[kb_]], W=[R[hk]])
                P.op('dve', lambda e, h=h, aa=aa, bb=bb: e.tensor_tensor_scan(rev(h[:, 0:S], S), rev(aa[:, 0:S], S), rev(bb[:, 0:S], S),
                                                                               h[:, S:S + 1], ALU.mult, ALU.add),
                     R=[R[ka_], R[kb_], R[hk]], W=[R[hk]])
        P.op('pool', lambda e: e.tensor_tensor(hf, hf, hb, ALU.add), R=[R['hf'], R['hb']], W=[R['hf']])
        P.op('dve', lambda e: e.tensor_tensor(ob, ga, hf, ALU.mult), R=[R['ga'], R['hf']], W=[R['ob']])
        P.dma('act', self.mixT[cc * 128:(cc + 1) * 128, :], ob, R=[R['ob']])
        P.barrier()


Builder.phase_rg = _phase_rg


def bcast_rows(ap_row, nparts=128):
    pat = [list(p) for p in ap_row.ap]
    return bass.AP(ap_row.tensor, ap_row.offset, [[0, nparts]] + pat)


def _phase_hy_filters(self, n, zT_d, dec_d, decb_d, Ccol, Scol, wk_d, Kf_d):
    P, M = self.P, self.M
    M.reset()
    nj = n // 128
    w1 = M.tile([64], F32)
    w2 = M.tile([64], F32)
    w3 = M.tile([2048], F32)
    sm = M.tile([3], F32)
    zT = M.tile([n], F32)
    h1 = M.tile([n], F32)
    h2 = M.tile([n], F32)
    tm = M.tile([512], F32)
    r_w, r_z, r_h1, r_h2, r_tm = Res(), Res(), Res(), Res(), Res()
    P.dma('sp', w1[0:33, :], self.f_w1, W=[r_w])
    P.dma('sp', w2[0:64, :], self.f_w2, W=[r_w])
    P.dma('sp', w3[0:64, :], self.f_w3, W=[r_w])
    P.dma('sp', sm[0:64, 0:1], self.f_b1.rearrange('(p o) -> p o', o=1), W=[r_w])
    P.dma('sp', sm[0:64, 1:2], self.f_freq.rearrange('(p o) -> p o', o=1), W=[r_w])
    P.dma('sp', sm[0:64, 2:3], self.f_b2.rearrange('(p o) -> p o', o=1), W=[r_w])
    P.dma('sp', zT[0:33, :], zT_d, W=[r_z])
    rps = [Res() for _ in range(8)]
    PI = math.pi

    def sin_layer(lhsT, kk, src, rsrc, bcol, dst, rdst):
        for bi, t0 in enumerate(range(0, n, 512)):
            T = min(512, n - t0)
            ps = M.bank(bi % 2)
            P.op('pe', lambda e, ps=ps, t0=t0, T=T: e.matmul(ps[0:64, 0:T], lhsT, src[0:kk, t0:t0 + T], start=True, stop=True),
                 R=[r_w, rsrc], W=[rps[bi % 2]])
            P.op('dve', lambda e, ps=ps, T=T: e.tensor_scalar(tm[0:64, 0:T], ps[0:64, 0:T], sm[0:64, bcol:bcol + 1],
                                                           sm[0:64, 1:2], ALU.add, ALU.mult), R=[rps[bi % 2], r_w, r_tm], W=[r_tm])
            for thr, op, mul in ((PI, ALU.is_gt, -2 * PI), (-PI, ALU.is_lt, 2 * PI)):
                P.op('dve', lambda e, T=T, t0=t0, thr=thr, op=op, mul=mul: e.tensor_scalar(
                    dst[0:64, t0:t0 + T], tm[0:64, 0:T], thr, mul, op, ALU.mult), R=[r_tm], W=[rdst])
                P.op('dve', lambda e, T=T, t0=t0: e.tensor_tensor(tm[0:64, 0:T], tm[0:64, 0:T], dst[0:64, t0:t0 + T], ALU.add),
                     R=[r_tm, rdst], W=[r_tm])
            P.op('act', lambda e, T=T, t0=t0: e.activation(dst[0:64, t0:t0 + T], tm[0:64, 0:T], AF.Sin), R=[r_tm], W=[rdst])

    sin_layer(w1[0:33, :], 33, zT, r_z, 0, h1, r_h1)
    sin_layer(w2[0:64, :], 64, h1, r_h1, 2, h2, r_h2)
    ks = M.tile([nj, 1024], BF16)
    kd = M.tile([nj, 1024], BF16)
    dec = [M.tile([512], F32) for _ in range(2)]
    decb = [M.tile([512], F32) for _ in range(2)]
    kf = [M.tile([512], F32) for _ in range(2)]
    kb = [M.tile([512], F32) for _ in range(2)]
    ab = [[M.tile([512], BF16) for _ in range(2)] for _ in range(2)]
    r_dec = [Res(), Res()]
    r_kf = [Res(), Res()]
    r_kb = [Res(), Res()]
    r_ab = [[Res(), Res()], [Res(), Res()]]
    r_ks = Res()
    r_nrm = Res()
    cnt = 0
    pending = []

    def flush():
        for fn in pending:
            fn()
        del pending[:]

    for j in range(nj):
        P.dma('sp', dec[j % 2], dec_d[j * 128:(j + 1) * 128, :], W=[r_dec[j % 2]])
        P.dma('sp', decb[j % 2], decb_d[j * 128:(j + 1) * 128, :], W=[r_dec[j % 2]])
        for o in range(2):
            q = cnt % 2
            cnt += 1
            todo = []
            for dr in range(2):
                bk = 2 + 2 * q + dr
                ps = M.bank(bk)
                c0 = o * 1024 + dr * 512
                P.op('pe', lambda e, ps=ps, j=j, c0=c0: e.matmul(ps, h2[0:64, j * 128:(j + 1) * 128], w3[0:64, c0:c0 + 512],
                                                             start=True, stop=True), R=[r_h2, r_w], W=[rps[bk]])
                dst, rd, dc = (kf[q], r_kf[q], dec[j % 2]) if dr == 0 else (kb[q], r_kb[q], decb[j % 2])
                P.op('dve', lambda e, ps=ps, dst=dst, dc=dc: e.tensor_tensor(dst, ps, dc, ALU.mult),
                     R=[rps[bk], r_dec[j % 2]], W=[rd])
                P.op('act', lambda e, dst=dst, q=q, dr=dr: e.activation(ab[q][dr], dst, AF.Abs), R=[rd], W=[r_ab[q][dr]])
                first = (j == 0 and dr == 0)
                last = (j == nj - 1 and dr == 1)

                def nm(o=o, q=q, dr=dr, first=first, last=last):
                    P.op('pe', lambda e: e.matmul(M.bank(6 + o), self.ones_bf, ab[q][dr], start=first, stop=last),
                         R=[r_ab[q][dr], self.r_ones], W=[r_nrm])
                todo.append(nm)
            flush()
            pending.extend(todo)
            P.op('dve', lambda e, q=q, j=j, o=o: e.tensor_tensor(ks[:, j, o * 512:(o + 1) * 512], kf[q], kb[q], ALU.add),
                 R=[r_kf[q], r_kb[q]], W=[r_ks])
            P.op('dve', lambda e, q=q, j=j, o=o: e.tensor_tensor(kd[:, j, o * 512:(o + 1) * 512], kb[q], kf[q], ALU.subtract),
                 R=[r_kf[q], r_kb[q]], W=[r_ks])
    flush()
    rn = M.tile([1024], F32)
    r_rn = Res()
    for o in range(2):
        P.op('dve', lambda e, o=o: e.reciprocal(rn[:, o * 512:(o + 1) * 512], M.bank(6 + o)), R=[r_nrm], W=[r_rn])
    wk = M.tile([nj], F32)
    P.dma('sp', wk, wk_d, W=[r_rn])
    cs = [[M.tile([nj, 128], BF16) for _ in range(2)] for _ in range(2)]
    r_cs = [Res(), Res()]
    ot = [M.tile([512], F32) for _ in range(4)]
    r_ot = [Res() for _ in range(4)]
    m = 0
    for f in range(nj):
        q = f % 2
        P.dma('sp', cs[q][0], Ccol[f], W=[r_cs[q]])
        P.dma('sp', cs[q][1], Scol[f], W=[r_cs[q]])
        for ri in range(2):
            src = ks if ri == 0 else kd
            for o in range(2):
                bk = m % 4
                ps = M.bank(bk)
                for j in range(nj):
                    P.op('pe', lambda e, ps=ps, q=q, ri=ri, j=j, o=o, src=src: e.matmul(
                        ps, cs[q][ri][:, j, :], src[:, j, o * 512:(o + 1) * 512], start=(j == 0), stop=(j == nj - 1)),
                        R=[r_cs[q], r_ks], W=[rps[bk]])
                P.op('dve', lambda e, ps=ps, bk=bk, f=f, o=o: e.scalar_tensor_tensor(
                    ot[bk], ps, wk[:, f:f + 1], rn[:, o * 512:(o + 1) * 512], ALU.mult, ALU.mult),
                    R=[rps[bk], r_rn], W=[r_ot[bk]])
                P.dma('act', Kf_d[o, ri, f * 128:(f + 1) * 128, :], ot[bk], R=[r_ot[bk]])
                m += 1
    P.barrier()


def _phase_hy_prep(self):
    P, M = self.P, self.M
    M.reset()
    zz = [M.tile([NT], F32) for _ in range(2)]
    zc = [M.tile([NT], F32) for _ in range(2)]
    cw = [M.tile([4], F32) for _ in range(2)]
    stg = [M.tile([4, 128], F32) for _ in range(2)]
    r_zz = [Res(), Res()]
    r_zc = [Res(), Res()]
    r_cw = [Res(), Res()]
    r_stg = [Res(), Res()]
    r_ps = [Res(), Res()]
    segs = [(0, S), (S, CT)]
    m = 0
    for ch in range(12):
        q = ch % 2
        g, cc = ch // 4, ch % 4
        fs = slice(ch * 128, (ch + 1) * 128)
        P.dma('sp', zz[q], self.z0[1024 + ch * 128:1024 + (ch + 1) * 128, :], W=[r_zz[q]])
        P.dma('sp', cw[q][:, 0:3], self.hy_conv_w[:, fs].rearrange('j p -> p j'), W=[r_cw[q]], allow_slow_non_contiguous=True)
        P.dma('sp', cw[q][:, 3:4], self.hy_conv_b[fs].rearrange('(p o) -> p o', o=1), W=[r_cw[q]])
        P.op('act', lambda e, q=q: e.activation(zc[q], zz[q], AF.Identity, bias=cw[q][:, 3:4], scale=cw[q][:, 1:2]),
             R=[r_zz[q], r_cw[q]], W=[r_zc[q]])
        for j in (0, 2):
            off = j - 1
            for (s0, n) in segs:
                lo = max(0, -off)
                hi = n - max(0, off)
                P.op('dve', lambda e, q=q, j=j, off=off, s0=s0, lo=lo, hi=hi: e.scalar_tensor_tensor(
                    zc[q][:, s0 + lo:s0 + hi], zz[q][:, s0 + lo + off:s0 + hi + off], cw[q][:, j:j + 1],
                    zc[q][:, s0 + lo:s0 + hi], ALU.mult, ALU.add), R=[r_zz[q], r_cw[q], r_zc[q]], W=[r_zc[q]])
        for tg in range(0, NT // 128, 4):
            ntl = min(4, NT // 128 - tg)
            b = m % 2
            m += 1
            ps = M.bank(b)
            for i in range(ntl):
                P.op('pe', lambda e, ps=ps, q=q, tg=tg, i=i: e.transpose(
                    ps[:, i * 128:(i + 1) * 128], zc[q][:, (tg + i) * 128:(tg + i + 1) * 128], self.id32),
                    R=[r_zc[q], self.r_id], W=[r_ps[b]])
            P.op('act' if b == 0 else 'dve',
                 (lambda e, ps=ps, b=b, ntl=ntl: e.copy(stg[b][:, 0:ntl, :], ps[:, 0:ntl * 128].rearrange('p (a c) -> p a c', c=128)))
                 if b == 0 else
                 (lambda e, ps=ps, b=b, ntl=ntl: e.tensor_copy(stg[b][:, 0:ntl, :], ps[:, 0:ntl * 128].rearrange('p (a c) -> p a c', c=128))),
                 R=[r_ps[b]], W=[r_stg[b]])
            P.dma('act', self.vtok[g, tg * 128:(tg + ntl) * 128, cc * 128:(cc + 1) * 128].rearrange('(a p) c -> p a c', p=128),
                  stg[b][:, 0:ntl, :], R=[r_stg[b]])
    P.barrier()


def _phase_hy_conv(self, n, tok0, Ccol, Scol, Kf_d):
    P, M = self.P, self.M
    M.reset()
    nj = n // 128
    u = M.tile([nj, 512], BF16)
    Y = M.tile([nj, 2, 512], BF16)
    r_u = [Res() for _ in range(nj)]
    r_u2 = [Res() for _ in range(nj)]
    r_Y = [Res() for _ in range(nj)]
    cs = [[M.tile([nj, 128], BF16) for _ in range(2)] for _ in range(3)]
    r_cs = [Res(), Res(), Res()]
    kk = [[M.tile([512], F32) for _ in range(2)] for _ in range(3)]
    r_kk = [Res(), Res(), Res()]
    t1 = [M.tile([512], F32) for _ in range(2)]
    t2 = [M.tile([512], F32) for _ in range(2)]
    t3 = [M.tile([512], F32) for _ in range(2)]
    t4 = [M.tile([512], F32) for _ in range(2)]
    r_t1 = [Res(), Res()]
    r_t2 = [Res(), Res()]
    r_t3 = [Res(), Res()]
    r_t4 = [Res(), Res()]
    biasb = M.tile([2, 512], F32)
    r_bias = Res()
    for o in range(2):
        P.dma('sp', biasb[:, o, :], bcast_rows(self.hy_bias[o]), W=[r_bias])
    uf = [M.tile([512], F32) for _ in range(3)]
    xg = [M.tile([512], F32) for _ in range(3)]
    un = [M.tile([512], F32) for _ in range(2)]
    ub = [M.tile([512], BF16) for _ in range(2)]
    ot = [M.tile([512], BF16) for _ in range(2)]
    r_uf = [Res(), Res(), Res()]
    r_xg = [Res(), Res(), Res()]
    r_un = [Res(), Res()]
    r_ub = [Res(), Res()]
    r_ot = [Res(), Res()]
    rps = [Res() for _ in range(8)]
    for j in range(nj):
        P.dma('pool', u[:, j, :], self.vtok[0, tok0 + j * 128:tok0 + (j + 1) * 128, :], W=[r_u[j]])
    for o in range(2):
        for f in range(nj):
            q3 = f % 3
            q = f % 2
            P.dma('sp', cs[q3][0], Ccol[f], W=[r_cs[q3]])
            P.dma('sp', cs[q3][1], Scol[f], W=[r_cs[q3]])
            for ri in range(2):
                P.dma('sp', kk[q3][ri], Kf_d[o, ri, f * 128:(f + 1) * 128, :], W=[r_kk[q3]])
            pr, pi = M.bank(2 * q), M.bank(2 * q + 1)
            for ri, ps in ((0, pr), (1, pi)):
                for j in range(nj):
                    P.op('pe', lambda e, ps=ps, q3=q3, ri=ri, j=j: e.matmul(ps, cs[q3][ri][:, j, :], u[:, j, :],
                                                                          start=(j == 0), stop=(j == nj - 1)),
                         R=[r_cs[q3], r_u[j]], W=[rps[2 * q + ri]])
            R_ = [rps[2 * q], rps[2 * q + 1], r_kk[q3]]
            P.op('dve', lambda e, q=q, q3=q3, pr=pr: e.tensor_tensor(t1[q], pr, kk[q3][0], ALU.mult), R=R_ + [r_t1[q]], W=[r_t1[q]])
            P.op('dve', lambda e, q=q, q3=q3, pi=pi: e.tensor_tensor(t2[q], pi, kk[q3][1], ALU.mult), R=R_ + [r_t2[q]], W=[r_t2[q]])
            P.op('dve', lambda e, q=q, q3=q3, pi=pi: e.tensor_tensor(t3[q], pi, kk[q3][0], ALU.mult), R=R_ + [r_t3[q]], W=[r_t3[q]])
            P.op('dve', lambda e, q=q, q3=q3, pr=pr: e.tensor_tensor(t4[q], pr, kk[q3][1], ALU.mult), R=R_ + [r_t4[q]], W=[r_t4[q]])
            P.op('pool', lambda e, q=q, f=f: e.tensor_tensor(Y[:, f, 0, :], t1[q], t2[q], ALU.add), R=[r_t1[q], r_t2[q]], W=[r_Y[f]])
            P.op('pool', lambda e, q=q, f=f: e.tensor_tensor(Y[:, f, 1, :], t3[q], t4[q], ALU.subtract), R=[r_t3[q], r_t4[q]], W=[r_Y[f]])
        for t in range(nj):
            q = t % 2
            q3 = t % 3
            P.dma('sp', cs[q3][0], Ccol[t], W=[r_cs[q3]])
            P.dma('sp', cs[q3][1], Scol[t], W=[r_cs[q3]])
            rows = slice(tok0 + t * 128, tok0 + (t + 1) * 128)
            if o == 0:
                P.dma('sp', uf[q3], self.vtok[0, rows, :], W=[r_uf[q3]])
            else:
                P.dma('sp', uf[q3], self.u2tok[rows, :], R=[r_u2[t]], W=[r_uf[q3]])
            P.dma('sp', xg[q3], self.vtok[1 + o, rows, :], W=[r_xg[q3]])
            ps = M.bank(4 + q)
            for ri in range(2):
                for j in range(nj):
                    P.op('pe', lambda e, ps=ps, q3=q3, ri=ri, j=j: e.matmul(ps, cs[q3][ri][:, j, :], Y[:, j, ri, :],
                                                                          start=(ri == 0 and j == 0), stop=(ri == 1 and j == nj - 1)),
                         R=[r_cs[q3], r_Y[j]], W=[rps[4 + q]])
            P.op('pool', lambda e, q=q, q3=q3, o=o: e.tensor_tensor(un[q], uf[q3], biasb[:, o, :], ALU.mult), R=[r_uf[q3], r_bias], W=[r_un[q]])
            P.op('dve', lambda e, q=q, ps=ps: e.tensor_tensor(un[q], un[q], ps, ALU.add), R=[r_un[q], rps[4 + q]], W=[r_un[q]])
            if o == 0:
                P.op('dve', lambda e, q=q, q3=q3: e.tensor_tensor(un[q], un[q], xg[q3], ALU.mult), R=[r_un[q], r_xg[q3]], W=[r_un[q]])
                P.dma('act', self.u2tok[rows, :], un[q], R=[r_un[q]], W=[r_u2[t]])
                P.op('act', lambda e, q=q, t=t: e.copy(u[:, t, :], un[q]), R=[r_un[q]], W=[r_u[t]])
            else:
                P.op('dve', lambda e, q=q, q3=q3: e.tensor_tensor(ub[q], un[q], xg[q3], ALU.mult), R=[r_un[q], r_xg[q3]], W=[r_ub[q]])
                pt = M.bank(6 + q, 512, BF16)
                for cc in range(4):
                    P.op('pe', lambda e, pt=pt, q=q, cc=cc: e.transpose(pt[:, cc * 128:(cc + 1) * 128], ub[q][:, cc * 128:(cc + 1) * 128], self.idbf),
                         R=[r_ub[q], self.r_idbf], W=[rps[6 + q]])
                P.op('act', lambda e, pt=pt, q=q: e.copy(ot[q], pt), R=[rps[6 + q]], W=[r_ot[q]])
                P.dma('act', self.mixT[512:1024, tok0 + t * 128:tok0 + (t + 1) * 128].rearrange('(a p) t -> p a t', p=128),
                      ot[q].rearrange('p (a t) -> p a t', a=4), R=[r_ot[q]])
    P.barrier()


Builder.phase_hy_filters = _phase_hy_filters
Builder.phase_hy_prep = _phase_hy_prep
Builder.phase_hy_conv = _phase_hy_conv


def hyena_consts():
    out = {}
    f32 = np.float32
    max_decay = math.log(1e-2) / 0.3
    min_decay = math.log(1e-2) / 1.5
    deltas = np.linspace(min_decay, max_decay, 512, dtype=f32)
    for n in (2048, 256):
        pos = np.arange(n, dtype=f32)
        t = np.linspace(0.0, 1.0, n, dtype=f32)[:, None]
        w = (f32(2.0 * math.pi) * pos / f32(n)).astype(f32)
        fr = np.linspace(1e-4, 15, 16, dtype=f32)
        ang = (w[:, None] * fr[None, :]).astype(f32)
        z = np.concatenate([t, np.cos(ang), -np.sin(ang)], axis=-1).astype(f32)
        out['zT%d' % n] = np.ascontiguousarray(z.T)
        dec = np.exp(-t * np.abs(deltas)[None, :]).astype(f32)
        out['dec%d' % n] = dec
        decb = dec.copy()
        decb[0] = 0.0
        out['decb%d' % n] = decb
        N = 2 * n - 1
        sk = (np.arange(n, dtype=np.int64)[:, None] * np.arange(n, dtype=np.int64)[None, :]) % N
        ang = 2.0 * np.pi * sk.astype(np.float64) / N
        nj = n // 128
        for nm, mat in (('C', np.cos(ang)), ('S', np.sin(ang))):
            m4 = mat.reshape(nj, 128, nj, 128).transpose(2, 1, 0, 3)
            out['%scol%d' % (nm, n)] = np.ascontiguousarray(m4).astype(ml_dtypes.bfloat16)
        wk = np.full(n, 2.0 / N, dtype=f32)
        wk[0] = 1.0 / N
        out['wk%d' % n] = np.ascontiguousarray(wk.reshape(nj, 128).T)
    return out


def _phase_out_proj(self, l, src_bf, w_dram, xsrc, xdst, blocks, prefetch=None):
    P, M = self.P, self.M
    M.reset()
    ntok = max(t0 + T for t0, T, c in blocks)
    mx = M.tile([8, ntok], BF16)
    wt = M.tile([8, D], BF16)
    r_mx, r_w = Res(), Res()
    for k in range(8):
        P.dma('sp', mx[:, k, :], src_bf[k * 128:(k + 1) * 128, 0:ntok], W=[r_mx])
        P.dma('pool', wt[:, k, :], w_dram[k * 128:(k + 1) * 128, :], W=[r_w])
    if prefetch is not None:
        self.prefetch_wup(prefetch)
    xb = [M.tile([512], F32) for _ in range(3)]
    ob = [M.tile([512], F32) for _ in range(3)]
    r_xb = [Res() for _ in range(3)]
    r_ob = [Res() for _ in range(3)]
    rps = [Res() for _ in range(4)]
    n = 0
    mod = self.mod[l]
    for dc in range(8):
        for (t0, T, c) in blocks:
            b = n % 4
            q = n % 3
            n += 1
            ps = M.bank(b)
            P.dma('sp', xb[q][:, 0:T], xsrc[dc * 128:(dc + 1) * 128, t0:t0 + T], W=[r_xb[q]])
            for k in range(8):
                P.op('pe', lambda e, ps=ps, k=k, dc=dc, t0=t0, T=T: e.matmul(
                    ps[:, 0:T], wt[:, k, dc * 128:(dc + 1) * 128], mx[:, k, t0:t0 + T], start=(k == 0), stop=(k == 7)),
                    R=[r_w, r_mx], W=[rps[b]])
            P.op('dve', lambda e, ps=ps, q=q, dc=dc, c=c, T=T: e.scalar_tensor_tensor(
                ob[q][:, 0:T], ps[:, 0:T], mod[:, 16 + dc, c:c + 1], xb[q][:, 0:T], ALU.mult, ALU.add),
                R=[rps[b], r_xb[q], self.r_mod[l]], W=[r_ob[q]])
            P.dma('act', xdst[dc * 128:(dc + 1) * 128, t0:t0 + T], ob[q][:, 0:T], R=[r_ob[q]])
    P.barrier()


def _phase_ffn(self, l, xsrc, xdst, segs, final_out=None):
    P, M = self.P, self.M
    M.reset()
    TB = 256
    NP = 22
    pre = getattr(self, 'wup', None)
    if pre is not None and pre[0] == l:
        wu, r_wu = pre[1], pre[2]
    else:
        wu = M.tile([8, 5632], BF16)
        r_wu = [Res() for _ in range(44)]
        for g in range(0, 5632, 2048):
            gw = min(2048, 5632 - g)
            for k in range(8):
                P.dma('pool', wu[:, k, g:g + gw], self.ffn_w_up[l, k * 128:(k + 1) * 128, g:g + gw],
                      W=[r_wu[j] for j in range(g // 128, (g + gw) // 128)])
    wd = M.tile([NP, D], BF16)
    r_wd = Res()
    for p in range(NP):
        P.dma('pool', wd[:, p, :], self.ffn_w_down[l, p * 128:(p + 1) * 128, :], W=[r_wd])
    cw = M.tile([44, 4], F32)
    r_cw = Res()
    P.dma('sp', cw, self.ffn_cw[l], W=[r_cw])
    NC = TB + 2
    xbs = [M.tile([8, NC], F32) for _ in range(2)]
    sq = M.tile([8, NC], BF16)
    h2 = M.tile([8, NC], BF16)
    rstd = M.tile([NC], F32)
    tmp = [M.tile([NC], F32) for _ in range(2)]
    mm = M.tile([NP, TB], BF16)
    cg = [M.tile([TB], F32) for _ in range(3)]
    cv = [M.tile([TB], F32) for _ in range(3)]
    xn = M.tile([8, TB], F32)
    r_xbs = [Res(), Res()]
    r_sq, r_h2, r_rs, r_xn = Res(), Res(), Res(), Res()
    r_tmp = [Res(), Res()]
    r_mm = [Res() for _ in range(NP)]
    r_cg = [Res(), Res(), Res()]
    r_cv = [Res(), Res(), Res()]
    rps = [Res() for _ in range(8)]
    A = self.A2[l]
    mod = self.mod[l]
    if final_out is not None:
        fg = M.tile([8], F32)
        r_fg = Res()
        P.dma('sp', fg, self.final_g.rearrange('(k p) -> p k', p=128), W=[r_fg], allow_slow_non_contiguous=True)
        sq2, r_sq2 = sq, r_sq
        fo, r_fo = xn, r_xn
    RM = [self.r_mod[l]]
    blist = []
    for (s0, s1, c) in segs:
        for t0 in range(s0, s1, TB):
            T = min(TB, s1 - t0)
            lo = max(s0, t0 - 1)
            hi = min(s1, t0 + T + 1)
            blist.append((s0, s1, c, t0, T, lo, hi))

    def load_x(i):
        s0, s1, c, t0, T, lo, hi = blist[i]
        P.dma('sp', xbs[i % 2][:, :, 0:hi - lo], xsrc[:, lo:hi].rearrange('(k p) t -> p k t', p=128), W=[r_xbs[i % 2]])

    load_x(0)
    for nblk, (s0, s1, c, t0, T, lo, hi) in enumerate(blist):
        if True:
            nc_ = hi - lo
            c0 = t0 - lo
            xb, r_xb = xbs[nblk % 2], r_xbs[nblk % 2]
            P.op('act', lambda e, nc_=nc_, xb=xb: e.activation(sq[:, :, 0:nc_], xb[:, :, 0:nc_], AF.Square), R=[r_xb], W=[r_sq])
            psn = M.bank(6)
            for k in range(8):
                P.op('pe', lambda e, k=k, nc_=nc_, psn=psn: e.matmul(psn[:, 0:nc_], self.ones_bf, sq[:, k, 0:nc_], start=(k == 0), stop=(k == 7)),
                     R=[r_sq, self.r_ones], W=[rps[6]])
            P.op('act', lambda e, nc_=nc_, psn=psn: e.activation(rstd[:, 0:nc_], psn[:, 0:nc_], AF.Sqrt, bias=EPS, scale=1.0 / D),
                 R=[rps[6]], W=[r_rs])
            P.op('dve', lambda e, nc_=nc_: e.reciprocal(rstd[:, 0:nc_], rstd[:, 0:nc_]), R=[r_rs], W=[r_rs])
            for k in range(8):
                q = k % 2
                P.op('pool', lambda e, k=k, q=q, nc_=nc_, xb=xb: e.tensor_tensor(
                    tmp[q][:, 0:nc_], xb[:, k, 0:nc_], rstd[:, 0:nc_], ALU.mult), R=[r_xb, r_rs], W=[r_tmp[q]])
                P.op('act', lambda e, k=k, q=q, nc_=nc_, c=c: e.activation(
                    h2[:, k, 0:nc_], tmp[q][:, 0:nc_], AF.Identity, bias=mod[:, 24 + k, c:c + 1], scale=A[:, k, c:c + 1]),
                    R=[r_tmp[q]] + RM, W=[r_h2])
            if nblk + 1 < len(blist):
                load_x(nblk + 1)
            a = 1 if t0 == s0 else 0
            bnd = 1 if t0 + T == s1 else 0

            def st_pe(p):
                q = p % 3
                for ch, bk in ((p, 2 * q), (NP + p, 2 * q + 1)):
                    ps = M.bank(bk)
                    for k in range(8):
                        P.op('pe', lambda e, ps=ps, k=k, ch=ch, nc_=nc_: e.matmul(
                            ps[:, 0:nc_], wu[:, k, ch * 128:(ch + 1) * 128], h2[:, k, 0:nc_], start=(k == 0), stop=(k == 7)),
                            R=[r_wu[ch], r_h2], W=[rps[bk]])

            def st_conv(p):
                q = p % 3
                pairs = ((p, M.bank(2 * q), 2 * q, cg[q], r_cg[q]), (NP + p, M.bank(2 * q + 1), 2 * q + 1, cv[q], r_cv[q]))
                for ch, ps, bk, dst, rd in pairs:
                    P.op('act', lambda e, ps=ps, ch=ch, dst=dst, T=T, c0=c0: e.activation(
                        dst[:, 0:T], ps[:, c0:c0 + T], AF.Identity, bias=cw[:, ch, 3:4], scale=cw[:, ch, 1:2]),
                        R=[rps[bk], r_cw], W=[rd])
                for ch, ps, bk, dst, rd in pairs:
                    P.op('dve', lambda e, ps=ps, ch=ch, dst=dst, T=T, c0=c0, a=a: e.scalar_tensor_tensor(
                        dst[:, a:T], ps[:, c0 - 1 + a:c0 - 1 + T], cw[:, ch, 0:1], dst[:, a:T], ALU.mult, ALU.add),
                        R=[rps[bk], r_cw, rd], W=[rd])
                for ch, ps, bk, dst, rd in pairs:
                    P.op('dve', lambda e, ps=ps, ch=ch, dst=dst, T=T, c0=c0, bnd=bnd: e.scalar_tensor_tensor(
                        dst[:, 0:T - bnd], ps[:, c0 + 1:c0 + 1 + T - bnd], cw[:, ch, 2:3], dst[:, 0:T - bnd], ALU.mult, ALU.add),
                        R=[rps[bk], r_cw, rd], W=[rd])

            def st_gate(p):
                q = p % 3
                P.op('act', lambda e, q=q, T=T: e.activation(cg[q][:, 0:T], cg[q][:, 0:T], AF.Silu), R=[r_cg[q]], W=[r_cg[q]])
                P.op('pool', lambda e, q=q, p=p, T=T: e.tensor_tensor(mm[:, p, 0:T], cg[q][:, 0:T], cv[q][:, 0:T], ALU.mult),
                     R=[r_cg[q], r_cv[q]], W=[r_mm[p]])

            for i in range(NP + 2):
                if i < NP:
                    st_pe(i)
                if 1 <= i <= NP:
                    st_conv(i - 1)
                if 2 <= i <= NP + 1:
                    st_gate(i - 2)
            for dc in range(8):
                bk = 6 + dc % 2
                ps = M.bank(bk)
                for p in range(NP):
                    P.op('pe', lambda e, ps=ps, p=p, dc=dc, T=T: e.matmul(
                        ps[:, 0:T], wd[:, p, dc * 128:(dc + 1) * 128], mm[:, p, 0:T], start=(p == 0), stop=(p == NP - 1)),
                        R=[r_wd, r_mm[p]], W=[rps[bk]])
                P.op('dve', lambda e, ps=ps, dc=dc, T=T, c0=c0, c=c, xb=xb: e.scalar_tensor_tensor(
                    xn[:, dc, 0:T], ps[:, 0:T], mod[:, 40 + dc, c:c + 1], xb[:, dc, c0:c0 + T], ALU.mult, ALU.add),
                    R=[rps[bk], r_xb] + RM, W=[r_xn])
            if final_out is None:
                P.dma('sp', xdst[:, t0:t0 + T].rearrange('(k p) t -> p k t', p=128), xn[:, :, 0:T], R=[r_xn])
            else:
                P.op('act', lambda e, T=T: e.activation(sq2[:, :, 0:T], xn[:, :, 0:T], AF.Square), R=[r_xn], W=[r_sq2])
                psn = M.bank(7)
                for k in range(8):
                    P.op('pe', lambda e, k=k, T=T, psn=psn: e.matmul(psn[:, 0:T], self.ones_bf, sq2[:, k, 0:T], start=(k == 0), stop=(k == 7)),
                         R=[r_sq2, self.r_ones], W=[rps[7]])
                P.op('act', lambda e, T=T, psn=psn: e.activation(rstd[:, 0:T], psn[:, 0:T], AF.Sqrt, bias=EPS, scale=1.0 / D),
                     R=[rps[7], r_rs], W=[r_rs])
                P.op('dve', lambda e, T=T: e.reciprocal(rstd[:, 0:T], rstd[:, 0:T]), R=[r_rs], W=[r_rs])
                for k in range(8):
                    P.op('dve', lambda e, k=k, T=T: e.scalar_tensor_tensor(
                        fo[:, k, 0:T], xn[:, k, 0:T], fg[:, k:k + 1], rstd[:, 0:T], ALU.mult, ALU.mult),
                        R=[r_xn, r_rs, r_fg], W=[r_fo])
                P.dma('sp', final_out[:, t0:t0 + T].rearrange('(k p) t -> p k t', p=128), fo[:, :, 0:T], R=[r_fo])
    P.barrier()
    if pre is not None and pre[0] == l:
        M.release_top()
        self.wup = None


def _prefetch_wup(self, l):
    P, M = self.P, self.M
    flat = M.reserve_top(8 * 5632 * 2)
    wu = flat.rearrange('p (k f) -> p k f', k=8)
    r_wu = [Res() for _ in range(44)]
    for g in range(0, 5632, 2048):
        gw = min(2048, 5632 - g)
        for k in range(8):
            P.dma('pool', wu[:, k, g:g + gw], self.ffn_w_up[l, k * 128:(k + 1) * 128, g:g + gw],
                  W=[r_wu[j] for j in range(g // 128, (g + gw) // 128)], bg=True)
    self.wup = (l, wu, r_wu)


Builder.prefetch_wup = _prefetch_wup
Builder.phase_out_proj = _phase_out_proj
Builder.phase_ffn = _phase_ffn


def _rms_feat(self, zt, nk, T, g, dst, rz, rg_, rdst, tmp, r_tmp, sq, r_sq, rstd, r_rs, psb, rps):
    P = self.P
    P.op('act', lambda e: e.activation(sq[:, 0:nk, 0:T], zt[:, :, 0:T], AF.Square), R=[rz], W=[r_sq])
    for k in range(nk):
        P.op('pe', lambda e, k=k: e.matmul(psb[:, 0:T], self.ones_bf, sq[:, k, 0:T], start=(k == 0), stop=(k == nk - 1)),
             R=[r_sq, self.r_ones], W=[rps])
    P.op('act', lambda e: e.activation(rstd[:, 0:T], psb[:, 0:T], AF.Sqrt, bias=EPS, scale=1.0 / (nk * 128)), R=[rps], W=[r_rs])
    P.op('dve', lambda e: e.reciprocal(rstd[:, 0:T], rstd[:, 0:T]), R=[r_rs], W=[r_rs])
    for k in range(nk):
        P.op('dve', lambda e, k=k: e.scalar_tensor_tensor(dst[:, k, :], zt[:, k, 0:T], g[:, k:k + 1], rstd[:, 0:T], ALU.mult, ALU.mult),
             R=[rz, r_rs, rg_], W=[rdst])


def _phase_mla_prep(self):
    P, M = self.P, self.M
    M.reset()
    z1 = self.z1
    wq = M.tile([4, 1536], BF16)
    wkv = M.tile([2, 2048], BF16)
    r_w = Res()
    for k in range(4):
        P.dma('pool', wq[:, k, :], self.od_w_uq[k * 128:(k + 1) * 128, :], W=[r_w])
    for k in range(2):
        for g in range(2):
            P.dma('pool', wkv[:, k, g * 1024:(g + 1) * 1024], self.od_w_ukv[k * 128:(k + 1) * 128, g * 1024:(g + 1) * 1024], W=[r_w])
    gq = M.tile([4], F32)
    gkv = M.tile([2], F32)
    rotm = M.tile([64], F32)
    cos2 = M.tile([S], F32)
    sin2 = M.tile([S], F32)
    r_c = Res()
    P.dma('sp', gq, self.od_q_norm_g.rearrange('(k p) -> p k', p=128), W=[r_c], allow_slow_non_contiguous=True)
    P.dma('sp', gkv, self.od_kv_norm_g.rearrange('(k p) -> p k', p=128), W=[r_c], allow_slow_non_contiguous=True)
    P.dma('sp', rotm[0:64, :], self.rotm, W=[r_c])
    P.dma('sp', cos2[0:64, :], self.cos2, W=[r_c])
    P.dma('sp', sin2[0:64, :], self.sin2, W=[r_c])
    qn = M.tile([4, S], BF16)
    ckv = M.tile([2, NT], BF16)
    r_qn = [Res() for _ in range(4)]
    r_ckv = [Res() for _ in range(5)]
    zt = [M.tile([4, 512], F32) for _ in range(2)]
    r_zt = [Res(), Res()]
    sq = M.tile([4, 512], BF16)
    rstd = M.tile([512], F32)
    tmp = None
    r_sq, r_rs = Res(), Res()
    rps = [Res() for _ in range(8)]
    blocks = token_blocks()
    for i, (t0, T, c) in enumerate(blocks[:4]):
        q = i % 2
        P.dma('sp', zt[q][:, :, 0:T], z1[0:512, t0:t0 + T].rearrange('(k p) t -> p k t', p=128), W=[r_zt[q]])
        self.rms_feat(zt[q], 4, T, gq, qn[:, :, t0:t0 + T], r_zt[q], r_c, r_qn[i], None, None, sq, r_sq, rstd, r_rs, M.bank(7), rps[7])
    for i, (t0, T, c) in enumerate(blocks):
        q = i % 2
        P.dma('sp', zt[q][:, 0:2, 0:T], z1[512:768, t0:t0 + T].rearrange('(k p) t -> p k t', p=128), W=[r_zt[q]])
        self.rms_feat(zt[q][:, 0:2, :], 2, T, gkv, ckv[:, :, t0:t0 + T], r_zt[q], r_c, r_ckv[i], None, None, sq, r_sq, rstd, r_rs, M.bank(7), rps[7])
    stb = [M.tile([512], BF16) for _ in range(4)]
    r_stb = [Res() for _ in range(4)]
    xr = [M.tile([512], F32) for _ in range(2)]
    r_xr = [Res(), Res()]
    tt_ = [M.tile([512], F32) for _ in range(2)]
    r_tt = [Res(), Res()]
    n = 0

    def rope_out(src_sb, r_src, t0, T, dst_dram, latent, q):
        nonlocal n
        b = n % 4
        n += 1
        if latent:
            pr = M.bank(4 + q)
            P.op('pe', lambda e: e.matmul(pr[0:64, 0:T], rotm[0:64, :], src_sb[0:64, 0:T], start=True, stop=True),
                 R=[r_src, r_c], W=[rps[4 + q]])
            P.op('dve', lambda e: e.tensor_tensor(tt_[q][0:64, 0:T], pr[0:64, 0:T], sin2[0:64, t0:t0 + T], ALU.mult),
                 R=[rps[4 + q], r_c], W=[r_tt[q]])
            P.op('dve', lambda e: e.tensor_tensor(src_sb[0:64, 0:T], src_sb[0:64, 0:T], cos2[0:64, t0:t0 + T], ALU.mult),
                 R=[r_src, r_c], W=[r_src])
            P.op('dve', lambda e: e.tensor_tensor(stb[b][0:64, 0:T], src_sb[0:64, 0:T], tt_[q][0:64, 0:T], ALU.add),
                 R=[r_src, r_tt[q]], W=[r_stb[b]])
        else:
            P.op('dve', lambda e: e.tensor_copy(stb[b][0:64, 0:T], src_sb[0:64, 0:T]), R=[r_src], W=[r_stb[b]])
        P.dma('act', dst_dram, stb[b][0:64, 0:T], R=[r_stb[b]])

    for h in range(8):
        for i, (t0, T, c) in enumerate(blocks[:4]):
            b = n % 4
            n += 1
            ps = M.bank(b)
            for k in range(4):
                P.op('pe', lambda e, ps=ps, k=k, h=h, t0=t0, T=T: e.matmul(
                    ps[:, 0:T], wq[:, k, h * 192:h * 192 + 128], qn[:, k, t0:t0 + T], start=(k == 0), stop=(k == 3)),
                    R=[r_w, r_qn[i]], W=[rps[b]])
            P.op('act', lambda e, ps=ps, b=b, T=T: e.copy(stb[b][:, 0:T], ps[:, 0:T]), R=[rps[b]], W=[r_stb[b]])
            P.dma('act', self.qT[h, 0:128, t0:t0 + T], stb[b][:, 0:T], R=[r_stb[b]])
            q = i % 2
            b2 = n % 4
            n += 1
            ps2 = M.bank(b2)
            for k in range(4):
                P.op('pe', lambda e, ps2=ps2, k=k, h=h, t0=t0, T=T: e.matmul(
                    ps2[0:64, 0:T], wq[:, k, h * 192 + 128:h * 192 + 192], qn[:, k, t0:t0 + T], start=(k == 0), stop=(k == 3)),
                    R=[r_w, r_qn[i]], W=[rps[b2]])
            P.op('act', lambda e, ps2=ps2, q=q, T=T: e.copy(xr[q][0:64, 0:T], ps2[0:64, 0:T]), R=[rps[b2]], W=[r_xr[q]])
            rope_out(xr[q], r_xr[q], t0, T, self.qT[h, 128:192, t0:t0 + T], True, q)
    for h in range(8):
        for i, (t0, T, c) in enumerate(blocks):
            b = n % 4
            n += 1
            ps = M.bank(b)
            for k in range(2):
                P.op('pe', lambda e, ps=ps, k=k, h=h, t0=t0, T=T: e.matmul(
                    ps[:, 0:T], wkv[:, k, h * 128:(h + 1) * 128], ckv[:, k, t0:t0 + T], start=(k == 0), stop=(k == 1)),
                    R=[r_w, r_ckv[i]], W=[rps[b]])
            P.op('act', lambda e, ps=ps, b=b, T=T: e.copy(stb[b][:, 0:T], ps[:, 0:T]), R=[rps[b]], W=[r_stb[b]])
            P.dma('act', self.kT[h, :, t0:t0 + T], stb[b][:, 0:T], R=[r_stb[b]])
    for tt in range(NT // 128):
        i = min(tt // 4, 4)
        for g in range(2):
            b = n % 4
            n += 1
            ps = M.bank(b)
            for k in range(2):
                P.op('pe', lambda e, ps=ps, k=k, g=g, tt=tt: e.matmul(
                    ps, ckv[:, k, tt * 128:(tt + 1) * 128], wkv[:, k, 1024 + g * 512:1024 + (g + 1) * 512], start=(k == 0), stop=(k == 1)),
                    R=[r_w, r_ckv[i]], W=[rps[b]])
            P.op('dve', lambda e, ps=ps, b=b: e.tensor_copy(stb[b], ps), R=[rps[b]], W=[r_stb[b]])
            P.dma('act', self.Vtok[tt * 128:(tt + 1) * 128, g * 512:(g + 1) * 512], stb[b], R=[r_stb[b]])
    for i, (t0, T, c) in enumerate(blocks):
        q = i % 2
        P.dma('sp', xr[q][0:64, 0:T], z1[768:832, t0:t0 + T], W=[r_xr[q]])
        rope_out(xr[q], r_xr[q], t0, T, self.krT[:, t0:t0 + T], c == 0, q)
    P.barrier()


def _phase_attn(self):
    P, M = self.P, self.M
    M.reset()
    SC = 192.0 ** -0.5
    V = M.tile([18, 1024], BF16)
    kra = M.tile([NT], BF16)
    colsel = M.tile([128], BF16)
    r_v, r_kra, r_cs = Res(), Res(), Res()
    for tt in range(18):
        P.dma('sp', V[:, tt, :], self.Vtok[tt * 128:(tt + 1) * 128, :], W=[r_v])
    P.op('dve', lambda e: e.memset(kra[64:128, :], 0.0), W=[r_kra])
    P.op('dve', lambda e: e.memset(kra[64:65, :], 1.0), W=[r_kra])
    P.dma('sp', kra[0:64, :], self.krT, W=[r_kra])
    P.op('dve', lambda e: e.memset(colsel, 0.0), W=[r_cs])
    P.op('dve', lambda e: e.memset(colsel[:, 64:65], 1.0), W=[r_cs])
    qn = [M.tile([S], BF16) for _ in range(2)]
    qrz = [M.tile([S], BF16) for _ in range(2)]
    kn = [M.tile([NT], BF16) for _ in range(2)]
    r_hd = [Res(), Res()]
    for i in range(2):
        P.op('dve', lambda e, i=i: e.memset(qrz[i][64:128, :], 0.0), W=[r_hd[i]])
    NB = 3
    qra = [M.tile([512], BF16) for _ in range(NB)]
    r_qra = [Res() for _ in range(NB)]
    mx = [M.tile([8], F32) for _ in range(4)]
    r_mx = [Res() for _ in range(4)]
    dg = [M.tile([128], BF16) for _ in range(2)]
    r_dg = [Res(), Res()]
    PTt = [M.tile([512], BF16) for _ in range(4)]
    r_PT = [Res() for _ in range(4)]
    rs = [M.tile([512], F32) for _ in range(2)]
    r_rs = [Res(), Res()]
    ob = [M.tile([512], BF16) for _ in range(2)]
    r_ob = [Res(), Res()]
    rps = [Res() for _ in range(8)]
    kblocks = token_blocks()
    cnt = {'sc': 0, 'mx': 0, 'dg': 0, 'p2': 0, 'pt': 0}
    if getattr(self, 'attn_hook', None):
        self.attn_hook()

    def load_head(h):
        i = h % 2
        P.dma('sp', qn[i], self.qT[h, 0:128, :], W=[r_hd[i]])
        P.dma('sp', qrz[i][0:64, :], self.qT[h, 128:192, :], W=[r_hd[i]])
        P.dma('sp', kn[i], self.kT[h], W=[r_hd[i]])

    def pass1(h, qb, bi):
        i = h % 2
        u = bi % NB
        P.dma('sp', qra[u][0:64, :], self.qT[h, 128:192, qb * 512:(qb + 1) * 512], W=[r_qra[u]])
        for qt in range(4):
            qs = slice(qb * 512 + qt * 128, qb * 512 + (qt + 1) * 128)
            m_ = cnt['mx'] % 4
            cnt['mx'] += 1
            for j, (k0, T, c) in enumerate(kblocks):
                b = cnt['sc'] % 3
                cnt['sc'] += 1
                ps = M.bank(b)
                P.op('pe', lambda e, ps=ps, i=i, qs=qs, k0=k0, T=T: e.matmul(ps[:, 0:T], qn[i][:, qs], kn[i][:, k0:k0 + T], start=True, stop=False),
                     R=[r_hd[i]], W=[rps[b]])
                P.op('pe', lambda e, ps=ps, i=i, qs=qs, k0=k0, T=T: e.matmul(ps[:, 0:T], qrz[i][:, qs], kra[:, k0:k0 + T], start=False, stop=True),
                     R=[r_hd[i], r_kra], W=[rps[b]])
                P.op('dve', lambda e, ps=ps, m_=m_, j=j, T=T: e.tensor_reduce(mx[m_][:, j:j + 1], ps[:, 0:T], AX.X, ALU.max),
                     R=[rps[b]], W=[r_mx[m_]])
                yield
            P.op('dve', lambda e, m_=m_: e.tensor_reduce(mx[m_][:, 5:6], mx[m_][:, 0:5], AX.X, ALU.max, negate=True),
                 R=[r_mx[m_]], W=[r_mx[m_]])
            g = cnt['dg'] % 2
            cnt['dg'] += 1
            P.op('dve', lambda e, g=g, m_=m_: e.tensor_scalar(dg[g], self.idbf, mx[m_][:, 5:6], None, ALU.mult),
                 R=[r_mx[m_], self.r_idbf], W=[r_dg[g]])
            P.op('pe', lambda e, g=g, qt=qt: e.matmul(M.bank(3)[:, qt * 128:(qt + 1) * 128], colsel, dg[g], start=True, stop=True),
                 R=[r_cs, r_dg[g]], W=[rps[3]])
        P.op('act', lambda e, u=u: e.copy(qra[u][64:128, :], M.bank(3)[64:128, :]), R=[rps[3]], W=[r_qra[u]])

    def pass2(h, qb, bi):
        i = h % 2
        u = bi % NB
        qcols = slice(qb * 512, (qb + 1) * 512)
        def emit_s(kt):
            b = 4 + kt % 2
            ps = M.bank(b)
            ks = slice(kt * 128, (kt + 1) * 128)
            P.op('pe', lambda e, ps=ps, i=i, ks=ks, qcols=qcols: e.matmul(ps, kn[i][:, ks], qn[i][:, qcols], start=True, stop=False),
                 R=[r_hd[i]], W=[rps[b]])
            P.op('pe', lambda e, ps=ps, ks=ks, u=u: e.matmul(ps, kra[:, ks], qra[u], start=False, stop=True),
                 R=[r_kra, r_qra[u]], W=[rps[b]])

        emit_s(0)
        for kt in range(18):
            if kt + 1 < 18:
                emit_s(kt + 1)
            yield
            b = 4 + kt % 2
            ps = M.bank(b)
            t = cnt['pt'] % 4
            cnt['pt'] += 1
            P.op('act', lambda e, ps=ps, t=t: e.activation(PTt[t], ps, AF.Exp, scale=SC), R=[rps[b]], W=[r_PT[t]])
            P.op('pe', lambda e, t=t, kt=kt: e.matmul(M.bank(6), self.ones_bf, PTt[t], start=(kt == 0), stop=(kt == 17)),
                 R=[r_PT[t], self.r_ones], W=[rps[6]])
            P.op('pe', lambda e, t=t, kt=kt, h=h: e.matmul(M.bank(7), V[:, kt, h * 128:(h + 1) * 128], PTt[t], start=(kt == 0), stop=(kt == 17)),
                 R=[r_PT[t], r_v], W=[rps[7]])
        o = bi % 2
        P.op('dve', lambda e, o=o: e.reciprocal(rs[o], M.bank(6)), R=[rps[6]], W=[r_rs[o]])
        P.op('dve', lambda e, o=o: e.tensor_tensor(ob[o], M.bank(7), rs[o], ALU.mult), R=[rps[7], r_rs[o]], W=[r_ob[o]])
        P.dma('sp', self.oT[h * 128:(h + 1) * 128, qcols], ob[o], R=[r_ob[o]])

    blocks = [(h, qb) for h in range(8) for qb in range(4)]
    def run_both(g2, g1):
        a, b_ = True, True
        while a or b_:
            if a:
                try:
                    next(g2)
                except StopIteration:
                    a = False
            if b_:
                try:
                    next(g1)
                except StopIteration:
                    b_ = False

    load_head(0)
    for _ in pass1(0, 0, 0):
        pass
    for bi, (h, qb) in enumerate(blocks):
        g2 = pass2(h, qb, bi)
        if bi + 1 < len(blocks):
            h1, qb1 = blocks[bi + 1]
            if qb1 == 0:
                pass
            g1 = pass1(h1, qb1, bi + 1)
        else:
            g1 = iter(())
        run_both(g2, g1)
        if qb == 0 and h + 1 < 8:
            load_head(h + 1)
    P.barrier()


Builder.rms_feat = _rms_feat
Builder.phase_mla_prep = _phase_mla_prep
Builder.phase_attn = _phase_attn


def mla_consts():
    f32 = np.float32
    n = S
    row = np.repeat(np.arange(n // 64, dtype=f32), 64)
    col = np.tile(np.arange(64, dtype=f32), n // 64)
    inv = (f32(10000.0) ** (-np.arange(16, dtype=f32) / f32(16))).astype(f32)
    ang = np.concatenate([row[:, None] * inv[None, :], col[:, None] * inv[None, :]], axis=-1).astype(f32)
    cos2 = np.repeat(np.cos(ang).astype(f32), 2, axis=1).T
    sin2 = np.repeat(np.sin(ang).astype(f32), 2, axis=1).T
    rotm = np.zeros((64, 64), f32)
    for i in range(32):
        rotm[2 * i + 1, 2 * i] = -1.0
        rotm[2 * i, 2 * i + 1] = 1.0
    return {'cos2': np.ascontiguousarray(cos2), 'sin2': np.ascontiguousarray(sin2), 'rotm': rotm}

def make_inputs(inputs, b):
    x = np.asarray(inputs['x'][b], np.float32)
    ctx = np.asarray(inputs['ctx'][b], np.float32)
    m = {}
    m['xT'] = np.ascontiguousarray(np.concatenate([x.T, ctx.T], axis=1))
    m['cvec'] = np.ascontiguousarray(np.stack([inputs['c'][b], inputs['c_ctx']]).astype(np.float32))
    m['ada_w'] = np.ascontiguousarray(inputs['ada_w'], np.float32)
    m['ada_b'] = np.ascontiguousarray(inputs['ada_b'], np.float32)
    m['norm1_g'] = np.ascontiguousarray(inputs['norm1_g'], np.float32)
    m['norm2_g'] = np.ascontiguousarray(inputs['norm2_g'], np.float32)
    m['ev_w_in'] = np.ascontiguousarray(inputs['ev_w_in'][0], np.float32)
    m['ident'] = np.eye(128, dtype=np.float32)
    f = lambda k: np.ascontiguousarray(inputs[k][0], np.float32)
    m['rg_conv_w'] = f('ev_rg_conv_w'); m['rg_conv_b'] = f('ev_rg_conv_b')
    m['rg_wa'] = f('ev_rg_wa'); m['rg_ba'] = f('ev_rg_ba'); m['rg_wx'] = f('ev_rg_wx'); m['rg_bx'] = f('ev_rg_bx')
    m['rg_lambda'] = f('ev_rg_lambda')
    m['hy_conv_w'] = f('ev_hy_conv_w'); m['hy_conv_b'] = f('ev_hy_conv_b')
    m['f_w1'] = f('ev_hy_f_w1'); m['f_b1'] = f('ev_hy_f_b1'); m['f_freq'] = f('ev_hy_f_freq')
    m['f_w2'] = f('ev_hy_f_w2'); m['f_b2'] = f('ev_hy_f_b2'); m['f_w3'] = f('ev_hy_f_w3')
    m['hy_bias'] = f('ev_hy_bias')
    m.update(HC())
    m['ev_w_out'] = f('ev_w_out')
    wi = np.zeros((D, 896), np.float32)
    wi[:, :832] = inputs['od_w_in'][0]
    m['od_w_in'] = wi
    m['od_q_norm_g'] = f('od_q_norm_g'); m['od_kv_norm_g'] = f('od_kv_norm_g')
    m['od_w_uq'] = f('od_w_uq'); m['od_w_o'] = f('od_w_o')
    wkv = np.asarray(inputs['od_w_ukv'][0], np.float32).reshape(256, 8, 256)
    m['od_w_ukv'] = np.ascontiguousarray(np.concatenate([wkv[:, :, :128].reshape(256, 1024), wkv[:, :, 128:].reshape(256, 1024)], axis=1))
    m.update(MC())
    for k in ('ffn_w_up', 'ffn_conv_w', 'ffn_conv_b', 'ffn_w_down', 'final_g'):
        m[k] = np.ascontiguousarray(inputs[k], np.float32)
    cwb = np.concatenate([np.asarray(inputs['ffn_conv_w'], np.float32), np.asarray(inputs['ffn_conv_b'], np.float32)[:, None, :]], axis=1)
    m['ffn_cw'] = np.ascontiguousarray(cwb.reshape(2, 4, 44, 128).transpose(0, 3, 2, 1))
    return m


_HC = {}


def HC():
    if not _HC:
        _HC.update(hyena_consts())
    return _HC


_MC = {}


def MC():
    if not _MC:
        _MC.update(mla_consts())
    return _MC


def kernel(**inputs):
    inputs = {k: np.asarray(v) for k, v in inputs.items()}
    nb = inputs['x'].shape[0]
    B = Builder(dbg=False)
    nc = B.build()
    shared = make_inputs(inputs, 0)
    in_maps = []
    for b in range(nb):
        m = dict(shared)
        x = np.asarray(inputs['x'][b], np.float32)
        ctx = np.asarray(inputs['ctx'][b], np.float32)
        m['xT'] = np.ascontiguousarray(np.concatenate([x.T, ctx.T], axis=1))
        m['cvec'] = np.ascontiguousarray(np.stack([inputs['c'][b], inputs['c_ctx']]).astype(np.float32))
        in_maps.append({k: v for k, v in m.items() if k in B.din})
    res = run_bass_kernel_spmd(nc, in_maps, core_ids=list(range(nb)))
    out = np.stack([np.asarray(res.results[b]['outT'], np.float32).T for b in range(nb)], axis=0)
    return np.ascontiguousarray(out)
```
